# Optimizing a Trainium2 kernel written in Bass

```python
import math
import jax, jax.numpy as jnp
from jax import lax
import numpy as np

D_MODEL = 1024
BATCH = 4
SEQ = 4096
DEPTH = 2

CHUNK = 64
EXPAND = 2
D_MIX = EXPAND * D_MODEL
N_GROUPS = 4
G = D_MIX // N_GROUPS
HEAD_DIM = 64
N_HEADS = G // HEAD_DIM
CONV_W = 4
LORA_W = 64
LORA_A = 64
RWKV_SHIFT = 3 * G + LORA_W + LORA_A
LRU_C = 8.0
W_DECAY_SCALE = 0.606531
ROPE_THETA = 10000.0
NORM_EPS = 1e-6
GN_EPS = 1e-5
D_IN = (2 * G + 3 * G + 2 * N_HEADS) + (RWKV_SHIFT + G) + (2 * G) + (4 * G)

kernel_name = "hybrid_parallel_mlstm_rwkv7_rglru_retention"


def rms_norm(x, g):
    xf = x.astype(jnp.float32)
    y = xf * lax.rsqrt(jnp.mean(xf * xf, -1, keepdims=True) + NORM_EPS)
    return (y * g.astype(jnp.float32)).astype(x.dtype)


def heads(t):
    return t.reshape(*t.shape[:-1], N_HEADS, HEAD_DIM)


def head_layer_norm(x, scale):
    xf = x.astype(jnp.float32)
    mu = jnp.mean(xf, -1, keepdims=True)
    var = jnp.mean(jnp.square(xf - mu), -1, keepdims=True)
    y = ((xf - mu) * lax.rsqrt(var + GN_EPS)).reshape(*x.shape[:-2], -1)
    return y * scale.astype(jnp.float32)


def causal_dwconv(x, w, b):
    c = x.shape[-1]
    y = lax.conv_general_dilated(
        x, w[:, None, :].astype(x.dtype), window_strides=(1,), padding=[(CONV_W - 1, 0)],
        dimension_numbers=("NWC", "WIO", "NWC"), feature_group_count=c)
    return y + b.astype(x.dtype)


def token_shift(x):
    return jnp.pad(x, ((0, 0), (1, 0), (0, 0)))[:, :-1]


def rope(x, cos, sin):
    half = HEAD_DIM // 2
    x1, x2 = x[..., :half], x[..., half:]
    return jnp.concatenate([x1 * cos - x2 * sin, x2 * cos + x1 * sin], -1)


def to_chunks(t):
    b, s, h, d = t.shape
    return t.reshape(b, s // CHUNK, CHUNK, h, d).transpose(1, 0, 3, 2, 4)


def from_chunks(t):
    nc, b, h, l, d = t.shape
    return t.transpose(1, 0, 3, 2, 4).reshape(b, nc * l, h, d)


def gate_chunks(t):
    b, s, h = t.shape
    return t.reshape(b, s // CHUNK, CHUNK, h).transpose(1, 0, 3, 2)


def mlstm_chunkwise(q, k, v, i_pre, f_pre):
    b, s, h, dh = q.shape
    f32 = jnp.float32
    qc = to_chunks(q.astype(f32))
    kc = to_chunks(k.astype(f32)) * (dh ** -0.5)
    vc = to_chunks(v.astype(f32))
    ic = gate_chunks(i_pre.astype(f32))
    lfc = gate_chunks(jax.nn.log_sigmoid(f_pre.astype(f32)))
    causal = jnp.tril(jnp.ones((CHUNK, CHUNK), bool))

    def step(carry, inp):
        c_st, n_st, m_st = carry
        qb, kb, vb, ib, lfb = inp
        cum = jnp.cumsum(lfb, -1)
        log_d = jnp.where(causal, cum[..., :, None] - cum[..., None, :] + ib[..., None, :], -jnp.inf)
        inter = cum + m_st[..., None]
        m_t = jnp.maximum(inter, jnp.max(log_d, -1))
        scores = jnp.einsum("bhld,bhsd->bhls", qb, kb) * jnp.exp(log_d - m_t[..., None])
        s_inter = jnp.exp(inter - m_t)
        num = jnp.einsum("bhls,bhsd->bhld", scores, vb) + s_inter[..., None] * jnp.einsum("bhld,bhed->bhle", qb, c_st)
        den = jnp.sum(scores, -1) + s_inter * jnp.einsum("bhld,bhd->bhl", qb, n_st)
        h_out = num / jnp.maximum(jnp.abs(den), jnp.exp(-m_t))[..., None]
        tot = cum[..., -1]
        log_w = tot[..., None] - cum + ib
        m_new = jnp.maximum(tot + m_st, jnp.max(log_w, -1))
        wj = jnp.exp(log_w - m_new[..., None])
        sc = jnp.exp(tot + m_st - m_new)
        c_new = sc[..., None, None] * c_st + jnp.einsum("bhl,bhle,bhld->bhed", wj, vb, kb)
        n_new = sc[..., None] * n_st + jnp.einsum("bhl,bhld->bhd", wj, kb)
        return (c_new, n_new, m_new), h_out

    init = (jnp.zeros((b, h, dh, dh), f32), jnp.zeros((b, h, dh), f32), jnp.zeros((b, h), f32))
    _, hc = lax.scan(step, init, (qc, kc, vc, ic, lfc))
    return from_chunks(hc)


def rwkv7_scan(r, w, k_t, v, kappa_hat, a):
    b, s, h, dh = r.shape

    def step(st, inp):
        r_t, w_t, k_tt, v_t, kh_t, a_t = inp
        sk = jnp.einsum("bhvk,bhk->bhv", st, kh_t)
        st = st * w_t[:, :, None, :] - sk[..., None] * (a_t * kh_t)[:, :, None, :] + v_t[..., None] * k_tt[:, :, None, :]
        return st, jnp.einsum("bhvk,bhk->bhv", st, r_t)

    xs = tuple(jnp.moveaxis(t, 1, 0) for t in (r, w, k_t, v, kappa_hat, a))
    _, y = lax.scan(step, jnp.zeros((b, h, dh, dh), jnp.float32), xs)
    return jnp.moveaxis(y, 0, 1)


def rglru(x, w_r, b_r, w_i, b_i, lam):
    xf = x.astype(jnp.float32)
    xh = heads(xf)
    r = jax.nn.sigmoid(jnp.einsum("bshi,hij->bshj", xh, w_r.astype(jnp.float32)).reshape(xf.shape) + b_r)
    i = jax.nn.sigmoid(jnp.einsum("bshi,hij->bshj", xh, w_i.astype(jnp.float32)).reshape(xf.shape) + b_i)
    log_a = -LRU_C * r * jax.nn.softplus(-lam.astype(jnp.float32))
    a = jnp.exp(log_a)
    u = jnp.sqrt(-jnp.expm1(2.0 * log_a)) * (i * xf)

    def combine(left, right):
        a1, b1 = left
        a2, b2 = right
        return a1 * a2, a2 * b1 + b2

    _, h = lax.associative_scan(combine, (a, u), axis=1)
    return h


def retention_chunkwise(q, k, v):
    b, s, h, dh = q.shape
    f32 = jnp.float32
    log_g = jnp.log1p(-jnp.exp2(-5.0 - jnp.arange(N_HEADS, dtype=f32)))
    idx = jnp.arange(CHUNK, dtype=f32)
    d_mat = jnp.exp(log_g[:, None, None] * jnp.abs(idx[:, None] - idx[None, :]))
    xi = jnp.exp(log_g[:, None] * (idx + 1.0))
    zeta = jnp.exp(log_g[:, None] * (CHUNK - 1.0 - idx))
    g_chunk = jnp.exp(log_g * CHUNK)

    def step(r_st, inp):
        qb, kb, vb = inp
        intra = jnp.einsum("bhld,bhsd->bhls", qb, kb) * d_mat
        o = jnp.einsum("bhls,bhse->bhle", intra, vb) + xi[..., None] * jnp.einsum("bhld,bhde->bhle", qb, r_st)
        r_st = g_chunk[:, None, None] * r_st + jnp.einsum("bhsd,bhse->bhde", kb * zeta[..., None], vb)
        return r_st, o

    xs = (to_chunks(q.astype(f32)), to_chunks(k.astype(f32)), to_chunks(v.astype(f32)))
    _, oc = lax.scan(step, jnp.zeros((b, h, dh, dh), f32), xs)
    return from_chunks(oc)


def split_at(t, sizes):
    idx = [int(i) for i in np.cumsum(sizes)[:-1]]
    return jnp.split(t, idx, axis=-1)


def setup_inputs(seed: int = 0) -> dict:
    key = jax.random.key(seed)
    ks = jax.random.split(key, 26)
    n = lambda k, shape: jax.random.normal(k, shape, jnp.float32)
    u_lam = jax.random.uniform(ks[24], (DEPTH, G), jnp.float32, 0.9, 0.999)
    s_lam = u_lam ** (1.0 / LRU_C)
    f_bias = jnp.linspace(3.0, 6.0, N_HEADS, dtype=jnp.float32)[None, :] + 0.1 * n(ks[8], (DEPTH, N_HEADS))
    return {
        "x": n(ks[0], (BATCH, SEQ, D_MODEL)),
        "norm_pre": 1.0 + 0.05 * n(ks[1], (DEPTH, D_MODEL)),
        "norm_post": 1.0 + 0.05 * n(ks[2], (DEPTH, D_MODEL)),
        "w_in": n(ks[3], (DEPTH, D_MODEL, D_IN)) * D_MODEL ** -0.5,
        "w_out": n(ks[4], (DEPTH, D_MIX, D_MODEL)) * D_MIX ** -0.5,
        "mlstm_conv_w": n(ks[5], (DEPTH, CONV_W, 2 * G)) * CONV_W ** -0.5,
        "mlstm_conv_b": 0.01 * n(ks[6], (DEPTH, 2 * G)),
        "mlstm_i_bias": 0.1 * n(ks[7], (DEPTH, N_HEADS)),
        "mlstm_f_bias": f_bias,
        "mlstm_norm_w": 1.0 + 0.05 * n(ks[9], (DEPTH, G)),
        "rwkv_mu": jax.random.uniform(ks[10], (DEPTH, RWKV_SHIFT), jnp.float32),
        "rwkv_w_up": 0.1 * n(ks[11], (DEPTH, LORA_W, G)),
        "rwkv_w0": jax.random.uniform(ks[12], (DEPTH, G), jnp.float32, -4.0, 2.0),
        "rwkv_a_up": 0.1 * n(ks[13], (DEPTH, LORA_A, G)),
        "rwkv_a0": 0.1 * n(ks[14], (DEPTH, G)),
        "rwkv_k_k": 0.85 + 0.05 * n(ks[15], (DEPTH, G)),
        "rwkv_k_a": 1.0 + 0.05 * n(ks[16], (DEPTH, G)),
        "rwkv_r_k": 0.1 * n(ks[17], (DEPTH, G)),
        "rwkv_gn_w": 1.0 + 0.05 * n(ks[18], (DEPTH, G)),
        "rwkv_gn_b": 0.01 * n(ks[19], (DEPTH, G)),
        "lru_conv_w": n(ks[20], (DEPTH, CONV_W, G)) * CONV_W ** -0.5,
        "lru_conv_b": 0.01 * n(ks[21], (DEPTH, G)),
        "lru_w_r": n(ks[22], (DEPTH, N_HEADS, HEAD_DIM, HEAD_DIM)) * HEAD_DIM ** -0.5,
        "lru_b_r": 0.01 * n(ks[23], (DEPTH, G)),
        "lru_w_i": n(ks[25], (DEPTH, N_HEADS, HEAD_DIM, HEAD_DIM)) * HEAD_DIM ** -0.5,
        "lru_b_i": 0.01 * n(jax.random.fold_in(ks[23], 1), (DEPTH, G)),
        "lru_lambda": jnp.log(s_lam) - jnp.log1p(-s_lam),
        "ret_norm_w": 1.0 + 0.05 * n(jax.random.fold_in(ks[24], 1), (DEPTH, G)),
    }


def reference(x, norm_pre, norm_post, w_in, w_out, mlstm_conv_w, mlstm_conv_b, mlstm_i_bias,
              mlstm_f_bias, mlstm_norm_w, rwkv_mu, rwkv_w_up, rwkv_w0, rwkv_a_up, rwkv_a0,
              rwkv_k_k, rwkv_k_a, rwkv_r_k, rwkv_gn_w, rwkv_gn_b, lru_conv_w, lru_conv_b,
              lru_w_r, lru_b_r, lru_w_i, lru_b_i, lru_lambda, ret_norm_w):
    f32 = jnp.float32
    seq = x.shape[1]
    half = HEAD_DIM // 2
    pos = jnp.arange(seq, dtype=f32)
    inv_freq = ROPE_THETA ** (-jnp.arange(half, dtype=f32) / half)
    ang = pos[:, None] * inv_freq[None, :]
    cos = jnp.cos(ang)[:, None, :]
    sin = jnp.sin(ang)[:, None, :]
    silu = lambda t: jax.nn.silu(t.astype(f32))

    for l in range(DEPTH):
        h = rms_norm(x, norm_pre[l])
        p = h @ w_in[l]
        (a_qk, a_v, a_o, a_z, a_i, a_f, b_sh, b_z, c_x, c_z, d_q, d_k, d_v, d_z) = split_at(
            p, [2 * G, G, G, G, N_HEADS, N_HEADS, RWKV_SHIFT, G, G, G, G, G, G, G])

        qk = jax.nn.silu(causal_dwconv(a_qk, mlstm_conv_w[l], mlstm_conv_b[l]))
        q_a, k_a = jnp.split(qk, 2, axis=-1)
        h_a = mlstm_chunkwise(heads(q_a), heads(k_a), heads(a_v),
                              a_i + mlstm_i_bias[l], a_f + mlstm_f_bias[l])
        h_a = h_a * jax.nn.sigmoid(heads(a_o).astype(f32))
        y_a = head_layer_norm(h_a, mlstm_norm_w[l]) * silu(a_z)

        b_sh = b_sh + rwkv_mu[l] * (token_shift(b_sh) - b_sh)
        r_b, k_b, v_b, w_lo, a_lo = split_at(b_sh.astype(f32), [G, G, G, LORA_W, LORA_A])
        decay_logit = rwkv_w0[l] + jnp.tanh(w_lo) @ rwkv_w_up[l].astype(f32)
        w_b = jnp.exp(-W_DECAY_SCALE * jax.nn.sigmoid(decay_logit))
        a_b = jax.nn.sigmoid(rwkv_a0[l] + a_lo @ rwkv_a_up[l].astype(f32))
        kappa = heads(k_b * rwkv_k_k[l])
        kappa_hat = kappa * lax.rsqrt(jnp.sum(kappa * kappa, -1, keepdims=True) + 1e-12)
        k_til = k_b * (1.0 + (a_b - 1.0) * rwkv_k_a[l])
        wkv = rwkv7_scan(heads(r_b), heads(w_b), heads(k_til), heads(v_b), kappa_hat, heads(a_b))
        bonus = jnp.sum(heads(r_b * rwkv_r_k[l] * k_til), -1, keepdims=True) * heads(v_b)
        y_b = (head_layer_norm(wkv, rwkv_gn_w[l]) + rwkv_gn_b[l] + bonus.reshape(bonus.shape[:-2] + (G,))) * silu(b_z)

        xc = causal_dwconv(c_x, lru_conv_w[l], lru_conv_b[l])
        y_c = rglru(xc, lru_w_r[l], lru_b_r[l], lru_w_i[l], lru_b_i[l], lru_lambda[l]) * silu(c_z)

        q_d = rope(heads(d_q).astype(f32), cos, sin)
        k_d = rope(heads(d_k).astype(f32), cos, sin) * (HEAD_DIM ** -0.5)
        o_d = retention_chunkwise(q_d, k_d, heads(d_v))
        y_d = head_layer_norm(o_d, ret_norm_w[l]) * silu(d_z)

        y = jnp.concatenate([y_a, y_b, y_c, y_d], axis=-1).astype(x.dtype) @ w_out[l]
        x = x + rms_norm(y, norm_post[l])
    return x
```

```python
import contextlib
import numpy as np
import concourse.bass as bass
import concourse.mybir as mybir
from concourse.bass_utils import run_bass_kernel_spmd

F32 = mybir.dt.float32
AF = mybir.ActivationFunctionType
ALU = mybir.AluOpType
AX = mybir.AxisListType

D_MODEL = 1024
SEQ = 4096
BATCH = 4
DEPTH = 2
G = 512
D_IN = 7824
T = 256
NTB = T // 128
NCH = T // 64
NCORES = 4


class Buf:
    def __init__(self, name, tile, psum=False):
        self.name = name
        self.tile = tile
        self.psum = psum
        self.acc = []
        self.dma_sem = None
        self.dma_ops = []

    def __getitem__(self, idx):
        return Ref(self, self.tile[idx])


def _box(ap):
    pat = ap.ap
    pstep = pat[0][0]
    off = int(ap.offset)
    p0 = off // pstep if pstep else 0
    f0 = off % pstep if pstep else off
    ext = 0
    for st, cnt in pat[1:]:
        ext += abs(st) * (cnt - 1)
    return (p0, p0 + pat[0][1], f0, f0 + ext + 1)


class Ref:
    def __init__(self, buf, ap, box=None):
        self.buf = buf
        self.ap = ap
        if box is None:
            box = _box(ap)
            if buf.psum:
                box = ((box[0] // 32) * 32, ((box[1] + 31) // 32) * 32, 0, 512)
        self.box = box

    def bc(self, shape):
        return Ref(self.buf, self.ap.to_broadcast(list(shape)), self.box)

    def __getitem__(self, idx):
        return Ref(self.buf, self.ap[idx])

    def rr(self, pat, **kw):
        return Ref(self.buf, self.ap.rearrange(pat, **kw), self.box)


def _ovl(a, b):
    return a[0] < b[1] and b[0] < a[1] and a[2] < b[3] and b[2] < a[3]


def _cov(a, b):
    return a[0] <= b[0] and a[1] >= b[1] and a[2] <= b[2] and a[3] >= b[3]


class Prog:
    ENG = ("pe", "act", "dve", "pool", "sp")

    def __init__(self, nc, stack):
        self.nc = nc
        self.stack = stack
        self.ops = []
        self.nbuf = 0

    def sb(self, name, shape):
        t = self.stack.enter_context(self.nc.sbuf_tensor("s_" + name, list(shape), F32))
        return Buf(name, t)

    def ps(self, name):
        t = self.stack.enter_context(self.nc.psum_tensor("p_" + name, [128, 512], F32))
        return Buf(name, t, psum=True)

    def op(self, eng, fn, outs, ins, dma=False):
        oid = len(self.ops)
        deps = set()
        for r, w in [(x, True) for x in outs] + [(x, False) for x in ins]:
            if r is None or not isinstance(r, Ref):
                continue
            b = r.buf
            for (bx, o2, w2) in b.acc:
                if (w or w2) and _ovl(bx, r.box):
                    deps.add(o2)
        for r, w in [(x, True) for x in outs] + [(x, False) for x in ins]:
            if r is None or not isinstance(r, Ref):
                continue
            b = r.buf
            if w:
                b.acc = [a for a in b.acc if not _cov(r.box, a[0])]
            b.acc.append((r.box, oid, w))
            if len(b.acc) > 48:
                merged = {}
                for (bx, o2, w2) in b.acc:
                    k = (self.ops[o2]["eng"] if o2 < oid else eng, w2)
                    if k in merged:
                        m = merged[k]
                        merged[k] = ((min(m[0][0], bx[0]), max(m[0][1], bx[1]), min(m[0][2], bx[2]),
                                      max(m[0][3], bx[3])), max(m[1], o2), w2)
                    else:
                        merged[k] = (bx, o2, w2)
                b.acc = list(merged.values())
        dbuf = None
        if dma:
            for r in list(outs) + list(ins):
                if isinstance(r, Ref):
                    dbuf = r.buf
            dbuf.dma_ops.append(oid)
        deps.discard(oid)
        self.ops.append(dict(eng=eng, fn=fn, deps=sorted(deps), dma=dma, dbuf=dbuf, sig=False))
        return oid

    def emit(self):
        nc = self.nc
        ops = self.ops
        for o in ops:
            for d in o["deps"]:
                od = ops[d]
                if od["dma"]:
                    continue
                if od["eng"] == o["eng"] and o["eng"] == "pe" and not o["dma"]:
                    continue
                od["sig"] = True
        sems = {e: self.stack.enter_context(nc.semaphore("sem_" + e)) for e in self.ENG}
        cnt = {e: 0 for e in self.ENG}
        for o in ops:
            if o["dma"]:
                b = o["dbuf"]
                if b.dma_sem is None:
                    b.dma_sem = self.stack.enter_context(nc.semaphore("dsem_" + b.name))
            elif o["sig"]:
                cnt[o["eng"]] += 1
                o["signo"] = cnt[o["eng"]]
        per_eng = {e: [] for e in self.ENG}
        for i, o in enumerate(ops):
            per_eng[o["eng"]].append(i)
        import bisect

        def gen(engname, e):
            waited = {}
            for i in per_eng[engname]:
                o = ops[i]
                need = {}
                for d in o["deps"]:
                    od = ops[d]
                    if od["dma"]:
                        b = od["dbuf"]
                        n = bisect.bisect_left(b.dma_ops, i)
                        key = ("d", id(b))
                        if need.get(key, (None, 0))[1] < 16 * n:
                            need[key] = (b.dma_sem, 16 * n)
                    else:
                        if od["eng"] == engname and engname == "pe" and not o["dma"]:
                            continue
                        key = ("e", od["eng"])
                        if need.get(key, (None, 0))[1] < od["signo"]:
                            need[key] = (sems[od["eng"]], od["signo"])
                for key, (sem, val) in need.items():
                    if waited.get(key, 0) >= val:
                        continue
                    e.wait_ge(sem, val)
                    waited[key] = val
                ins = o["fn"](e)
                if o["dma"]:
                    ins.then_inc(o["dbuf"].dma_sem, 16)
                elif o["sig"]:
                    ins.then_inc(sems[engname], 1)

        with nc.Block() as block:
            @block.tensor
            def _(e):
                gen("pe", e)

            @block.scalar
            def _(e):
                gen("act", e)

            @block.vector
            def _(e):
                gen("dve", e)

            @block.gpsimd
            def _(e):
                gen("pool", e)

            @block.sync
            def _(e):
                gen("sp", e)
                seen = set()
                for o in ops:
                    if o["dma"] and id(o["dbuf"]) not in seen:
                        seen.add(id(o["dbuf"]))
                        e.wait_ge(o["dbuf"].dma_sem, 16 * len(o["dbuf"].dma_ops))


def _a(x):
    return x.ap if isinstance(x, Ref) else x


def _fm4(v):
    return np.ascontiguousarray(v.reshape(-1, 128).T)


class CMap:
    def __init__(self):
        self.off = {}
        self.n = 0

    def add(self, name, w):
        self.off[name] = (self.n, w)
        self.n += w


def build_cmap():
    c = CMap()
    c.add("gpre", 8)
    c.add("a_cw", 32); c.add("a_cb", 8); c.add("a_ib", 1); c.add("a_fb", 1); c.add("a_nw", 4)
    c.add("b_mu", 13); c.add("b_w0", 4); c.add("b_a0", 4); c.add("b_kk", 4); c.add("b_ka", 4)
    c.add("b_rk", 4); c.add("b_gw", 4); c.add("b_gb", 4)
    c.add("c_cw", 16); c.add("c_cb", 4); c.add("c_br", 4); c.add("c_bi", 4); c.add("c_lam", 4)
    c.add("d_nw", 4)
    return c


CM = build_cmap()

def _cols(a, b):
    return list(range(a, b))


def build_slabs():
    slabs = []
    def fm(name, start):
        slabs.append((name, _cols(start, start + 512)))
    fm("Aq", 0); fm("Ak", 512); fm("Az", 2048)
    slabs.append(("Ag", _cols(2560, 2576) + [-1] * (512 - 16)))
    fm("Br", 2576); fm("Bk", 3088); fm("Bv", 3600)
    slabs.append(("Bl", _cols(4112, 4240) + [-1] * (512 - 128)))
    fm("Bz", 4240); fm("Cx", 4752); fm("Cz", 5264); fm("Dz", 7312)
    fm("Av", 1024); fm("Ao", 1536); fm("Dq", 5776); fm("Dk", 6288); fm("Dv", 6800)
    return slabs


SLABS = build_slabs()
SLAB_ID = {s[0]: i for i, s in enumerate(SLABS)}
NSLAB = len(SLABS)


def host_prepare(inp):
    f = np.float32
    w_in = inp["w_in"]
    w_in_p = np.concatenate([w_in, np.zeros((DEPTH, D_MODEL, 1), f)], axis=2)
    wt = np.empty((DEPTH, NSLAB, 128, 8, 512), f)
    for si, (name, cols) in enumerate(SLABS):
        blk = w_in_p[:, :, cols]
        wt[:, si] = blk.reshape(DEPTH, 8, 128, 512).transpose(0, 2, 1, 3)
    w_out = inp["w_out"]
    wo = np.ascontiguousarray(w_out.reshape(DEPTH, 4, 4, 128, 1024).transpose(0, 1, 3, 2, 4))
    cv = np.zeros((DEPTH, 128, CM.n), f)
    def put(name, arr):
        o, w = CM.off[name]
        cv[:, :, o:o + w] = arr
    put("gpre", inp["norm_pre"].reshape(DEPTH, 8, 128).transpose(0, 2, 1))
    cw = inp["mlstm_conv_w"]
    put("a_cw", cw.reshape(DEPTH, 4, 8, 128).transpose(0, 3, 1, 2).reshape(DEPTH, 128, 32))
    put("a_cb", inp["mlstm_conv_b"].reshape(DEPTH, 8, 128).transpose(0, 2, 1))
    ib = np.zeros((DEPTH, 128, 1), f); ib[:, 0:8, 0] = inp["mlstm_i_bias"]; put("a_ib", ib)
    fb = np.zeros((DEPTH, 128, 1), f); fb[:, 0:8, 0] = inp["mlstm_f_bias"]; put("a_fb", fb)
    def fm4(x):
        return x.reshape(DEPTH, -1, 128).transpose(0, 2, 1)
    put("a_nw", fm4(inp["mlstm_norm_w"]))
    mu = inp["rwkv_mu"]
    put("b_mu", fm4(mu))
    for k, nm in [("b_w0", "rwkv_w0"), ("b_a0", "rwkv_a0"), ("b_kk", "rwkv_k_k"), ("b_ka", "rwkv_k_a"),
                  ("b_rk", "rwkv_r_k"), ("b_gw", "rwkv_gn_w"), ("b_gb", "rwkv_gn_b"),
                  ("c_cb", "lru_conv_b"), ("c_br", "lru_b_r"), ("c_bi", "lru_b_i"), ("c_lam", "lru_lambda"),
                  ("d_nw", "ret_norm_w")]:
        put(k, fm4(inp[nm]))
    lw = inp["lru_conv_w"]
    put("c_cw", lw.reshape(DEPTH, 4, 4, 128).transpose(0, 3, 1, 2).reshape(DEPTH, 128, 16))
    lora = np.concatenate([inp["rwkv_w_up"], inp["rwkv_a_up"]], axis=1)
    lru = np.zeros((DEPTH, 128, 2, 4, 128), f)
    for which, nm in enumerate(["lru_w_r", "lru_w_i"]):
        w = inp[nm]
        for g in range(4):
            for j in range(2):
                lru[:, 64 * j:64 * j + 64, which, g, 64 * j:64 * j + 64] = w[:, 2 * g + j]
    gpost = np.ascontiguousarray(np.broadcast_to(inp["norm_post"][:, None, :], (DEPTH, 128, D_MODEL)))
    return dict(wt=wt, wo=wo, cv=cv, lora=np.ascontiguousarray(lora), lru=lru, gpost=gpost)


def host_tables():
    f = np.float32
    t = {}
    t["ident"] = np.eye(128, dtype=f)
    hd = 64
    log_g = np.log1p(-np.exp2(-5.0 - np.arange(8, dtype=np.float64)))
    idx = np.arange(64, dtype=np.float64)
    dmat = np.exp(log_g[:, None, None] * np.abs(idx[:, None] - idx[None, :])) * hd ** -0.5
    xi = np.exp(log_g[:, None] * (idx + 1.0))
    zeta = np.exp(log_g[:, None] * (63.0 - idx)) * hd ** -0.5
    gch = np.exp(log_g * 64.0)
    hs2h = [2 * (s % 4) + (s // 4) for s in range(8)]
    dm = np.zeros((128, 8, 64));
    for s in range(8):
        dm[0:64, s] = dmat[hs2h[s]]; dm[64:128, s] = dmat[hs2h[s]]
    t["d_dmat"] = dm.astype(f)
    t["d_xi"] = np.tile(xi.T, (2, 1)).astype(f)
    t["d_zeta"] = np.tile(zeta.T, (2, 1)).astype(f)
    gc = np.zeros((128, 4))
    for g in range(4):
        for j in range(2):
            gc[64 * j:64 * j + 64, g] = gch[2 * g + j]
    t["d_gch"] = gc.astype(f)
    mk = np.zeros((128, 8, 64))
    sidx = np.arange(128) % 64
    mk[:] = np.where(sidx[:, None, None] <= np.arange(64)[None, None, :], 0.0, -30000.0)
    t["a_mask"] = mk.reshape(128, 512).astype(f)
    sel = np.zeros((8, 512 + 4 + 128))
    for hp_ in range(8):
        for hs in range(8):
            if hp_ == hs2h[hs]:
                sel[hp_, hs * 64:(hs + 1) * 64] = 1.0
        sel[hp_, 512 + hp_ // 2] = 1.0
        jj = hp_ % 2
        sel[hp_, 516 + 64 * jj:516 + 64 * jj + 64] = 1.0
    t["a_sel"] = sel.astype(f)
    bo = np.zeros((128, 128)); bo[0:64, 0:64] = 1.0; bo[64:128, 64:128] = 1.0
    t["b_ones"] = bo.astype(f)
    i2 = np.zeros((128, 64)); i2[0:64] = np.eye(64); i2[64:128] = np.eye(64)
    t["b_i2"] = i2.astype(f)
    rr_ = (np.arange(128) % 64)[:, None]; cc_ = np.arange(64)[None, :]
    m5 = np.zeros((128, 5, 64))
    m5[:, 0] = rr_ > cc_; m5[:, 1] = cc_ > rr_; m5[:, 2] = cc_ > rr_; m5[:, 3] = cc_ >= rr_; m5[:, 4] = cc_ >= rr_
    t["b_m5"] = m5.astype(f)
    half = 32
    pos = np.arange(SEQ, dtype=np.float32)
    inv_freq = (np.float32(10000.0) ** (-np.arange(half, dtype=np.float32) / np.float32(half))).astype(np.float32)
    ang = (pos[:, None] * inv_freq[None, :]).astype(np.float32).astype(np.float64)
    t["rope"] = np.concatenate([np.cos(ang), np.sin(ang)], axis=1).astype(f)
    return t


def build_program(ntiles=SEQ // T, nlayers=DEPTH, mixers="ABCD"):
    nc = bass.Bass("TRN2", target_bir_lowering=False)
    stack = contextlib.ExitStack()
    P = Prog(nc, stack)
    dram = {}

    def din(name, shape):
        dram[name] = nc.dram_tensor(name, list(shape), F32, kind="ExternalInput").ap()
        return dram[name]

    x_d = din("x", [SEQ, D_MODEL])
    wt_d = din("wt", [DEPTH, NSLAB, 128, 8, 512])
    wo_d = din("wo", [DEPTH, 4, 128, 4, 1024])
    cv_d = din("cv", [DEPTH, 128, CM.n])
    lora_d = din("lora", [DEPTH, 128, 512])
    lru_d = din("lru", [DEPTH, 128, 2, 4, 128])
    gpost_d = din("gpost", [DEPTH, 128, D_MODEL])
    ident_d = din("ident", [128, 128])
    dmat_d = din("d_dmat", [128, 8, 64])
    dxi_d = din("d_xi", [128, 8])
    dzeta_d = din("d_zeta", [128, 8])
    dgch_d = din("d_gch", [128, 4])
    rope_d = din("rope", [SEQ, 64])
    bones_d = din("b_ones", [128, 128])
    bi2_d = din("b_i2", [128, 64])
    bm5_d = din("b_m5", [128, 5, 64])
    amask_d = din("a_mask", [128, 512])
    asel_d = din("a_sel", [8, 644])
    y_d = nc.dram_tensor("y", [SEQ, D_MODEL], F32, kind="ExternalOutput").ap()

    ident = P.sb("ident", [128, 128])
    cv = [P.sb(f"cv{l}", [128, CM.n]) for l in range(DEPTH)]
    lru1 = P.sb("lru", [128, 2, 4, 128]); lru = [lru1, lru1]
    gpost1 = P.sb("gpost", [128, D_MODEL]); gpost = [gpost1, gpost1]
    lora_up = [P.sb(f"lora_up{l}", [128, 512]) for l in range(DEPTH)]
    b_ones = P.sb("b_ones", [128, 128]); b_i2 = P.sb("b_i2", [128, 1, 64]); b_m5 = P.sb("b_m5", [128, 5, 64])
    b_hist = [P.sb(f"b_hist{l}", [128, 13]) for l in range(DEPTH)]
    b_S = [P.sb(f"b_S{l}", [128, 4, 64]) for l in range(DEPTH)]
    gLt = P.sb("gLt", [128, 4, NCH]); PRall = [P.sb(f"PR{i}", [128, 5, 8, 64]) for i in range(NTB)]; Y2 = P.sb("Y2", [128, 8, 64])
    K2g = P.sb("K2g", [128, 2, 4, 64])
    ones = P.sb("ones", [128, T]); zeros = P.sb("zeros", [128, T])
    xt = [P.sb(f"xt{i}", [128, NTB, D_MODEL]) for i in range(1)]
    hT = P.sb("hT", [128, 8, T])
    yT = P.sb("yT", [128, 16, T])
    slab = [P.sb(f"slab{i}", [128, 8, 512]) for i in range(2)]
    sm = P.sb("small", [128, 64])
    scr = P.sb("scr", [128, D_MODEL])
    scr2 = P.sb("scr2", [128, D_MODEL])
    FS = [P.sb(f"fs{i}", [128, 4, T + 4]) for i in range(7)]
    c_hist = [P.sb(f"c_hist{l}", [128, 4, 3]) for l in range(DEPTH)]
    c_state = [P.sb(f"c_state{l}", [128, 4]) for l in range(DEPTH)]
    c_coef = [P.sb(f"c_coef{l}", [128, 8]) for l in range(DEPTH)]
    PB = [P.ps(f"pb{i}") for i in range(8)]
    TM = [P.sb(f"tm{i}", [128, NTB, 8, 64]) for i in range(6)]
    d_dmat = P.sb("d_dmat", [128, 8, 64]); d_xi = P.sb("d_xi", [128, 4, 2, 1]); d_zeta = P.sb("d_zeta", [128, 8, 1])
    d_gch = P.sb("d_gch", [128, 4, 1]); rope_sb = P.sb("rope_sb", [128, NTB, 1, 64])
    d_R = [P.sb(f"d_R{l}", [128, 4, 64]) for l in range(DEPTH)]
    a_mask = P.sb("a_mask", [128, 512]); a_sel = P.sb("a_sel", [8, 644])
    ga = P.sb("ga", [8, 12, T + 1]); scl = P.sb("scl", [8, NCH]); rhs_sc = P.sb("rhs_sc", [8, NCH, 4])
    scb = P.sb("scb", [128, NCH, 4, 1]); negPx = P.sb("negPx", [8, 2, 8, 64]); ET = P.sb("ET", [128, 8, 64])
    tmsc = P.sb("tmsc", [128, NTB, 3, 8]); hnum = P.sb("hnum", [128, 8, 64]); vw = P.sb("vw", [128, 8, 64])
    a_hist = [P.sb(f"a_hist{l}", [128, 8, 3]) for l in range(DEPTH)]
    a_C = [P.sb(f"a_C{l}", [128, 4, 64]) for l in range(DEPTH)]
    a_n = [P.sb(f"a_n{l}", [128, 4, 1]) for l in range(DEPTH)]
    a_car = [P.sb(f"a_car{l}", [8, 4]) for l in range(DEPTH)]
    AT = P.sb("AT", [128, 8, 64]); tmp4 = P.sb("tmp4", [128, 4, 2, 64]); lnst = P.sb("lnst", [128, 64])
    state = dict(slab_i=0, pb_i=0)

    import os
    cut = int(os.environ.get("KCUT", "9"))
    def dma(out, in_):
        return P.op("sp", lambda e: e.dma_start(out=_a(out), in_=_a(in_)), [out], [in_], dma=True)

    def mm(out, lhsT, rhs, start=True, stop=True):
        P.op("pe", lambda e: e.matmul(_a(out), _a(lhsT), _a(rhs), start=start, stop=stop),
             [out], [lhsT, rhs] + ([] if start else [out]))

    def tr(out, in_, idn):
        P.op("pe", lambda e: e.transpose(_a(out), _a(in_), _a(idn)), [out], [in_, idn])

    def act(out, in_, func, bias=None, scale=1.0, accum=None, eng="act"):
        kw = {}
        if bias is not None:
            kw["bias"] = _a(bias)
        if accum is not None:
            kw["accum_out"] = _a(accum)
        P.op("act", lambda e: e.activation(_a(out), _a(in_), func, scale=_a(scale), **kw),
             [out, accum], [in_, bias, scale])

    def tt(out, a, b, op, eng="dve"):
        P.op(eng, lambda e: e.tensor_tensor(_a(out), _a(a), _a(b), op), [out], [a, b])

    def ts(out, a, s1, s2, op0, op1=ALU.bypass, eng="dve"):
        P.op(eng, lambda e: e.tensor_scalar(_a(out), _a(a), _a(s1), _a(s2), op0, op1), [out], [a, s1, s2])

    def stt(out, a, s, b, op0, op1):
        P.op("dve", lambda e: e.scalar_tensor_tensor(_a(out), _a(a), _a(s), _a(b), op0, op1), [out], [a, s, b])

    def scan(out, d0, d1, init, op0, op1):
        P.op("dve", lambda e: e.tensor_tensor_scan(_a(out), _a(d0), _a(d1), _a(init), op0, op1),
             [out], [d0, d1, init])

    def cp(out, in_, eng="dve"):
        P.op(eng, lambda e: e.tensor_copy(_a(out), _a(in_)), [out], [in_])

    def recip(out, in_):
        P.op("dve", lambda e: e.reciprocal(_a(out), _a(in_)), [out], [in_])

    def memset(out, val, eng="dve"):
        P.op(eng, lambda e: e.memset(_a(out), val), [out], [])

    def reduce(out, in_, op, axis=AX.X):
        P.op("dve", lambda e: e.tensor_reduce(_a(out), _a(in_), axis, op), [out], [in_])

    def next_pb():
        state["pb_i"] = (state["pb_i"] + 1) % 2
        return PB[state["pb_i"]]

    def load_slab(l, name, ncols=512):
        s = slab[state["slab_i"]]
        state["slab_i"] = (state["slab_i"] + 1) % 2
        dma(s[:, :, 0:ncols], wt_d[l, SLAB_ID[name], :, :, 0:ncols])
        return s

    def load_wo(l, si):
        s = slab[state["slab_i"]]
        state["slab_i"] = (state["slab_i"] + 1) % 2
        dma(s[:, :, :], wo_d[l, si].rearrange("p f n -> p (f n)").rearrange("p (a b) -> p a b", a=8))
        return s

    def cvc(l, name, i=0):
        o, w = CM.off[name]
        return cv[l][:, o + i:o + i + 1]

    def fm_proj(s, gi, out_cols=T):
        pb = next_pb()
        for kc in range(8):
            mm(pb[:, 0:T], s[:, kc, gi * 128:(gi + 1) * 128], hT[:, kc, :], start=(kc == 0), stop=(kc == 7))
        return pb

    dma(ident[:, :], ident_d)
    for l in range(DEPTH):
        dma(cv[l][:, :], cv_d[l])
        dma(lora_up[l][:, :], lora_d[l])
        memset(b_hist[l][:, :], 0.0)
        memset(b_S[l][:, :, :], 0.0)
    dma(d_dmat[:, :, :], dmat_d)
    dma(d_xi[:, :, :, :], dxi_d.rearrange("p (g j o) -> p g j o", g=4, j=2))
    dma(d_zeta[:, :, :], dzeta_d.rearrange("p (h o) -> p h o", o=1))
    dma(d_gch[:, :, :], dgch_d.rearrange("p (g o) -> p g o", o=1))
    dma(a_mask[:, :], amask_d)
    dma(a_sel[:, :], asel_d)
    for l in range(DEPTH):
        memset(d_R[l][:, :, :], 0.0)
        memset(a_hist[l][:, :, :], 0.0)
        memset(a_C[l][:, :, :], 0.0)
        memset(a_n[l][:, :, :], 0.0)
        memset(a_car[l][:, :], 0.0)
        ts(a_car[l][0:8, 2:3], cv[l][0:8, CM.off["a_fb"][0]:CM.off["a_fb"][0] + 1], -1.0, None, ALU.mult)
    dma(b_ones[:, :], bones_d)
    dma(b_i2[:, 0, :], bi2_d)
    dma(b_m5[:, :, :], bm5_d)
    memset(ones[:, :], 1.0)
    memset(zeros[:, :], 0.0)
    for l in range(DEPTH):
        memset(c_hist[l][:, :, :], 0.0)
        memset(c_state[l][:, :], 0.0)
        o, w = CM.off["c_lam"]
        act(sm[:, 0:4], cv[l][:, o:o + 4], AF.Exp, scale=-1.0)
        act(sm[:, 4:8], sm[:, 0:4], AF.Ln, bias=ones[:, 0:1])
        ts(c_coef[l][:, 0:4], sm[:, 4:8], -8.0, None, ALU.mult)
        ts(c_coef[l][:, 4:8], sm[:, 4:8], -16.0, None, ALU.mult)

    def rmsnorm_pre(l, X):
        for tb in range(NTB):
            act(scr[:, :], X[:, tb, :], AF.Square)
            reduce(sm[:, 8 + tb:9 + tb], scr[:, :], ALU.add)
        ts(sm[:, 12:12 + NTB], sm[:, 8:8 + NTB], 1.0 / D_MODEL, 1e-6, ALU.mult, ALU.add)
        act(sm[:, 16:16 + NTB], sm[:, 12:12 + NTB], AF.Sqrt)
        recip(sm[:, 20:20 + NTB], sm[:, 16:16 + NTB])
        for tb in range(NTB):
            act(scr[:, :], X[:, tb, :], AF.Copy, scale=sm[:, 20 + tb:21 + tb])
            for half in range(2):
                pb = next_pb()
                for q in range(4):
                    kc = half * 4 + q
                    tr(pb[:, q * 128:(q + 1) * 128], scr[:, kc * 128:(kc + 1) * 128], ident[:, :])
                for q in range(4):
                    kc = half * 4 + q
                    ts(hT[:, kc, tb * 128:(tb + 1) * 128], pb[:, q * 128:(q + 1) * 128],
                       cvc(l, "gpre", kc), None, ALU.mult)

    def mixer_C(l):
        cx, xc, rg, ig, aa, uu, hh = FS[0], FS[1], FS[2], FS[3], FS[4], FS[5], FS[6]
        dma(lru[l][:, :, :, :], lru_d[l])
        wx = load_slab(l, "Cx")
        for g in range(4):
            pb = fm_proj(wx, g)
            cp(cx[:, g, 0:3], c_hist[l][:, g, :])
            act(cx[:, g, 3:3 + T], pb[:, 0:T], AF.Copy)
            cp(c_hist[l][:, g, :], cx[:, g, T:T + 3])
            o, w = CM.off["c_cw"]
            ts(xc[:, g, 0:T], cx[:, g, 0:T], cv[l][:, o + g:o + g + 1], cvc(l, "c_cb", g), ALU.mult, ALU.add)
            for j in range(1, 4):
                stt(xc[:, g, 0:T], cx[:, g, j:j + T], cv[l][:, o + 4 * j + g:o + 4 * j + g + 1], xc[:, g, 0:T],
                    ALU.mult, ALU.add)
        for g in range(4):
            pb = next_pb()
            mm(pb[:, 0:T], lru[l][:, 0, g, :], xc[:, g, 0:T])
            act(rg[:, g, 0:T], pb[:, 0:T], AF.Sigmoid, bias=cvc(l, "c_br", g))
            pb = next_pb()
            mm(pb[:, 0:T], lru[l][:, 1, g, :], xc[:, g, 0:T])
            act(ig[:, g, 0:T], pb[:, 0:T], AF.Sigmoid, bias=cvc(l, "c_bi", g))
        for g in range(4):
            act(aa[:, g, 0:T], rg[:, g, 0:T], AF.Exp, scale=c_coef[l][:, g:g + 1])
            act(uu[:, g, 0:T], rg[:, g, 0:T], AF.Exp, scale=c_coef[l][:, 4 + g:5 + g])
        ts(uu[:, :, 0:T], uu[:, :, 0:T], -1.0, 1.0, ALU.mult, ALU.add)
        act(uu[:, :, 0:T], uu[:, :, 0:T], AF.Sqrt)
        tt(ig[:, :, 0:T], ig[:, :, 0:T], xc[:, :, 0:T], ALU.mult)
        tt(uu[:, :, 0:T], uu[:, :, 0:T], ig[:, :, 0:T], ALU.mult)
        for g in range(4):
            scan(hh[:, g, 0:T], aa[:, g, 0:T], uu[:, g, 0:T], c_state[l][:, g:g + 1], ALU.mult, ALU.add)
            cp(c_state[l][:, g:g + 1], hh[:, g, T - 1:T])
        wz = load_slab(l, "Cz")
        for g in range(4):
            pb = fm_proj(wz, g)
            act(rg[:, g, 0:T], pb[:, 0:T], AF.Silu)
            tt(yT[:, 8 + g, :], hh[:, g, 0:T], rg[:, g, 0:T], ALU.mult)

    def tm_proj(sl, tb):
        pb = next_pb()
        for kc in range(8):
            mm(pb[:, :], hT[:, kc, tb * 128:(tb + 1) * 128], sl[:, kc, :], start=(kc == 0), stop=(kc == 7))
        return pb

    def to_fm(dst, src, tb):
        pb = next_pb()
        for g in range(4):
            tr(pb[:, g * 128:(g + 1) * 128], src[:, tb, 2 * g:2 * g + 2, 0:64], ident[:, :])
        cp(dst[:, 0:4, tb * 128:(tb + 1) * 128], pb[:, :].rr("p (a b) -> p a b", a=4))

    def to_tm(dst, src, tb):
        pb = next_pb()
        for g in range(4):
            tr(pb[:, g * 128:(g + 1) * 128], src[:, g, tb * 128:(tb + 1) * 128], ident[:, :])
        cp(dst[:, tb, :, 0:64], pb[:, :].rr("p (h e) -> p h e", h=8))

    def head_ln(dst, src, tb):
        X3 = src[:, tb, :, 0:64]
        D3 = dst[:, tb, :, 0:64]
        reduce(lnst[:, 0:8], X3, ALU.add)
        stt(D3, lnst[:, 0:8].rr("p (h o) -> p h o", o=1).bc([128, 8, 64]), -1.0 / 64, X3, ALU.mult, ALU.add)
        tt(scr[:, 0:512].rr("p (h e) -> p h e", h=8), D3, D3, ALU.mult)
        reduce(lnst[:, 8:16], scr[:, 0:512].rr("p (h e) -> p h e", h=8), ALU.add)
        ts(lnst[:, 16:24], lnst[:, 8:16], 1.0 / 64, 1e-5, ALU.mult, ALU.add)
        act(lnst[:, 24:32], lnst[:, 16:24], AF.Sqrt)
        recip(lnst[:, 32:40], lnst[:, 24:32])
        tt(D3, D3, lnst[:, 32:40].rr("p (h o) -> p h o", o=1).bc([128, 8, 64]), ALU.mult)

    def mixer_D(l, ti):
        qr, kr, vv, kz, osb, xn = TM[0], TM[1], TM[2], TM[3], TM[4], TM[5]
        qT, kT, sz = FS[0], FS[1], FS[2]
        S = [PB[3], PB[4]]; O = [PB[5], PB[6]]; ST = [PB[2], PB[7]]
        dma(rope_sb[:, :, :, :], rope_d.rearrange("(n tb p) (o e) -> n p tb o e", tb=NTB, p=128, o=1)[ti])
        for name, dst in (("Dq", qr), ("Dk", kr)):
            sl = load_slab(l, name)
            for tb in range(NTB):
                pb = tm_proj(sl, tb)
                act(scr[:, 0:512], pb[:, :], AF.Copy)
                raw = scr[:, 0:512].rr("p (h e) -> p h e", h=8)
                x1, x2 = raw[:, :, 0:32], raw[:, :, 32:64]
                cos = rope_sb[:, tb, :, 0:32].bc([128, 8, 32]); sin = rope_sb[:, tb, :, 32:64].bc([128, 8, 32])
                t1 = scr2[:, 0:256].rr("p (h e) -> p h e", h=8); t2 = scr2[:, 256:512].rr("p (h e) -> p h e", h=8)
                d1, d2 = dst[:, tb, :, 0:32], dst[:, tb, :, 32:64]
                tt(d1, x1, cos, ALU.mult); tt(t1, x2, sin, ALU.mult); tt(d1, d1, t1, ALU.subtract)
                tt(d2, x2, cos, ALU.mult); tt(t2, x1, sin, ALU.mult); tt(d2, d2, t2, ALU.add)
                to_fm(qT if name == "Dq" else kT, dst, tb)
        sl = load_slab(l, "Dv")
        for tb in range(NTB):
            pb = tm_proj(sl, tb)
            act(vv[:, tb, :, 0:64], pb[:, :].rr("p (h e) -> p h e", h=8), AF.Copy)
            tt(kz[:, tb, :, 0:64], kr[:, tb, :, 0:64], d_zeta[:, :, :].bc([128, 8, 64]), ALU.mult)
        sl = load_slab(l, "Dz")
        for g in range(4):
            pb = fm_proj(sl, g)
            act(sz[:, g, 0:T], pb[:, 0:T], AF.Silu)
        for c in range(NCH):
            tb, p = c // 2, c % 2
            rows = slice(64 * p, 64 * p + 64)
            cols = slice(c * 64, (c + 1) * 64)
            for j in range(2):
                jr = slice(64 * j, 64 * j + 64)
                for g in range(4):
                    mm(S[j][rows, g * 64:(g + 1) * 64], kT[jr, g, cols], qT[jr, g, cols])
                    mm(S[j][rows, 256 + g * 64:256 + (g + 1) * 64], qT[jr, g, cols], d_R[l][jr, g, :])
            for j in range(2):
                tt(AT[rows, 4 * j:4 * j + 4, :], S[j][rows, 0:256].rr("p (a b) -> p a b", a=4),
                   d_dmat[rows, 4 * j:4 * j + 4, :], ALU.mult)
                tt(tmp4[rows, :, j, 0:64], S[j][rows, 256:512].rr("p (a b) -> p a b", a=4),
                   d_xi[rows, :, j, :].bc([64, 4, 64]), ALU.mult)
            for h in range(8):
                g, j = h // 2, h % 2
                mm(O[p][rows, h * 64:(h + 1) * 64], AT[rows, j * 4 + g, :], vv[rows, tb, h, 0:64])
            tt(osb[rows, tb, :, 0:64], O[p][rows, :].rr("p (h e) -> p h e", h=8),
               tmp4[rows, :, :, 0:64].rr("p g j e -> p (g j) e"), ALU.add)
            for h in range(8):
                g, j = h // 2, h % 2
                mm(ST[p][64 * j:64 * j + 64, g * 64:(g + 1) * 64], kz[rows, tb, h, 0:64], vv[rows, tb, h, 0:64])
            tt(d_R[l][:, :, :], d_R[l][:, :, :], d_gch[:, :, :].bc([128, 4, 64]), ALU.mult)
            tt(d_R[l][:, :, :], d_R[l][:, :, :], ST[p][:, 0:256].rr("p (a b) -> p a b", a=4), ALU.add)
        for tb in range(NTB):
            head_ln(xn, osb, tb)
            pb = next_pb()
            for g in range(4):
                tr(pb[:, g * 128:(g + 1) * 128], xn[:, tb, 2 * g:2 * g + 2, 0:64], ident[:, :])
            for g in range(4):
                stt(yT[:, 12 + g, tb * 128:(tb + 1) * 128], pb[:, g * 128:(g + 1) * 128], cvc(l, "d_nw", g),
                    sz[:, g, tb * 128:(tb + 1) * 128], ALU.mult, ALU.mult)

    def conv_silu(l, sl, g8, dst, gdst, cx, whichhist, cwname, cbname, ngroups_total, silu=True):
        pass

    def mixer_A(l, ti):
        cx, qT, kT, sz = FS[0], FS[1], FS[2], FS[3]
        ktm, vv, so, osb, xn = TM[0], TM[1], TM[2], TM[3], TM[4]
        S = [PB[3], PB[4]]; O = [PB[5], PB[6]]; ST = [PB[2], PB[7]]; JX = [PB[0], PB[1]]
        R_I, R_SP, R_F, R_G, R_P, R_MU, R_PE, R_SI, R_WJ, R_NM, R_NP = range(11)
        ocw = CM.off["a_cw"][0]
        for which, (name, dstT) in enumerate((("Aq", qT), ("Ak", kT))):
            sl = load_slab(l, name)
            for g in range(4):
                g8 = which * 4 + g
                pb = fm_proj(sl, g)
                cp(cx[:, g, 0:3], a_hist[l][:, g8, :])
                act(cx[:, g, 3:3 + T], pb[:, 0:T], AF.Copy)
                cp(a_hist[l][:, g8, :], cx[:, g, T:T + 3])
                ts(dstT[:, g, 0:T], cx[:, g, 0:T], cv[l][:, ocw + g8:ocw + g8 + 1], cvc(l, "a_cb", g8),
                   ALU.mult, ALU.add)
                for j in range(1, 4):
                    stt(dstT[:, g, 0:T], cx[:, g, j:j + T], cv[l][:, ocw + 8 * j + g8:ocw + 8 * j + g8 + 1],
                        dstT[:, g, 0:T], ALU.mult, ALU.add)
                act(dstT[:, g, 0:T], dstT[:, g, 0:T], AF.Silu)
        sl = load_slab(l, "Az")
        for g in range(4):
            pb = fm_proj(sl, g)
            act(sz[:, g, 0:T], pb[:, 0:T], AF.Silu)
        acut = int(os.environ.get("ACUT", "99"))
        if acut < 1:
            return
        sl = load_slab(l, "Ag", ncols=128)
        pbi = next_pb()
        for kc in range(8):
            mm(pbi[0:8, 0:T], sl[:, kc, 0:8], hT[:, kc, :], start=(kc == 0), stop=(kc == 7))
        act(ga[0:8, R_I, 0:T], pbi[0:8, 0:T], AF.Identity, bias=cv[l][0:8, CM.off["a_ib"][0]:CM.off["a_ib"][0] + 1])
        gcut = int(os.environ.get("GCUT", "99"))
        if gcut < 1:
            return
        pbf = next_pb()
        for kc in range(8):
            mm(pbf[0:8, 0:T], sl[:, kc, 8:16], hT[:, kc, :], start=(kc == 0), stop=(kc == 7))
        act(ga[0:8, R_SP, 0:T], pbf[0:8, 0:T], AF.Exp, bias=a_car[l][0:8, 2:3], scale=-1.0)
        act(ga[0:8, R_SP, 0:T], ga[0:8, R_SP, 0:T], AF.Ln, bias=ones[0:8, 0:1])
        if gcut < 2:
            return
        scan(ga[0:8, R_F, 0:T], ones[0:8, 0:T], ga[0:8, R_SP, 0:T], a_car[l][0:8, 0:1], ALU.mult, ALU.subtract)
        cp(a_car[l][0:8, 0:1], ga[0:8, R_F, T - 1:T])
        tt(ga[0:8, R_G, 0:T], ga[0:8, R_I, 0:T], ga[0:8, R_F, 0:T], ALU.subtract)
        if gcut < 3:
            return
        cp(ga[0:8, R_P, 0:1], a_car[l][0:8, 1:2])
        scan(ga[0:8, R_P, 1:T + 1], ones[0:8, 0:T], ga[0:8, R_G, 0:T], a_car[l][0:8, 1:2], ALU.mult, ALU.max)
        cp(a_car[l][0:8, 1:2], ga[0:8, R_P, T:T + 1])
        if gcut < 4:
            return
        for c in range(NCH):
            cols = slice(c * 64, (c + 1) * 64)
            ts(ga[0:8, R_MU, cols], zeros[0:8, 0:64], ga[0:8, R_P, c * 64:c * 64 + 1], None, ALU.add)
            ts(ga[0:8, R_PE, cols], zeros[0:8, 0:64], ga[0:8, R_P, c * 64 + 64:c * 64 + 65], None, ALU.add)
            tt(scl[0:8, c:c + 1], ga[0:8, R_P, c * 64:c * 64 + 1], ga[0:8, R_P, c * 64 + 64:c * 64 + 65], ALU.subtract)
        if gcut < 5:
            return
        Pv = ga[0:8, R_P, 1:T + 1]
        tt(ga[0:8, R_SI, 0:T], ga[0:8, R_MU, 0:T], Pv, ALU.subtract)
        tt(ga[0:8, R_WJ, 0:T], ga[0:8, R_G, 0:T], ga[0:8, R_PE, 0:T], ALU.subtract)
        stt(ga[0:8, R_NM, 0:T], ga[0:8, R_F, 0:T], -1.0, Pv, ALU.mult, ALU.subtract)
        ts(ga[0:8, R_NP, 0:T], Pv, -1.0, None, ALU.mult)
        if acut < 2:
            return
        for tb in range(NTB):
            pb = next_pb()
            for r, R in enumerate((R_SI, R_WJ, R_NM)):
                tr(pb[:, r * 8:(r + 1) * 8], ga[0:8, R, tb * 128:(tb + 1) * 128], ident[0:8, 0:8])
            act(tmsc[:, tb, :, :], pb[:, 0:24].rr("p (a b) -> p a b", a=3), AF.Exp)
        if acut < 3:
            return
        tt(rhs_sc[0:8, :, :], scl[0:8, :].rr("p (c o) -> p c o", o=1).bc([8, NCH, 4]),
           a_sel[0:8, 512:516].rr("p (o g) -> p o g", o=1).bc([8, NCH, 4]), ALU.mult)
        pb = next_pb()
        mm(pb[:, 0:NCH * 4], a_sel[0:8, 516:644], rhs_sc[0:8, :, :].rr("p c g -> p (c g)"))
        act(scb[:, :, :, :].rr("p c g o -> p (c g o)"), pb[:, 0:NCH * 4], AF.Exp)
        if acut < 4:
            return
        for tb in range(NTB):
            to_tm(ktm, kT, tb)
        sl = load_slab(l, "Av")
        for tb in range(NTB):
            pb = tm_proj(sl, tb)
            act(vv[:, tb, :, :], pb[:, :].rr("p (h e) -> p h e", h=8), AF.Copy)
        sl = load_slab(l, "Ao")
        for tb in range(NTB):
            pb = tm_proj(sl, tb)
            act(so[:, tb, :, :], pb[:, :].rr("p (h e) -> p h e", h=8), AF.Sigmoid)
        for tb in range(NTB):
            E = next_pb()
            mm(E[:, :], ident[:, :], a_mask[:, :], start=True, stop=False)
            mm(E[:, :], ga[0:8, R_G, tb * 128:(tb + 1) * 128], a_sel[0:8, 0:512], start=False, stop=False)
            for p in range(2):
                c = tb * 2 + p
                tt(negPx[0:8, p, :, :], ga[0:8, R_NP:R_NP + 1, c * 64:(c + 1) * 64].bc([8, 8, 64]),
                   a_sel[0:8, 0:512].rr("p (h e) -> p h e", h=8), ALU.mult)
                mm(E[64 * p:64 * p + 64, :], ones[0:8, 0:64], negPx[0:8, p, :, :].rr("p h e -> p (h e)"),
                   start=False, stop=True)
            act(ET[:, :, :].rr("p h e -> p (h e)"), E[:, :], AF.Exp)
            if acut < 5:
                continue
            for p in range(2):
                c = tb * 2 + p
                rows = slice(64 * p, 64 * p + 64)
                cols = slice(c * 64, (c + 1) * 64)
                sI4 = tmsc[:, tb, 0, :].rr("p (g j o) -> p g j o", g=4, j=2)
                for j in range(2):
                    jr = slice(64 * j, 64 * j + 64)
                    for g in range(4):
                        mm(S[j][rows, g * 64:(g + 1) * 64], kT[jr, g, cols], qT[jr, g, cols])
                        mm(JX[j][rows, g * 64:(g + 1) * 64], qT[jr, g, cols], a_C[l][jr, g, :])
                        mm(JX[j][rows, 256 + g:257 + g], qT[jr, g, cols], a_n[l][jr, g, :])
                for j in range(2):
                    stt(AT[rows, 4 * j:4 * j + 4, :], S[j][rows, 0:256].rr("p (a b) -> p a b", a=4), 0.125,
                        ET[rows, 4 * j:4 * j + 4, :], ALU.mult, ALU.mult)
                    tt(tmp4[rows, :, j, :], JX[j][rows, 0:256].rr("p (a b) -> p a b", a=4),
                       sI4[rows, :, j, :].bc([64, 4, 64]), ALU.mult)
                    tt(lnst[rows, 40:48].rr("p (g j) -> p g j", j=2)[:, :, j], JX[j][rows, 256:260],
                       tmsc[rows, tb, 0, :].rr("p (g j) -> p g j", j=2)[:, :, j], ALU.mult)
                for h in range(8):
                    g, j = h // 2, h % 2
                    mm(O[p][rows, h * 64:(h + 1) * 64], AT[rows, j * 4 + g, :], vv[rows, tb, h, :])
                    mm(ST[p][rows, 384 + h:385 + h], AT[rows, j * 4 + g, :], ones[rows, 0:1])
                tt(hnum[rows, :, :], O[p][rows, :].rr("p (h e) -> p h e", h=8),
                   tmp4[rows, :, :, :].rr("p g j e -> p (g j) e"), ALU.add)
                tt(lnst[rows, 48:56], ST[p][rows, 384:392], lnst[rows, 40:48], ALU.add)
                stt(lnst[rows, 48:56], lnst[rows, 48:56], -1.0, lnst[rows, 48:56], ALU.mult, ALU.max)
                tt(lnst[rows, 48:56], lnst[rows, 48:56], tmsc[rows, tb, 2, :], ALU.max)
                recip(lnst[rows, 56:64], lnst[rows, 48:56])
                tt(hnum[rows, :, :], hnum[rows, :, :],
                   lnst[rows, 56:64].rr("p (h o) -> p h o", o=1).bc([64, 8, 64]), ALU.mult)
                tt(osb[rows, tb, :, :], hnum[rows, :, :], so[rows, tb, :, :], ALU.mult)
                tt(vw[rows, :, :], vv[rows, tb, :, :],
                   tmsc[rows, tb, 1, :].rr("p (h o) -> p h o", o=1).bc([64, 8, 64]), ALU.mult)
                for h in range(8):
                    g, j = h // 2, h % 2
                    jr = slice(64 * j, 64 * j + 64)
                    mm(ST[p][jr, g * 64:(g + 1) * 64], ktm[rows, tb, h, :], vw[rows, h, :])
                    mm(ST[p][jr, 256 + g:257 + g], ktm[rows, tb, h, :], tmsc[rows, tb, 1, h:h + 1])
                tt(a_C[l][:, :, :], a_C[l][:, :, :], scb[:, c, :, :].bc([128, 4, 64]), ALU.mult)
                stt(a_C[l][:, :, :], ST[p][:, 0:256].rr("p (a b) -> p a b", a=4), 0.125, a_C[l][:, :, :],
                    ALU.mult, ALU.add)
                tt(a_n[l][:, :, :], a_n[l][:, :, :], scb[:, c, :, :], ALU.mult)
                stt(a_n[l][:, :, :], ST[p][:, 256:260].rr("p (a b) -> p a b", b=1), 0.125, a_n[l][:, :, :],
                    ALU.mult, ALU.add)
        for tb in range(NTB):
            head_ln(xn, osb, tb)
            pb = next_pb()
            for g in range(4):
                tr(pb[:, g * 128:(g + 1) * 128], xn[:, tb, 2 * g:2 * g + 2, 0:64], ident[:, :])
            for g in range(4):
                stt(yT[:, g, tb * 128:(tb + 1) * 128], pb[:, g * 128:(g + 1) * 128], cvc(l, "a_nw", g),
                    sz[:, g, tb * 128:(tb + 1) * 128], ALU.mult, ALU.mult)

    def mixer_B(l, ti):
        rS, kS, vS, sz = FS[0], FS[1], FS[2], FS[3]
        RhT, AlT, bon = rS, kS, vS
        pools = (FS[4], FS[5], FS[6])

        def slot(i):
            return pools[i // 4][:, i % 4, :]
        raw, lora, aT, lw, lc, kt, kh, beta, eg, egi, egm, tA = [slot(i) for i in range(12)]
        Vtm, Btm, Ktm, wkv, xn = TM[0], TM[1], TM[2], TM[3], TM[4]
        BJ = [PB[3], PB[4]]; PA = [PB[2], PB[7]]; PQ = [PB[5], PB[6]]; PX = [PB[0], PB[1]]
        Pn, Qn, Xm, W2 = ET, hnum, vw, AT
        omu = CM.off["b_mu"][0]

        def shift_mix(dst, pb, hidx, mucol, nrows=128):
            cp(raw[:, 0:1], b_hist[l][:, hidx:hidx + 1])
            act(raw[:, 1:T + 1], pb[:, 0:T], AF.Copy)
            cp(b_hist[l][:, hidx:hidx + 1], raw[:, T:T + 1])
            tt(tA[:, 0:T], raw[:, 0:T], raw[:, 1:T + 1], ALU.subtract)
            stt(dst, tA[:, 0:T], cv[l][:, omu + mucol:omu + mucol + 1], raw[:, 1:T + 1], ALU.mult, ALU.add)

        sl = load_slab(l, "Bl", ncols=128)
        pb = fm_proj(sl, 0)
        shift_mix(lora[:, 0:T], pb, 12, 12)
        act(lora[0:64, 0:T], lora[0:64, 0:T], AF.Tanh)
        for wi, (name, dst) in enumerate((("Br", rS), ("Bk", kS), ("Bv", vS))):
            sl = load_slab(l, name)
            for g in range(4):
                pb = fm_proj(sl, g)
                shift_mix(dst[:, g, 0:T], pb, wi * 4 + g, wi * 4 + g)
        sl = load_slab(l, "Bz")
        for g in range(4):
            pb = fm_proj(sl, g)
            act(sz[:, g, 0:T], pb[:, 0:T], AF.Silu)
        for g in range(4):
            gc = slice(g * 128, (g + 1) * 128)
            pb = next_pb()
            mm(pb[:, 0:T], lora_up[l][0:64, gc], lora[0:64, 0:T])
            act(lw[:, 0:T], pb[:, 0:T], AF.Sigmoid, bias=cvc(l, "b_w0", g))
            ts(lw[:, 0:T], lw[:, 0:T], -0.606531, None, ALU.mult)
            pb = next_pb()
            mm(pb[:, 0:T], lora_up[l][64:128, gc], lora[64:128, 0:T])
            act(aT[:, 0:T], pb[:, 0:T], AF.Sigmoid, bias=cvc(l, "b_a0", g))
            for c in range(NCH):
                cols = slice(c * 64, (c + 1) * 64)
                scan(lc[:, cols], ones[:, 0:64], lw[:, cols], 0.0, ALU.mult, ALU.add)
            act(eg[:, 0:T], lc[:, 0:T], AF.Exp)
            act(egi[:, 0:T], lc[:, 0:T], AF.Exp, scale=-1.0)
            tt(tA[:, 0:T], lc[:, 0:T], lw[:, 0:T], ALU.subtract)
            act(egm[:, 0:T], tA[:, 0:T], AF.Exp)
            for c in range(NCH):
                cp(gLt[:, g, c:c + 1], eg[:, c * 64 + 63:c * 64 + 64])
            ts(kh[:, 0:T], kS[:, g, 0:T], cvc(l, "b_kk", g), None, ALU.mult)
            tt(tA[:, 0:T], kh[:, 0:T], kh[:, 0:T], ALU.mult)
            pb = next_pb()
            mm(pb[:, 0:T], b_ones[:, :], tA[:, 0:T])
            ts(tA[:, 0:T], pb[:, 0:T], 1e-12, None, ALU.add)
            act(tA[:, 0:T], tA[:, 0:T], AF.Sqrt)
            recip(tA[:, 0:T], tA[:, 0:T])
            tt(kh[:, 0:T], kh[:, 0:T], tA[:, 0:T], ALU.mult)
            ts(tA[:, 0:T], aT[:, 0:T], -1.0, cvc(l, "b_ka", g), ALU.add, ALU.mult)
            stt(kt[:, 0:T], tA[:, 0:T], 1.0, kS[:, g, 0:T], ALU.add, ALU.mult)
            for tb in range(NTB):
                pbt = next_pb()
                tr(pbt[:, 0:128], vS[:, g, tb * 128:(tb + 1) * 128], ident[:, :])
                cp(Vtm[:, tb, 2 * g:2 * g + 2, :], pbt[:, 0:128].rr("p (a b) -> p a b", a=2))
            stt(tA[:, 0:T], rS[:, g, 0:T], cvc(l, "b_rk", g), kt[:, 0:T], ALU.mult, ALU.mult)
            pbb = next_pb()
            mm(pbb[:, 0:T], b_ones[:, :], tA[:, 0:T])
            tt(bon[:, g, 0:T], pbb[:, 0:T], vS[:, g, 0:T], ALU.mult)
            tt(beta[:, 0:T], aT[:, 0:T], kh[:, 0:T], ALU.mult)
            tt(beta[:, 0:T], beta[:, 0:T], egi[:, 0:T], ALU.mult)
            tt(AlT[:, g, 0:T], kh[:, 0:T], egm[:, 0:T], ALU.mult)
            tt(kt[:, 0:T], kt[:, 0:T], egi[:, 0:T], ALU.mult)
            tt(RhT[:, g, 0:T], rS[:, g, 0:T], eg[:, 0:T], ALU.mult)
            for tb in range(NTB):
                pbt = next_pb()
                tr(pbt[:, 0:128], beta[:, tb * 128:(tb + 1) * 128], ident[:, :])
                tr(pbt[:, 128:256], kt[:, tb * 128:(tb + 1) * 128], ident[:, :])
                cp(Btm[:, tb, 2 * g:2 * g + 2, :], pbt[:, 0:128].rr("p (a b) -> p a b", a=2))
                cp(Ktm[:, tb, 2 * g:2 * g + 2, :], pbt[:, 128:256].rr("p (a b) -> p a b", a=2))
            for tb in range(NTB):
                for j in range(2):
                    jr = slice(64 * j, 64 * j + 64)
                    for p in range(2):
                        c = tb * 2 + p
                        rows = slice(64 * p, 64 * p + 64)
                        cols = slice(c * 64, (c + 1) * 64)
                        A_, B_, K_, R_ = AlT[jr, g, cols], beta[jr, cols], kt[jr, cols], RhT[jr, g, cols]
                        mm(BJ[j][rows, 0:64], A_, B_)
                        mm(BJ[j][rows, 64:128], B_, A_)
                        mm(BJ[j][rows, 128:192], K_, A_)
                        mm(BJ[j][rows, 192:256], B_, R_)
                        mm(BJ[j][rows, 256:320], K_, R_)
                    tt(PRall[tb][:, :, 2 * g + j, :], BJ[j][:, 0:320].rr("p (k t) -> p k t", k=5), b_m5[:, :, :],
                       ALU.mult)
        for tb in range(NTB):
            PRt = PRall[tb]
            P0, Q0, MkT, NbT, NkT = (PRt[:, k, :, :] for k in range(5))
            Wsb, Usb = PRt[:, 2, :, :], PRt[:, 4, :, :]
            for p in range(2):
                rows = slice(64 * p, 64 * p + 64)
                c = tb * 2 + p
                for h in range(8):
                    g, j = h // 2, h % 2
                    mm(PA[p][rows, h * 64:(h + 1) * 64], MkT[rows, h, :], Vtm[rows, tb, h, :])
                    mm(PQ[p][rows, h * 64:(h + 1) * 64], NkT[rows, h, :], Vtm[rows, tb, h, :])
                    mm(PX[p][64 * j:64 * j + 64, g * 64:(g + 1) * 64], Ktm[rows, tb, h, :], Vtm[rows, tb, h, :])
                act(W2[rows, :, :], PA[p][rows, :].rr("p (h e) -> p h e", h=8), AF.Copy)
                act(Y2[rows, :, :], PQ[p][rows, :].rr("p (h e) -> p h e", h=8), AF.Copy)
                tt(K2g[:, p, :, :], PX[p][:, 0:256].rr("p (a b) -> p a b", a=4),
                   gLt[:, :, c:c + 1].bc([128, 4, 64]), ALU.mult)
            tt(Xm[:, :, :], b_i2[:, :, :].bc([128, 8, 64]), Q0, ALU.subtract)
            Pc, Qc = P0, Q0
            nxt = [(Pn, Qn), (P0, Q0)]
            for i in range(1, 6):
                Pd, Qd = nxt[(i - 1) % 2]
                for p in range(2):
                    rows = slice(64 * p, 64 * p + 64)
                    for h in range(8):
                        hc = slice(h * 64, (h + 1) * 64)
                        mm(PA[p][rows, hc], Qc[rows, h, :], Pc[rows, h, :])
                        if i < 5:
                            mm(PQ[p][rows, hc], Pc[rows, h, :], Qc[rows, h, :])
                for p in range(2):
                    rows = slice(64 * p, 64 * p + 64)
                    act(Pd[rows, :, :], PA[p][rows, :].rr("p (h e) -> p h e", h=8), AF.Copy)
                    if i < 5:
                        cp(Qd[rows, :, :], PQ[p][rows, :].rr("p (h e) -> p h e", h=8))
                Pc, Qc = Pd, Qd
                for p in range(2):
                    rows = slice(64 * p, 64 * p + 64)
                    for h in range(8):
                        mm(PX[p][rows, h * 64:(h + 1) * 64], Pc[rows, h, :], Xm[rows, h, :])
                for p in range(2):
                    rows = slice(64 * p, 64 * p + 64)
                    tt(Xm[rows, :, :], Xm[rows, :, :], PX[p][rows, :].rr("p (h e) -> p h e", h=8), ALU.add)
            for p in range(2):
                c = tb * 2 + p
                rows = slice(64 * p, 64 * p + 64)
                cols = slice(c * 64, (c + 1) * 64)
                for j in range(2):
                    jr = slice(64 * j, 64 * j + 64)
                    for g in range(4):
                        mm(BJ[j][rows, g * 64:(g + 1) * 64], AlT[jr, g, cols], b_S[l][jr, g, :])
                        mm(BJ[j][rows, 256 + g * 64:256 + (g + 1) * 64], RhT[jr, g, cols], b_S[l][jr, g, :])
                W24 = W2[rows, :, :].rr("p (g j) e -> p g j e", j=2)
                Ws4 = Wsb[rows, :, :].rr("p (g j) e -> p g j e", j=2)
                Y24 = Y2[rows, :, :].rr("p (g j) e -> p g j e", j=2)
                for j in range(2):
                    tt(Ws4[:, :, j, :], BJ[j][rows, 0:256].rr("p (a b) -> p a b", a=4), W24[:, :, j, :], ALU.add)
                    tt(tmp4[rows, :, j, :], BJ[j][rows, 256:512].rr("p (a b) -> p a b", a=4), Y24[:, :, j, :],
                       ALU.add)
                for h in range(8):
                    mm(PX[p][rows, h * 64:(h + 1) * 64], Xm[rows, h, :], Wsb[rows, h, :])
                act(Usb[rows, :, :], PX[p][rows, :].rr("p (h e) -> p h e", h=8), AF.Copy)
                for h in range(8):
                    g, j = h // 2, h % 2
                    mm(PA[p][rows, h * 64:(h + 1) * 64], NbT[rows, h, :], Usb[rows, h, :])
                    mm(PQ[p][64 * j:64 * j + 64, g * 64:(g + 1) * 64], Btm[rows, tb, h, :], Usb[rows, h, :])
                tt(wkv[rows, tb, :, :], tmp4[rows, :, :, :].rr("p g j e -> p (g j) e"),
                   PA[p][rows, :].rr("p (h e) -> p h e", h=8), ALU.subtract)
                tt(b_S[l][:, :, :], b_S[l][:, :, :], PQ[p][:, 0:256].rr("p (a b) -> p a b", a=4), ALU.subtract)
                tt(b_S[l][:, :, :], b_S[l][:, :, :], gLt[:, :, c:c + 1].bc([128, 4, 64]), ALU.mult)
                tt(b_S[l][:, :, :], b_S[l][:, :, :], K2g[:, p, :, :], ALU.add)
        ogw, ogb = CM.off["b_gw"][0], CM.off["b_gb"][0]
        for tb in range(NTB):
            head_ln(xn, wkv, tb)
            pb = next_pb()
            for g in range(4):
                tr(pb[:, g * 128:(g + 1) * 128], xn[:, tb, 2 * g:2 * g + 2, 0:64], ident[:, :])
            for g in range(4):
                tc_ = slice(tb * 128, (tb + 1) * 128)
                ts(scr[:, 0:128], pb[:, g * 128:(g + 1) * 128], cv[l][:, ogw + g:ogw + g + 1],
                   cv[l][:, ogb + g:ogb + g + 1], ALU.mult, ALU.add)
                tt(scr[:, 0:128], scr[:, 0:128], bon[:, g, tc_], ALU.add)
                tt(yT[:, 4 + g, tc_], scr[:, 0:128], sz[:, g, tc_], ALU.mult)

    def out_proj_residual(l, X):
        dma(gpost[l][:, :], gpost_d[l])
        for si in range(4):
            s = load_wo(l, si)
            for tb in range(NTB):
                for half in range(2):
                    pb = PB[4 + tb * 2 + half]
                    for q in range(4):
                        fc = si * 4 + q
                        mm(pb[:, :], yT[:, fc, tb * 128:(tb + 1) * 128], s[:, q * 2 + half, :],
                           start=(fc == 0), stop=(fc == 15))
        if cut < 4:
            return
        for tb in range(NTB):
            for half in range(2):
                act(scr[:, half * 512:(half + 1) * 512], PB[4 + tb * 2 + half][:, :], AF.Copy)
            if cut < 5:
                continue
            act(scr2[:, :], scr[:, :], AF.Square)
            reduce(sm[:, 24:25], scr2[:, :], ALU.add)
            ts(sm[:, 25:26], sm[:, 24:25], 1.0 / D_MODEL, 1e-6, ALU.mult, ALU.add)
            act(sm[:, 26:27], sm[:, 25:26], AF.Sqrt)
            recip(sm[:, 27:28], sm[:, 26:27])
            if cut < 6:
                continue
            stt(scr2[:, :], scr[:, :], sm[:, 27:28], gpost[l][:, :], ALU.mult, ALU.mult)
            if cut < 7:
                continue
            tt(X[:, tb, :], X[:, tb, :], scr2[:, :], ALU.add)

    xv = x_d.rearrange("(n tb p) d -> n p tb d", tb=NTB, p=128)
    yv = y_d.rearrange("(n tb p) d -> n p tb d", tb=NTB, p=128)
    for ti in range(ntiles):
        X = xt[0]
        dma(X[:, :, :], xv[ti])
        for l in range(nlayers):
            if cut >= 1:
                rmsnorm_pre(l, X)
            memset(yT[:, :, :], 0.0)
            if "C" in mixers and cut >= 2:
                mixer_C(l)
            if "D" in mixers:
                mixer_D(l, ti)
            if "A" in mixers:
                mixer_A(l, ti)
            if "B" in mixers:
                mixer_B(l, ti)
            if cut >= 3:
                out_proj_residual(l, X)
        dma(yv[ti], X[:, :, :])

    P.emit()
    return nc, stack


def kernel(**inputs):
    inputs = {k: np.asarray(v) for k, v in inputs.items()}
    return run(inputs)


def make_in_map(xc, hp, tb):
    m = dict(x=xc, wt=hp["wt"], wo=hp["wo"], cv=hp["cv"], lora=hp["lora"], lru=hp["lru"], gpost=hp["gpost"])
    m.update(tb)
    return m


def run(inputs, ntiles=SEQ // T, nlayers=DEPTH, mixers="ABCD"):
    hp = host_prepare(inputs)
    tb = host_tables()
    nc, stack = build_program(ntiles, nlayers, mixers)
    x = np.ascontiguousarray(inputs["x"], dtype=np.float32)
    in_maps = [make_in_map(x[c], hp, tb) for c in range(NCORES)]
    with stack:
        res = run_bass_kernel_spmd(nc, in_maps, core_ids=list(range(NCORES)))
    out = np.stack([np.asarray(r["y"]) for r in res.results], axis=0)
    return out.astype(np.float32)
```

```python
import contextlib
import numpy as np
import concourse.bass as bass
import concourse.mybir as mybir
from concourse.bass_utils import run_bass_kernel_spmd

F32 = mybir.dt.float32
BF16 = mybir.dt.bfloat16
AF = mybir.ActivationFunctionType
ALU = mybir.AluOpType
AX = mybir.AxisListType

D_MODEL = 1024
SEQ = 4096
BATCH = 4
DEPTH = 2
G = 512
D_IN = 7824
T = 256
NTB = T // 128
NCH = T // 64
NCORES = 4
NSLABBUF = 3


class Buf:
    def __init__(self, name, tile, psum=False):
        self.name = name
        self.tile = tile
        self.psum = psum
        self.acc = []
        self.dma_sem = None
        self.dma_ops = []

    def __getitem__(self, idx):
        return Ref(self, self.tile[idx])


def _box(ap):
    pat = ap.ap
    pstep = pat[0][0]
    off = int(ap.offset)
    p0 = off // pstep if pstep else 0
    f0 = off % pstep if pstep else off
    ext = 0
    for st, cnt in pat[1:]:
        ext += abs(st) * (cnt - 1)
    return (p0, p0 + pat[0][1], f0, f0 + ext + 1)


class Ref:
    def __init__(self, buf, ap, box=None):
        self.buf = buf
        self.ap = ap
        if box is None:
            box = _box(ap)
            if buf.psum:
                box = ((box[0] // 32) * 32, ((box[1] + 31) // 32) * 32, 0, 512)
        self.box = box

    def bc(self, shape):
        return Ref(self.buf, self.ap.to_broadcast(list(shape)), self.box)

    def __getitem__(self, idx):
        return Ref(self.buf, self.ap[idx])

    def rr(self, pat, **kw):
        return Ref(self.buf, self.ap.rearrange(pat, **kw), self.box)


def _ovl(a, b):
    return a[0] < b[1] and b[0] < a[1] and a[2] < b[3] and b[2] < a[3]


def _cov(a, b):
    return a[0] <= b[0] and a[1] >= b[1] and a[2] <= b[2] and a[3] >= b[3]


class Prog:
    ENG = ("pe", "act", "dve", "pool", "sp")

    def __init__(self, nc, stack):
        self.nc = nc
        self.stack = stack
        self.ops = []
        self.nbuf = 0

    def sb(self, name, shape, dtype=F32):
        t = self.stack.enter_context(self.nc.sbuf_tensor("s_" + name, list(shape), dtype))
        return Buf(name, t)

    def ps(self, name):
        t = self.stack.enter_context(self.nc.psum_tensor("p_" + name, [128, 512], F32))
        return Buf(name, t, psum=True)

    def op(self, eng, fn, outs, ins, dma=False):
        oid = len(self.ops)
        deps = set()
        for r, w in [(x, True) for x in outs] + [(x, False) for x in ins]:
            if r is None or not isinstance(r, Ref):
                continue
            b = r.buf
            for (bx, o2, w2) in b.acc:
                if (w or w2) and _ovl(bx, r.box):
                    deps.add(o2)
        for r, w in [(x, True) for x in outs] + [(x, False) for x in ins]:
            if r is None or not isinstance(r, Ref):
                continue
            b = r.buf
            if w:
                b.acc = [a for a in b.acc if not _cov(r.box, a[0])]
            b.acc.append((r.box, oid, w))
            if len(b.acc) > 48:
                merged = {}
                for (bx, o2, w2) in b.acc:
                    k = (self.ops[o2]["eng"] if o2 < oid else eng, w2)
                    if k in merged:
                        m = merged[k]
                        merged[k] = ((min(m[0][0], bx[0]), max(m[0][1], bx[1]), min(m[0][2], bx[2]),
                                      max(m[0][3], bx[3])), max(m[1], o2), w2)
                    else:
                        merged[k] = (bx, o2, w2)
                b.acc = list(merged.values())
        dbuf = None
        if dma:
            for r in list(outs) + list(ins):
                if isinstance(r, Ref):
                    dbuf = r.buf
            dbuf.dma_ops.append(oid)
        deps.discard(oid)
        self.ops.append(dict(eng=eng, fn=fn, deps=sorted(deps), dma=dma, dbuf=dbuf, sig=False))
        return oid

    def emit(self):
        nc = self.nc
        ops = self.ops
        for o in ops:
            for d in o["deps"]:
                od = ops[d]
                if od["dma"]:
                    continue
                if od["eng"] == o["eng"] and o["eng"] == "pe" and not o["dma"]:
                    continue
                od["sig"] = True
        sems = {e: self.stack.enter_context(nc.semaphore("sem_" + e)) for e in self.ENG}
        cnt = {e: 0 for e in self.ENG}
        for o in ops:
            if o["dma"]:
                b = o["dbuf"]
                if b.dma_sem is None:
                    b.dma_sem = self.stack.enter_context(nc.semaphore("dsem_" + b.name))
            elif o["sig"]:
                cnt[o["eng"]] += 1
                o["signo"] = cnt[o["eng"]]
        per_eng = {e: [] for e in self.ENG}
        for i, o in enumerate(ops):
            per_eng[o["eng"]].append(i)
        import bisect

        def gen(engname, e):
            waited = {}
            for i in per_eng[engname]:
                o = ops[i]
                need = {}
                for d in o["deps"]:
                    od = ops[d]
                    if od["dma"]:
                        b = od["dbuf"]
                        n = bisect.bisect_left(b.dma_ops, i)
                        key = ("d", id(b))
                        if need.get(key, (None, 0))[1] < 16 * n:
                            need[key] = (b.dma_sem, 16 * n)
                    else:
                        if od["eng"] == engname and engname == "pe" and not o["dma"]:
                            continue
                        key = ("e", od["eng"])
                        if need.get(key, (None, 0))[1] < od["signo"]:
                            need[key] = (sems[od["eng"]], od["signo"])
                for key, (sem, val) in need.items():
                    if waited.get(key, 0) >= val:
                        continue
                    e.wait_ge(sem, val)
                    waited[key] = val
                ins = o["fn"](e)
                if o["dma"]:
                    ins.then_inc(o["dbuf"].dma_sem, 16)
                elif o["sig"]:
                    ins.then_inc(sems[engname], 1)

        with nc.Block() as block:
            @block.tensor
            def _(e):
                gen("pe", e)

            @block.scalar
            def _(e):
                gen("act", e)

            @block.vector
            def _(e):
                gen("dve", e)

            @block.gpsimd
            def _(e):
                gen("pool", e)

            @block.sync
            def _(e):
                gen("sp", e)
                seen = set()
                for o in ops:
                    if o["dma"] and id(o["dbuf"]) not in seen:
                        seen.add(id(o["dbuf"]))
                        e.wait_ge(o["dbuf"].dma_sem, 16 * len(o["dbuf"].dma_ops))


def _a(x):
    return x.ap if isinstance(x, Ref) else x


def _fm4(v):
    return np.ascontiguousarray(v.reshape(-1, 128).T)


class CMap:
    def __init__(self):
        self.off = {}
        self.n = 0

    def add(self, name, w):
        self.off[name] = (self.n, w)
        self.n += w


def build_cmap():
    c = CMap()
    c.add("gpre", 8)
    c.add("a_cw", 32); c.add("a_cb", 8); c.add("a_ib", 1); c.add("a_fb", 1); c.add("a_nw", 4)
    c.add("b_mu", 13); c.add("b_w0", 4); c.add("b_a0", 4); c.add("b_kk", 4); c.add("b_ka", 4)
    c.add("b_rk", 4); c.add("b_gw", 4); c.add("b_gb", 4)
    c.add("c_cw", 16); c.add("c_cb", 4); c.add("c_br", 4); c.add("c_bi", 4); c.add("c_lam", 4)
    c.add("d_nw", 4)
    return c


CM = build_cmap()

def _cols(a, b):
    return list(range(a, b))


def build_slabs():
    slabs = []
    def fm(name, start):
        slabs.append((name, _cols(start, start + 512)))
    fm("Aq", 0); fm("Ak", 512); fm("Az", 2048)
    slabs.append(("Ag", _cols(2560, 2576) + [-1] * (512 - 16)))
    fm("Br", 2576); fm("Bk", 3088); fm("Bv", 3600)
    slabs.append(("Bl", _cols(4112, 4240) + [-1] * (512 - 128)))
    fm("Bz", 4240); fm("Cx", 4752); fm("Cz", 5264); fm("Dz", 7312)
    fm("Av", 1024); fm("Ao", 1536); fm("Dq", 5776); fm("Dk", 6288); fm("Dv", 6800)
    return slabs


SLABS = build_slabs()
SLAB_ID = {s[0]: i for i, s in enumerate(SLABS)}
NSLAB = len(SLABS)


def host_prepare(inp):
    f = np.float32
    w_in = inp["w_in"]
    w_in_p = np.concatenate([w_in, np.zeros((DEPTH, D_MODEL, 1), f)], axis=2)
    wt = np.empty((DEPTH, NSLAB, 128, 8, 512), f)
    for si, (name, cols) in enumerate(SLABS):
        blk = w_in_p[:, :, cols]
        wt[:, si] = blk.reshape(DEPTH, 8, 128, 512).transpose(0, 2, 1, 3)
    w_out = inp["w_out"]
    wo = np.ascontiguousarray(w_out.reshape(DEPTH, 4, 4, 128, 1024).transpose(0, 1, 3, 2, 4))
    cv = np.zeros((DEPTH, 128, CM.n), f)
    def put(name, arr):
        o, w = CM.off[name]
        cv[:, :, o:o + w] = arr
    put("gpre", inp["norm_pre"].reshape(DEPTH, 8, 128).transpose(0, 2, 1))
    cw = inp["mlstm_conv_w"]
    put("a_cw", cw.reshape(DEPTH, 4, 8, 128).transpose(0, 3, 1, 2).reshape(DEPTH, 128, 32))
    put("a_cb", inp["mlstm_conv_b"].reshape(DEPTH, 8, 128).transpose(0, 2, 1))
    ib = np.zeros((DEPTH, 128, 1), f); ib[:, 0:8, 0] = inp["mlstm_i_bias"]; put("a_ib", ib)
    fb = np.zeros((DEPTH, 128, 1), f); fb[:, 0:8, 0] = inp["mlstm_f_bias"]; put("a_fb", fb)
    def fm4(x):
        return x.reshape(DEPTH, -1, 128).transpose(0, 2, 1)
    put("a_nw", fm4(inp["mlstm_norm_w"]))
    mu = inp["rwkv_mu"]
    put("b_mu", fm4(mu))
    for k, nm in [("b_w0", "rwkv_w0"), ("b_a0", "rwkv_a0"), ("b_kk", "rwkv_k_k"), ("b_ka", "rwkv_k_a"),
                  ("b_rk", "rwkv_r_k"), ("b_gw", "rwkv_gn_w"), ("b_gb", "rwkv_gn_b"),
                  ("c_cb", "lru_conv_b"), ("c_br", "lru_b_r"), ("c_bi", "lru_b_i"), ("c_lam", "lru_lambda"),
                  ("d_nw", "ret_norm_w")]:
        put(k, fm4(inp[nm]))
    lw = inp["lru_conv_w"]
    put("c_cw", lw.reshape(DEPTH, 4, 4, 128).transpose(0, 3, 1, 2).reshape(DEPTH, 128, 16))
    lora = np.concatenate([inp["rwkv_w_up"], inp["rwkv_a_up"]], axis=1)
    lru = np.zeros((DEPTH, 128, 2, 4, 128), f)
    for which, nm in enumerate(["lru_w_r", "lru_w_i"]):
        w = inp[nm]
        for g in range(4):
            for j in range(2):
                lru[:, 64 * j:64 * j + 64, which, g, 64 * j:64 * j + 64] = w[:, 2 * g + j]
    gpost = np.ascontiguousarray(np.broadcast_to(inp["norm_post"][:, None, :], (DEPTH, 128, D_MODEL)))
    return dict(wt=wt, wo=wo, cv=cv, lora=np.ascontiguousarray(lora), lru=lru, gpost=gpost)


def host_tables():
    f = np.float32
    t = {}
    t["ident"] = np.eye(128, dtype=f)
    hd = 64
    log_g = np.log1p(-np.exp2(-5.0 - np.arange(8, dtype=np.float64)))
    idx = np.arange(64, dtype=np.float64)
    dmat = np.exp(log_g[:, None, None] * np.abs(idx[:, None] - idx[None, :])) * hd ** -0.5
    xi = np.exp(log_g[:, None] * (idx + 1.0))
    zeta = np.exp(log_g[:, None] * (63.0 - idx)) * hd ** -0.5
    gch = np.exp(log_g * 64.0)
    hs2h = [2 * (s % 4) + (s // 4) for s in range(8)]
    dm = np.zeros((128, 8, 64));
    for s in range(8):
        dm[0:64, s] = dmat[hs2h[s]]; dm[64:128, s] = dmat[hs2h[s]]
    t["d_dmat"] = dm.astype(f)
    t["d_xi"] = np.tile(xi.T, (2, 1)).astype(f)
    t["d_zeta"] = np.tile(zeta.T, (2, 1)).astype(f)
    gc = np.zeros((128, 4))
    for g in range(4):
        for j in range(2):
            gc[64 * j:64 * j + 64, g] = gch[2 * g + j]
    t["d_gch"] = gc.astype(f)
    mk = np.zeros((128, 8, 64))
    sidx = np.arange(128) % 64
    mk[:] = np.where(sidx[:, None, None] <= np.arange(64)[None, None, :], 0.0, -30000.0)
    t["a_mask"] = mk.reshape(128, 512).astype(f)
    sel = np.zeros((8, 512 + 4 + 128))
    for hp_ in range(8):
        for hs in range(8):
            if hp_ == hs2h[hs]:
                sel[hp_, hs * 64:(hs + 1) * 64] = 1.0
        sel[hp_, 512 + hp_ // 2] = 1.0
        jj = hp_ % 2
        sel[hp_, 516 + 64 * jj:516 + 64 * jj + 64] = 1.0
    t["a_sel"] = sel.astype(f)
    bo = np.zeros((128, 128)); bo[0:64, 0:64] = 1.0; bo[64:128, 64:128] = 1.0
    t["b_ones"] = bo.astype(f)
    i2 = np.zeros((128, 64)); i2[0:64] = np.eye(64); i2[64:128] = np.eye(64)
    t["b_i2"] = i2.astype(f)
    rr_ = (np.arange(128) % 64)[:, None]; cc_ = np.arange(64)[None, :]
    m5 = np.zeros((128, 5, 64))
    m5[:, 0] = rr_ > cc_; m5[:, 1] = cc_ > rr_; m5[:, 2] = cc_ > rr_; m5[:, 3] = cc_ >= rr_; m5[:, 4] = cc_ >= rr_
    t["b_m5"] = m5.astype(f)
    half = 32
    pos = np.arange(SEQ, dtype=np.float32)
    inv_freq = (np.float32(10000.0) ** (-np.arange(half, dtype=np.float32) / np.float32(half))).astype(np.float32)
    ang = (pos[:, None] * inv_freq[None, :]).astype(np.float32).astype(np.float64)
    t["rope"] = np.concatenate([np.cos(ang), np.sin(ang)], axis=1).astype(f)
    return t


def build_program(ntiles=SEQ // T, nlayers=DEPTH, mixers="ABCD"):
    nc = bass.Bass("TRN2", target_bir_lowering=False)
    stack = contextlib.ExitStack()
    P = Prog(nc, stack)
    dram = {}

    def din(name, shape):
        dram[name] = nc.dram_tensor(name, list(shape), F32, kind="ExternalInput").ap()
        return dram[name]

    x_d = din("x", [SEQ, D_MODEL])
    wt_d = din("wt", [DEPTH, NSLAB, 128, 8, 512])
    wo_d = din("wo", [DEPTH, 4, 128, 4, 1024])
    cv_d = din("cv", [DEPTH, 128, CM.n])
    lora_d = din("lora", [DEPTH, 128, 512])
    lru_d = din("lru", [DEPTH, 128, 2, 4, 128])
    gpost_d = din("gpost", [DEPTH, 128, D_MODEL])
    ident_d = din("ident", [128, 128])
    dmat_d = din("d_dmat", [128, 8, 64])
    dxi_d = din("d_xi", [128, 8])
    dzeta_d = din("d_zeta", [128, 8])
    dgch_d = din("d_gch", [128, 4])
    rope_d = din("rope", [SEQ, 64])
    bones_d = din("b_ones", [128, 128])
    bi2_d = din("b_i2", [128, 64])
    bm5_d = din("b_m5", [128, 5, 64])
    amask_d = din("a_mask", [128, 512])
    asel_d = din("a_sel", [8, 644])
    y_d = nc.dram_tensor("y", [SEQ, D_MODEL], F32, kind="ExternalOutput").ap()

    ident = P.sb("ident", [128, 128])
    cv = [P.sb(f"cv{l}", [128, CM.n]) for l in range(DEPTH)]
    lru1 = P.sb("lru", [128, 2, 4, 128]); lru = [lru1, lru1]
    gpost1 = P.sb("gpost", [128, D_MODEL]); gpost = [gpost1, gpost1]
    lora_up = [P.sb(f"lora_up{l}", [128, 512]) for l in range(DEPTH)]
    b_ones = P.sb("b_ones", [128, 128]); b_i2 = P.sb("b_i2", [128, 1, 64]); b_m5 = P.sb("b_m5", [128, 5, 64])
    b_hist = [P.sb(f"b_hist{l}", [128, 13]) for l in range(DEPTH)]
    b_S = [P.sb(f"b_S{l}", [128, 4, 64]) for l in range(DEPTH)]
    gLt = P.sb("gLt", [128, 4, NCH]); PRall = [P.sb(f"PR{i}", [128, 5, 8, 64]) for i in range(NTB)]; Y2 = P.sb("Y2", [128, 8, 64])
    K2g = P.sb("K2g", [128, 2, 4, 64])
    ones = P.sb("ones", [128, T]); zeros = P.sb("zeros", [128, T])
    xt = [P.sb(f"xt{i}", [128, NTB, D_MODEL]) for i in range(1)]
    hT = P.sb("hT", [128, 8, T], BF16)
    yT = P.sb("yT", [128, 16, T], BF16)
    slab = [P.sb(f"slab{i}", [128, 8, 512], BF16) for i in range(NSLABBUF)]
    sm = P.sb("small", [128, 64])
    scr = P.sb("scr", [128, D_MODEL])
    scr2 = P.sb("scr2", [128, D_MODEL])
    FS = [P.sb(f"fs{i}", [128, 4, T + 4]) for i in range(7)]
    c_hist = [P.sb(f"c_hist{l}", [128, 4, 3]) for l in range(DEPTH)]
    c_state = [P.sb(f"c_state{l}", [128, 4]) for l in range(DEPTH)]
    c_coef = [P.sb(f"c_coef{l}", [128, 8]) for l in range(DEPTH)]
    PB = [P.ps(f"pb{i}") for i in range(8)]
    TM = [P.sb(f"tm{i}", [128, NTB, 8, 64]) for i in range(6)]
    d_dmat = P.sb("d_dmat", [128, 8, 64]); d_xi = P.sb("d_xi", [128, 4, 2, 1]); d_zeta = P.sb("d_zeta", [128, 8, 1])
    d_gch = P.sb("d_gch", [128, 4, 1]); rope_sb = P.sb("rope_sb", [128, NTB, 1, 64])
    d_R = [P.sb(f"d_R{l}", [128, 4, 64]) for l in range(DEPTH)]
    a_mask = P.sb("a_mask", [128, 512]); a_sel = P.sb("a_sel", [8, 644])
    ga = P.sb("ga", [8, 12, T + 1]); scl = P.sb("scl", [8, NCH]); rhs_sc = P.sb("rhs_sc", [8, NCH, 4])
    scb = P.sb("scb", [128, NCH, 4, 1]); negPx = P.sb("negPx", [8, 2, 8, 64]); ET = P.sb("ET", [128, 8, 64])
    tmsc = P.sb("tmsc", [128, NTB, 3, 8]); hnum = P.sb("hnum", [128, 8, 64]); vw = P.sb("vw", [128, 8, 64])
    a_hist = [P.sb(f"a_hist{l}", [128, 8, 3]) for l in range(DEPTH)]
    a_C = [P.sb(f"a_C{l}", [128, 4, 64]) for l in range(DEPTH)]
    a_n = [P.sb(f"a_n{l}", [128, 4, 1]) for l in range(DEPTH)]
    a_car = [P.sb(f"a_car{l}", [8, 4]) for l in range(DEPTH)]
    AT = P.sb("AT", [128, 8, 64]); tmp4 = P.sb("tmp4", [128, 4, 2, 64]); lnst = P.sb("lnst", [128, 64])
    state = dict(slab_i=0, pb_i=0)

    import os
    cut = int(os.environ.get("KCUT", "9"))
    def dma(out, in_, eng="sp"):
        return P.op(eng, lambda e: e.dma_start(out=_a(out), in_=_a(in_)), [out], [in_], dma=True)

    def mm(out, lhsT, rhs, start=True, stop=True):
        P.op("pe", lambda e: e.matmul(_a(out), _a(lhsT), _a(rhs), start=start, stop=stop),
             [out], [lhsT, rhs] + ([] if start else [out]))

    def tr(out, in_, idn):
        P.op("pe", lambda e: e.transpose(_a(out), _a(in_), _a(idn)), [out], [in_, idn])

    def act(out, in_, func, bias=None, scale=1.0, accum=None, eng="act"):
        kw = {}
        if bias is not None:
            kw["bias"] = _a(bias)
        if accum is not None:
            kw["accum_out"] = _a(accum)
        P.op("act", lambda e: e.activation(_a(out), _a(in_), func, scale=_a(scale), **kw),
             [out, accum], [in_, bias, scale])

    def tt(out, a, b, op, eng="dve"):
        P.op(eng, lambda e: e.tensor_tensor(_a(out), _a(a), _a(b), op), [out], [a, b])

    def ts(out, a, s1, s2, op0, op1=ALU.bypass, eng="dve"):
        P.op(eng, lambda e: e.tensor_scalar(_a(out), _a(a), _a(s1), _a(s2), op0, op1), [out], [a, s1, s2])

    def stt(out, a, s, b, op0, op1):
        P.op("dve", lambda e: e.scalar_tensor_tensor(_a(out), _a(a), _a(s), _a(b), op0, op1), [out], [a, s, b])

    def scan(out, d0, d1, init, op0, op1):
        P.op("dve", lambda e: e.tensor_tensor_scan(_a(out), _a(d0), _a(d1), _a(init), op0, op1),
             [out], [d0, d1, init])

    def cp(out, in_, eng="dve"):
        P.op(eng, lambda e: e.tensor_copy(_a(out), _a(in_)), [out], [in_])

    def recip(out, in_):
        P.op("dve", lambda e: e.reciprocal(_a(out), _a(in_)), [out], [in_])

    def memset(out, val, eng="dve"):
        P.op(eng, lambda e: e.memset(_a(out), val), [out], [])

    def reduce(out, in_, op, axis=AX.X):
        P.op("dve", lambda e: e.tensor_reduce(_a(out), _a(in_), axis, op), [out], [in_])

    def next_pb():
        state["pb_i"] = (state["pb_i"] + 1) % 2
        return PB[state["pb_i"]]

    def load_slab(l, name, ncols=512):
        s = slab[state["slab_i"]]
        state["slab_i"] = (state["slab_i"] + 1) % NSLABBUF
        dma(s[:, :, 0:ncols], wt_d[l, SLAB_ID[name], :, :, 0:ncols], eng="pool")
        return s

    def load_wo(l, si):
        s = slab[state["slab_i"]]
        state["slab_i"] = (state["slab_i"] + 1) % NSLABBUF
        dma(s[:, :, :], wo_d[l, si].rearrange("p f n -> p (f n)").rearrange("p (a b) -> p a b", a=8), eng="pool")
        return s

    def cvc(l, name, i=0):
        o, w = CM.off[name]
        return cv[l][:, o + i:o + i + 1]

    def fm_proj(s, gi, out_cols=T):
        pb = next_pb()
        for kc in range(8):
            mm(pb[:, 0:T], s[:, kc, gi * 128:(gi + 1) * 128], hT[:, kc, :], start=(kc == 0), stop=(kc == 7))
        return pb

    dma(ident[:, :], ident_d)
    for l in range(DEPTH):
        dma(cv[l][:, :], cv_d[l])
        dma(lora_up[l][:, :], lora_d[l])
        memset(b_hist[l][:, :], 0.0)
        memset(b_S[l][:, :, :], 0.0)
    dma(d_dmat[:, :, :], dmat_d)
    dma(d_xi[:, :, :, :], dxi_d.rearrange("p (g j o) -> p g j o", g=4, j=2))
    dma(d_zeta[:, :, :], dzeta_d.rearrange("p (h o) -> p h o", o=1))
    dma(d_gch[:, :, :], dgch_d.rearrange("p (g o) -> p g o", o=1))
    dma(a_mask[:, :], amask_d)
    dma(a_sel[:, :], asel_d)
    for l in range(DEPTH):
        memset(d_R[l][:, :, :], 0.0)
        memset(a_hist[l][:, :, :], 0.0)
        memset(a_C[l][:, :, :], 0.0)
        memset(a_n[l][:, :, :], 0.0)
        memset(a_car[l][:, :], 0.0)
        ts(a_car[l][0:8, 2:3], cv[l][0:8, CM.off["a_fb"][0]:CM.off["a_fb"][0] + 1], -1.0, None, ALU.mult)
    dma(b_ones[:, :], bones_d)
    dma(b_i2[:, 0, :], bi2_d)
    dma(b_m5[:, :, :], bm5_d)
    memset(ones[:, :], 1.0)
    memset(zeros[:, :], 0.0)
    for l in range(DEPTH):
        memset(c_hist[l][:, :, :], 0.0)
        memset(c_state[l][:, :], 0.0)
        o, w = CM.off["c_lam"]
        act(sm[:, 0:4], cv[l][:, o:o + 4], AF.Exp, scale=-1.0)
        act(sm[:, 4:8], sm[:, 0:4], AF.Ln, bias=ones[:, 0:1])
        ts(c_coef[l][:, 0:4], sm[:, 4:8], -8.0, None, ALU.mult)
        ts(c_coef[l][:, 4:8], sm[:, 4:8], -16.0, None, ALU.mult)

    def rmsnorm_pre(l, X):
        for tb in range(NTB):
            act(scr[:, :], X[:, tb, :], AF.Square)
            reduce(sm[:, 8 + tb:9 + tb], scr[:, :], ALU.add)
        ts(sm[:, 12:12 + NTB], sm[:, 8:8 + NTB], 1.0 / D_MODEL, 1e-6, ALU.mult, ALU.add)
        act(sm[:, 16:16 + NTB], sm[:, 12:12 + NTB], AF.Sqrt)
        recip(sm[:, 20:20 + NTB], sm[:, 16:16 + NTB])
        for tb in range(NTB):
            act(scr[:, :], X[:, tb, :], AF.Copy, scale=sm[:, 20 + tb:21 + tb])
            for half in range(2):
                pb = next_pb()
                for q in range(4):
                    kc = half * 4 + q
                    tr(pb[:, q * 128:(q + 1) * 128], scr[:, kc * 128:(kc + 1) * 128], ident[:, :])
                for q in range(4):
                    kc = half * 4 + q
                    ts(hT[:, kc, tb * 128:(tb + 1) * 128], pb[:, q * 128:(q + 1) * 128],
                       cvc(l, "gpre", kc), None, ALU.mult)

    def mixer_C(l):
        cx, xc, rg, ig, aa, uu, hh = FS[0], FS[1], FS[2], FS[3], FS[4], FS[5], FS[6]
        dma(lru[l][:, :, :, :], lru_d[l])
        wx = load_slab(l, "Cx")
        for g in range(4):
            pb = fm_proj(wx, g)
            cp(cx[:, g, 0:3], c_hist[l][:, g, :])
            act(cx[:, g, 3:3 + T], pb[:, 0:T], AF.Copy)
            cp(c_hist[l][:, g, :], cx[:, g, T:T + 3])
            o, w = CM.off["c_cw"]
            ts(xc[:, g, 0:T], cx[:, g, 0:T], cv[l][:, o + g:o + g + 1], cvc(l, "c_cb", g), ALU.mult, ALU.add)
            for j in range(1, 4):
                stt(xc[:, g, 0:T], cx[:, g, j:j + T], cv[l][:, o + 4 * j + g:o + 4 * j + g + 1], xc[:, g, 0:T],
                    ALU.mult, ALU.add)
        for g in range(4):
            pb = next_pb()
            mm(pb[:, 0:T], lru[l][:, 0, g, :], xc[:, g, 0:T])
            act(rg[:, g, 0:T], pb[:, 0:T], AF.Sigmoid, bias=cvc(l, "c_br", g))
            pb = next_pb()
            mm(pb[:, 0:T], lru[l][:, 1, g, :], xc[:, g, 0:T])
            act(ig[:, g, 0:T], pb[:, 0:T], AF.Sigmoid, bias=cvc(l, "c_bi", g))
        for g in range(4):
            act(aa[:, g, 0:T], rg[:, g, 0:T], AF.Exp, scale=c_coef[l][:, g:g + 1])
            act(uu[:, g, 0:T], rg[:, g, 0:T], AF.Exp, scale=c_coef[l][:, 4 + g:5 + g])
        ts(uu[:, :, 0:T], uu[:, :, 0:T], -1.0, 1.0, ALU.mult, ALU.add)
        act(uu[:, :, 0:T], uu[:, :, 0:T], AF.Sqrt)
        tt(ig[:, :, 0:T], ig[:, :, 0:T], xc[:, :, 0:T], ALU.mult)
        tt(uu[:, :, 0:T], uu[:, :, 0:T], ig[:, :, 0:T], ALU.mult)
        for g in range(4):
            scan(hh[:, g, 0:T], aa[:, g, 0:T], uu[:, g, 0:T], c_state[l][:, g:g + 1], ALU.mult, ALU.add)
            cp(c_state[l][:, g:g + 1], hh[:, g, T - 1:T])
        wz = load_slab(l, "Cz")
        for g in range(4):
            pb = fm_proj(wz, g)
            act(rg[:, g, 0:T], pb[:, 0:T], AF.Silu)
            tt(yT[:, 8 + g, :], hh[:, g, 0:T], rg[:, g, 0:T], ALU.mult)

    def tm_proj(sl, tb):
        pb = next_pb()
        for kc in range(8):
            mm(pb[:, :], hT[:, kc, tb * 128:(tb + 1) * 128], sl[:, kc, :], start=(kc == 0), stop=(kc == 7))
        return pb

    def to_fm(dst, src, tb):
        pb = next_pb()
        for g in range(4):
            tr(pb[:, g * 128:(g + 1) * 128], src[:, tb, 2 * g:2 * g + 2, 0:64], ident[:, :])
        cp(dst[:, 0:4, tb * 128:(tb + 1) * 128], pb[:, :].rr("p (a b) -> p a b", a=4))

    def to_tm(dst, src, tb):
        pb = next_pb()
        for g in range(4):
            tr(pb[:, g * 128:(g + 1) * 128], src[:, g, tb * 128:(tb + 1) * 128], ident[:, :])
        cp(dst[:, tb, :, 0:64], pb[:, :].rr("p (h e) -> p h e", h=8))

    def head_ln(dst, src, tb):
        X3 = src[:, tb, :, 0:64]
        D3 = dst[:, tb, :, 0:64]
        reduce(lnst[:, 0:8], X3, ALU.add)
        stt(D3, lnst[:, 0:8].rr("p (h o) -> p h o", o=1).bc([128, 8, 64]), -1.0 / 64, X3, ALU.mult, ALU.add)
        tt(scr[:, 0:512].rr("p (h e) -> p h e", h=8), D3, D3, ALU.mult)
        reduce(lnst[:, 8:16], scr[:, 0:512].rr("p (h e) -> p h e", h=8), ALU.add)
        ts(lnst[:, 16:24], lnst[:, 8:16], 1.0 / 64, 1e-5, ALU.mult, ALU.add)
        act(lnst[:, 24:32], lnst[:, 16:24], AF.Sqrt)
        recip(lnst[:, 32:40], lnst[:, 24:32])
        tt(D3, D3, lnst[:, 32:40].rr("p (h o) -> p h o", o=1).bc([128, 8, 64]), ALU.mult)

    def mixer_D(l, ti):
        qr, kr, vv, kz, osb, xn = TM[0], TM[1], TM[2], TM[3], TM[4], TM[5]
        qT, kT, sz = FS[0], FS[1], FS[2]
        S = [PB[3], PB[4]]; O = [PB[5], PB[6]]; ST = [PB[2], PB[7]]
        dma(rope_sb[:, :, :, :], rope_d.rearrange("(n tb p) (o e) -> n p tb o e", tb=NTB, p=128, o=1)[ti])
        for name, dst in (("Dq", qr), ("Dk", kr)):
            sl = load_slab(l, name)
            for tb in range(NTB):
                pb = tm_proj(sl, tb)
                act(scr[:, 0:512], pb[:, :], AF.Copy)
                raw = scr[:, 0:512].rr("p (h e) -> p h e", h=8)
                x1, x2 = raw[:, :, 0:32], raw[:, :, 32:64]
                cos = rope_sb[:, tb, :, 0:32].bc([128, 8, 32]); sin = rope_sb[:, tb, :, 32:64].bc([128, 8, 32])
                t1 = scr2[:, 0:256].rr("p (h e) -> p h e", h=8); t2 = scr2[:, 256:512].rr("p (h e) -> p h e", h=8)
                d1, d2 = dst[:, tb, :, 0:32], dst[:, tb, :, 32:64]
                tt(d1, x1, cos, ALU.mult); tt(t1, x2, sin, ALU.mult); tt(d1, d1, t1, ALU.subtract)
                tt(d2, x2, cos, ALU.mult); tt(t2, x1, sin, ALU.mult); tt(d2, d2, t2, ALU.add)
                to_fm(qT if name == "Dq" else kT, dst, tb)
        sl = load_slab(l, "Dv")
        for tb in range(NTB):
            pb = tm_proj(sl, tb)
            act(vv[:, tb, :, 0:64], pb[:, :].rr("p (h e) -> p h e", h=8), AF.Copy)
            tt(kz[:, tb, :, 0:64], kr[:, tb, :, 0:64], d_zeta[:, :, :].bc([128, 8, 64]), ALU.mult)
        sl = load_slab(l, "Dz")
        for g in range(4):
            pb = fm_proj(sl, g)
            act(sz[:, g, 0:T], pb[:, 0:T], AF.Silu)
        for c in range(NCH):
            tb, p = c // 2, c % 2
            rows = slice(64 * p, 64 * p + 64)
            cols = slice(c * 64, (c + 1) * 64)
            for j in range(2):
                jr = slice(64 * j, 64 * j + 64)
                for g in range(4):
                    mm(S[j][rows, g * 64:(g + 1) * 64], kT[jr, g, cols], qT[jr, g, cols])
                    mm(S[j][rows, 256 + g * 64:256 + (g + 1) * 64], qT[jr, g, cols], d_R[l][jr, g, :])
            for j in range(2):
                tt(AT[rows, 4 * j:4 * j + 4, :], S[j][rows, 0:256].rr("p (a b) -> p a b", a=4),
                   d_dmat[rows, 4 * j:4 * j + 4, :], ALU.mult)
                tt(tmp4[rows, :, j, 0:64], S[j][rows, 256:512].rr("p (a b) -> p a b", a=4),
                   d_xi[rows, :, j, :].bc([64, 4, 64]), ALU.mult)
            for h in range(8):
                g, j = h // 2, h % 2
                mm(O[p][rows, h * 64:(h + 1) * 64], AT[rows, j * 4 + g, :], vv[rows, tb, h, 0:64])
            tt(osb[rows, tb, :, 0:64], O[p][rows, :].rr("p (h e) -> p h e", h=8),
               tmp4[rows, :, :, 0:64].rr("p g j e -> p (g j) e"), ALU.add)
            for h in range(8):
                g, j = h // 2, h % 2
                mm(ST[p][64 * j:64 * j + 64, g * 64:(g + 1) * 64], kz[rows, tb, h, 0:64], vv[rows, tb, h, 0:64])
            tt(d_R[l][:, :, :], d_R[l][:, :, :], d_gch[:, :, :].bc([128, 4, 64]), ALU.mult)
            tt(d_R[l][:, :, :], d_R[l][:, :, :], ST[p][:, 0:256].rr("p (a b) -> p a b", a=4), ALU.add)
        for tb in range(NTB):
            head_ln(xn, osb, tb)
            pb = next_pb()
            for g in range(4):
                tr(pb[:, g * 128:(g + 1) * 128], xn[:, tb, 2 * g:2 * g + 2, 0:64], ident[:, :])
            for g in range(4):
                stt(yT[:, 12 + g, tb * 128:(tb + 1) * 128], pb[:, g * 128:(g + 1) * 128], cvc(l, "d_nw", g),
                    sz[:, g, tb * 128:(tb + 1) * 128], ALU.mult, ALU.mult)

    def conv_silu(l, sl, g8, dst, gdst, cx, whichhist, cwname, cbname, ngroups_total, silu=True):
        pass

    def mixer_A(l, ti):
        cx, qT, kT, sz = FS[0], FS[1], FS[2], FS[3]
        ktm, vv, so, osb, xn = TM[0], TM[1], TM[2], TM[3], TM[4]
        S = [PB[3], PB[4]]; O = [PB[5], PB[6]]; ST = [PB[2], PB[7]]; JX = [PB[0], PB[1]]
        R_I, R_SP, R_F, R_G, R_P, R_MU, R_PE, R_SI, R_WJ, R_NM, R_NP = range(11)
        ocw = CM.off["a_cw"][0]
        for which, (name, dstT) in enumerate((("Aq", qT), ("Ak", kT))):
            sl = load_slab(l, name)
            for g in range(4):
                g8 = which * 4 + g
                pb = fm_proj(sl, g)
                cp(cx[:, g, 0:3], a_hist[l][:, g8, :])
                act(cx[:, g, 3:3 + T], pb[:, 0:T], AF.Copy)
                cp(a_hist[l][:, g8, :], cx[:, g, T:T + 3])
                ts(dstT[:, g, 0:T], cx[:, g, 0:T], cv[l][:, ocw + g8:ocw + g8 + 1], cvc(l, "a_cb", g8),
                   ALU.mult, ALU.add)
                for j in range(1, 4):
                    stt(dstT[:, g, 0:T], cx[:, g, j:j + T], cv[l][:, ocw + 8 * j + g8:ocw + 8 * j + g8 + 1],
                        dstT[:, g, 0:T], ALU.mult, ALU.add)
                act(dstT[:, g, 0:T], dstT[:, g, 0:T], AF.Silu)
        sl = load_slab(l, "Az")
        for g in range(4):
            pb = fm_proj(sl, g)
            act(sz[:, g, 0:T], pb[:, 0:T], AF.Silu)
        acut = int(os.environ.get("ACUT", "99"))
        if acut < 1:
            return
        sl = load_slab(l, "Ag", ncols=128)
        pbi = next_pb()
        for kc in range(8):
            mm(pbi[0:8, 0:T], sl[:, kc, 0:8], hT[:, kc, :], start=(kc == 0), stop=(kc == 7))
        act(ga[0:8, R_I, 0:T], pbi[0:8, 0:T], AF.Identity, bias=cv[l][0:8, CM.off["a_ib"][0]:CM.off["a_ib"][0] + 1])
        gcut = int(os.environ.get("GCUT", "99"))
        if gcut < 1:
            return
        pbf = next_pb()
        for kc in range(8):
            mm(pbf[0:8, 0:T], sl[:, kc, 8:16], hT[:, kc, :], start=(kc == 0), stop=(kc == 7))
        act(ga[0:8, R_SP, 0:T], pbf[0:8, 0:T], AF.Exp, bias=a_car[l][0:8, 2:3], scale=-1.0)
        act(ga[0:8, R_SP, 0:T], ga[0:8, R_SP, 0:T], AF.Ln, bias=ones[0:8, 0:1])
        if gcut < 2:
            return
        scan(ga[0:8, R_F, 0:T], ones[0:8, 0:T], ga[0:8, R_SP, 0:T], a_car[l][0:8, 0:1], ALU.mult, ALU.subtract)
        cp(a_car[l][0:8, 0:1], ga[0:8, R_F, T - 1:T])
        tt(ga[0:8, R_G, 0:T], ga[0:8, R_I, 0:T], ga[0:8, R_F, 0:T], ALU.subtract)
        if gcut < 3:
            return
        cp(ga[0:8, R_P, 0:1], a_car[l][0:8, 1:2])
        scan(ga[0:8, R_P, 1:T + 1], ones[0:8, 0:T], ga[0:8, R_G, 0:T], a_car[l][0:8, 1:2], ALU.mult, ALU.max)
        cp(a_car[l][0:8, 1:2], ga[0:8, R_P, T:T + 1])
        if gcut < 4:
            return
        for c in range(NCH):
            cols = slice(c * 64, (c + 1) * 64)
            ts(ga[0:8, R_MU, cols], zeros[0:8, 0:64], ga[0:8, R_P, c * 64:c * 64 + 1], None, ALU.add)
            ts(ga[0:8, R_PE, cols], zeros[0:8, 0:64], ga[0:8, R_P, c * 64 + 64:c * 64 + 65], None, ALU.add)
            tt(scl[0:8, c:c + 1], ga[0:8, R_P, c * 64:c * 64 + 1], ga[0:8, R_P, c * 64 + 64:c * 64 + 65], ALU.subtract)
        if gcut < 5:
            return
        Pv = ga[0:8, R_P, 1:T + 1]
        tt(ga[0:8, R_SI, 0:T], ga[0:8, R_MU, 0:T], Pv, ALU.subtract)
        tt(ga[0:8, R_WJ, 0:T], ga[0:8, R_G, 0:T], ga[0:8, R_PE, 0:T], ALU.subtract)
        stt(ga[0:8, R_NM, 0:T], ga[0:8, R_F, 0:T], -1.0, Pv, ALU.mult, ALU.subtract)
        ts(ga[0:8, R_NP, 0:T], Pv, -1.0, None, ALU.mult)
        if acut < 2:
            return
        for tb in range(NTB):
            pb = next_pb()
            for r, R in enumerate((R_SI, R_WJ, R_NM)):
                tr(pb[:, r * 8:(r + 1) * 8], ga[0:8, R, tb * 128:(tb + 1) * 128], ident[0:8, 0:8])
            act(tmsc[:, tb, :, :], pb[:, 0:24].rr("p (a b) -> p a b", a=3), AF.Exp)
        if acut < 3:
            return
        tt(rhs_sc[0:8, :, :], scl[0:8, :].rr("p (c o) -> p c o", o=1).bc([8, NCH, 4]),
           a_sel[0:8, 512:516].rr("p (o g) -> p o g", o=1).bc([8, NCH, 4]), ALU.mult)
        pb = next_pb()
        mm(pb[:, 0:NCH * 4], a_sel[0:8, 516:644], rhs_sc[0:8, :, :].rr("p c g -> p (c g)"))
        act(scb[:, :, :, :].rr("p c g o -> p (c g o)"), pb[:, 0:NCH * 4], AF.Exp)
        if acut < 4:
            return
        for tb in range(NTB):
            to_tm(ktm, kT, tb)
        sl = load_slab(l, "Av")
        for tb in range(NTB):
            pb = tm_proj(sl, tb)
            act(vv[:, tb, :, :], pb[:, :].rr("p (h e) -> p h e", h=8), AF.Copy)
        sl = load_slab(l, "Ao")
        for tb in range(NTB):
            pb = tm_proj(sl, tb)
            act(so[:, tb, :, :], pb[:, :].rr("p (h e) -> p h e", h=8), AF.Sigmoid)
        for tb in range(NTB):
            E = next_pb()
            mm(E[:, :], ident[:, :], a_mask[:, :], start=True, stop=False)
            mm(E[:, :], ga[0:8, R_G, tb * 128:(tb + 1) * 128], a_sel[0:8, 0:512], start=False, stop=False)
            for p in range(2):
                c = tb * 2 + p
                tt(negPx[0:8, p, :, :], ga[0:8, R_NP:R_NP + 1, c * 64:(c + 1) * 64].bc([8, 8, 64]),
                   a_sel[0:8, 0:512].rr("p (h e) -> p h e", h=8), ALU.mult)
                mm(E[64 * p:64 * p + 64, :], ones[0:8, 0:64], negPx[0:8, p, :, :].rr("p h e -> p (h e)"),
                   start=False, stop=True)
            act(ET[:, :, :].rr("p h e -> p (h e)"), E[:, :], AF.Exp)
            if acut < 5:
                continue
            for p in range(2):
                c = tb * 2 + p
                rows = slice(64 * p, 64 * p + 64)
                cols = slice(c * 64, (c + 1) * 64)
                sI4 = tmsc[:, tb, 0, :].rr("p (g j o) -> p g j o", g=4, j=2)
                for j in range(2):
                    jr = slice(64 * j, 64 * j + 64)
                    for g in range(4):
                        mm(S[j][rows, g * 64:(g + 1) * 64], kT[jr, g, cols], qT[jr, g, cols])
                        mm(JX[j][rows, g * 64:(g + 1) * 64], qT[jr, g, cols], a_C[l][jr, g, :])
                        mm(JX[j][rows, 256 + g:257 + g], qT[jr, g, cols], a_n[l][jr, g, :])
                for j in range(2):
                    stt(AT[rows, 4 * j:4 * j + 4, :], S[j][rows, 0:256].rr("p (a b) -> p a b", a=4), 0.125,
                        ET[rows, 4 * j:4 * j + 4, :], ALU.mult, ALU.mult)
                    tt(tmp4[rows, :, j, :], JX[j][rows, 0:256].rr("p (a b) -> p a b", a=4),
                       sI4[rows, :, j, :].bc([64, 4, 64]), ALU.mult)
                    tt(lnst[rows, 40:48].rr("p (g j) -> p g j", j=2)[:, :, j], JX[j][rows, 256:260],
                       tmsc[rows, tb, 0, :].rr("p (g j) -> p g j", j=2)[:, :, j], ALU.mult)
                for h in range(8):
                    g, j = h // 2, h % 2
                    mm(O[p][rows, h * 64:(h + 1) * 64], AT[rows, j * 4 + g, :], vv[rows, tb, h, :])
                    mm(ST[p][rows, 384 + h:385 + h], AT[rows, j * 4 + g, :], ones[rows, 0:1])
                tt(hnum[rows, :, :], O[p][rows, :].rr("p (h e) -> p h e", h=8),
                   tmp4[rows, :, :, :].rr("p g j e -> p (g j) e"), ALU.add)
                tt(lnst[rows, 48:56], ST[p][rows, 384:392], lnst[rows, 40:48], ALU.add)
                stt(lnst[rows, 48:56], lnst[rows, 48:56], -1.0, lnst[rows, 48:56], ALU.mult, ALU.max)
                tt(lnst[rows, 48:56], lnst[rows, 48:56], tmsc[rows, tb, 2, :], ALU.max)
                recip(lnst[rows, 56:64], lnst[rows, 48:56])
                tt(hnum[rows, :, :], hnum[rows, :, :],
                   lnst[rows, 56:64].rr("p (h o) -> p h o", o=1).bc([64, 8, 64]), ALU.mult)
                tt(osb[rows, tb, :, :], hnum[rows, :, :], so[rows, tb, :, :], ALU.mult)
                tt(vw[rows, :, :], vv[rows, tb, :, :],
                   tmsc[rows, tb, 1, :].rr("p (h o) -> p h o", o=1).bc([64, 8, 64]), ALU.mult)
                for h in range(8):
                    g, j = h // 2, h % 2
                    jr = slice(64 * j, 64 * j + 64)
                    mm(ST[p][jr, g * 64:(g + 1) * 64], ktm[rows, tb, h, :], vw[rows, h, :])
                    mm(ST[p][jr, 256 + g:257 + g], ktm[rows, tb, h, :], tmsc[rows, tb, 1, h:h + 1])
                tt(a_C[l][:, :, :], a_C[l][:, :, :], scb[:, c, :, :].bc([128, 4, 64]), ALU.mult)
                stt(a_C[l][:, :, :], ST[p][:, 0:256].rr("p (a b) -> p a b", a=4), 0.125, a_C[l][:, :, :],
                    ALU.mult, ALU.add)
                tt(a_n[l][:, :, :], a_n[l][:, :, :], scb[:, c, :, :], ALU.mult)
                stt(a_n[l][:, :, :], ST[p][:, 256:260].rr("p (a b) -> p a b", b=1), 0.125, a_n[l][:, :, :],
                    ALU.mult, ALU.add)
        for tb in range(NTB):
            head_ln(xn, osb, tb)
            pb = next_pb()
            for g in range(4):
                tr(pb[:, g * 128:(g + 1) * 128], xn[:, tb, 2 * g:2 * g + 2, 0:64], ident[:, :])
            for g in range(4):
                stt(yT[:, g, tb * 128:(tb + 1) * 128], pb[:, g * 128:(g + 1) * 128], cvc(l, "a_nw", g),
                    sz[:, g, tb * 128:(tb + 1) * 128], ALU.mult, ALU.mult)

    def mixer_B(l, ti):
        rS, kS, vS, sz = FS[0], FS[1], FS[2], FS[3]
        RhT, AlT, bon = rS, kS, vS
        pools = (FS[4], FS[5], FS[6])

        def slot(i):
            return pools[i // 4][:, i % 4, :]
        raw, lora, aT, lw, lc, kt, kh, beta, eg, egi, egm, tA = [slot(i) for i in range(12)]
        Vtm, Btm, Ktm, wkv, xn = TM[0], TM[1], TM[2], TM[3], TM[4]
        BJ = [PB[3], PB[4]]; PA = [PB[2], PB[7]]; PQ = [PB[5], PB[6]]; PX = [PB[0], PB[1]]
        Pn, Qn, Xm, W2 = ET, hnum, vw, AT
        omu = CM.off["b_mu"][0]

        def shift_mix(dst, pb, hidx, mucol, nrows=128):
            cp(raw[:, 0:1], b_hist[l][:, hidx:hidx + 1])
            act(raw[:, 1:T + 1], pb[:, 0:T], AF.Copy)
            cp(b_hist[l][:, hidx:hidx + 1], raw[:, T:T + 1])
            tt(tA[:, 0:T], raw[:, 0:T], raw[:, 1:T + 1], ALU.subtract)
            stt(dst, tA[:, 0:T], cv[l][:, omu + mucol:omu + mucol + 1], raw[:, 1:T + 1], ALU.mult, ALU.add)

        sl = load_slab(l, "Bl", ncols=128)
        pb = fm_proj(sl, 0)
        shift_mix(lora[:, 0:T], pb, 12, 12)
        act(lora[0:64, 0:T], lora[0:64, 0:T], AF.Tanh)
        for wi, (name, dst) in enumerate((("Br", rS), ("Bk", kS), ("Bv", vS))):
            sl = load_slab(l, name)
            for g in range(4):
                pb = fm_proj(sl, g)
                shift_mix(dst[:, g, 0:T], pb, wi * 4 + g, wi * 4 + g)
        sl = load_slab(l, "Bz")
        for g in range(4):
            pb = fm_proj(sl, g)
            act(sz[:, g, 0:T], pb[:, 0:T], AF.Silu)
        for g in range(4):
            gc = slice(g * 128, (g + 1) * 128)
            pb = next_pb()
            mm(pb[:, 0:T], lora_up[l][0:64, gc], lora[0:64, 0:T])
            act(lw[:, 0:T], pb[:, 0:T], AF.Sigmoid, bias=cvc(l, "b_w0", g))
            ts(lw[:, 0:T], lw[:, 0:T], -0.606531, None, ALU.mult)
            pb = next_pb()
            mm(pb[:, 0:T], lora_up[l][64:128, gc], lora[64:128, 0:T])
            act(aT[:, 0:T], pb[:, 0:T], AF.Sigmoid, bias=cvc(l, "b_a0", g))
            for c in range(NCH):
                cols = slice(c * 64, (c + 1) * 64)
                scan(lc[:, cols], ones[:, 0:64], lw[:, cols], 0.0, ALU.mult, ALU.add)
            act(eg[:, 0:T], lc[:, 0:T], AF.Exp)
            act(egi[:, 0:T], lc[:, 0:T], AF.Exp, scale=-1.0)
            tt(tA[:, 0:T], lc[:, 0:T], lw[:, 0:T], ALU.subtract)
            act(egm[:, 0:T], tA[:, 0:T], AF.Exp)
            for c in range(NCH):
                cp(gLt[:, g, c:c + 1], eg[:, c * 64 + 63:c * 64 + 64])
            ts(kh[:, 0:T], kS[:, g, 0:T], cvc(l, "b_kk", g), None, ALU.mult)
            tt(tA[:, 0:T], kh[:, 0:T], kh[:, 0:T], ALU.mult)
            pb = next_pb()
            mm(pb[:, 0:T], b_ones[:, :], tA[:, 0:T])
            ts(tA[:, 0:T], pb[:, 0:T], 1e-12, None, ALU.add)
            act(tA[:, 0:T], tA[:, 0:T], AF.Sqrt)
            recip(tA[:, 0:T], tA[:, 0:T])
            tt(kh[:, 0:T], kh[:, 0:T], tA[:, 0:T], ALU.mult)
            ts(tA[:, 0:T], aT[:, 0:T], -1.0, cvc(l, "b_ka", g), ALU.add, ALU.mult)
            stt(kt[:, 0:T], tA[:, 0:T], 1.0, kS[:, g, 0:T], ALU.add, ALU.mult)
            for tb in range(NTB):
                pbt = next_pb()
                tr(pbt[:, 0:128], vS[:, g, tb * 128:(tb + 1) * 128], ident[:, :])
                cp(Vtm[:, tb, 2 * g:2 * g + 2, :], pbt[:, 0:128].rr("p (a b) -> p a b", a=2))
            stt(tA[:, 0:T], rS[:, g, 0:T], cvc(l, "b_rk", g), kt[:, 0:T], ALU.mult, ALU.mult)
            pbb = next_pb()
            mm(pbb[:, 0:T], b_ones[:, :], tA[:, 0:T])
            tt(bon[:, g, 0:T], pbb[:, 0:T], vS[:, g, 0:T], ALU.mult)
            tt(beta[:, 0:T], aT[:, 0:T], kh[:, 0:T], ALU.mult)
            tt(beta[:, 0:T], beta[:, 0:T], egi[:, 0:T], ALU.mult)
            tt(AlT[:, g, 0:T], kh[:, 0:T], egm[:, 0:T], ALU.mult)
            tt(kt[:, 0:T], kt[:, 0:T], egi[:, 0:T], ALU.mult)
            tt(RhT[:, g, 0:T], rS[:, g, 0:T], eg[:, 0:T], ALU.mult)
            for tb in range(NTB):
                pbt = next_pb()
                tr(pbt[:, 0:128], beta[:, tb * 128:(tb + 1) * 128], ident[:, :])
                tr(pbt[:, 128:256], kt[:, tb * 128:(tb + 1) * 128], ident[:, :])
                cp(Btm[:, tb, 2 * g:2 * g + 2, :], pbt[:, 0:128].rr("p (a b) -> p a b", a=2))
                cp(Ktm[:, tb, 2 * g:2 * g + 2, :], pbt[:, 128:256].rr("p (a b) -> p a b", a=2))
            for tb in range(NTB):
                for j in range(2):
                    jr = slice(64 * j, 64 * j + 64)
                    for p in range(2):
                        c = tb * 2 + p
                        rows = slice(64 * p, 64 * p + 64)
                        cols = slice(c * 64, (c + 1) * 64)
                        A_, B_, K_, R_ = AlT[jr, g, cols], beta[jr, cols], kt[jr, cols], RhT[jr, g, cols]
                        mm(BJ[j][rows, 0:64], A_, B_)
                        mm(BJ[j][rows, 64:128], B_, A_)
                        mm(BJ[j][rows, 128:192], K_, A_)
                        mm(BJ[j][rows, 192:256], B_, R_)
                        mm(BJ[j][rows, 256:320], K_, R_)
                    tt(PRall[tb][:, :, 2 * g + j, :], BJ[j][:, 0:320].rr("p (k t) -> p k t", k=5), b_m5[:, :, :],
                       ALU.mult)
        for tb in range(NTB):
            PRt = PRall[tb]
            P0, Q0, MkT, NbT, NkT = (PRt[:, k, :, :] for k in range(5))
            Wsb, Usb = PRt[:, 2, :, :], PRt[:, 4, :, :]
            for p in range(2):
                rows = slice(64 * p, 64 * p + 64)
                c = tb * 2 + p
                for h in range(8):
                    g, j = h // 2, h % 2
                    mm(PA[p][rows, h * 64:(h + 1) * 64], MkT[rows, h, :], Vtm[rows, tb, h, :])
                    mm(PQ[p][rows, h * 64:(h + 1) * 64], NkT[rows, h, :], Vtm[rows, tb, h, :])
                    mm(PX[p][64 * j:64 * j + 64, g * 64:(g + 1) * 64], Ktm[rows, tb, h, :], Vtm[rows, tb, h, :])
                act(W2[rows, :, :], PA[p][rows, :].rr("p (h e) -> p h e", h=8), AF.Copy)
                act(Y2[rows, :, :], PQ[p][rows, :].rr("p (h e) -> p h e", h=8), AF.Copy)
                tt(K2g[:, p, :, :], PX[p][:, 0:256].rr("p (a b) -> p a b", a=4),
                   gLt[:, :, c:c + 1].bc([128, 4, 64]), ALU.mult)
            tt(Xm[:, :, :], b_i2[:, :, :].bc([128, 8, 64]), Q0, ALU.subtract)
            Pc, Qc = P0, Q0
            nxt = [(Pn, Qn), (P0, Q0)]
            for i in range(1, 6):
                Pd, Qd = nxt[(i - 1) % 2]
                for p in range(2):
                    rows = slice(64 * p, 64 * p + 64)
                    for h in range(8):
                        hc = slice(h * 64, (h + 1) * 64)
                        mm(PA[p][rows, hc], Qc[rows, h, :], Pc[rows, h, :])
                        if i < 5:
                            mm(PQ[p][rows, hc], Pc[rows, h, :], Qc[rows, h, :])
                for p in range(2):
                    rows = slice(64 * p, 64 * p + 64)
                    act(Pd[rows, :, :], PA[p][rows, :].rr("p (h e) -> p h e", h=8), AF.Copy)
                    if i < 5:
                        cp(Qd[rows, :, :], PQ[p][rows, :].rr("p (h e) -> p h e", h=8))
                Pc, Qc = Pd, Qd
                for p in range(2):
                    rows = slice(64 * p, 64 * p + 64)
                    for h in range(8):
                        mm(PX[p][rows, h * 64:(h + 1) * 64], Pc[rows, h, :], Xm[rows, h, :])
                for p in range(2):
                    rows = slice(64 * p, 64 * p + 64)
                    tt(Xm[rows, :, :], Xm[rows, :, :], PX[p][rows, :].rr("p (h e) -> p h e", h=8), ALU.add)
            for p in range(2):
                c = tb * 2 + p
                rows = slice(64 * p, 64 * p + 64)
                cols = slice(c * 64, (c + 1) * 64)
                for j in range(2):
                    jr = slice(64 * j, 64 * j + 64)
                    for g in range(4):
                        mm(BJ[j][rows, g * 64:(g + 1) * 64], AlT[jr, g, cols], b_S[l][jr, g, :])
                        mm(BJ[j][rows, 256 + g * 64:256 + (g + 1) * 64], RhT[jr, g, cols], b_S[l][jr, g, :])
                W24 = W2[rows, :, :].rr("p (g j) e -> p g j e", j=2)
                Ws4 = Wsb[rows, :, :].rr("p (g j) e -> p g j e", j=2)
                Y24 = Y2[rows, :, :].rr("p (g j) e -> p g j e", j=2)
                for j in range(2):
                    tt(Ws4[:, :, j, :], BJ[j][rows, 0:256].rr("p (a b) -> p a b", a=4), W24[:, :, j, :], ALU.add)
                    tt(tmp4[rows, :, j, :], BJ[j][rows, 256:512].rr("p (a b) -> p a b", a=4), Y24[:, :, j, :],
                       ALU.add)
                for h in range(8):
                    mm(PX[p][rows, h * 64:(h + 1) * 64], Xm[rows, h, :], Wsb[rows, h, :])
                act(Usb[rows, :, :], PX[p][rows, :].rr("p (h e) -> p h e", h=8), AF.Copy)
                for h in range(8):
                    g, j = h // 2, h % 2
                    mm(PA[p][rows, h * 64:(h + 1) * 64], NbT[rows, h, :], Usb[rows, h, :])
                    mm(PQ[p][64 * j:64 * j + 64, g * 64:(g + 1) * 64], Btm[rows, tb, h, :], Usb[rows, h, :])
                tt(wkv[rows, tb, :, :], tmp4[rows, :, :, :].rr("p g j e -> p (g j) e"),
                   PA[p][rows, :].rr("p (h e) -> p h e", h=8), ALU.subtract)
                tt(b_S[l][:, :, :], b_S[l][:, :, :], PQ[p][:, 0:256].rr("p (a b) -> p a b", a=4), ALU.subtract)
                tt(b_S[l][:, :, :], b_S[l][:, :, :], gLt[:, :, c:c + 1].bc([128, 4, 64]), ALU.mult)
                tt(b_S[l][:, :, :], b_S[l][:, :, :], K2g[:, p, :, :], ALU.add)
        ogw, ogb = CM.off["b_gw"][0], CM.off["b_gb"][0]
        for tb in range(NTB):
            head_ln(xn, wkv, tb)
            pb = next_pb()
            for g in range(4):
                tr(pb[:, g * 128:(g + 1) * 128], xn[:, tb, 2 * g:2 * g + 2, 0:64], ident[:, :])
            for g in range(4):
                tc_ = slice(tb * 128, (tb + 1) * 128)
                ts(scr[:, 0:128], pb[:, g * 128:(g + 1) * 128], cv[l][:, ogw + g:ogw + g + 1],
                   cv[l][:, ogb + g:ogb + g + 1], ALU.mult, ALU.add)
                tt(scr[:, 0:128], scr[:, 0:128], bon[:, g, tc_], ALU.add)
                tt(yT[:, 4 + g, tc_], scr[:, 0:128], sz[:, g, tc_], ALU.mult)

    def out_proj_residual(l, X):
        dma(gpost[l][:, :], gpost_d[l])
        for si in range(4):
            s = load_wo(l, si)
            for tb in range(NTB):
                for half in range(2):
                    pb = PB[4 + tb * 2 + half]
                    for q in range(4):
                        fc = si * 4 + q
                        mm(pb[:, :], yT[:, fc, tb * 128:(tb + 1) * 128], s[:, q * 2 + half, :],
                           start=(fc == 0), stop=(fc == 15))
        if cut < 4:
            return
        for tb in range(NTB):
            for half in range(2):
                act(scr[:, half * 512:(half + 1) * 512], PB[4 + tb * 2 + half][:, :], AF.Copy)
            if cut < 5:
                continue
            act(scr2[:, :], scr[:, :], AF.Square)
            reduce(sm[:, 24:25], scr2[:, :], ALU.add)
            ts(sm[:, 25:26], sm[:, 24:25], 1.0 / D_MODEL, 1e-6, ALU.mult, ALU.add)
            act(sm[:, 26:27], sm[:, 25:26], AF.Sqrt)
            recip(sm[:, 27:28], sm[:, 26:27])
            if cut < 6:
                continue
            stt(scr2[:, :], scr[:, :], sm[:, 27:28], gpost[l][:, :], ALU.mult, ALU.mult)
            if cut < 7:
                continue
            tt(X[:, tb, :], X[:, tb, :], scr2[:, :], ALU.add)

    xv = x_d.rearrange("(n tb p) d -> n p tb d", tb=NTB, p=128)
    yv = y_d.rearrange("(n tb p) d -> n p tb d", tb=NTB, p=128)
    for ti in range(ntiles):
        X = xt[0]
        dma(X[:, :, :], xv[ti])
        for l in range(nlayers):
            if cut >= 1:
                rmsnorm_pre(l, X)
            memset(yT[:, :, :], 0.0)
            if "C" in mixers and cut >= 2:
                mixer_C(l)
            if "D" in mixers:
                mixer_D(l, ti)
            if "A" in mixers:
                mixer_A(l, ti)
            if "B" in mixers:
                mixer_B(l, ti)
            if cut >= 3:
                out_proj_residual(l, X)
        dma(yv[ti], X[:, :, :])

    P.emit()
    return nc, stack


def kernel(**inputs):
    inputs = {k: np.asarray(v) for k, v in inputs.items()}
    return run(inputs)


def make_in_map(xc, hp, tb):
    m = dict(x=xc, wt=hp["wt"], wo=hp["wo"], cv=hp["cv"], lora=hp["lora"], lru=hp["lru"], gpost=hp["gpost"])
    m.update(tb)
    return m


def run(inputs, ntiles=SEQ // T, nlayers=DEPTH, mixers="ABCD"):
    hp = host_prepare(inputs)
    tb = host_tables()
    nc, stack = build_program(ntiles, nlayers, mixers)
    x = np.ascontiguousarray(inputs["x"], dtype=np.float32)
    in_maps = [make_in_map(x[c], hp, tb) for c in range(NCORES)]
    with stack:
        res = run_bass_kernel_spmd(nc, in_maps, core_ids=list(range(NCORES)))
    out = np.stack([np.asarray(r["y"]) for r in res.results], axis=0)
    return out.astype(np.float32)
```

```python
import contextlib
import numpy as np
import concourse.bass as bass
import concourse.mybir as mybir
from concourse.bass_utils import run_bass_kernel_spmd

F32 = mybir.dt.float32
BF16 = mybir.dt.bfloat16
AF = mybir.ActivationFunctionType
ALU = mybir.AluOpType
AX = mybir.AxisListType

D_MODEL = 1024
SEQ = 4096
BATCH = 4
DEPTH = 2
G = 512
D_IN = 7824
T = 256
NTB = T // 128
NCH = T // 64
NCORES = 4
NSLABBUF = 3


class Buf:
    def __init__(self, name, tile, psum=False):
        self.name = name
        self.tile = tile
        self.psum = psum
        self.acc = []
        self.dma_sem = None
        self.dma_ops = []

    def __getitem__(self, idx):
        return Ref(self, self.tile[idx])


def _box(ap):
    pat = ap.ap
    pstep = pat[0][0]
    off = int(ap.offset)
    p0 = off // pstep if pstep else 0
    f0 = off % pstep if pstep else off
    ext = 0
    for st, cnt in pat[1:]:
        ext += abs(st) * (cnt - 1)
    return (p0, p0 + pat[0][1], f0, f0 + ext + 1)


class Ref:
    def __init__(self, buf, ap, box=None):
        self.buf = buf
        self.ap = ap
        if box is None:
            box = _box(ap)
            if buf.psum:
                box = ((box[0] // 32) * 32, ((box[1] + 31) // 32) * 32, 0, 512)
        self.box = box

    def bc(self, shape):
        return Ref(self.buf, self.ap.to_broadcast(list(shape)), self.box)

    def __getitem__(self, idx):
        return Ref(self.buf, self.ap[idx])

    def rr(self, pat, **kw):
        return Ref(self.buf, self.ap.rearrange(pat, **kw), self.box)


def _ovl(a, b):
    return a[0] < b[1] and b[0] < a[1] and a[2] < b[3] and b[2] < a[3]


def _cov(a, b):
    return a[0] <= b[0] and a[1] >= b[1] and a[2] <= b[2] and a[3] >= b[3]


class Prog:
    ENG = ("pe", "act", "dve", "pool", "sp")

    def __init__(self, nc, stack):
        self.nc = nc
        self.stack = stack
        self.ops = []
        self.nbuf = 0

    def sb(self, name, shape, dtype=F32):
        t = self.stack.enter_context(self.nc.sbuf_tensor("s_" + name, list(shape), dtype))
        return Buf(name, t)

    def ps(self, name):
        t = self.stack.enter_context(self.nc.psum_tensor("p_" + name, [128, 512], F32))
        return Buf(name, t, psum=True)

    def op(self, eng, fn, outs, ins, dma=False):
        oid = len(self.ops)
        deps = set()
        for r, w in [(x, True) for x in outs] + [(x, False) for x in ins]:
            if r is None or not isinstance(r, Ref):
                continue
            b = r.buf
            for (bx, o2, w2) in b.acc:
                if (w or w2) and _ovl(bx, r.box):
                    deps.add(o2)
        for r, w in [(x, True) for x in outs] + [(x, False) for x in ins]:
            if r is None or not isinstance(r, Ref):
                continue
            b = r.buf
            if w:
                b.acc = [a for a in b.acc if not _cov(r.box, a[0])]
            b.acc.append((r.box, oid, w))
            if len(b.acc) > 48:
                merged = {}
                for (bx, o2, w2) in b.acc:
                    k = (self.ops[o2]["eng"] if o2 < oid else eng, w2)
                    if k in merged:
                        m = merged[k]
                        merged[k] = ((min(m[0][0], bx[0]), max(m[0][1], bx[1]), min(m[0][2], bx[2]),
                                      max(m[0][3], bx[3])), max(m[1], o2), w2)
                    else:
                        merged[k] = (bx, o2, w2)
                b.acc = list(merged.values())
        dbuf = None
        if dma:
            for r in list(outs) + list(ins):
                if isinstance(r, Ref):
                    dbuf = r.buf
            dbuf.dma_ops.append(oid)
        deps.discard(oid)
        self.ops.append(dict(eng=eng, fn=fn, deps=sorted(deps), dma=dma, dbuf=dbuf, sig=False))
        return oid

    def emit(self):
        nc = self.nc
        ops = self.ops
        for o in ops:
            for d in o["deps"]:
                od = ops[d]
                if od["dma"]:
                    continue
                if od["eng"] == o["eng"] and o["eng"] == "pe" and not o["dma"]:
                    continue
                od["sig"] = True
        sems = {e: self.stack.enter_context(nc.semaphore("sem_" + e)) for e in self.ENG}
        cnt = {e: 0 for e in self.ENG}
        for o in ops:
            if o["dma"]:
                b = o["dbuf"]
                if b.dma_sem is None:
                    b.dma_sem = self.stack.enter_context(nc.semaphore("dsem_" + b.name))
            elif o["sig"]:
                cnt[o["eng"]] += 1
                o["signo"] = cnt[o["eng"]]
        per_eng = {e: [] for e in self.ENG}
        for i, o in enumerate(ops):
            per_eng[o["eng"]].append(i)
        import bisect

        def gen(engname, e):
            waited = {}
            for i in per_eng[engname]:
                o = ops[i]
                need = {}
                for d in o["deps"]:
                    od = ops[d]
                    if od["dma"]:
                        b = od["dbuf"]
                        n = bisect.bisect_left(b.dma_ops, i)
                        key = ("d", id(b))
                        if need.get(key, (None, 0))[1] < 16 * n:
                            need[key] = (b.dma_sem, 16 * n)
                    else:
                        if od["eng"] == engname and engname == "pe" and not o["dma"]:
                            continue
                        key = ("e", od["eng"])
                        if need.get(key, (None, 0))[1] < od["signo"]:
                            need[key] = (sems[od["eng"]], od["signo"])
                for key, (sem, val) in need.items():
                    if waited.get(key, 0) >= val:
                        continue
                    e.wait_ge(sem, val)
                    waited[key] = val
                ins = o["fn"](e)
                if o["dma"]:
                    ins.then_inc(o["dbuf"].dma_sem, 16)
                elif o["sig"]:
                    ins.then_inc(sems[engname], 1)

        with nc.Block() as block:
            @block.tensor
            def _(e):
                gen("pe", e)

            @block.scalar
            def _(e):
                gen("act", e)

            @block.vector
            def _(e):
                gen("dve", e)

            @block.gpsimd
            def _(e):
                gen("pool", e)

            @block.sync
            def _(e):
                gen("sp", e)
                seen = set()
                for o in ops:
                    if o["dma"] and id(o["dbuf"]) not in seen:
                        seen.add(id(o["dbuf"]))
                        e.wait_ge(o["dbuf"].dma_sem, 16 * len(o["dbuf"].dma_ops))


def _a(x):
    return x.ap if isinstance(x, Ref) else x


def _fm4(v):
    return np.ascontiguousarray(v.reshape(-1, 128).T)


class CMap:
    def __init__(self):
        self.off = {}
        self.n = 0

    def add(self, name, w):
        self.off[name] = (self.n, w)
        self.n += w


def build_cmap():
    c = CMap()
    c.add("gpre", 8)
    c.add("a_cw", 32); c.add("a_cb", 8); c.add("a_ib", 1); c.add("a_fb", 1); c.add("a_nw", 4)
    c.add("b_mu", 13); c.add("b_w0", 4); c.add("b_a0", 4); c.add("b_kk", 4); c.add("b_ka", 4)
    c.add("b_rk", 4); c.add("b_gw", 4); c.add("b_gb", 4)
    c.add("c_cw", 16); c.add("c_cb", 4); c.add("c_br", 4); c.add("c_bi", 4); c.add("c_lam", 4)
    c.add("d_nw", 4)
    return c


CM = build_cmap()

def _cols(a, b):
    return list(range(a, b))


def build_slabs():
    slabs = []
    def fm(name, start):
        slabs.append((name, _cols(start, start + 512)))
    fm("Aq", 0); fm("Ak", 512); fm("Az", 2048)
    slabs.append(("Ag", _cols(2560, 2576) + [-1] * (512 - 16)))
    fm("Br", 2576); fm("Bk", 3088); fm("Bv", 3600)
    slabs.append(("Bl", _cols(4112, 4240) + [-1] * (512 - 128)))
    fm("Bz", 4240); fm("Cx", 4752); fm("Cz", 5264); fm("Dz", 7312)
    fm("Av", 1024); fm("Ao", 1536); fm("Dq", 5776); fm("Dk", 6288); fm("Dv", 6800)
    return slabs


SLABS = build_slabs()
SLAB_ID = {s[0]: i for i, s in enumerate(SLABS)}
NSLAB = len(SLABS)


def host_prepare(inp):
    f = np.float32
    w_in = inp["w_in"]
    w_in_p = np.concatenate([w_in, np.zeros((DEPTH, D_MODEL, 1), f)], axis=2)
    wt = np.empty((DEPTH, NSLAB, 128, 8, 512), f)
    for si, (name, cols) in enumerate(SLABS):
        blk = w_in_p[:, :, cols]
        wt[:, si] = blk.reshape(DEPTH, 8, 128, 512).transpose(0, 2, 1, 3)
    w_out = inp["w_out"]
    wo = np.ascontiguousarray(w_out.reshape(DEPTH, 4, 4, 128, 1024).transpose(0, 1, 3, 2, 4))
    cv = np.zeros((DEPTH, 128, CM.n), f)
    def put(name, arr):
        o, w = CM.off[name]
        cv[:, :, o:o + w] = arr
    put("gpre", inp["norm_pre"].reshape(DEPTH, 8, 128).transpose(0, 2, 1))
    cw = inp["mlstm_conv_w"]
    put("a_cw", cw.reshape(DEPTH, 4, 8, 128).transpose(0, 3, 1, 2).reshape(DEPTH, 128, 32))
    put("a_cb", inp["mlstm_conv_b"].reshape(DEPTH, 8, 128).transpose(0, 2, 1))
    ib = np.zeros((DEPTH, 128, 1), f); ib[:, 0:8, 0] = inp["mlstm_i_bias"]; put("a_ib", ib)
    fb = np.zeros((DEPTH, 128, 1), f); fb[:, 0:8, 0] = inp["mlstm_f_bias"]; put("a_fb", fb)
    def fm4(x):
        return x.reshape(DEPTH, -1, 128).transpose(0, 2, 1)
    put("a_nw", fm4(inp["mlstm_norm_w"]))
    mu = inp["rwkv_mu"]
    put("b_mu", fm4(mu))
    for k, nm in [("b_w0", "rwkv_w0"), ("b_a0", "rwkv_a0"), ("b_kk", "rwkv_k_k"), ("b_ka", "rwkv_k_a"),
                  ("b_rk", "rwkv_r_k"), ("b_gw", "rwkv_gn_w"), ("b_gb", "rwkv_gn_b"),
                  ("c_cb", "lru_conv_b"), ("c_br", "lru_b_r"), ("c_bi", "lru_b_i"), ("c_lam", "lru_lambda"),
                  ("d_nw", "ret_norm_w")]:
        put(k, fm4(inp[nm]))
    lw = inp["lru_conv_w"]
    put("c_cw", lw.reshape(DEPTH, 4, 4, 128).transpose(0, 3, 1, 2).reshape(DEPTH, 128, 16))
    lora = np.concatenate([inp["rwkv_w_up"], inp["rwkv_a_up"]], axis=1)
    lru = np.zeros((DEPTH, 128, 2, 4, 128), f)
    for which, nm in enumerate(["lru_w_r", "lru_w_i"]):
        w = inp[nm]
        for g in range(4):
            for j in range(2):
                lru[:, 64 * j:64 * j + 64, which, g, 64 * j:64 * j + 64] = w[:, 2 * g + j]
    gpost = np.ascontiguousarray(np.broadcast_to(inp["norm_post"][:, None, :], (DEPTH, 128, D_MODEL)))
    return dict(wt=wt, wo=wo, cv=cv, lora=np.ascontiguousarray(lora), lru=lru, gpost=gpost)


def host_tables():
    f = np.float32
    t = {}
    t["ident"] = np.eye(128, dtype=f)
    hd = 64
    log_g = np.log1p(-np.exp2(-5.0 - np.arange(8, dtype=np.float64)))
    idx = np.arange(64, dtype=np.float64)
    dmat = np.exp(log_g[:, None, None] * np.abs(idx[:, None] - idx[None, :])) * hd ** -0.5
    xi = np.exp(log_g[:, None] * (idx + 1.0))
    zeta = np.exp(log_g[:, None] * (63.0 - idx)) * hd ** -0.5
    gch = np.exp(log_g * 64.0)
    hs2h = [2 * (s % 4) + (s // 4) for s in range(8)]
    dm = np.zeros((128, 8, 64));
    for s in range(8):
        dm[0:64, s] = dmat[hs2h[s]]; dm[64:128, s] = dmat[hs2h[s]]
    t["d_dmat"] = dm.astype(f)
    t["d_xi"] = np.tile(xi.T, (2, 1)).astype(f)
    t["d_zeta"] = np.tile(zeta.T, (2, 1)).astype(f)
    gc = np.zeros((128, 4))
    for g in range(4):
        for j in range(2):
            gc[64 * j:64 * j + 64, g] = gch[2 * g + j]
    t["d_gch"] = gc.astype(f)
    mk = np.zeros((128, 8, 64))
    sidx = np.arange(128) % 64
    mk[:] = np.where(sidx[:, None, None] <= np.arange(64)[None, None, :], 0.0, -30000.0)
    t["a_mask"] = mk.reshape(128, 512).astype(f)
    sel = np.zeros((8, 512 + 4 + 128))
    for hp_ in range(8):
        for hs in range(8):
            if hp_ == hs2h[hs]:
                sel[hp_, hs * 64:(hs + 1) * 64] = 1.0
        sel[hp_, 512 + hp_ // 2] = 1.0
        jj = hp_ % 2
        sel[hp_, 516 + 64 * jj:516 + 64 * jj + 64] = 1.0
    t["a_sel"] = sel.astype(f)
    bo = np.zeros((128, 128)); bo[0:64, 0:64] = 1.0; bo[64:128, 64:128] = 1.0
    t["b_ones"] = bo.astype(f)
    i2 = np.zeros((128, 64)); i2[0:64] = np.eye(64); i2[64:128] = np.eye(64)
    t["b_i2"] = i2.astype(f)
    rr_ = (np.arange(128) % 64)[:, None]; cc_ = np.arange(64)[None, :]
    m5 = np.zeros((128, 5, 64))
    m5[:, 0] = rr_ > cc_; m5[:, 1] = cc_ > rr_; m5[:, 2] = cc_ > rr_; m5[:, 3] = cc_ >= rr_; m5[:, 4] = cc_ >= rr_
    t["b_m5"] = m5.astype(f)
    half = 32
    pos = np.arange(SEQ, dtype=np.float32)
    inv_freq = (np.float32(10000.0) ** (-np.arange(half, dtype=np.float32) / np.float32(half))).astype(np.float32)
    ang = (pos[:, None] * inv_freq[None, :]).astype(np.float32).astype(np.float64)
    t["rope"] = np.concatenate([np.cos(ang), np.sin(ang)], axis=1).astype(f)
    return t


def build_program(ntiles=SEQ // T, nlayers=DEPTH, mixers="ABCD"):
    nc = bass.Bass("TRN2", target_bir_lowering=False)
    stack = contextlib.ExitStack()
    P = Prog(nc, stack)
    dram = {}

    def din(name, shape):
        dram[name] = nc.dram_tensor(name, list(shape), F32, kind="ExternalInput").ap()
        return dram[name]

    x_d = din("x", [SEQ, D_MODEL])
    wt_d = din("wt", [DEPTH, NSLAB, 128, 8, 512])
    wo_d = din("wo", [DEPTH, 4, 128, 4, 1024])
    cv_d = din("cv", [DEPTH, 128, CM.n])
    lora_d = din("lora", [DEPTH, 128, 512])
    lru_d = din("lru", [DEPTH, 128, 2, 4, 128])
    gpost_d = din("gpost", [DEPTH, 128, D_MODEL])
    ident_d = din("ident", [128, 128])
    dmat_d = din("d_dmat", [128, 8, 64])
    dxi_d = din("d_xi", [128, 8])
    dzeta_d = din("d_zeta", [128, 8])
    dgch_d = din("d_gch", [128, 4])
    rope_d = din("rope", [SEQ, 64])
    bones_d = din("b_ones", [128, 128])
    bi2_d = din("b_i2", [128, 64])
    bm5_d = din("b_m5", [128, 5, 64])
    amask_d = din("a_mask", [128, 512])
    asel_d = din("a_sel", [8, 644])
    y_d = nc.dram_tensor("y", [SEQ, D_MODEL], F32, kind="ExternalOutput").ap()

    ident = P.sb("ident", [128, 128])
    cv = [P.sb(f"cv{l}", [128, CM.n]) for l in range(DEPTH)]
    lru1 = P.sb("lru", [128, 2, 4, 128]); lru = [lru1, lru1]
    gpost1 = P.sb("gpost", [128, D_MODEL]); gpost = [gpost1, gpost1]
    lora_up = [P.sb(f"lora_up{l}", [128, 512]) for l in range(DEPTH)]
    b_ones = P.sb("b_ones", [128, 128]); b_i2 = P.sb("b_i2", [128, 1, 64]); b_m5 = P.sb("b_m5", [128, 5, 64])
    b_hist = [P.sb(f"b_hist{l}", [128, 13]) for l in range(DEPTH)]
    b_S = [P.sb(f"b_S{l}", [128, 4, 64]) for l in range(DEPTH)]
    ident16 = P.sb("ident16", [128, 128], BF16)
    AlT16 = P.sb("AlT16", [128, 4, T], BF16); RhT16 = P.sb("RhT16", [128, 4, T], BF16)
    bk16 = P.sb("bk16", [128, 2, T], BF16)
    bm16 = [P.sb(f"bm16_{i}", [128, 8, 64], BF16) for i in range(3)]
    b_S16 = P.sb("b_S16", [128, 4, 64], BF16)
    a_n16 = P.sb("a_n16", [128, 4, 1], BF16); wj16 = P.sb("wj16", [128, NTB, 8], BF16)
    ones16 = P.sb("ones16", [128, 64], BF16)
    gLt = P.sb("gLt", [128, 4, NCH]); PRall = [P.sb(f"PR{i}", [128, 5, 8, 64], BF16) for i in range(NTB)]; Y2 = P.sb("Y2", [128, 8, 64])
    K2g = P.sb("K2g", [128, 2, 4, 64])
    ones = P.sb("ones", [128, T]); zeros = P.sb("zeros", [128, T])
    xt = [P.sb(f"xt{i}", [128, NTB, D_MODEL]) for i in range(1)]
    hT = P.sb("hT", [128, 8, T], BF16)
    yT = P.sb("yT", [128, 16, T], BF16)
    slab = [P.sb(f"slab{i}", [128, 8, 512], BF16) for i in range(NSLABBUF)]
    sm = P.sb("small", [128, 64])
    scr = P.sb("scr", [128, D_MODEL])
    scr2 = P.sb("scr2", [128, D_MODEL])
    FS = [P.sb(f"fs{i}", [128, 4, T + 4]) for i in range(7)]
    c_hist = [P.sb(f"c_hist{l}", [128, 4, 3]) for l in range(DEPTH)]
    c_state = [P.sb(f"c_state{l}", [128, 4]) for l in range(DEPTH)]
    c_coef = [P.sb(f"c_coef{l}", [128, 8]) for l in range(DEPTH)]
    PB = [P.ps(f"pb{i}") for i in range(8)]
    TM = [P.sb(f"tm{i}", [128, NTB, 8, 64]) for i in range(6)]
    TM16 = [P.sb(f"tm16_{i}", [128, NTB, 8, 64], BF16) for i in range(3)]
    d_dmat = P.sb("d_dmat", [128, 8, 64]); d_xi = P.sb("d_xi", [128, 4, 2, 1]); d_zeta = P.sb("d_zeta", [128, 8, 1])
    d_gch = P.sb("d_gch", [128, 4, 1]); rope_sb = P.sb("rope_sb", [128, NTB, 1, 64])
    d_R = [P.sb(f"d_R{l}", [128, 4, 64]) for l in range(DEPTH)]
    a_mask = P.sb("a_mask", [128, 512]); a_sel = P.sb("a_sel", [8, 644])
    ga = P.sb("ga", [8, 12, T + 1]); scl = P.sb("scl", [8, NCH]); rhs_sc = P.sb("rhs_sc", [8, NCH, 4])
    scb = P.sb("scb", [128, NCH, 4, 1]); negPx = P.sb("negPx", [8, 2, 8, 64]); ET = P.sb("ET", [128, 8, 64])
    tmsc = P.sb("tmsc", [128, NTB, 3, 8]); hnum = P.sb("hnum", [128, 8, 64]); vw = P.sb("vw", [128, 8, 64])
    a_hist = [P.sb(f"a_hist{l}", [128, 8, 3]) for l in range(DEPTH)]
    a_C = [P.sb(f"a_C{l}", [128, 4, 64]) for l in range(DEPTH)]
    a_n = [P.sb(f"a_n{l}", [128, 4, 1]) for l in range(DEPTH)]
    a_car = [P.sb(f"a_car{l}", [8, 4]) for l in range(DEPTH)]
    AT = P.sb("AT", [128, 8, 64]); tmp4 = P.sb("tmp4", [128, 4, 2, 64]); lnst = P.sb("lnst", [128, 64])
    state = dict(slab_i=0, pb_i=0)

    import os
    cut = int(os.environ.get("KCUT", "9"))
    def dma(out, in_, eng="sp"):
        return P.op(eng, lambda e: e.dma_start(out=_a(out), in_=_a(in_)), [out], [in_], dma=True)

    def mm(out, lhsT, rhs, start=True, stop=True):
        P.op("pe", lambda e: e.matmul(_a(out), _a(lhsT), _a(rhs), start=start, stop=stop),
             [out], [lhsT, rhs] + ([] if start else [out]))

    def tr(out, in_, idn):
        P.op("pe", lambda e: e.transpose(_a(out), _a(in_), _a(idn)), [out], [in_, idn])

    def act(out, in_, func, bias=None, scale=1.0, accum=None, eng="act"):
        kw = {}
        if bias is not None:
            kw["bias"] = _a(bias)
        if accum is not None:
            kw["accum_out"] = _a(accum)
        P.op("act", lambda e: e.activation(_a(out), _a(in_), func, scale=_a(scale), **kw),
             [out, accum], [in_, bias, scale])

    def tt(out, a, b, op, eng="dve"):
        P.op(eng, lambda e: e.tensor_tensor(_a(out), _a(a), _a(b), op), [out], [a, b])

    def ts(out, a, s1, s2, op0, op1=ALU.bypass, eng="dve"):
        P.op(eng, lambda e: e.tensor_scalar(_a(out), _a(a), _a(s1), _a(s2), op0, op1), [out], [a, s1, s2])

    def stt(out, a, s, b, op0, op1):
        P.op("dve", lambda e: e.scalar_tensor_tensor(_a(out), _a(a), _a(s), _a(b), op0, op1), [out], [a, s, b])

    def scan(out, d0, d1, init, op0, op1):
        P.op("dve", lambda e: e.tensor_tensor_scan(_a(out), _a(d0), _a(d1), _a(init), op0, op1),
             [out], [d0, d1, init])

    def cp(out, in_, eng="dve"):
        P.op(eng, lambda e: e.tensor_copy(_a(out), _a(in_)), [out], [in_])

    def recip(out, in_):
        P.op("dve", lambda e: e.reciprocal(_a(out), _a(in_)), [out], [in_])

    def memset(out, val, eng="dve"):
        P.op(eng, lambda e: e.memset(_a(out), val), [out], [])

    def reduce(out, in_, op, axis=AX.X):
        P.op("dve", lambda e: e.tensor_reduce(_a(out), _a(in_), axis, op), [out], [in_])

    def next_pb():
        state["pb_i"] = (state["pb_i"] + 1) % 2
        return PB[state["pb_i"]]

    def load_slab(l, name, ncols=512):
        s = slab[state["slab_i"]]
        state["slab_i"] = (state["slab_i"] + 1) % NSLABBUF
        dma(s[:, :, 0:ncols], wt_d[l, SLAB_ID[name], :, :, 0:ncols], eng="pool")
        return s

    def load_wo(l, si):
        s = slab[state["slab_i"]]
        state["slab_i"] = (state["slab_i"] + 1) % NSLABBUF
        dma(s[:, :, :], wo_d[l, si].rearrange("p f n -> p (f n)").rearrange("p (a b) -> p a b", a=8), eng="pool")
        return s

    def cvc(l, name, i=0):
        o, w = CM.off[name]
        return cv[l][:, o + i:o + i + 1]

    def fm_proj(s, gi, out_cols=T):
        pb = next_pb()
        for kc in range(8):
            mm(pb[:, 0:T], s[:, kc, gi * 128:(gi + 1) * 128], hT[:, kc, :], start=(kc == 0), stop=(kc == 7))
        return pb

    dma(ident[:, :], ident_d)
    for l in range(DEPTH):
        dma(cv[l][:, :], cv_d[l])
        dma(lora_up[l][:, :], lora_d[l])
        memset(b_hist[l][:, :], 0.0)
        memset(b_S[l][:, :, :], 0.0)
    dma(d_dmat[:, :, :], dmat_d)
    dma(d_xi[:, :, :, :], dxi_d.rearrange("p (g j o) -> p g j o", g=4, j=2))
    dma(d_zeta[:, :, :], dzeta_d.rearrange("p (h o) -> p h o", o=1))
    dma(d_gch[:, :, :], dgch_d.rearrange("p (g o) -> p g o", o=1))
    dma(a_mask[:, :], amask_d)
    dma(a_sel[:, :], asel_d)
    for l in range(DEPTH):
        memset(d_R[l][:, :, :], 0.0)
        memset(a_hist[l][:, :, :], 0.0)
        memset(a_C[l][:, :, :], 0.0)
        memset(a_n[l][:, :, :], 0.0)
        memset(a_car[l][:, :], 0.0)
        ts(a_car[l][0:8, 2:3], cv[l][0:8, CM.off["a_fb"][0]:CM.off["a_fb"][0] + 1], -1.0, None, ALU.mult)
    dma(b_ones[:, :], bones_d)
    dma(b_i2[:, 0, :], bi2_d)
    dma(b_m5[:, :, :], bm5_d)
    memset(ones[:, :], 1.0)
    cp(ident16[:, :], ident[:, :])
    memset(ones16[:, :], 1.0)
    memset(zeros[:, :], 0.0)
    for l in range(DEPTH):
        memset(c_hist[l][:, :, :], 0.0)
        memset(c_state[l][:, :], 0.0)
        o, w = CM.off["c_lam"]
        act(sm[:, 0:4], cv[l][:, o:o + 4], AF.Exp, scale=-1.0)
        act(sm[:, 4:8], sm[:, 0:4], AF.Ln, bias=ones[:, 0:1])
        ts(c_coef[l][:, 0:4], sm[:, 4:8], -8.0, None, ALU.mult)
        ts(c_coef[l][:, 4:8], sm[:, 4:8], -16.0, None, ALU.mult)

    def rmsnorm_pre(l, X):
        for tb in range(NTB):
            act(scr[:, :], X[:, tb, :], AF.Square)
            reduce(sm[:, 8 + tb:9 + tb], scr[:, :], ALU.add)
        ts(sm[:, 12:12 + NTB], sm[:, 8:8 + NTB], 1.0 / D_MODEL, 1e-6, ALU.mult, ALU.add)
        act(sm[:, 16:16 + NTB], sm[:, 12:12 + NTB], AF.Sqrt)
        recip(sm[:, 20:20 + NTB], sm[:, 16:16 + NTB])
        for tb in range(NTB):
            act(scr[:, :], X[:, tb, :], AF.Copy, scale=sm[:, 20 + tb:21 + tb])
            for half in range(2):
                pb = next_pb()
                for q in range(4):
                    kc = half * 4 + q
                    tr(pb[:, q * 128:(q + 1) * 128], scr[:, kc * 128:(kc + 1) * 128], ident[:, :])
                for q in range(4):
                    kc = half * 4 + q
                    ts(hT[:, kc, tb * 128:(tb + 1) * 128], pb[:, q * 128:(q + 1) * 128],
                       cvc(l, "gpre", kc), None, ALU.mult)

    def mixer_C(l):
        cx, xc, rg, ig, aa, uu, hh = FS[0], FS[1], FS[2], FS[3], FS[4], FS[5], FS[6]
        dma(lru[l][:, :, :, :], lru_d[l])
        wx = load_slab(l, "Cx")
        for g in range(4):
            pb = fm_proj(wx, g)
            cp(cx[:, g, 0:3], c_hist[l][:, g, :])
            act(cx[:, g, 3:3 + T], pb[:, 0:T], AF.Copy)
            cp(c_hist[l][:, g, :], cx[:, g, T:T + 3])
            o, w = CM.off["c_cw"]
            ts(xc[:, g, 0:T], cx[:, g, 0:T], cv[l][:, o + g:o + g + 1], cvc(l, "c_cb", g), ALU.mult, ALU.add)
            for j in range(1, 4):
                stt(xc[:, g, 0:T], cx[:, g, j:j + T], cv[l][:, o + 4 * j + g:o + 4 * j + g + 1], xc[:, g, 0:T],
                    ALU.mult, ALU.add)
        for g in range(4):
            pb = next_pb()
            mm(pb[:, 0:T], lru[l][:, 0, g, :], xc[:, g, 0:T])
            act(rg[:, g, 0:T], pb[:, 0:T], AF.Sigmoid, bias=cvc(l, "c_br", g))
            pb = next_pb()
            mm(pb[:, 0:T], lru[l][:, 1, g, :], xc[:, g, 0:T])
            act(ig[:, g, 0:T], pb[:, 0:T], AF.Sigmoid, bias=cvc(l, "c_bi", g))
        for g in range(4):
            act(aa[:, g, 0:T], rg[:, g, 0:T], AF.Exp, scale=c_coef[l][:, g:g + 1])
            act(uu[:, g, 0:T], rg[:, g, 0:T], AF.Exp, scale=c_coef[l][:, 4 + g:5 + g])
        ts(uu[:, :, 0:T], uu[:, :, 0:T], -1.0, 1.0, ALU.mult, ALU.add)
        act(uu[:, :, 0:T], uu[:, :, 0:T], AF.Sqrt)
        tt(ig[:, :, 0:T], ig[:, :, 0:T], xc[:, :, 0:T], ALU.mult)
        tt(uu[:, :, 0:T], uu[:, :, 0:T], ig[:, :, 0:T], ALU.mult)
        for g in range(4):
            scan(hh[:, g, 0:T], aa[:, g, 0:T], uu[:, g, 0:T], c_state[l][:, g:g + 1], ALU.mult, ALU.add)
            cp(c_state[l][:, g:g + 1], hh[:, g, T - 1:T])
        wz = load_slab(l, "Cz")
        for g in range(4):
            pb = fm_proj(wz, g)
            act(rg[:, g, 0:T], pb[:, 0:T], AF.Silu)
            tt(yT[:, 8 + g, :], hh[:, g, 0:T], rg[:, g, 0:T], ALU.mult)

    def tm_proj(sl, tb):
        pb = next_pb()
        for kc in range(8):
            mm(pb[:, :], hT[:, kc, tb * 128:(tb + 1) * 128], sl[:, kc, :], start=(kc == 0), stop=(kc == 7))
        return pb

    def to_fm(dst, src, tb):
        pb = next_pb()
        for g in range(4):
            tr(pb[:, g * 128:(g + 1) * 128], src[:, tb, 2 * g:2 * g + 2, 0:64], ident[:, :])
        cp(dst[:, 0:4, tb * 128:(tb + 1) * 128], pb[:, :].rr("p (a b) -> p a b", a=4))

    def to_tm(dst, src, tb):
        pb = next_pb()
        for g in range(4):
            tr(pb[:, g * 128:(g + 1) * 128], src[:, g, tb * 128:(tb + 1) * 128], ident[:, :])
        cp(dst[:, tb, :, 0:64], pb[:, :].rr("p (h e) -> p h e", h=8))

    def head_ln(dst, src, tb):
        X3 = src[:, tb, :, 0:64]
        D3 = dst[:, tb, :, 0:64]
        reduce(lnst[:, 0:8], X3, ALU.add)
        stt(D3, lnst[:, 0:8].rr("p (h o) -> p h o", o=1).bc([128, 8, 64]), -1.0 / 64, X3, ALU.mult, ALU.add)
        tt(scr[:, 0:512].rr("p (h e) -> p h e", h=8), D3, D3, ALU.mult)
        reduce(lnst[:, 8:16], scr[:, 0:512].rr("p (h e) -> p h e", h=8), ALU.add)
        ts(lnst[:, 16:24], lnst[:, 8:16], 1.0 / 64, 1e-5, ALU.mult, ALU.add)
        act(lnst[:, 24:32], lnst[:, 16:24], AF.Sqrt)
        recip(lnst[:, 32:40], lnst[:, 24:32])
        tt(D3, D3, lnst[:, 32:40].rr("p (h o) -> p h o", o=1).bc([128, 8, 64]), ALU.mult)

    def mixer_D(l, ti):
        qr, kr, osb, xn = TM[0], TM[1], TM[4], TM[5]
        vv, kz = TM16[1], TM16[2]
        qT, kT, sz = AlT16, RhT16, FS[2]
        AT = bm16[0]
        d_R16 = b_S16
        cp(d_R16[:, :, :], d_R[l][:, :, :])
        S = [PB[3], PB[4]]; O = [PB[5], PB[6]]; ST = [PB[2], PB[7]]
        dma(rope_sb[:, :, :, :], rope_d.rearrange("(n tb p) (o e) -> n p tb o e", tb=NTB, p=128, o=1)[ti])
        for name, dst in (("Dq", qr), ("Dk", kr)):
            sl = load_slab(l, name)
            for tb in range(NTB):
                pb = tm_proj(sl, tb)
                act(scr[:, 0:512], pb[:, :], AF.Copy)
                raw = scr[:, 0:512].rr("p (h e) -> p h e", h=8)
                x1, x2 = raw[:, :, 0:32], raw[:, :, 32:64]
                cos = rope_sb[:, tb, :, 0:32].bc([128, 8, 32]); sin = rope_sb[:, tb, :, 32:64].bc([128, 8, 32])
                t1 = scr2[:, 0:256].rr("p (h e) -> p h e", h=8); t2 = scr2[:, 256:512].rr("p (h e) -> p h e", h=8)
                d1, d2 = dst[:, tb, :, 0:32], dst[:, tb, :, 32:64]
                tt(d1, x1, cos, ALU.mult); tt(t1, x2, sin, ALU.mult); tt(d1, d1, t1, ALU.subtract)
                tt(d2, x2, cos, ALU.mult); tt(t2, x1, sin, ALU.mult); tt(d2, d2, t2, ALU.add)
                to_fm(qT if name == "Dq" else kT, dst, tb)
        sl = load_slab(l, "Dv")
        for tb in range(NTB):
            pb = tm_proj(sl, tb)
            act(vv[:, tb, :, 0:64], pb[:, :].rr("p (h e) -> p h e", h=8), AF.Copy)
            tt(kz[:, tb, :, 0:64], kr[:, tb, :, 0:64], d_zeta[:, :, :].bc([128, 8, 64]), ALU.mult)
        sl = load_slab(l, "Dz")
        for g in range(4):
            pb = fm_proj(sl, g)
            act(sz[:, g, 0:T], pb[:, 0:T], AF.Silu)
        for c in range(NCH):
            tb, p = c // 2, c % 2
            rows = slice(64 * p, 64 * p + 64)
            cols = slice(c * 64, (c + 1) * 64)
            for j in range(2):
                jr = slice(64 * j, 64 * j + 64)
                for g in range(4):
                    mm(S[j][rows, g * 64:(g + 1) * 64], kT[jr, g, cols], qT[jr, g, cols])
                    mm(S[j][rows, 256 + g * 64:256 + (g + 1) * 64], qT[jr, g, cols], d_R16[jr, g, :])
            for j in range(2):
                tt(AT[rows, 4 * j:4 * j + 4, :], S[j][rows, 0:256].rr("p (a b) -> p a b", a=4),
                   d_dmat[rows, 4 * j:4 * j + 4, :], ALU.mult)
                tt(tmp4[rows, :, j, 0:64], S[j][rows, 256:512].rr("p (a b) -> p a b", a=4),
                   d_xi[rows, :, j, :].bc([64, 4, 64]), ALU.mult)
            for h in range(8):
                g, j = h // 2, h % 2
                mm(O[p][rows, h * 64:(h + 1) * 64], AT[rows, j * 4 + g, :], vv[rows, tb, h, 0:64])
            tt(osb[rows, tb, :, 0:64], O[p][rows, :].rr("p (h e) -> p h e", h=8),
               tmp4[rows, :, :, 0:64].rr("p g j e -> p (g j) e"), ALU.add)
            for h in range(8):
                g, j = h // 2, h % 2
                mm(ST[p][64 * j:64 * j + 64, g * 64:(g + 1) * 64], kz[rows, tb, h, 0:64], vv[rows, tb, h, 0:64])
            tt(d_R[l][:, :, :], d_R[l][:, :, :], d_gch[:, :, :].bc([128, 4, 64]), ALU.mult)
            tt(d_R[l][:, :, :], d_R[l][:, :, :], ST[p][:, 0:256].rr("p (a b) -> p a b", a=4), ALU.add)
            cp(d_R16[:, :, :], d_R[l][:, :, :])
        for tb in range(NTB):
            head_ln(xn, osb, tb)
            pb = next_pb()
            for g in range(4):
                tr(pb[:, g * 128:(g + 1) * 128], xn[:, tb, 2 * g:2 * g + 2, 0:64], ident[:, :])
            for g in range(4):
                stt(yT[:, 12 + g, tb * 128:(tb + 1) * 128], pb[:, g * 128:(g + 1) * 128], cvc(l, "d_nw", g),
                    sz[:, g, tb * 128:(tb + 1) * 128], ALU.mult, ALU.mult)

    def conv_silu(l, sl, g8, dst, gdst, cx, whichhist, cwname, cbname, ngroups_total, silu=True):
        pass

    def mixer_A(l, ti):
        cx, qT32, kT32, sz = FS[0], FS[1], FS[2], FS[3]
        qT, kT = AlT16, RhT16
        so, osb, xn = TM[2], TM[3], TM[4]
        ktm, vv = TM16[0], TM16[1]
        AT, vw = bm16[0], bm16[1]
        a_C16 = b_S16
        cp(a_C16[:, :, :], a_C[l][:, :, :])
        cp(a_n16[:, :, :], a_n[l][:, :, :])
        S = [PB[3], PB[4]]; O = [PB[5], PB[6]]; ST = [PB[2], PB[7]]; JX = [PB[0], PB[1]]
        R_I, R_SP, R_F, R_G, R_P, R_MU, R_PE, R_SI, R_WJ, R_NM, R_NP = range(11)
        ocw = CM.off["a_cw"][0]
        for which, (name, dstT, dst16) in enumerate((("Aq", qT32, qT), ("Ak", kT32, kT))):
            sl = load_slab(l, name)
            for g in range(4):
                g8 = which * 4 + g
                pb = fm_proj(sl, g)
                cp(cx[:, g, 0:3], a_hist[l][:, g8, :])
                act(cx[:, g, 3:3 + T], pb[:, 0:T], AF.Copy)
                cp(a_hist[l][:, g8, :], cx[:, g, T:T + 3])
                ts(dstT[:, g, 0:T], cx[:, g, 0:T], cv[l][:, ocw + g8:ocw + g8 + 1], cvc(l, "a_cb", g8),
                   ALU.mult, ALU.add)
                for j in range(1, 4):
                    stt(dstT[:, g, 0:T], cx[:, g, j:j + T], cv[l][:, ocw + 8 * j + g8:ocw + 8 * j + g8 + 1],
                        dstT[:, g, 0:T], ALU.mult, ALU.add)
                act(dst16[:, g, 0:T], dstT[:, g, 0:T], AF.Silu)
        sl = load_slab(l, "Az")
        for g in range(4):
            pb = fm_proj(sl, g)
            act(sz[:, g, 0:T], pb[:, 0:T], AF.Silu)
        acut = int(os.environ.get("ACUT", "99"))
        if acut < 1:
            return
        sl = load_slab(l, "Ag", ncols=128)
        pbi = next_pb()
        for kc in range(8):
            mm(pbi[0:8, 0:T], sl[:, kc, 0:8], hT[:, kc, :], start=(kc == 0), stop=(kc == 7))
        act(ga[0:8, R_I, 0:T], pbi[0:8, 0:T], AF.Identity, bias=cv[l][0:8, CM.off["a_ib"][0]:CM.off["a_ib"][0] + 1])
        gcut = int(os.environ.get("GCUT", "99"))
        if gcut < 1:
            return
        pbf = next_pb()
        for kc in range(8):
            mm(pbf[0:8, 0:T], sl[:, kc, 8:16], hT[:, kc, :], start=(kc == 0), stop=(kc == 7))
        act(ga[0:8, R_SP, 0:T], pbf[0:8, 0:T], AF.Exp, bias=a_car[l][0:8, 2:3], scale=-1.0)
        act(ga[0:8, R_SP, 0:T], ga[0:8, R_SP, 0:T], AF.Ln, bias=ones[0:8, 0:1])
        if gcut < 2:
            return
        scan(ga[0:8, R_F, 0:T], ones[0:8, 0:T], ga[0:8, R_SP, 0:T], a_car[l][0:8, 0:1], ALU.mult, ALU.subtract)
        cp(a_car[l][0:8, 0:1], ga[0:8, R_F, T - 1:T])
        tt(ga[0:8, R_G, 0:T], ga[0:8, R_I, 0:T], ga[0:8, R_F, 0:T], ALU.subtract)
        if gcut < 3:
            return
        cp(ga[0:8, R_P, 0:1], a_car[l][0:8, 1:2])
        scan(ga[0:8, R_P, 1:T + 1], ones[0:8, 0:T], ga[0:8, R_G, 0:T], a_car[l][0:8, 1:2], ALU.mult, ALU.max)
        cp(a_car[l][0:8, 1:2], ga[0:8, R_P, T:T + 1])
        if gcut < 4:
            return
        for c in range(NCH):
            cols = slice(c * 64, (c + 1) * 64)
            ts(ga[0:8, R_MU, cols], zeros[0:8, 0:64], ga[0:8, R_P, c * 64:c * 64 + 1], None, ALU.add)
            ts(ga[0:8, R_PE, cols], zeros[0:8, 0:64], ga[0:8, R_P, c * 64 + 64:c * 64 + 65], None, ALU.add)
            tt(scl[0:8, c:c + 1], ga[0:8, R_P, c * 64:c * 64 + 1], ga[0:8, R_P, c * 64 + 64:c * 64 + 65], ALU.subtract)
        if gcut < 5:
            return
        Pv = ga[0:8, R_P, 1:T + 1]
        tt(ga[0:8, R_SI, 0:T], ga[0:8, R_MU, 0:T], Pv, ALU.subtract)
        tt(ga[0:8, R_WJ, 0:T], ga[0:8, R_G, 0:T], ga[0:8, R_PE, 0:T], ALU.subtract)
        stt(ga[0:8, R_NM, 0:T], ga[0:8, R_F, 0:T], -1.0, Pv, ALU.mult, ALU.subtract)
        ts(ga[0:8, R_NP, 0:T], Pv, -1.0, None, ALU.mult)
        if acut < 2:
            return
        for tb in range(NTB):
            pb = next_pb()
            for r, R in enumerate((R_SI, R_WJ, R_NM)):
                tr(pb[:, r * 8:(r + 1) * 8], ga[0:8, R, tb * 128:(tb + 1) * 128], ident[0:8, 0:8])
            act(tmsc[:, tb, :, :], pb[:, 0:24].rr("p (a b) -> p a b", a=3), AF.Exp)
        if acut < 3:
            return
        tt(rhs_sc[0:8, :, :], scl[0:8, :].rr("p (c o) -> p c o", o=1).bc([8, NCH, 4]),
           a_sel[0:8, 512:516].rr("p (o g) -> p o g", o=1).bc([8, NCH, 4]), ALU.mult)
        pb = next_pb()
        mm(pb[:, 0:NCH * 4], a_sel[0:8, 516:644], rhs_sc[0:8, :, :].rr("p c g -> p (c g)"))
        act(scb[:, :, :, :].rr("p c g o -> p (c g o)"), pb[:, 0:NCH * 4], AF.Exp)
        if acut < 4:
            return
        for tb in range(NTB):
            pbt = next_pb()
            for g in range(4):
                mm(pbt[:, g * 128:(g + 1) * 128], kT[:, g, tb * 128:(tb + 1) * 128], ident16[:, :])
            cp(ktm[:, tb, :, :], pbt[:, :].rr("p (h e) -> p h e", h=8))
            cp(wj16[:, tb, :], tmsc[:, tb, 1, :])
        sl = load_slab(l, "Av")
        for tb in range(NTB):
            pb = tm_proj(sl, tb)
            act(vv[:, tb, :, :], pb[:, :].rr("p (h e) -> p h e", h=8), AF.Copy)
        sl = load_slab(l, "Ao")
        for tb in range(NTB):
            pb = tm_proj(sl, tb)
            act(so[:, tb, :, :], pb[:, :].rr("p (h e) -> p h e", h=8), AF.Sigmoid)
        for tb in range(NTB):
            E = next_pb()
            mm(E[:, :], ident[:, :], a_mask[:, :], start=True, stop=False)
            mm(E[:, :], ga[0:8, R_G, tb * 128:(tb + 1) * 128], a_sel[0:8, 0:512], start=False, stop=False)
            for p in range(2):
                c = tb * 2 + p
                tt(negPx[0:8, p, :, :], ga[0:8, R_NP:R_NP + 1, c * 64:(c + 1) * 64].bc([8, 8, 64]),
                   a_sel[0:8, 0:512].rr("p (h e) -> p h e", h=8), ALU.mult)
                mm(E[64 * p:64 * p + 64, :], ones[0:8, 0:64], negPx[0:8, p, :, :].rr("p h e -> p (h e)"),
                   start=False, stop=True)
            act(ET[:, :, :].rr("p h e -> p (h e)"), E[:, :], AF.Exp)
            if acut < 5:
                continue
            for p in range(2):
                c = tb * 2 + p
                rows = slice(64 * p, 64 * p + 64)
                cols = slice(c * 64, (c + 1) * 64)
                sI4 = tmsc[:, tb, 0, :].rr("p (g j o) -> p g j o", g=4, j=2)
                for j in range(2):
                    jr = slice(64 * j, 64 * j + 64)
                    for g in range(4):
                        mm(S[j][rows, g * 64:(g + 1) * 64], kT[jr, g, cols], qT[jr, g, cols])
                        mm(JX[j][rows, g * 64:(g + 1) * 64], qT[jr, g, cols], a_C16[jr, g, :])
                        mm(JX[j][rows, 256 + g:257 + g], qT[jr, g, cols], a_n16[jr, g, :])
                for j in range(2):
                    stt(AT[rows, 4 * j:4 * j + 4, :], S[j][rows, 0:256].rr("p (a b) -> p a b", a=4), 0.125,
                        ET[rows, 4 * j:4 * j + 4, :], ALU.mult, ALU.mult)
                    tt(tmp4[rows, :, j, :], JX[j][rows, 0:256].rr("p (a b) -> p a b", a=4),
                       sI4[rows, :, j, :].bc([64, 4, 64]), ALU.mult)
                    tt(lnst[rows, 40:48].rr("p (g j) -> p g j", j=2)[:, :, j], JX[j][rows, 256:260],
                       tmsc[rows, tb, 0, :].rr("p (g j) -> p g j", j=2)[:, :, j], ALU.mult)
                for h in range(8):
                    g, j = h // 2, h % 2
                    mm(O[p][rows, h * 64:(h + 1) * 64], AT[rows, j * 4 + g, :], vv[rows, tb, h, :])
                    mm(ST[p][rows, 384 + h:385 + h], AT[rows, j * 4 + g, :], ones16[rows, 0:1])
                tt(hnum[rows, :, :], O[p][rows, :].rr("p (h e) -> p h e", h=8),
                   tmp4[rows, :, :, :].rr("p g j e -> p (g j) e"), ALU.add)
                tt(lnst[rows, 48:56], ST[p][rows, 384:392], lnst[rows, 40:48], ALU.add)
                stt(lnst[rows, 48:56], lnst[rows, 48:56], -1.0, lnst[rows, 48:56], ALU.mult, ALU.max)
                tt(lnst[rows, 48:56], lnst[rows, 48:56], tmsc[rows, tb, 2, :], ALU.max)
                recip(lnst[rows, 56:64], lnst[rows, 48:56])
                tt(hnum[rows, :, :], hnum[rows, :, :],
                   lnst[rows, 56:64].rr("p (h o) -> p h o", o=1).bc([64, 8, 64]), ALU.mult)
                tt(osb[rows, tb, :, :], hnum[rows, :, :], so[rows, tb, :, :], ALU.mult)
                tt(vw[rows, :, :], vv[rows, tb, :, :],
                   tmsc[rows, tb, 1, :].rr("p (h o) -> p h o", o=1).bc([64, 8, 64]), ALU.mult)
                for h in range(8):
                    g, j = h // 2, h % 2
                    jr = slice(64 * j, 64 * j + 64)
                    mm(ST[p][jr, g * 64:(g + 1) * 64], ktm[rows, tb, h, :], vw[rows, h, :])
                    mm(ST[p][jr, 256 + g:257 + g], ktm[rows, tb, h, :], wj16[rows, tb, h:h + 1])
                tt(a_C[l][:, :, :], a_C[l][:, :, :], scb[:, c, :, :].bc([128, 4, 64]), ALU.mult)
                stt(a_C[l][:, :, :], ST[p][:, 0:256].rr("p (a b) -> p a b", a=4), 0.125, a_C[l][:, :, :],
                    ALU.mult, ALU.add)
                tt(a_n[l][:, :, :], a_n[l][:, :, :], scb[:, c, :, :], ALU.mult)
                stt(a_n[l][:, :, :], ST[p][:, 256:260].rr("p (a b) -> p a b", b=1), 0.125, a_n[l][:, :, :],
                    ALU.mult, ALU.add)
                cp(a_C16[:, :, :], a_C[l][:, :, :])
                cp(a_n16[:, :, :], a_n[l][:, :, :])
        for tb in range(NTB):
            head_ln(xn, osb, tb)
            pb = next_pb()
            for g in range(4):
                tr(pb[:, g * 128:(g + 1) * 128], xn[:, tb, 2 * g:2 * g + 2, 0:64], ident[:, :])
            for g in range(4):
                stt(yT[:, g, tb * 128:(tb + 1) * 128], pb[:, g * 128:(g + 1) * 128], cvc(l, "a_nw", g),
                    sz[:, g, tb * 128:(tb + 1) * 128], ALU.mult, ALU.mult)

    def mixer_B(l, ti):
        rS, kS, vS, sz = FS[0], FS[1], FS[2], FS[3]
        RhT, AlT, bon = rS, kS, vS
        pools = (FS[4], FS[5], FS[6])

        def slot(i):
            return pools[i // 4][:, i % 4, :]
        raw, lora, aT, lw, lc, kt, kh, beta, eg, egi, egm, tA = [slot(i) for i in range(12)]
        wkv, xn = TM[3], TM[4]
        Vtm, Btm, Ktm = TM16[0], TM16[1], TM16[2]
        BJ = [PB[3], PB[4]]; PA = [PB[2], PB[7]]; PQ = [PB[5], PB[6]]; PX = [PB[0], PB[1]]
        Pn, Qn, Xm = bm16
        W2 = AT
        beta16, kt16 = bk16[:, 0, :], bk16[:, 1, :]
        omu = CM.off["b_mu"][0]

        def shift_mix(dst, pb, hidx, mucol, nrows=128):
            cp(raw[:, 0:1], b_hist[l][:, hidx:hidx + 1])
            act(raw[:, 1:T + 1], pb[:, 0:T], AF.Copy)
            cp(b_hist[l][:, hidx:hidx + 1], raw[:, T:T + 1])
            tt(tA[:, 0:T], raw[:, 0:T], raw[:, 1:T + 1], ALU.subtract)
            stt(dst, tA[:, 0:T], cv[l][:, omu + mucol:omu + mucol + 1], raw[:, 1:T + 1], ALU.mult, ALU.add)

        cp(b_S16[:, :, :], b_S[l][:, :, :])
        sl = load_slab(l, "Bl", ncols=128)
        pb = fm_proj(sl, 0)
        shift_mix(lora[:, 0:T], pb, 12, 12)
        act(lora[0:64, 0:T], lora[0:64, 0:T], AF.Tanh)
        for wi, (name, dst) in enumerate((("Br", rS), ("Bk", kS), ("Bv", vS))):
            sl = load_slab(l, name)
            for g in range(4):
                pb = fm_proj(sl, g)
                shift_mix(dst[:, g, 0:T], pb, wi * 4 + g, wi * 4 + g)
        sl = load_slab(l, "Bz")
        for g in range(4):
            pb = fm_proj(sl, g)
            act(sz[:, g, 0:T], pb[:, 0:T], AF.Silu)
        bcut = int(os.environ.get("BCUT", "99"))
        if bcut < 1:
            return
        for g in range(4):
            gc = slice(g * 128, (g + 1) * 128)
            pb = next_pb()
            mm(pb[:, 0:T], lora_up[l][0:64, gc], lora[0:64, 0:T])
            act(lw[:, 0:T], pb[:, 0:T], AF.Sigmoid, bias=cvc(l, "b_w0", g))
            ts(lw[:, 0:T], lw[:, 0:T], -0.606531, None, ALU.mult)
            pb = next_pb()
            mm(pb[:, 0:T], lora_up[l][64:128, gc], lora[64:128, 0:T])
            act(aT[:, 0:T], pb[:, 0:T], AF.Sigmoid, bias=cvc(l, "b_a0", g))
            for c in range(NCH):
                cols = slice(c * 64, (c + 1) * 64)
                scan(lc[:, cols], ones[:, 0:64], lw[:, cols], 0.0, ALU.mult, ALU.add)
            act(eg[:, 0:T], lc[:, 0:T], AF.Exp)
            act(egi[:, 0:T], lc[:, 0:T], AF.Exp, scale=-1.0)
            tt(tA[:, 0:T], lc[:, 0:T], lw[:, 0:T], ALU.subtract)
            act(egm[:, 0:T], tA[:, 0:T], AF.Exp)
            for c in range(NCH):
                cp(gLt[:, g, c:c + 1], eg[:, c * 64 + 63:c * 64 + 64])
            ts(kh[:, 0:T], kS[:, g, 0:T], cvc(l, "b_kk", g), None, ALU.mult)
            tt(tA[:, 0:T], kh[:, 0:T], kh[:, 0:T], ALU.mult)
            pb = next_pb()
            mm(pb[:, 0:T], b_ones[:, :], tA[:, 0:T])
            ts(tA[:, 0:T], pb[:, 0:T], 1e-12, None, ALU.add)
            act(tA[:, 0:T], tA[:, 0:T], AF.Sqrt)
            recip(tA[:, 0:T], tA[:, 0:T])
            tt(kh[:, 0:T], kh[:, 0:T], tA[:, 0:T], ALU.mult)
            ts(tA[:, 0:T], aT[:, 0:T], -1.0, cvc(l, "b_ka", g), ALU.add, ALU.mult)
            stt(kt[:, 0:T], tA[:, 0:T], 1.0, kS[:, g, 0:T], ALU.add, ALU.mult)
            for tb in range(NTB):
                pbt = next_pb()
                tr(pbt[:, 0:128], vS[:, g, tb * 128:(tb + 1) * 128], ident[:, :])
                cp(Vtm[:, tb, 2 * g:2 * g + 2, :], pbt[:, 0:128].rr("p (a b) -> p a b", a=2))
            stt(tA[:, 0:T], rS[:, g, 0:T], cvc(l, "b_rk", g), kt[:, 0:T], ALU.mult, ALU.mult)
            pbb = next_pb()
            mm(pbb[:, 0:T], b_ones[:, :], tA[:, 0:T])
            tt(bon[:, g, 0:T], pbb[:, 0:T], vS[:, g, 0:T], ALU.mult)
            tt(beta[:, 0:T], aT[:, 0:T], kh[:, 0:T], ALU.mult)
            tt(beta16[:, 0:T], beta[:, 0:T], egi[:, 0:T], ALU.mult)
            tt(AlT16[:, g, 0:T], kh[:, 0:T], egm[:, 0:T], ALU.mult)
            tt(kt16[:, 0:T], kt[:, 0:T], egi[:, 0:T], ALU.mult)
            tt(RhT16[:, g, 0:T], rS[:, g, 0:T], eg[:, 0:T], ALU.mult)
            for tb in range(NTB):
                pbt = next_pb()
                mm(pbt[:, 0:128], beta16[:, tb * 128:(tb + 1) * 128], ident16[:, :])
                mm(pbt[:, 128:256], kt16[:, tb * 128:(tb + 1) * 128], ident16[:, :])
                cp(Btm[:, tb, 2 * g:2 * g + 2, :], pbt[:, 0:128].rr("p (a b) -> p a b", a=2))
                cp(Ktm[:, tb, 2 * g:2 * g + 2, :], pbt[:, 128:256].rr("p (a b) -> p a b", a=2))
            for tb in range(NTB):
                for j in range(2):
                    jr = slice(64 * j, 64 * j + 64)
                    for p in range(2):
                        c = tb * 2 + p
                        rows = slice(64 * p, 64 * p + 64)
                        cols = slice(c * 64, (c + 1) * 64)
                        A_, B_, K_, R_ = AlT16[jr, g, cols], beta16[jr, cols], kt16[jr, cols], RhT16[jr, g, cols]
                        mm(BJ[j][rows, 0:64], A_, B_)
                        mm(BJ[j][rows, 64:128], B_, A_)
                        mm(BJ[j][rows, 128:192], K_, A_)
                        mm(BJ[j][rows, 192:256], B_, R_)
                        mm(BJ[j][rows, 256:320], K_, R_)
                    tt(PRall[tb][:, :, 2 * g + j, :], BJ[j][:, 0:320].rr("p (k t) -> p k t", k=5), b_m5[:, :, :],
                       ALU.mult)
        if bcut < 2:
            return
        for tb in range(NTB):
            PRt = PRall[tb]
            P0, Q0, MkT, NbT, NkT = (PRt[:, k, :, :] for k in range(5))
            Wsb, Usb = PRt[:, 2, :, :], PRt[:, 4, :, :]
            for p in range(2):
                rows = slice(64 * p, 64 * p + 64)
                c = tb * 2 + p
                for h in range(8):
                    g, j = h // 2, h % 2
                    mm(PA[p][rows, h * 64:(h + 1) * 64], MkT[rows, h, :], Vtm[rows, tb, h, :])
                    mm(PQ[p][rows, h * 64:(h + 1) * 64], NkT[rows, h, :], Vtm[rows, tb, h, :])
                    mm(PX[p][64 * j:64 * j + 64, g * 64:(g + 1) * 64], Ktm[rows, tb, h, :], Vtm[rows, tb, h, :])
                act(W2[rows, :, :], PA[p][rows, :].rr("p (h e) -> p h e", h=8), AF.Copy)
                act(Y2[rows, :, :], PQ[p][rows, :].rr("p (h e) -> p h e", h=8), AF.Copy)
                tt(K2g[:, p, :, :], PX[p][:, 0:256].rr("p (a b) -> p a b", a=4),
                   gLt[:, :, c:c + 1].bc([128, 4, 64]), ALU.mult)
            tt(Xm[:, :, :], b_i2[:, :, :].bc([128, 8, 64]), Q0, ALU.subtract)
            Pc, Qc = P0, Q0
            nxt = [(Pn, Qn), (P0, Q0)]
            for i in range(1, 6):
                Pd, Qd = nxt[(i - 1) % 2]
                for p in range(2):
                    rows = slice(64 * p, 64 * p + 64)
                    for h in range(8):
                        hc = slice(h * 64, (h + 1) * 64)
                        mm(PA[p][rows, hc], Qc[rows, h, :], Pc[rows, h, :])
                        if i < 5:
                            mm(PQ[p][rows, hc], Pc[rows, h, :], Qc[rows, h, :])
                for p in range(2):
                    rows = slice(64 * p, 64 * p + 64)
                    act(Pd[rows, :, :], PA[p][rows, :].rr("p (h e) -> p h e", h=8), AF.Copy)
                    if i < 5:
                        cp(Qd[rows, :, :], PQ[p][rows, :].rr("p (h e) -> p h e", h=8))
                Pc, Qc = Pd, Qd
                for p in range(2):
                    rows = slice(64 * p, 64 * p + 64)
                    for h in range(8):
                        mm(PX[p][rows, h * 64:(h + 1) * 64], Pc[rows, h, :], Xm[rows, h, :])
                for p in range(2):
                    rows = slice(64 * p, 64 * p + 64)
                    tt(Xm[rows, :, :], Xm[rows, :, :], PX[p][rows, :].rr("p (h e) -> p h e", h=8), ALU.add)
            if bcut < 3:
                continue
            for p in range(2):
                c = tb * 2 + p
                rows = slice(64 * p, 64 * p + 64)
                cols = slice(c * 64, (c + 1) * 64)
                for j in range(2):
                    jr = slice(64 * j, 64 * j + 64)
                    for g in range(4):
                        mm(BJ[j][rows, g * 64:(g + 1) * 64], AlT16[jr, g, cols], b_S16[jr, g, :])
                        mm(BJ[j][rows, 256 + g * 64:256 + (g + 1) * 64], RhT16[jr, g, cols], b_S16[jr, g, :])
                W24 = W2[rows, :, :].rr("p (g j) e -> p g j e", j=2)
                Ws4 = Wsb[rows, :, :].rr("p (g j) e -> p g j e", j=2)
                Y24 = Y2[rows, :, :].rr("p (g j) e -> p g j e", j=2)
                for j in range(2):
                    tt(Ws4[:, :, j, :], BJ[j][rows, 0:256].rr("p (a b) -> p a b", a=4), W24[:, :, j, :], ALU.add)
                    tt(tmp4[rows, :, j, :], BJ[j][rows, 256:512].rr("p (a b) -> p a b", a=4), Y24[:, :, j, :],
                       ALU.add)
                for h in range(8):
                    mm(PX[p][rows, h * 64:(h + 1) * 64], Xm[rows, h, :], Wsb[rows, h, :])
                act(Usb[rows, :, :], PX[p][rows, :].rr("p (h e) -> p h e", h=8), AF.Copy)
                for h in range(8):
                    g, j = h // 2, h % 2
                    mm(PA[p][rows, h * 64:(h + 1) * 64], NbT[rows, h, :], Usb[rows, h, :])
                    mm(PQ[p][64 * j:64 * j + 64, g * 64:(g + 1) * 64], Btm[rows, tb, h, :], Usb[rows, h, :])
                tt(wkv[rows, tb, :, :], tmp4[rows, :, :, :].rr("p g j e -> p (g j) e"),
                   PA[p][rows, :].rr("p (h e) -> p h e", h=8), ALU.subtract)
                tt(b_S[l][:, :, :], b_S[l][:, :, :], PQ[p][:, 0:256].rr("p (a b) -> p a b", a=4), ALU.subtract)
                tt(b_S[l][:, :, :], b_S[l][:, :, :], gLt[:, :, c:c + 1].bc([128, 4, 64]), ALU.mult)
                tt(b_S[l][:, :, :], b_S[l][:, :, :], K2g[:, p, :, :], ALU.add)
                cp(b_S16[:, :, :], b_S[l][:, :, :])
        ogw, ogb = CM.off["b_gw"][0], CM.off["b_gb"][0]
        for tb in range(NTB):
            head_ln(xn, wkv, tb)
            pb = next_pb()
            for g in range(4):
                tr(pb[:, g * 128:(g + 1) * 128], xn[:, tb, 2 * g:2 * g + 2, 0:64], ident[:, :])
            for g in range(4):
                tc_ = slice(tb * 128, (tb + 1) * 128)
                ts(scr[:, 0:128], pb[:, g * 128:(g + 1) * 128], cv[l][:, ogw + g:ogw + g + 1],
                   cv[l][:, ogb + g:ogb + g + 1], ALU.mult, ALU.add)
                tt(scr[:, 0:128], scr[:, 0:128], bon[:, g, tc_], ALU.add)
                tt(yT[:, 4 + g, tc_], scr[:, 0:128], sz[:, g, tc_], ALU.mult)

    def out_proj_residual(l, X):
        dma(gpost[l][:, :], gpost_d[l])
        for si in range(4):
            s = load_wo(l, si)
            for tb in range(NTB):
                for half in range(2):
                    pb = PB[4 + tb * 2 + half]
                    for q in range(4):
                        fc = si * 4 + q
                        mm(pb[:, :], yT[:, fc, tb * 128:(tb + 1) * 128], s[:, q * 2 + half, :],
                           start=(fc == 0), stop=(fc == 15))
        if cut < 4:
            return
        for tb in range(NTB):
            for half in range(2):
                act(scr[:, half * 512:(half + 1) * 512], PB[4 + tb * 2 + half][:, :], AF.Copy)
            if cut < 5:
                continue
            act(scr2[:, :], scr[:, :], AF.Square)
            reduce(sm[:, 24:25], scr2[:, :], ALU.add)
            ts(sm[:, 25:26], sm[:, 24:25], 1.0 / D_MODEL, 1e-6, ALU.mult, ALU.add)
            act(sm[:, 26:27], sm[:, 25:26], AF.Sqrt)
            recip(sm[:, 27:28], sm[:, 26:27])
            if cut < 6:
                continue
            stt(scr2[:, :], scr[:, :], sm[:, 27:28], gpost[l][:, :], ALU.mult, ALU.mult)
            if cut < 7:
                continue
            tt(X[:, tb, :], X[:, tb, :], scr2[:, :], ALU.add)

    xv = x_d.rearrange("(n tb p) d -> n p tb d", tb=NTB, p=128)
    yv = y_d.rearrange("(n tb p) d -> n p tb d", tb=NTB, p=128)
    for ti in range(ntiles):
        X = xt[0]
        dma(X[:, :, :], xv[ti])
        for l in range(nlayers):
            if cut >= 1:
                rmsnorm_pre(l, X)
            memset(yT[:, :, :], 0.0)
            if "C" in mixers and cut >= 2:
                mixer_C(l)
            if "D" in mixers:
                mixer_D(l, ti)
            if "A" in mixers:
                mixer_A(l, ti)
            if "B" in mixers:
                mixer_B(l, ti)
            if cut >= 3:
                out_proj_residual(l, X)
        dma(yv[ti], X[:, :, :])

    P.emit()
    return nc, stack


def kernel(**inputs):
    inputs = {k: np.asarray(v) for k, v in inputs.items()}
    return run(inputs)


def make_in_map(xc, hp, tb):
    m = dict(x=xc, wt=hp["wt"], wo=hp["wo"], cv=hp["cv"], lora=hp["lora"], lru=hp["lru"], gpost=hp["gpost"])
    m.update(tb)
    return m


def run(inputs, ntiles=SEQ // T, nlayers=DEPTH, mixers="ABCD"):
    hp = host_prepare(inputs)
    tb = host_tables()
    nc, stack = build_program(ntiles, nlayers, mixers)
    x = np.ascontiguousarray(inputs["x"], dtype=np.float32)
    in_maps = [make_in_map(x[c], hp, tb) for c in range(NCORES)]
    with stack:
        res = run_bass_kernel_spmd(nc, in_maps, core_ids=list(range(NCORES)))
    out = np.stack([np.asarray(r["y"]) for r in res.results], axis=0)
    return out.astype(np.float32)
```

```python
import contextlib
import numpy as np
import concourse.bass as bass
import concourse.mybir as mybir
from concourse.bass_utils import run_bass_kernel_spmd

F32 = mybir.dt.float32
BF16 = mybir.dt.bfloat16
AF = mybir.ActivationFunctionType
ALU = mybir.AluOpType
AX = mybir.AxisListType

D_MODEL = 1024
SEQ = 4096
BATCH = 4
DEPTH = 2
G = 512
D_IN = 7824
T = 256
NTB = T // 128
NCH = T // 64
NCORES = 4
NSLABBUF = 3
import os as _os
INORDER = tuple(_os.environ.get("INORDER", "pe").split(","))


class Buf:
    def __init__(self, name, tile, psum=False):
        self.name = name
        self.tile = tile
        self.psum = psum
        self.acc = []
        self.dma_sem = None
        self.dma_ops = []

    def __getitem__(self, idx):
        return Ref(self, self.tile[idx])


def _box(ap):
    pat = ap.ap
    pstep = pat[0][0]
    off = int(ap.offset)
    p0 = off // pstep if pstep else 0
    f0 = off % pstep if pstep else off
    ext = 0
    for st, cnt in pat[1:]:
        ext += abs(st) * (cnt - 1)
    return (p0, p0 + pat[0][1], f0, f0 + ext + 1)


class Ref:
    def __init__(self, buf, ap, box=None):
        self.buf = buf
        self.ap = ap
        if box is None:
            box = _box(ap)
            if buf.psum:
                box = ((box[0] // 32) * 32, ((box[1] + 31) // 32) * 32, 0, 512)
        self.box = box

    def bc(self, shape):
        return Ref(self.buf, self.ap.to_broadcast(list(shape)), self.box)

    def __getitem__(self, idx):
        return Ref(self.buf, self.ap[idx])

    def rr(self, pat, **kw):
        return Ref(self.buf, self.ap.rearrange(pat, **kw), self.box)


def _ovl(a, b):
    return a[0] < b[1] and b[0] < a[1] and a[2] < b[3] and b[2] < a[3]


def _cov(a, b):
    return a[0] <= b[0] and a[1] >= b[1] and a[2] <= b[2] and a[3] >= b[3]


class Prog:
    ENG = ("pe", "act", "dve", "pool", "sp")

    def __init__(self, nc, stack):
        self.nc = nc
        self.stack = stack
        self.ops = []
        self.nbuf = 0

    def sb(self, name, shape, dtype=F32):
        t = self.stack.enter_context(self.nc.sbuf_tensor("s_" + name, list(shape), dtype))
        return Buf(name, t)

    def ps(self, name):
        t = self.stack.enter_context(self.nc.psum_tensor("p_" + name, [128, 512], F32))
        return Buf(name, t, psum=True)

    def op(self, eng, fn, outs, ins, dma=False):
        oid = len(self.ops)
        deps = set()
        for r, w in [(x, True) for x in outs] + [(x, False) for x in ins]:
            if r is None or not isinstance(r, Ref):
                continue
            b = r.buf
            for (bx, o2, w2) in b.acc:
                if (w or w2) and _ovl(bx, r.box):
                    deps.add(o2)
        for r, w in [(x, True) for x in outs] + [(x, False) for x in ins]:
            if r is None or not isinstance(r, Ref):
                continue
            b = r.buf
            if w:
                b.acc = [a for a in b.acc if not _cov(r.box, a[0])]
            b.acc.append((r.box, oid, w))
            if len(b.acc) > 48:
                merged = {}
                for (bx, o2, w2) in b.acc:
                    k = (self.ops[o2]["eng"] if o2 < oid else eng, w2)
                    if k in merged:
                        m = merged[k]
                        merged[k] = ((min(m[0][0], bx[0]), max(m[0][1], bx[1]), min(m[0][2], bx[2]),
                                      max(m[0][3], bx[3])), max(m[1], o2), w2)
                    else:
                        merged[k] = (bx, o2, w2)
                b.acc = list(merged.values())
        dbuf = None
        if dma:
            for r in list(outs) + list(ins):
                if isinstance(r, Ref):
                    dbuf = r.buf
            dbuf.dma_ops.append(oid)
        deps.discard(oid)
        self.ops.append(dict(eng=eng, fn=fn, deps=sorted(deps), dma=dma, dbuf=dbuf, sig=False))
        return oid

    def emit(self):
        nc = self.nc
        ops = self.ops
        for o in ops:
            for d in o["deps"]:
                od = ops[d]
                if od["dma"]:
                    continue
                if od["eng"] == o["eng"] and o["eng"] in INORDER and not o["dma"]:
                    continue
                od["sig"] = True
        sems = {e: self.stack.enter_context(nc.semaphore("sem_" + e)) for e in self.ENG}
        cnt = {e: 0 for e in self.ENG}
        for o in ops:
            if o["dma"]:
                b = o["dbuf"]
                if b.dma_sem is None:
                    b.dma_sem = self.stack.enter_context(nc.semaphore("dsem_" + b.name))
            elif o["sig"]:
                cnt[o["eng"]] += 1
                o["signo"] = cnt[o["eng"]]
        per_eng = {e: [] for e in self.ENG}
        for i, o in enumerate(ops):
            per_eng[o["eng"]].append(i)
        import bisect

        def gen(engname, e):
            waited = {}
            for i in per_eng[engname]:
                o = ops[i]
                need = {}
                for d in o["deps"]:
                    od = ops[d]
                    if od["dma"]:
                        b = od["dbuf"]
                        n = bisect.bisect_left(b.dma_ops, i)
                        key = ("d", id(b))
                        if need.get(key, (None, 0))[1] < 16 * n:
                            need[key] = (b.dma_sem, 16 * n)
                    else:
                        if od["eng"] == engname and engname in INORDER and not o["dma"]:
                            continue
                        key = ("e", od["eng"])
                        if need.get(key, (None, 0))[1] < od["signo"]:
                            need[key] = (sems[od["eng"]], od["signo"])
                for key, (sem, val) in need.items():
                    if waited.get(key, 0) >= val:
                        continue
                    e.wait_ge(sem, val)
                    waited[key] = val
                ins = o["fn"](e)
                if o["dma"]:
                    ins.then_inc(o["dbuf"].dma_sem, 16)
                elif o["sig"]:
                    ins.then_inc(sems[engname], 1)

        with nc.Block() as block:
            @block.tensor
            def _(e):
                gen("pe", e)

            @block.scalar
            def _(e):
                gen("act", e)

            @block.vector
            def _(e):
                gen("dve", e)

            @block.gpsimd
            def _(e):
                gen("pool", e)

            @block.sync
            def _(e):
                gen("sp", e)
                seen = set()
                for o in ops:
                    if o["dma"] and id(o["dbuf"]) not in seen:
                        seen.add(id(o["dbuf"]))
                        e.wait_ge(o["dbuf"].dma_sem, 16 * len(o["dbuf"].dma_ops))


def _a(x):
    return x.ap if isinstance(x, Ref) else x


def _fm4(v):
    return np.ascontiguousarray(v.reshape(-1, 128).T)


class CMap:
    def __init__(self):
        self.off = {}
        self.n = 0

    def add(self, name, w):
        self.off[name] = (self.n, w)
        self.n += w


def build_cmap():
    c = CMap()
    c.add("gpre", 8)
    c.add("a_cw", 32); c.add("a_cb", 8); c.add("a_ib", 1); c.add("a_fb", 1); c.add("a_nw", 4)
    c.add("b_mu", 13); c.add("b_w0", 4); c.add("b_a0", 4); c.add("b_kk", 4); c.add("b_ka", 4)
    c.add("b_rk", 4); c.add("b_gw", 4); c.add("b_gb", 4)
    c.add("c_cw", 16); c.add("c_cb", 4); c.add("c_br", 4); c.add("c_bi", 4); c.add("c_lam", 4)
    c.add("d_nw", 4)
    return c


CM = build_cmap()

def _cols(a, b):
    return list(range(a, b))


def build_slabs():
    slabs = []
    def fm(name, start):
        slabs.append((name, _cols(start, start + 512)))
    fm("Aq", 0); fm("Ak", 512); fm("Az", 2048)
    slabs.append(("Ag", _cols(2560, 2576) + [-1] * (512 - 16)))
    fm("Br", 2576); fm("Bk", 3088); fm("Bv", 3600)
    slabs.append(("Bl", _cols(4112, 4240) + [-1] * (512 - 128)))
    fm("Bz", 4240); fm("Cx", 4752); fm("Cz", 5264); fm("Dz", 7312)
    fm("Av", 1024); fm("Ao", 1536); fm("Dq", 5776); fm("Dk", 6288); fm("Dv", 6800)
    return slabs


SLABS = build_slabs()
SLAB_ID = {s[0]: i for i, s in enumerate(SLABS)}
NSLAB = len(SLABS)


def host_prepare(inp):
    f = np.float32
    w_in = inp["w_in"]
    w_in_p = np.concatenate([w_in, np.zeros((DEPTH, D_MODEL, 1), f)], axis=2)
    wt = np.empty((DEPTH, NSLAB, 128, 8, 512), f)
    for si, (name, cols) in enumerate(SLABS):
        blk = w_in_p[:, :, cols]
        wt[:, si] = blk.reshape(DEPTH, 8, 128, 512).transpose(0, 2, 1, 3)
    w_out = inp["w_out"]
    wo = np.ascontiguousarray(w_out.reshape(DEPTH, 4, 4, 128, 1024).transpose(0, 1, 3, 2, 4))
    cv = np.zeros((DEPTH, 128, CM.n), f)
    def put(name, arr):
        o, w = CM.off[name]
        cv[:, :, o:o + w] = arr
    put("gpre", inp["norm_pre"].reshape(DEPTH, 8, 128).transpose(0, 2, 1))
    cw = inp["mlstm_conv_w"]
    put("a_cw", cw.reshape(DEPTH, 4, 8, 128).transpose(0, 3, 1, 2).reshape(DEPTH, 128, 32))
    put("a_cb", inp["mlstm_conv_b"].reshape(DEPTH, 8, 128).transpose(0, 2, 1))
    ib = np.zeros((DEPTH, 128, 1), f); ib[:, 0:8, 0] = inp["mlstm_i_bias"]; put("a_ib", ib)
    fb = np.zeros((DEPTH, 128, 1), f); fb[:, 0:8, 0] = inp["mlstm_f_bias"]; put("a_fb", fb)
    def fm4(x):
        return x.reshape(DEPTH, -1, 128).transpose(0, 2, 1)
    put("a_nw", fm4(inp["mlstm_norm_w"]))
    mu = inp["rwkv_mu"]
    put("b_mu", fm4(mu))
    for k, nm in [("b_w0", "rwkv_w0"), ("b_a0", "rwkv_a0"), ("b_kk", "rwkv_k_k"), ("b_ka", "rwkv_k_a"),
                  ("b_rk", "rwkv_r_k"), ("b_gw", "rwkv_gn_w"), ("b_gb", "rwkv_gn_b"),
                  ("c_cb", "lru_conv_b"), ("c_br", "lru_b_r"), ("c_bi", "lru_b_i"), ("c_lam", "lru_lambda"),
                  ("d_nw", "ret_norm_w")]:
        put(k, fm4(inp[nm]))
    lw = inp["lru_conv_w"]
    put("c_cw", lw.reshape(DEPTH, 4, 4, 128).transpose(0, 3, 1, 2).reshape(DEPTH, 128, 16))
    lora = np.concatenate([inp["rwkv_w_up"], inp["rwkv_a_up"]], axis=1)
    lru = np.zeros((DEPTH, 128, 2, 4, 128), f)
    for which, nm in enumerate(["lru_w_r", "lru_w_i"]):
        w = inp[nm]
        for g in range(4):
            for j in range(2):
                lru[:, 64 * j:64 * j + 64, which, g, 64 * j:64 * j + 64] = w[:, 2 * g + j]
    gpost = np.ascontiguousarray(np.broadcast_to(inp["norm_post"][:, None, :], (DEPTH, 128, D_MODEL)))
    return dict(wt=wt, wo=wo, cv=cv, lora=np.ascontiguousarray(lora), lru=lru, gpost=gpost)


def host_tables():
    f = np.float32
    t = {}
    t["ident"] = np.eye(128, dtype=f)
    hd = 64
    log_g = np.log1p(-np.exp2(-5.0 - np.arange(8, dtype=np.float64)))
    idx = np.arange(64, dtype=np.float64)
    dmat = np.exp(log_g[:, None, None] * np.abs(idx[:, None] - idx[None, :])) * hd ** -0.5
    xi = np.exp(log_g[:, None] * (idx + 1.0))
    zeta = np.exp(log_g[:, None] * (63.0 - idx)) * hd ** -0.5
    gch = np.exp(log_g * 64.0)
    hs2h = [2 * (s % 4) + (s // 4) for s in range(8)]
    dm = np.zeros((128, 8, 64));
    for s in range(8):
        dm[0:64, s] = dmat[hs2h[s]]; dm[64:128, s] = dmat[hs2h[s]]
    t["d_dmat"] = dm.astype(f)
    t["d_xi"] = np.tile(xi.T, (2, 1)).astype(f)
    t["d_zeta"] = np.tile(zeta.T, (2, 1)).astype(f)
    gc = np.zeros((128, 4))
    for g in range(4):
        for j in range(2):
            gc[64 * j:64 * j + 64, g] = gch[2 * g + j]
    t["d_gch"] = gc.astype(f)
    mk = np.zeros((128, 8, 64))
    sidx = np.arange(128) % 64
    mk[:] = np.where(sidx[:, None, None] <= np.arange(64)[None, None, :], 0.0, -30000.0)
    t["a_mask"] = mk.reshape(128, 512).astype(f)
    sel = np.zeros((8, 512 + 4 + 128))
    for hp_ in range(8):
        for hs in range(8):
            if hp_ == hs2h[hs]:
                sel[hp_, hs * 64:(hs + 1) * 64] = 1.0
        sel[hp_, 512 + hp_ // 2] = 1.0
        jj = hp_ % 2
        sel[hp_, 516 + 64 * jj:516 + 64 * jj + 64] = 1.0
    t["a_sel"] = sel.astype(f)
    bo = np.zeros((128, 128)); bo[0:64, 0:64] = 1.0; bo[64:128, 64:128] = 1.0
    t["b_ones"] = bo.astype(f)
    i2 = np.zeros((128, 64)); i2[0:64] = np.eye(64); i2[64:128] = np.eye(64)
    t["b_i2"] = i2.astype(f)
    rr_ = (np.arange(128) % 64)[:, None]; cc_ = np.arange(64)[None, :]
    m5 = np.zeros((128, 5, 64))
    m5[:, 0] = rr_ > cc_; m5[:, 1] = cc_ > rr_; m5[:, 2] = cc_ > rr_; m5[:, 3] = cc_ >= rr_; m5[:, 4] = cc_ >= rr_
    t["b_m5"] = m5.astype(f)
    half = 32
    pos = np.arange(SEQ, dtype=np.float32)
    inv_freq = (np.float32(10000.0) ** (-np.arange(half, dtype=np.float32) / np.float32(half))).astype(np.float32)
    ang = (pos[:, None] * inv_freq[None, :]).astype(np.float32).astype(np.float64)
    t["rope"] = np.concatenate([np.cos(ang), np.sin(ang)], axis=1).astype(f)
    return t


def build_program(ntiles=SEQ // T, nlayers=DEPTH, mixers="ABCD"):
    nc = bass.Bass("TRN2", target_bir_lowering=False)
    stack = contextlib.ExitStack()
    P = Prog(nc, stack)
    dram = {}

    def din(name, shape):
        dram[name] = nc.dram_tensor(name, list(shape), F32, kind="ExternalInput").ap()
        return dram[name]

    x_d = din("x", [SEQ, D_MODEL])
    wt_d = din("wt", [DEPTH, NSLAB, 128, 8, 512])
    wo_d = din("wo", [DEPTH, 4, 128, 4, 1024])
    cv_d = din("cv", [DEPTH, 128, CM.n])
    lora_d = din("lora", [DEPTH, 128, 512])
    lru_d = din("lru", [DEPTH, 128, 2, 4, 128])
    gpost_d = din("gpost", [DEPTH, 128, D_MODEL])
    ident_d = din("ident", [128, 128])
    dmat_d = din("d_dmat", [128, 8, 64])
    dxi_d = din("d_xi", [128, 8])
    dzeta_d = din("d_zeta", [128, 8])
    dgch_d = din("d_gch", [128, 4])
    rope_d = din("rope", [SEQ, 64])
    bones_d = din("b_ones", [128, 128])
    bi2_d = din("b_i2", [128, 64])
    bm5_d = din("b_m5", [128, 5, 64])
    amask_d = din("a_mask", [128, 512])
    asel_d = din("a_sel", [8, 644])
    y_d = nc.dram_tensor("y", [SEQ, D_MODEL], F32, kind="ExternalOutput").ap()

    ident = P.sb("ident", [128, 128])
    cv = [P.sb(f"cv{l}", [128, CM.n]) for l in range(DEPTH)]
    lru1 = P.sb("lru", [128, 2, 4, 128]); lru = [lru1, lru1]
    gpost1 = P.sb("gpost", [128, D_MODEL]); gpost = [gpost1, gpost1]
    lora_up = [P.sb(f"lora_up{l}", [128, 512]) for l in range(DEPTH)]
    b_ones = P.sb("b_ones", [128, 128]); b_i2 = P.sb("b_i2", [128, 1, 64]); b_m5 = P.sb("b_m5", [128, 5, 64])
    b_hist = [P.sb(f"b_hist{l}", [128, 13]) for l in range(DEPTH)]
    b_S = [P.sb(f"b_S{l}", [128, 4, 64]) for l in range(DEPTH)]
    ident16 = P.sb("ident16", [128, 128], BF16)
    AlT16 = P.sb("AlT16", [128, 4, T], BF16); RhT16 = P.sb("RhT16", [128, 4, T], BF16)
    bk16 = P.sb("bk16", [128, 2, T], BF16)
    bm16 = [P.sb(f"bm16_{i}", [128, 8, 64], BF16) for i in range(3)]
    b_S16 = P.sb("b_S16", [128, 4, 64], BF16)
    a_n16 = P.sb("a_n16", [128, 4, 1], BF16); wj16 = P.sb("wj16", [128, NTB, 8], BF16)
    ones16 = P.sb("ones16", [128, 64], BF16); eps12 = P.sb("eps12", [128, 1])
    gLt = P.sb("gLt", [128, 4, NCH]); PRall = [P.sb(f"PR{i}", [128, 5, 8, 64], BF16) for i in range(NTB)]; Y2 = P.sb("Y2", [128, 8, 64])
    K2g = P.sb("K2g", [128, 2, 4, 64])
    ones = P.sb("ones", [128, T]); zeros = P.sb("zeros", [128, T])
    xt = [P.sb(f"xt{i}", [128, NTB, D_MODEL]) for i in range(1)]
    hT = P.sb("hT", [128, 8, T], BF16)
    yT = P.sb("yT", [128, 16, T], BF16)
    slab = [P.sb(f"slab{i}", [128, 8, 512], BF16) for i in range(NSLABBUF)]
    sm = P.sb("small", [128, 64])
    scr = P.sb("scr", [128, D_MODEL])
    scr2 = P.sb("scr2", [128, D_MODEL])
    FS = [P.sb(f"fs{i}", [128, 4, T + 4]) for i in range(7)]
    c_hist = [P.sb(f"c_hist{l}", [128, 4, 3]) for l in range(DEPTH)]
    c_state = [P.sb(f"c_state{l}", [128, 4]) for l in range(DEPTH)]
    c_coef = [P.sb(f"c_coef{l}", [128, 8]) for l in range(DEPTH)]
    PB = [P.ps(f"pb{i}") for i in range(8)]
    TM = [P.sb(f"tm{i}", [128, NTB, 8, 64]) for i in range(6)]
    TM16 = [P.sb(f"tm16_{i}", [128, NTB, 8, 64], BF16) for i in range(3)]
    d_dmat = P.sb("d_dmat", [128, 8, 64]); d_xi = P.sb("d_xi", [128, 4, 2, 1]); d_zeta = P.sb("d_zeta", [128, 8, 1])
    d_gch = P.sb("d_gch", [128, 4, 1]); rope_sb = P.sb("rope_sb", [128, NTB, 1, 64])
    d_R = [P.sb(f"d_R{l}", [128, 4, 64]) for l in range(DEPTH)]
    a_mask = P.sb("a_mask", [128, 512]); a_sel = P.sb("a_sel", [8, 644])
    ga = P.sb("ga", [8, 12, T + 1]); scl = P.sb("scl", [8, NCH]); rhs_sc = P.sb("rhs_sc", [8, NCH, 4])
    scb = P.sb("scb", [128, NCH, 4, 1]); negPx = P.sb("negPx", [8, 2, 8, 64]); ET = P.sb("ET", [128, 8, 64])
    tmsc = P.sb("tmsc", [128, NTB, 3, 8]); hnum = P.sb("hnum", [128, 8, 64]); vw = P.sb("vw", [128, 8, 64])
    a_hist = [P.sb(f"a_hist{l}", [128, 8, 3]) for l in range(DEPTH)]
    a_C = [P.sb(f"a_C{l}", [128, 4, 64]) for l in range(DEPTH)]
    a_n = [P.sb(f"a_n{l}", [128, 4, 1]) for l in range(DEPTH)]
    a_car = [P.sb(f"a_car{l}", [8, 4]) for l in range(DEPTH)]
    AT = P.sb("AT", [128, 8, 64]); tmp4 = P.sb("tmp4", [128, 4, 2, 64]); lnst = P.sb("lnst", [128, 64])
    state = dict(slab_i=0, pb_i=0)

    import os
    cut = int(os.environ.get("KCUT", "9"))
    def dma(out, in_, eng="sp"):
        return P.op(eng, lambda e: e.dma_start(out=_a(out), in_=_a(in_)), [out], [in_], dma=True)

    def mm(out, lhsT, rhs, start=True, stop=True):
        P.op("pe", lambda e: e.matmul(_a(out), _a(lhsT), _a(rhs), start=start, stop=stop),
             [out], [lhsT, rhs] + ([] if start else [out]))

    def tr(out, in_, idn):
        P.op("pe", lambda e: e.transpose(_a(out), _a(in_), _a(idn)), [out], [in_, idn])

    def act(out, in_, func, bias=None, scale=1.0, accum=None, eng="act"):
        kw = {}
        if bias is not None:
            kw["bias"] = _a(bias)
        if accum is not None:
            kw["accum_out"] = _a(accum)
        P.op("act", lambda e: e.activation(_a(out), _a(in_), func, scale=_a(scale), **kw),
             [out, accum], [in_, bias, scale])

    def tt(out, a, b, op, eng="dve"):
        P.op(eng, lambda e: e.tensor_tensor(_a(out), _a(a), _a(b), op), [out], [a, b])

    def ts(out, a, s1, s2, op0, op1=ALU.bypass, eng="dve"):
        P.op(eng, lambda e: e.tensor_scalar(_a(out), _a(a), _a(s1), _a(s2), op0, op1), [out], [a, s1, s2])

    def stt(out, a, s, b, op0, op1):
        P.op("dve", lambda e: e.scalar_tensor_tensor(_a(out), _a(a), _a(s), _a(b), op0, op1), [out], [a, s, b])

    def scan(out, d0, d1, init, op0, op1):
        P.op("dve", lambda e: e.tensor_tensor_scan(_a(out), _a(d0), _a(d1), _a(init), op0, op1),
             [out], [d0, d1, init])

    def cp(out, in_, eng="dve"):
        if isinstance(in_, Ref) and in_.buf.psum and eng == "dve":
            return act(out, in_, AF.Copy)
        P.op(eng, lambda e: e.tensor_copy(_a(out), _a(in_)), [out], [in_])

    def recip(out, in_):
        P.op("dve", lambda e: e.reciprocal(_a(out), _a(in_)), [out], [in_])

    def memset(out, val, eng="dve"):
        P.op(eng, lambda e: e.memset(_a(out), val), [out], [])

    def reduce(out, in_, op, axis=AX.X):
        P.op("dve", lambda e: e.tensor_reduce(_a(out), _a(in_), axis, op), [out], [in_])

    def interleave(gens):
        gens = list(gens)
        while gens:
            for gz in list(gens):
                try:
                    next(gz)
                except StopIteration:
                    gens.remove(gz)

    def next_pb():
        state["pb_i"] = (state["pb_i"] + 1) % 2
        return PB[state["pb_i"]]

    def load_slab(l, name, ncols=512):
        s = slab[state["slab_i"]]
        state["slab_i"] = (state["slab_i"] + 1) % NSLABBUF
        dma(s[:, :, 0:ncols], wt_d[l, SLAB_ID[name], :, :, 0:ncols], eng="pool")
        return s

    def load_wo(l, si):
        s = slab[state["slab_i"]]
        state["slab_i"] = (state["slab_i"] + 1) % NSLABBUF
        dma(s[:, :, :], wo_d[l, si].rearrange("p f n -> p (f n)").rearrange("p (a b) -> p a b", a=8), eng="pool")
        return s

    def cvc(l, name, i=0):
        o, w = CM.off[name]
        return cv[l][:, o + i:o + i + 1]

    def fm_proj(s, gi, out_cols=T):
        pb = next_pb()
        for kc in range(8):
            mm(pb[:, 0:T], s[:, kc, gi * 128:(gi + 1) * 128], hT[:, kc, :], start=(kc == 0), stop=(kc == 7))
        return pb

    dma(ident[:, :], ident_d)
    for l in range(DEPTH):
        dma(cv[l][:, :], cv_d[l])
        dma(lora_up[l][:, :], lora_d[l])
        memset(b_hist[l][:, :], 0.0)
        memset(b_S[l][:, :, :], 0.0)
    dma(d_dmat[:, :, :], dmat_d)
    dma(d_xi[:, :, :, :], dxi_d.rearrange("p (g j o) -> p g j o", g=4, j=2))
    dma(d_zeta[:, :, :], dzeta_d.rearrange("p (h o) -> p h o", o=1))
    dma(d_gch[:, :, :], dgch_d.rearrange("p (g o) -> p g o", o=1))
    dma(a_mask[:, :], amask_d)
    dma(a_sel[:, :], asel_d)
    for l in range(DEPTH):
        memset(d_R[l][:, :, :], 0.0)
        memset(a_hist[l][:, :, :], 0.0)
        memset(a_C[l][:, :, :], 0.0)
        memset(a_n[l][:, :, :], 0.0)
        memset(a_car[l][:, :], 0.0)
        ts(a_car[l][0:8, 2:3], cv[l][0:8, CM.off["a_fb"][0]:CM.off["a_fb"][0] + 1], -1.0, None, ALU.mult)
    dma(b_ones[:, :], bones_d)
    dma(b_i2[:, 0, :], bi2_d)
    dma(b_m5[:, :, :], bm5_d)
    memset(ones[:, :], 1.0)
    cp(ident16[:, :], ident[:, :])
    memset(ones16[:, :], 1.0)
    memset(eps12[:, :], 1e-12)
    memset(zeros[:, :], 0.0)
    for l in range(DEPTH):
        memset(c_hist[l][:, :, :], 0.0)
        memset(c_state[l][:, :], 0.0)
        o, w = CM.off["c_lam"]
        act(sm[:, 0:4], cv[l][:, o:o + 4], AF.Exp, scale=-1.0)
        act(sm[:, 4:8], sm[:, 0:4], AF.Ln, bias=ones[:, 0:1])
        ts(c_coef[l][:, 0:4], sm[:, 4:8], -8.0, None, ALU.mult)
        ts(c_coef[l][:, 4:8], sm[:, 4:8], -16.0, None, ALU.mult)

    def rmsnorm_pre(l, X):
        for tb in range(NTB):
            act(scr[:, :], X[:, tb, :], AF.Square)
            reduce(sm[:, 8 + tb:9 + tb], scr[:, :], ALU.add)
        ts(sm[:, 12:12 + NTB], sm[:, 8:8 + NTB], 1.0 / D_MODEL, 1e-6, ALU.mult, ALU.add)
        act(sm[:, 16:16 + NTB], sm[:, 12:12 + NTB], AF.Sqrt)
        recip(sm[:, 20:20 + NTB], sm[:, 16:16 + NTB])
        for tb in range(NTB):
            act(scr[:, :], X[:, tb, :], AF.Copy, scale=sm[:, 20 + tb:21 + tb])
            for half in range(2):
                pb = next_pb()
                for q in range(4):
                    kc = half * 4 + q
                    tr(pb[:, q * 128:(q + 1) * 128], scr[:, kc * 128:(kc + 1) * 128], ident[:, :])
                for q in range(4):
                    kc = half * 4 + q
                    ts(hT[:, kc, tb * 128:(tb + 1) * 128], pb[:, q * 128:(q + 1) * 128],
                       cvc(l, "gpre", kc), None, ALU.mult)

    def mixer_C(l):
        cx, xc, rg, ig, aa, uu, hh = FS[0], FS[1], FS[2], FS[3], FS[4], FS[5], FS[6]
        dma(lru[l][:, :, :, :], lru_d[l])
        wx = load_slab(l, "Cx")

        def cstream(g):
            pb = fm_proj(wx, g)
            cp(cx[:, g, 0:3], c_hist[l][:, g, :])
            act(cx[:, g, 3:3 + T], pb[:, 0:T], AF.Copy)
            yield
            cp(c_hist[l][:, g, :], cx[:, g, T:T + 3])
            o, w = CM.off["c_cw"]
            ts(xc[:, g, 0:T], cx[:, g, 0:T], cv[l][:, o + g:o + g + 1], cvc(l, "c_cb", g), ALU.mult, ALU.add)
            yield
            for j in range(1, 4):
                stt(xc[:, g, 0:T], cx[:, g, j:j + T], cv[l][:, o + 4 * j + g:o + 4 * j + g + 1], xc[:, g, 0:T],
                    ALU.mult, ALU.add)
                yield
        interleave([cstream(g) for g in range(4)])
        for g in range(4):
            pb = next_pb()
            mm(pb[:, 0:T], lru[l][:, 0, g, :], xc[:, g, 0:T])
            act(rg[:, g, 0:T], pb[:, 0:T], AF.Sigmoid, bias=cvc(l, "c_br", g))
            pb = next_pb()
            mm(pb[:, 0:T], lru[l][:, 1, g, :], xc[:, g, 0:T])
            act(ig[:, g, 0:T], pb[:, 0:T], AF.Sigmoid, bias=cvc(l, "c_bi", g))
        for g in range(4):
            act(aa[:, g, 0:T], rg[:, g, 0:T], AF.Exp, scale=c_coef[l][:, g:g + 1])
            act(uu[:, g, 0:T], rg[:, g, 0:T], AF.Exp, scale=c_coef[l][:, 4 + g:5 + g])
        ts(uu[:, :, 0:T], uu[:, :, 0:T], -1.0, 1.0, ALU.mult, ALU.add)
        act(uu[:, :, 0:T], uu[:, :, 0:T], AF.Sqrt)
        tt(ig[:, :, 0:T], ig[:, :, 0:T], xc[:, :, 0:T], ALU.mult)
        tt(uu[:, :, 0:T], uu[:, :, 0:T], ig[:, :, 0:T], ALU.mult)
        for g in range(4):
            scan(hh[:, g, 0:T], aa[:, g, 0:T], uu[:, g, 0:T], c_state[l][:, g:g + 1], ALU.mult, ALU.add)
            cp(c_state[l][:, g:g + 1], hh[:, g, T - 1:T])
        wz = load_slab(l, "Cz")
        for g in range(4):
            pb = fm_proj(wz, g)
            act(rg[:, g, 0:T], pb[:, 0:T], AF.Silu)
            tt(yT[:, 8 + g, :], hh[:, g, 0:T], rg[:, g, 0:T], ALU.mult)

    def tm_proj(sl, tb):
        pb = next_pb()
        for kc in range(8):
            mm(pb[:, :], hT[:, kc, tb * 128:(tb + 1) * 128], sl[:, kc, :], start=(kc == 0), stop=(kc == 7))
        return pb

    def to_fm(dst, src, tb):
        pb = next_pb()
        for g in range(4):
            tr(pb[:, g * 128:(g + 1) * 128], src[:, tb, 2 * g:2 * g + 2, 0:64], ident[:, :])
        cp(dst[:, 0:4, tb * 128:(tb + 1) * 128], pb[:, :].rr("p (a b) -> p a b", a=4))

    def to_tm(dst, src, tb):
        pb = next_pb()
        for g in range(4):
            tr(pb[:, g * 128:(g + 1) * 128], src[:, g, tb * 128:(tb + 1) * 128], ident[:, :])
        cp(dst[:, tb, :, 0:64], pb[:, :].rr("p (h e) -> p h e", h=8))

    def head_ln(dst, src, tb):
        X3 = src[:, tb, :, 0:64]
        D3 = dst[:, tb, :, 0:64]
        reduce(lnst[:, 0:8], X3, ALU.add)
        stt(D3, lnst[:, 0:8].rr("p (h o) -> p h o", o=1).bc([128, 8, 64]), -1.0 / 64, X3, ALU.mult, ALU.add)
        tt(scr[:, 0:512].rr("p (h e) -> p h e", h=8), D3, D3, ALU.mult)
        reduce(lnst[:, 8:16], scr[:, 0:512].rr("p (h e) -> p h e", h=8), ALU.add)
        ts(lnst[:, 16:24], lnst[:, 8:16], 1.0 / 64, 1e-5, ALU.mult, ALU.add)
        act(lnst[:, 24:32], lnst[:, 16:24], AF.Sqrt)
        recip(lnst[:, 32:40], lnst[:, 24:32])
        tt(D3, D3, lnst[:, 32:40].rr("p (h o) -> p h o", o=1).bc([128, 8, 64]), ALU.mult)

    def mixer_D(l, ti):
        qr, kr, osb, xn = TM[0], TM[1], TM[4], TM[5]
        vv, kz = TM16[1], TM16[2]
        qT, kT, sz = AlT16, RhT16, FS[2]
        AT = bm16[0]
        d_R16 = b_S16
        cp(d_R16[:, :, :], d_R[l][:, :, :])
        S = [PB[3], PB[4]]; O = [PB[5], PB[6]]; ST = [PB[2], PB[7]]
        dma(rope_sb[:, :, :, :], rope_d.rearrange("(n tb p) (o e) -> n p tb o e", tb=NTB, p=128, o=1)[ti])
        for name, dst in (("Dq", qr), ("Dk", kr)):
            sl = load_slab(l, name)
            for tb in range(NTB):
                pb = tm_proj(sl, tb)
                act(scr[:, 0:512], pb[:, :], AF.Copy)
                raw = scr[:, 0:512].rr("p (h e) -> p h e", h=8)
                x1, x2 = raw[:, :, 0:32], raw[:, :, 32:64]
                cos = rope_sb[:, tb, :, 0:32].bc([128, 8, 32]); sin = rope_sb[:, tb, :, 32:64].bc([128, 8, 32])
                t1 = scr2[:, 0:256].rr("p (h e) -> p h e", h=8); t2 = scr2[:, 256:512].rr("p (h e) -> p h e", h=8)
                d1, d2 = dst[:, tb, :, 0:32], dst[:, tb, :, 32:64]
                tt(d1, x1, cos, ALU.mult); tt(t1, x2, sin, ALU.mult); tt(d1, d1, t1, ALU.subtract)
                tt(d2, x2, cos, ALU.mult); tt(t2, x1, sin, ALU.mult); tt(d2, d2, t2, ALU.add)
                to_fm(qT if name == "Dq" else kT, dst, tb)
        sl = load_slab(l, "Dv")
        for tb in range(NTB):
            pb = tm_proj(sl, tb)
            act(vv[:, tb, :, 0:64], pb[:, :].rr("p (h e) -> p h e", h=8), AF.Copy)
            tt(kz[:, tb, :, 0:64], kr[:, tb, :, 0:64], d_zeta[:, :, :].bc([128, 8, 64]), ALU.mult)
        sl = load_slab(l, "Dz")
        for g in range(4):
            pb = fm_proj(sl, g)
            act(sz[:, g, 0:T], pb[:, 0:T], AF.Silu)
        for c in range(NCH):
            tb, p = c // 2, c % 2
            rows = slice(64 * p, 64 * p + 64)
            cols = slice(c * 64, (c + 1) * 64)
            for j in range(2):
                jr = slice(64 * j, 64 * j + 64)
                for g in range(4):
                    mm(S[j][rows, g * 64:(g + 1) * 64], kT[jr, g, cols], qT[jr, g, cols])
                    mm(S[j][rows, 256 + g * 64:256 + (g + 1) * 64], qT[jr, g, cols], d_R16[jr, g, :])
            for j in range(2):
                tt(AT[rows, 4 * j:4 * j + 4, :], S[j][rows, 0:256].rr("p (a b) -> p a b", a=4),
                   d_dmat[rows, 4 * j:4 * j + 4, :], ALU.mult)
                tt(tmp4[rows, :, j, 0:64], S[j][rows, 256:512].rr("p (a b) -> p a b", a=4),
                   d_xi[rows, :, j, :].bc([64, 4, 64]), ALU.mult)
            for h in range(8):
                g, j = h // 2, h % 2
                mm(O[p][rows, h * 64:(h + 1) * 64], AT[rows, j * 4 + g, :], vv[rows, tb, h, 0:64])
            tt(osb[rows, tb, :, 0:64], O[p][rows, :].rr("p (h e) -> p h e", h=8),
               tmp4[rows, :, :, 0:64].rr("p g j e -> p (g j) e"), ALU.add)
            for h in range(8):
                g, j = h // 2, h % 2
                mm(ST[p][64 * j:64 * j + 64, g * 64:(g + 1) * 64], kz[rows, tb, h, 0:64], vv[rows, tb, h, 0:64])
            tt(d_R[l][:, :, :], d_R[l][:, :, :], d_gch[:, :, :].bc([128, 4, 64]), ALU.mult)
            tt(d_R[l][:, :, :], d_R[l][:, :, :], ST[p][:, 0:256].rr("p (a b) -> p a b", a=4), ALU.add)
            cp(d_R16[:, :, :], d_R[l][:, :, :])
        for tb in range(NTB):
            head_ln(xn, osb, tb)
            pb = next_pb()
            for g in range(4):
                tr(pb[:, g * 128:(g + 1) * 128], xn[:, tb, 2 * g:2 * g + 2, 0:64], ident[:, :])
            for g in range(4):
                stt(yT[:, 12 + g, tb * 128:(tb + 1) * 128], pb[:, g * 128:(g + 1) * 128], cvc(l, "d_nw", g),
                    sz[:, g, tb * 128:(tb + 1) * 128], ALU.mult, ALU.mult)

    def conv_silu(l, sl, g8, dst, gdst, cx, whichhist, cwname, cbname, ngroups_total, silu=True):
        pass

    def mixer_A(l, ti):
        cx, qT32, kT32, sz = FS[0], FS[1], FS[2], FS[3]
        qT, kT = AlT16, RhT16
        so, osb, xn = TM[2], TM[3], TM[4]
        ktm, vv = TM16[0], TM16[1]
        AT, vw = bm16[0], bm16[1]
        a_C16 = b_S16
        cp(a_C16[:, :, :], a_C[l][:, :, :])
        cp(a_n16[:, :, :], a_n[l][:, :, :])
        S = [PB[3], PB[4]]; O = [PB[5], PB[6]]; ST = [PB[2], PB[7]]; JX = [PB[0], PB[1]]
        R_I, R_SP, R_F, R_G, R_P, R_MU, R_PE, R_SI, R_WJ, R_NM, R_NP = range(11)
        ocw = CM.off["a_cw"][0]
        for which, (name, dstT, dst16) in enumerate((("Aq", qT32, qT), ("Ak", kT32, kT))):
            sl = load_slab(l, name)

            def astream(g, which=which, sl=sl, dstT=dstT, dst16=dst16):
                g8 = which * 4 + g
                pb = fm_proj(sl, g)
                cp(cx[:, g, 0:3], a_hist[l][:, g8, :])
                act(cx[:, g, 3:3 + T], pb[:, 0:T], AF.Copy)
                yield
                cp(a_hist[l][:, g8, :], cx[:, g, T:T + 3])
                ts(dstT[:, g, 0:T], cx[:, g, 0:T], cv[l][:, ocw + g8:ocw + g8 + 1], cvc(l, "a_cb", g8),
                   ALU.mult, ALU.add)
                yield
                for j in range(1, 4):
                    stt(dstT[:, g, 0:T], cx[:, g, j:j + T], cv[l][:, ocw + 8 * j + g8:ocw + 8 * j + g8 + 1],
                        dstT[:, g, 0:T], ALU.mult, ALU.add)
                    yield
                act(dst16[:, g, 0:T], dstT[:, g, 0:T], AF.Silu)
                yield
            interleave([astream(g) for g in range(4)])
        sl = load_slab(l, "Az")
        for g in range(4):
            pb = fm_proj(sl, g)
            act(sz[:, g, 0:T], pb[:, 0:T], AF.Silu)
        acut = int(os.environ.get("ACUT", "99"))
        if acut < 1:
            return
        sl = load_slab(l, "Ag", ncols=128)
        pbi = next_pb()
        for kc in range(8):
            mm(pbi[0:8, 0:T], sl[:, kc, 0:8], hT[:, kc, :], start=(kc == 0), stop=(kc == 7))
        act(ga[0:8, R_I, 0:T], pbi[0:8, 0:T], AF.Identity, bias=cv[l][0:8, CM.off["a_ib"][0]:CM.off["a_ib"][0] + 1])
        gcut = int(os.environ.get("GCUT", "99"))
        if gcut < 1:
            return
        pbf = next_pb()
        for kc in range(8):
            mm(pbf[0:8, 0:T], sl[:, kc, 8:16], hT[:, kc, :], start=(kc == 0), stop=(kc == 7))
        act(ga[0:8, R_SP, 0:T], pbf[0:8, 0:T], AF.Exp, bias=a_car[l][0:8, 2:3], scale=-1.0)
        act(ga[0:8, R_SP, 0:T], ga[0:8, R_SP, 0:T], AF.Ln, bias=ones[0:8, 0:1])
        if gcut < 2:
            return
        scan(ga[0:8, R_F, 0:T], ones[0:8, 0:T], ga[0:8, R_SP, 0:T], a_car[l][0:8, 0:1], ALU.mult, ALU.subtract)
        cp(a_car[l][0:8, 0:1], ga[0:8, R_F, T - 1:T])
        tt(ga[0:8, R_G, 0:T], ga[0:8, R_I, 0:T], ga[0:8, R_F, 0:T], ALU.subtract)
        if gcut < 3:
            return
        cp(ga[0:8, R_P, 0:1], a_car[l][0:8, 1:2])
        scan(ga[0:8, R_P, 1:T + 1], ones[0:8, 0:T], ga[0:8, R_G, 0:T], a_car[l][0:8, 1:2], ALU.mult, ALU.max)
        cp(a_car[l][0:8, 1:2], ga[0:8, R_P, T:T + 1])
        if gcut < 4:
            return
        for c in range(NCH):
            cols = slice(c * 64, (c + 1) * 64)
            ts(ga[0:8, R_MU, cols], zeros[0:8, 0:64], ga[0:8, R_P, c * 64:c * 64 + 1], None, ALU.add)
            ts(ga[0:8, R_PE, cols], zeros[0:8, 0:64], ga[0:8, R_P, c * 64 + 64:c * 64 + 65], None, ALU.add)
            tt(scl[0:8, c:c + 1], ga[0:8, R_P, c * 64:c * 64 + 1], ga[0:8, R_P, c * 64 + 64:c * 64 + 65], ALU.subtract)
        if gcut < 5:
            return
        Pv = ga[0:8, R_P, 1:T + 1]
        tt(ga[0:8, R_SI, 0:T], ga[0:8, R_MU, 0:T], Pv, ALU.subtract)
        tt(ga[0:8, R_WJ, 0:T], ga[0:8, R_G, 0:T], ga[0:8, R_PE, 0:T], ALU.subtract)
        stt(ga[0:8, R_NM, 0:T], ga[0:8, R_F, 0:T], -1.0, Pv, ALU.mult, ALU.subtract)
        ts(ga[0:8, R_NP, 0:T], Pv, -1.0, None, ALU.mult)
        if acut < 2:
            return
        for tb in range(NTB):
            pb = next_pb()
            for r, R in enumerate((R_SI, R_WJ, R_NM)):
                tr(pb[:, r * 8:(r + 1) * 8], ga[0:8, R, tb * 128:(tb + 1) * 128], ident[0:8, 0:8])
            act(tmsc[:, tb, :, :], pb[:, 0:24].rr("p (a b) -> p a b", a=3), AF.Exp)
        if acut < 3:
            return
        tt(rhs_sc[0:8, :, :], scl[0:8, :].rr("p (c o) -> p c o", o=1).bc([8, NCH, 4]),
           a_sel[0:8, 512:516].rr("p (o g) -> p o g", o=1).bc([8, NCH, 4]), ALU.mult)
        pb = next_pb()
        mm(pb[:, 0:NCH * 4], a_sel[0:8, 516:644], rhs_sc[0:8, :, :].rr("p c g -> p (c g)"))
        act(scb[:, :, :, :].rr("p c g o -> p (c g o)"), pb[:, 0:NCH * 4], AF.Exp)
        if acut < 4:
            return
        for tb in range(NTB):
            pbt = next_pb()
            for g in range(4):
                mm(pbt[:, g * 128:(g + 1) * 128], kT[:, g, tb * 128:(tb + 1) * 128], ident16[:, :])
            cp(ktm[:, tb, :, :], pbt[:, :].rr("p (h e) -> p h e", h=8))
            cp(wj16[:, tb, :], tmsc[:, tb, 1, :])
        sl = load_slab(l, "Av")
        for tb in range(NTB):
            pb = tm_proj(sl, tb)
            act(vv[:, tb, :, :], pb[:, :].rr("p (h e) -> p h e", h=8), AF.Copy)
        sl = load_slab(l, "Ao")
        for tb in range(NTB):
            pb = tm_proj(sl, tb)
            act(so[:, tb, :, :], pb[:, :].rr("p (h e) -> p h e", h=8), AF.Sigmoid)
        for tb in range(NTB):
            E = next_pb()
            mm(E[:, :], ident[:, :], a_mask[:, :], start=True, stop=False)
            mm(E[:, :], ga[0:8, R_G, tb * 128:(tb + 1) * 128], a_sel[0:8, 0:512], start=False, stop=False)
            for p in range(2):
                c = tb * 2 + p
                tt(negPx[0:8, p, :, :], ga[0:8, R_NP:R_NP + 1, c * 64:(c + 1) * 64].bc([8, 8, 64]),
                   a_sel[0:8, 0:512].rr("p (h e) -> p h e", h=8), ALU.mult)
                mm(E[64 * p:64 * p + 64, :], ones[0:8, 0:64], negPx[0:8, p, :, :].rr("p h e -> p (h e)"),
                   start=False, stop=True)
            act(ET[:, :, :].rr("p h e -> p (h e)"), E[:, :], AF.Exp)
            if acut < 5:
                continue
            for p in range(2):
                c = tb * 2 + p
                rows = slice(64 * p, 64 * p + 64)
                cols = slice(c * 64, (c + 1) * 64)
                sI4 = tmsc[:, tb, 0, :].rr("p (g j o) -> p g j o", g=4, j=2)
                for j in range(2):
                    jr = slice(64 * j, 64 * j + 64)
                    for g in range(4):
                        mm(S[j][rows, g * 64:(g + 1) * 64], kT[jr, g, cols], qT[jr, g, cols])
                        mm(JX[j][rows, g * 64:(g + 1) * 64], qT[jr, g, cols], a_C16[jr, g, :])
                        mm(JX[j][rows, 256 + g:257 + g], qT[jr, g, cols], a_n16[jr, g, :])
                for j in range(2):
                    stt(AT[rows, 4 * j:4 * j + 4, :], S[j][rows, 0:256].rr("p (a b) -> p a b", a=4), 0.125,
                        ET[rows, 4 * j:4 * j + 4, :], ALU.mult, ALU.mult)
                    tt(tmp4[rows, :, j, :], JX[j][rows, 0:256].rr("p (a b) -> p a b", a=4),
                       sI4[rows, :, j, :].bc([64, 4, 64]), ALU.mult)
                    tt(lnst[rows, 40:48].rr("p (g j) -> p g j", j=2)[:, :, j], JX[j][rows, 256:260],
                       tmsc[rows, tb, 0, :].rr("p (g j) -> p g j", j=2)[:, :, j], ALU.mult)
                for h in range(8):
                    g, j = h // 2, h % 2
                    mm(O[p][rows, h * 64:(h + 1) * 64], AT[rows, j * 4 + g, :], vv[rows, tb, h, :])
                    mm(ST[p][rows, 384 + h:385 + h], AT[rows, j * 4 + g, :], ones16[rows, 0:1])
                tt(hnum[rows, :, :], O[p][rows, :].rr("p (h e) -> p h e", h=8),
                   tmp4[rows, :, :, :].rr("p g j e -> p (g j) e"), ALU.add)
                tt(lnst[rows, 48:56], ST[p][rows, 384:392], lnst[rows, 40:48], ALU.add)
                stt(lnst[rows, 48:56], lnst[rows, 48:56], -1.0, lnst[rows, 48:56], ALU.mult, ALU.max)
                tt(lnst[rows, 48:56], lnst[rows, 48:56], tmsc[rows, tb, 2, :], ALU.max)
                recip(lnst[rows, 56:64], lnst[rows, 48:56])
                tt(hnum[rows, :, :], hnum[rows, :, :],
                   lnst[rows, 56:64].rr("p (h o) -> p h o", o=1).bc([64, 8, 64]), ALU.mult)
                tt(osb[rows, tb, :, :], hnum[rows, :, :], so[rows, tb, :, :], ALU.mult)
                tt(vw[rows, :, :], vv[rows, tb, :, :],
                   tmsc[rows, tb, 1, :].rr("p (h o) -> p h o", o=1).bc([64, 8, 64]), ALU.mult)
                for h in range(8):
                    g, j = h // 2, h % 2
                    jr = slice(64 * j, 64 * j + 64)
                    mm(ST[p][jr, g * 64:(g + 1) * 64], ktm[rows, tb, h, :], vw[rows, h, :])
                    mm(ST[p][jr, 256 + g:257 + g], ktm[rows, tb, h, :], wj16[rows, tb, h:h + 1])
                tt(a_C[l][:, :, :], a_C[l][:, :, :], scb[:, c, :, :].bc([128, 4, 64]), ALU.mult)
                stt(a_C[l][:, :, :], ST[p][:, 0:256].rr("p (a b) -> p a b", a=4), 0.125, a_C[l][:, :, :],
                    ALU.mult, ALU.add)
                tt(a_n[l][:, :, :], a_n[l][:, :, :], scb[:, c, :, :], ALU.mult)
                stt(a_n[l][:, :, :], ST[p][:, 256:260].rr("p (a b) -> p a b", b=1), 0.125, a_n[l][:, :, :],
                    ALU.mult, ALU.add)
                cp(a_C16[:, :, :], a_C[l][:, :, :])
                cp(a_n16[:, :, :], a_n[l][:, :, :])
        for tb in range(NTB):
            head_ln(xn, osb, tb)
            pb = next_pb()
            for g in range(4):
                tr(pb[:, g * 128:(g + 1) * 128], xn[:, tb, 2 * g:2 * g + 2, 0:64], ident[:, :])
            for g in range(4):
                stt(yT[:, g, tb * 128:(tb + 1) * 128], pb[:, g * 128:(g + 1) * 128], cvc(l, "a_nw", g),
                    sz[:, g, tb * 128:(tb + 1) * 128], ALU.mult, ALU.mult)

    def mixer_B(l, ti):
        rS, kS, vS, sz = FS[0], FS[1], FS[2], FS[3]
        RhT, AlT, bon = rS, kS, vS
        pools = (FS[4], FS[5], FS[6])

        def slot(i):
            return pools[i // 4][:, i % 4, :]
        raw, lora, aT, lw, lc, kt, kh, beta, eg, egi, egm, tA = [slot(i) for i in range(12)]
        wkv, xn = TM[3], TM[4]
        Vtm, Btm, Ktm = TM16[0], TM16[1], TM16[2]
        BJ = [PB[3], PB[4]]; PA = [PB[2], PB[7]]; PQ = [PB[5], PB[6]]; PX = [PB[0], PB[1]]
        Pn, Qn, Xm = bm16
        W2 = AT
        beta16, kt16 = bk16[:, 0, :], bk16[:, 1, :]
        omu = CM.off["b_mu"][0]

        def shift_mix(dst, sl, gi, hidx, mucol, raw_, tA_):
            pb = fm_proj(sl, gi)
            cp(raw_[:, 0:1], b_hist[l][:, hidx:hidx + 1])
            act(raw_[:, 1:T + 1], pb[:, 0:T], AF.Copy)
            yield
            cp(b_hist[l][:, hidx:hidx + 1], raw_[:, T:T + 1])
            tt(tA_[:, 0:T], raw_[:, 0:T], raw_[:, 1:T + 1], ALU.subtract)
            yield
            stt(dst, tA_[:, 0:T], cv[l][:, omu + mucol:omu + mucol + 1], raw_[:, 1:T + 1], ALU.mult, ALU.add)
            yield

        cp(b_S16[:, :, :], b_S[l][:, :, :])
        sl = load_slab(l, "Bl", ncols=128)
        interleave([shift_mix(lora[:, 0:T], sl, 0, 12, 12, raw, tA)])
        act(lora[0:64, 0:T], lora[0:64, 0:T], AF.Tanh)
        tslots = [(slot(2 + 2 * i), slot(3 + 2 * i)) for i in range(4)]
        for wi, (name, dst) in enumerate((("Br", rS), ("Bk", kS), ("Bv", vS))):
            sl = load_slab(l, name)
            interleave([shift_mix(dst[:, g, 0:T], sl, g, wi * 4 + g, wi * 4 + g, tslots[g][0], tslots[g][1])
                        for g in range(4)])
        sl = load_slab(l, "Bz")
        for g in range(4):
            pb = fm_proj(sl, g)
            act(sz[:, g, 0:T], pb[:, 0:T], AF.Silu)
        bcut = int(os.environ.get("BCUT", "99"))
        if bcut < 1:
            return
        for g in range(4):
            gc = slice(g * 128, (g + 1) * 128)
            pb = next_pb()
            mm(pb[:, 0:T], lora_up[l][0:64, gc], lora[0:64, 0:T])
            act(lw[:, 0:T], pb[:, 0:T], AF.Sigmoid, bias=cvc(l, "b_w0", g))
            pb = next_pb()
            mm(pb[:, 0:T], lora_up[l][64:128, gc], lora[64:128, 0:T])
            act(aT[:, 0:T], pb[:, 0:T], AF.Sigmoid, bias=cvc(l, "b_a0", g))
            for c in range(NCH):
                cols = slice(c * 64, (c + 1) * 64)
                scan(lc[:, cols], ones[:, 0:64], lw[:, cols], 0.0, ALU.mult, ALU.add)
            act(eg[:, 0:T], lc[:, 0:T], AF.Exp, scale=-0.606531)
            act(egi[:, 0:T], lc[:, 0:T], AF.Exp, scale=0.606531)
            tt(tA[:, 0:T], lc[:, 0:T], lw[:, 0:T], ALU.subtract)
            act(egm[:, 0:T], tA[:, 0:T], AF.Exp, scale=-0.606531)
            cp(gLt[:, g, :], eg[:, 63:T:64])
            ts(kh[:, 0:T], kS[:, g, 0:T], cvc(l, "b_kk", g), None, ALU.mult)
            tt(tA[:, 0:T], kh[:, 0:T], kh[:, 0:T], ALU.mult)
            pb = next_pb()
            mm(pb[:, 0:T], b_ones[:, :], tA[:, 0:T])
            act(tA[:, 0:T], pb[:, 0:T], AF.Sqrt, bias=eps12[:, 0:1])
            recip(tA[:, 0:T], tA[:, 0:T])
            tt(kh[:, 0:T], kh[:, 0:T], tA[:, 0:T], ALU.mult)
            ts(tA[:, 0:T], aT[:, 0:T], -1.0, cvc(l, "b_ka", g), ALU.add, ALU.mult)
            stt(kt[:, 0:T], tA[:, 0:T], 1.0, kS[:, g, 0:T], ALU.add, ALU.mult)
            for tb in range(NTB):
                pbt = next_pb()
                tr(pbt[:, 0:128], vS[:, g, tb * 128:(tb + 1) * 128], ident[:, :])
                cp(Vtm[:, tb, 2 * g:2 * g + 2, :], pbt[:, 0:128].rr("p (a b) -> p a b", a=2))
            stt(tA[:, 0:T], rS[:, g, 0:T], cvc(l, "b_rk", g), kt[:, 0:T], ALU.mult, ALU.mult)
            pbb = next_pb()
            mm(pbb[:, 0:T], b_ones[:, :], tA[:, 0:T])
            tt(bon[:, g, 0:T], pbb[:, 0:T], vS[:, g, 0:T], ALU.mult)
            tt(beta[:, 0:T], aT[:, 0:T], kh[:, 0:T], ALU.mult)
            tt(beta16[:, 0:T], beta[:, 0:T], egi[:, 0:T], ALU.mult)
            tt(AlT16[:, g, 0:T], kh[:, 0:T], egm[:, 0:T], ALU.mult)
            tt(kt16[:, 0:T], kt[:, 0:T], egi[:, 0:T], ALU.mult)
            tt(RhT16[:, g, 0:T], rS[:, g, 0:T], eg[:, 0:T], ALU.mult)
            for tb in range(NTB):
                pbt = next_pb()
                mm(pbt[:, 0:128], beta16[:, tb * 128:(tb + 1) * 128], ident16[:, :])
                mm(pbt[:, 128:256], kt16[:, tb * 128:(tb + 1) * 128], ident16[:, :])
                cp(Btm[:, tb, 2 * g:2 * g + 2, :], pbt[:, 0:128].rr("p (a b) -> p a b", a=2))
                cp(Ktm[:, tb, 2 * g:2 * g + 2, :], pbt[:, 128:256].rr("p (a b) -> p a b", a=2))
            for tb in range(NTB):
                for j in range(2):
                    jr = slice(64 * j, 64 * j + 64)
                    for p in range(2):
                        c = tb * 2 + p
                        rows = slice(64 * p, 64 * p + 64)
                        cols = slice(c * 64, (c + 1) * 64)
                        A_, B_, K_, R_ = AlT16[jr, g, cols], beta16[jr, cols], kt16[jr, cols], RhT16[jr, g, cols]
                        mm(BJ[j][rows, 0:64], A_, B_)
                        mm(BJ[j][rows, 64:128], B_, A_)
                        mm(BJ[j][rows, 128:192], K_, A_)
                        mm(BJ[j][rows, 192:256], B_, R_)
                        mm(BJ[j][rows, 256:320], K_, R_)
                    tt(PRall[tb][:, :, 2 * g + j, :], BJ[j][:, 0:320].rr("p (k t) -> p k t", k=5), b_m5[:, :, :],
                       ALU.mult)
        if bcut < 2:
            return
        for tb in range(NTB):
            PRt = PRall[tb]
            P0, Q0, MkT, NbT, NkT = (PRt[:, k, :, :] for k in range(5))
            Wsb, Usb = PRt[:, 2, :, :], PRt[:, 4, :, :]
            for p in range(2):
                rows = slice(64 * p, 64 * p + 64)
                c = tb * 2 + p
                for h in range(8):
                    g, j = h // 2, h % 2
                    mm(PA[p][rows, h * 64:(h + 1) * 64], MkT[rows, h, :], Vtm[rows, tb, h, :])
                    mm(PQ[p][rows, h * 64:(h + 1) * 64], NkT[rows, h, :], Vtm[rows, tb, h, :])
                    mm(PX[p][64 * j:64 * j + 64, g * 64:(g + 1) * 64], Ktm[rows, tb, h, :], Vtm[rows, tb, h, :])
                act(W2[rows, :, :], PA[p][rows, :].rr("p (h e) -> p h e", h=8), AF.Copy)
                act(Y2[rows, :, :], PQ[p][rows, :].rr("p (h e) -> p h e", h=8), AF.Copy)
                tt(K2g[:, p, :, :], PX[p][:, 0:256].rr("p (a b) -> p a b", a=4),
                   gLt[:, :, c:c + 1].bc([128, 4, 64]), ALU.mult)
            tt(Xm[:, :, :], b_i2[:, :, :].bc([128, 8, 64]), Q0, ALU.subtract)
            Pc, Qc = P0, Q0
            nxt = [(Pn, Qn), (P0, Q0)]
            for i in range(1, 6):
                Pd, Qd = nxt[(i - 1) % 2]
                for p in range(2):
                    rows = slice(64 * p, 64 * p + 64)
                    for h in range(8):
                        hc = slice(h * 64, (h + 1) * 64)
                        mm(PA[p][rows, hc], Qc[rows, h, :], Pc[rows, h, :])
                        if i < 5:
                            mm(PQ[p][rows, hc], Pc[rows, h, :], Qc[rows, h, :])
                for p in range(2):
                    rows = slice(64 * p, 64 * p + 64)
                    act(Pd[rows, :, :], PA[p][rows, :].rr("p (h e) -> p h e", h=8), AF.Copy)
                    if i < 5:
                        cp(Qd[rows, :, :], PQ[p][rows, :].rr("p (h e) -> p h e", h=8))
                Pc, Qc = Pd, Qd
                for p in range(2):
                    rows = slice(64 * p, 64 * p + 64)
                    for h in range(8):
                        mm(PX[p][rows, h * 64:(h + 1) * 64], Pc[rows, h, :], Xm[rows, h, :])
                for p in range(2):
                    rows = slice(64 * p, 64 * p + 64)
                    tt(Xm[rows, :, :], Xm[rows, :, :], PX[p][rows, :].rr("p (h e) -> p h e", h=8), ALU.add)
            if bcut < 3:
                continue
            for p in range(2):
                c = tb * 2 + p
                rows = slice(64 * p, 64 * p + 64)
                cols = slice(c * 64, (c + 1) * 64)
                for j in range(2):
                    jr = slice(64 * j, 64 * j + 64)
                    for g in range(4):
                        mm(BJ[j][rows, g * 64:(g + 1) * 64], AlT16[jr, g, cols], b_S16[jr, g, :])
                        mm(BJ[j][rows, 256 + g * 64:256 + (g + 1) * 64], RhT16[jr, g, cols], b_S16[jr, g, :])
                W24 = W2[rows, :, :].rr("p (g j) e -> p g j e", j=2)
                Ws4 = Wsb[rows, :, :].rr("p (g j) e -> p g j e", j=2)
                Y24 = Y2[rows, :, :].rr("p (g j) e -> p g j e", j=2)
                for j in range(2):
                    tt(Ws4[:, :, j, :], BJ[j][rows, 0:256].rr("p (a b) -> p a b", a=4), W24[:, :, j, :], ALU.add)
                    tt(tmp4[rows, :, j, :], BJ[j][rows, 256:512].rr("p (a b) -> p a b", a=4), Y24[:, :, j, :],
                       ALU.add)
                for h in range(8):
                    mm(PX[p][rows, h * 64:(h + 1) * 64], Xm[rows, h, :], Wsb[rows, h, :])
                act(Usb[rows, :, :], PX[p][rows, :].rr("p (h e) -> p h e", h=8), AF.Copy)
                for h in range(8):
                    g, j = h // 2, h % 2
                    mm(PA[p][rows, h * 64:(h + 1) * 64], NbT[rows, h, :], Usb[rows, h, :])
                    mm(PQ[p][64 * j:64 * j + 64, g * 64:(g + 1) * 64], Btm[rows, tb, h, :], Usb[rows, h, :])
                tt(wkv[rows, tb, :, :], tmp4[rows, :, :, :].rr("p g j e -> p (g j) e"),
                   PA[p][rows, :].rr("p (h e) -> p h e", h=8), ALU.subtract)
                tt(b_S[l][:, :, :], b_S[l][:, :, :], PQ[p][:, 0:256].rr("p (a b) -> p a b", a=4), ALU.subtract)
                tt(b_S[l][:, :, :], b_S[l][:, :, :], gLt[:, :, c:c + 1].bc([128, 4, 64]), ALU.mult)
                tt(b_S[l][:, :, :], b_S[l][:, :, :], K2g[:, p, :, :], ALU.add)
                cp(b_S16[:, :, :], b_S[l][:, :, :])
        ogw, ogb = CM.off["b_gw"][0], CM.off["b_gb"][0]
        for tb in range(NTB):
            head_ln(xn, wkv, tb)
            pb = next_pb()
            for g in range(4):
                tr(pb[:, g * 128:(g + 1) * 128], xn[:, tb, 2 * g:2 * g + 2, 0:64], ident[:, :])
            for g in range(4):
                tc_ = slice(tb * 128, (tb + 1) * 128)
                ts(scr[:, 0:128], pb[:, g * 128:(g + 1) * 128], cv[l][:, ogw + g:ogw + g + 1],
                   cv[l][:, ogb + g:ogb + g + 1], ALU.mult, ALU.add)
                tt(scr[:, 0:128], scr[:, 0:128], bon[:, g, tc_], ALU.add)
                tt(yT[:, 4 + g, tc_], scr[:, 0:128], sz[:, g, tc_], ALU.mult)

    def out_proj_residual(l, X):
        dma(gpost[l][:, :], gpost_d[l])
        for si in range(4):
            s = load_wo(l, si)
            for tb in range(NTB):
                for half in range(2):
                    pb = PB[4 + tb * 2 + half]
                    for q in range(4):
                        fc = si * 4 + q
                        mm(pb[:, :], yT[:, fc, tb * 128:(tb + 1) * 128], s[:, q * 2 + half, :],
                           start=(fc == 0), stop=(fc == 15))
        if cut < 4:
            return
        for tb in range(NTB):
            for half in range(2):
                act(scr[:, half * 512:(half + 1) * 512], PB[4 + tb * 2 + half][:, :], AF.Copy)
            if cut < 5:
                continue
            act(scr2[:, :], scr[:, :], AF.Square)
            reduce(sm[:, 24:25], scr2[:, :], ALU.add)
            ts(sm[:, 25:26], sm[:, 24:25], 1.0 / D_MODEL, 1e-6, ALU.mult, ALU.add)
            act(sm[:, 26:27], sm[:, 25:26], AF.Sqrt)
            recip(sm[:, 27:28], sm[:, 26:27])
            if cut < 6:
                continue
            stt(scr2[:, :], scr[:, :], sm[:, 27:28], gpost[l][:, :], ALU.mult, ALU.mult)
            if cut < 7:
                continue
            tt(X[:, tb, :], X[:, tb, :], scr2[:, :], ALU.add)

    xv = x_d.rearrange("(n tb p) d -> n p tb d", tb=NTB, p=128)
    yv = y_d.rearrange("(n tb p) d -> n p tb d", tb=NTB, p=128)
    for ti in range(ntiles):
        X = xt[0]
        dma(X[:, :, :], xv[ti])
        for l in range(nlayers):
            if cut >= 1:
                rmsnorm_pre(l, X)
            memset(yT[:, :, :], 0.0)
            if "C" in mixers and cut >= 2:
                mixer_C(l)
            if "D" in mixers:
                mixer_D(l, ti)
            if "A" in mixers:
                mixer_A(l, ti)
            if "B" in mixers:
                mixer_B(l, ti)
            if cut >= 3:
                out_proj_residual(l, X)
        dma(yv[ti], X[:, :, :])

    P.emit()
    return nc, stack


def kernel(**inputs):
    inputs = {k: np.asarray(v) for k, v in inputs.items()}
    return run(inputs)


def make_in_map(xc, hp, tb):
    m = dict(x=xc, wt=hp["wt"], wo=hp["wo"], cv=hp["cv"], lora=hp["lora"], lru=hp["lru"], gpost=hp["gpost"])
    m.update(tb)
    return m


def run(inputs, ntiles=SEQ // T, nlayers=DEPTH, mixers="ABCD"):
    hp = host_prepare(inputs)
    tb = host_tables()
    nc, stack = build_program(ntiles, nlayers, mixers)
    x = np.ascontiguousarray(inputs["x"], dtype=np.float32)
    in_maps = [make_in_map(x[c], hp, tb) for c in range(NCORES)]
    with stack:
        res = run_bass_kernel_spmd(nc, in_maps, core_ids=list(range(NCORES)))
    out = np.stack([np.asarray(r["y"]) for r in res.results], axis=0)
    return out.astype(np.float32)
```

```python
import contextlib
import numpy as np
import concourse.bass as bass
import concourse.mybir as mybir
from concourse.bass_utils import run_bass_kernel_spmd

F32 = mybir.dt.float32
BF16 = mybir.dt.bfloat16
AF = mybir.ActivationFunctionType
ALU = mybir.AluOpType
AX = mybir.AxisListType

D_MODEL = 1024
SEQ = 4096
BATCH = 4
DEPTH = 2
G = 512
D_IN = 7824
T = 256
NTB = T // 128
NCH = T // 64
NCORES = 4
NSLABBUF = 3
import os as _os
INORDER = tuple(_os.environ.get("INORDER", "pe").split(","))


class Buf:
    def __init__(self, name, tile, psum=False):
        self.name = name
        self.tile = tile
        self.psum = psum
        self.acc = []
        self.dma_sem = None
        self.dma_ops = []

    def __getitem__(self, idx):
        return Ref(self, self.tile[idx])


def _box(ap):
    pat = ap.ap
    pstep = pat[0][0]
    off = int(ap.offset)
    p0 = off // pstep if pstep else 0
    f0 = off % pstep if pstep else off
    ext = 0
    for st, cnt in pat[1:]:
        ext += abs(st) * (cnt - 1)
    return (p0, p0 + pat[0][1], f0, f0 + ext + 1)


class Ref:
    def __init__(self, buf, ap, box=None):
        self.buf = buf
        self.ap = ap
        if box is None:
            box = _box(ap)
            if buf.psum:
                box = ((box[0] // 32) * 32, ((box[1] + 31) // 32) * 32, 0, 512)
        self.box = box

    def bc(self, shape):
        return Ref(self.buf, self.ap.to_broadcast(list(shape)), self.box)

    def __getitem__(self, idx):
        return Ref(self.buf, self.ap[idx])

    def rr(self, pat, **kw):
        return Ref(self.buf, self.ap.rearrange(pat, **kw), self.box)


def _ovl(a, b):
    return a[0] < b[1] and b[0] < a[1] and a[2] < b[3] and b[2] < a[3]


def _cov(a, b):
    return a[0] <= b[0] and a[1] >= b[1] and a[2] <= b[2] and a[3] >= b[3]


class Prog:
    ENG = ("pe", "act", "dve", "pool", "sp")

    def __init__(self, nc, stack):
        self.nc = nc
        self.stack = stack
        self.ops = []
        self.nbuf = 0

    def sb(self, name, shape, dtype=F32):
        t = self.stack.enter_context(self.nc.sbuf_tensor("s_" + name, list(shape), dtype))
        return Buf(name, t)

    def ps(self, name):
        t = self.stack.enter_context(self.nc.psum_tensor("p_" + name, [128, 512], F32))
        return Buf(name, t, psum=True)

    def op(self, eng, fn, outs, ins, dma=False):
        oid = len(self.ops)
        deps = set()
        for r, w in [(x, True) for x in outs] + [(x, False) for x in ins]:
            if r is None or not isinstance(r, Ref):
                continue
            b = r.buf
            for (bx, o2, w2) in b.acc:
                if (w or w2) and _ovl(bx, r.box):
                    deps.add(o2)
        for r, w in [(x, True) for x in outs] + [(x, False) for x in ins]:
            if r is None or not isinstance(r, Ref):
                continue
            b = r.buf
            if w:
                b.acc = [a for a in b.acc if not _cov(r.box, a[0])]
            b.acc.append((r.box, oid, w))
            if len(b.acc) > 48:
                merged = {}
                for (bx, o2, w2) in b.acc:
                    k = (self.ops[o2]["eng"] if o2 < oid else eng, w2)
                    if k in merged:
                        m = merged[k]
                        merged[k] = ((min(m[0][0], bx[0]), max(m[0][1], bx[1]), min(m[0][2], bx[2]),
                                      max(m[0][3], bx[3])), max(m[1], o2), w2)
                    else:
                        merged[k] = (bx, o2, w2)
                b.acc = list(merged.values())
        dbuf = None
        if dma:
            for r in list(outs) + list(ins):
                if isinstance(r, Ref):
                    dbuf = r.buf
            dbuf.dma_ops.append(oid)
        deps.discard(oid)
        self.ops.append(dict(eng=eng, fn=fn, deps=sorted(deps), dma=dma, dbuf=dbuf, sig=False))
        return oid

    def emit(self):
        nc = self.nc
        ops = self.ops
        for o in ops:
            for d in o["deps"]:
                od = ops[d]
                if od["dma"]:
                    continue
                if od["eng"] == o["eng"] and o["eng"] in INORDER and not o["dma"]:
                    continue
                od["sig"] = True
        sems = {e: self.stack.enter_context(nc.semaphore("sem_" + e)) for e in self.ENG}
        cnt = {e: 0 for e in self.ENG}
        for o in ops:
            if o["dma"]:
                b = o["dbuf"]
                if b.dma_sem is None:
                    b.dma_sem = self.stack.enter_context(nc.semaphore("dsem_" + b.name))
            elif o["sig"]:
                cnt[o["eng"]] += 1
                o["signo"] = cnt[o["eng"]]
        per_eng = {e: [] for e in self.ENG}
        for i, o in enumerate(ops):
            per_eng[o["eng"]].append(i)
        import bisect

        def gen(engname, e):
            waited = {}
            for i in per_eng[engname]:
                o = ops[i]
                need = {}
                for d in o["deps"]:
                    od = ops[d]
                    if od["dma"]:
                        b = od["dbuf"]
                        n = bisect.bisect_left(b.dma_ops, i)
                        key = ("d", id(b))
                        if need.get(key, (None, 0))[1] < 16 * n:
                            need[key] = (b.dma_sem, 16 * n)
                    else:
                        if od["eng"] == engname and engname in INORDER and not o["dma"]:
                            continue
                        key = ("e", od["eng"])
                        if need.get(key, (None, 0))[1] < od["signo"]:
                            need[key] = (sems[od["eng"]], od["signo"])
                for key, (sem, val) in need.items():
                    if waited.get(key, 0) >= val:
                        continue
                    e.wait_ge(sem, val)
                    waited[key] = val
                ins = o["fn"](e)
                if o["dma"]:
                    ins.then_inc(o["dbuf"].dma_sem, 16)
                elif o["sig"]:
                    ins.then_inc(sems[engname], 1)

        with nc.Block() as block:
            @block.tensor
            def _(e):
                gen("pe", e)

            @block.scalar
            def _(e):
                gen("act", e)

            @block.vector
            def _(e):
                gen("dve", e)

            @block.gpsimd
            def _(e):
                gen("pool", e)

            @block.sync
            def _(e):
                gen("sp", e)
                seen = set()
                for o in ops:
                    if o["dma"] and id(o["dbuf"]) not in seen:
                        seen.add(id(o["dbuf"]))
                        e.wait_ge(o["dbuf"].dma_sem, 16 * len(o["dbuf"].dma_ops))


def _a(x):
    return x.ap if isinstance(x, Ref) else x


def _fm4(v):
    return np.ascontiguousarray(v.reshape(-1, 128).T)


class CMap:
    def __init__(self):
        self.off = {}
        self.n = 0

    def add(self, name, w):
        self.off[name] = (self.n, w)
        self.n += w


def build_cmap():
    c = CMap()
    c.add("gpre", 8)
    c.add("a_cw", 32); c.add("a_cb", 8); c.add("a_ib", 1); c.add("a_fb", 1); c.add("a_nw", 4)
    c.add("b_mu", 13); c.add("b_w0", 4); c.add("b_a0", 4); c.add("b_kk", 4); c.add("b_ka", 4)
    c.add("b_rk", 4); c.add("b_gw", 4); c.add("b_gb", 4)
    c.add("c_cw", 16); c.add("c_cb", 4); c.add("c_br", 4); c.add("c_bi", 4); c.add("c_lam", 4)
    c.add("d_nw", 4)
    return c


CM = build_cmap()

def _cols(a, b):
    return list(range(a, b))


def build_slabs():
    slabs = []
    def fm(name, start):
        slabs.append((name, _cols(start, start + 512)))
    fm("Aq", 0); fm("Ak", 512); fm("Az", 2048)
    slabs.append(("Ag", _cols(2560, 2576) + [-1] * (512 - 16)))
    fm("Br", 2576); fm("Bk", 3088); fm("Bv", 3600)
    slabs.append(("Bl", _cols(4112, 4240) + [-1] * (512 - 128)))
    fm("Bz", 4240); fm("Cx", 4752); fm("Cz", 5264); fm("Dz", 7312)
    fm("Av", 1024); fm("Ao", 1536); fm("Dq", 5776); fm("Dk", 6288); fm("Dv", 6800)
    return slabs


SLABS = build_slabs()
SLAB_ID = {s[0]: i for i, s in enumerate(SLABS)}
NSLAB = len(SLABS)


def host_prepare(inp):
    f = np.float32
    w_in = inp["w_in"]
    w_in_p = np.concatenate([w_in, np.zeros((DEPTH, D_MODEL, 1), f)], axis=2)
    wt = np.empty((DEPTH, NSLAB, 128, 8, 512), f)
    for si, (name, cols) in enumerate(SLABS):
        blk = w_in_p[:, :, cols]
        wt[:, si] = blk.reshape(DEPTH, 8, 128, 512).transpose(0, 2, 1, 3)
    w_out = inp["w_out"]
    wo = np.ascontiguousarray(w_out.reshape(DEPTH, 4, 4, 128, 1024).transpose(0, 1, 3, 2, 4))
    cv = np.zeros((DEPTH, 128, CM.n), f)
    def put(name, arr):
        o, w = CM.off[name]
        cv[:, :, o:o + w] = arr
    put("gpre", inp["norm_pre"].reshape(DEPTH, 8, 128).transpose(0, 2, 1))
    cw = inp["mlstm_conv_w"]
    put("a_cw", cw.reshape(DEPTH, 4, 8, 128).transpose(0, 3, 1, 2).reshape(DEPTH, 128, 32))
    put("a_cb", inp["mlstm_conv_b"].reshape(DEPTH, 8, 128).transpose(0, 2, 1))
    ib = np.zeros((DEPTH, 128, 1), f); ib[:, 0:8, 0] = inp["mlstm_i_bias"]; put("a_ib", ib)
    fb = np.zeros((DEPTH, 128, 1), f); fb[:, 0:8, 0] = inp["mlstm_f_bias"]; put("a_fb", fb)
    def fm4(x):
        return x.reshape(DEPTH, -1, 128).transpose(0, 2, 1)
    put("a_nw", fm4(inp["mlstm_norm_w"]))
    mu = inp["rwkv_mu"]
    put("b_mu", fm4(mu))
    for k, nm in [("b_w0", "rwkv_w0"), ("b_a0", "rwkv_a0"), ("b_kk", "rwkv_k_k"), ("b_ka", "rwkv_k_a"),
                  ("b_rk", "rwkv_r_k"), ("b_gw", "rwkv_gn_w"), ("b_gb", "rwkv_gn_b"),
                  ("c_cb", "lru_conv_b"), ("c_br", "lru_b_r"), ("c_bi", "lru_b_i"), ("c_lam", "lru_lambda"),
                  ("d_nw", "ret_norm_w")]:
        put(k, fm4(inp[nm]))
    lw = inp["lru_conv_w"]
    put("c_cw", lw.reshape(DEPTH, 4, 4, 128).transpose(0, 3, 1, 2).reshape(DEPTH, 128, 16))
    lora = np.concatenate([inp["rwkv_w_up"], inp["rwkv_a_up"]], axis=1)
    lru = np.zeros((DEPTH, 128, 2, 4, 128), f)
    for which, nm in enumerate(["lru_w_r", "lru_w_i"]):
        w = inp[nm]
        for g in range(4):
            for j in range(2):
                lru[:, 64 * j:64 * j + 64, which, g, 64 * j:64 * j + 64] = w[:, 2 * g + j]
    gpost = np.ascontiguousarray(np.broadcast_to(inp["norm_post"][:, None, :], (DEPTH, 128, D_MODEL)))
    return dict(wt=wt, wo=wo, cv=cv, lora=np.ascontiguousarray(lora), lru=lru, gpost=gpost)


def host_tables():
    f = np.float32
    t = {}
    t["ident"] = np.eye(128, dtype=f)
    hd = 64
    log_g = np.log1p(-np.exp2(-5.0 - np.arange(8, dtype=np.float64)))
    idx = np.arange(64, dtype=np.float64)
    dmat = np.exp(log_g[:, None, None] * np.abs(idx[:, None] - idx[None, :])) * hd ** -0.5
    xi = np.exp(log_g[:, None] * (idx + 1.0))
    zeta = np.exp(log_g[:, None] * (63.0 - idx)) * hd ** -0.5
    gch = np.exp(log_g * 64.0)
    hs2h = [2 * (s % 4) + (s // 4) for s in range(8)]
    dm = np.zeros((128, 8, 64));
    for s in range(8):
        dm[0:64, s] = dmat[hs2h[s]]; dm[64:128, s] = dmat[hs2h[s]]
    t["d_dmat"] = dm.astype(f)
    t["d_xi"] = np.tile(xi.T, (2, 1)).astype(f)
    t["d_zeta"] = np.tile(zeta.T, (2, 1)).astype(f)
    gc = np.zeros((128, 4))
    for g in range(4):
        for j in range(2):
            gc[64 * j:64 * j + 64, g] = gch[2 * g + j]
    t["d_gch"] = gc.astype(f)
    mk = np.zeros((128, 8, 64))
    sidx = np.arange(128) % 64
    mk[:] = np.where(sidx[:, None, None] <= np.arange(64)[None, None, :], 0.0, -30000.0)
    t["a_mask"] = mk.reshape(128, 512).astype(f)
    sel = np.zeros((8, 512 + 4 + 128))
    for hp_ in range(8):
        for hs in range(8):
            if hp_ == hs2h[hs]:
                sel[hp_, hs * 64:(hs + 1) * 64] = 1.0
        sel[hp_, 512 + hp_ // 2] = 1.0
        jj = hp_ % 2
        sel[hp_, 516 + 64 * jj:516 + 64 * jj + 64] = 1.0
    t["a_sel"] = sel.astype(f)
    bo = np.zeros((128, 128)); bo[0:64, 0:64] = 1.0; bo[64:128, 64:128] = 1.0
    t["b_ones"] = bo.astype(f)
    i2 = np.zeros((128, 64)); i2[0:64] = np.eye(64); i2[64:128] = np.eye(64)
    t["b_i2"] = i2.astype(f)
    rr_ = (np.arange(128) % 64)[:, None]; cc_ = np.arange(64)[None, :]
    m5 = np.zeros((128, 5, 64))
    m5[:, 0] = rr_ > cc_; m5[:, 1] = cc_ > rr_; m5[:, 2] = cc_ > rr_; m5[:, 3] = cc_ >= rr_; m5[:, 4] = cc_ >= rr_
    t["b_m5"] = m5.astype(f)
    half = 32
    pos = np.arange(SEQ, dtype=np.float32)
    inv_freq = (np.float32(10000.0) ** (-np.arange(half, dtype=np.float32) / np.float32(half))).astype(np.float32)
    ang = (pos[:, None] * inv_freq[None, :]).astype(np.float32).astype(np.float64)
    t["rope"] = np.concatenate([np.cos(ang), np.sin(ang)], axis=1).astype(f)
    return t


def build_program(ntiles=SEQ // T, nlayers=DEPTH, mixers="ABCD"):
    nc = bass.Bass("TRN2", target_bir_lowering=False)
    stack = contextlib.ExitStack()
    P = Prog(nc, stack)
    dram = {}

    def din(name, shape):
        dram[name] = nc.dram_tensor(name, list(shape), F32, kind="ExternalInput").ap()
        return dram[name]

    x_d = din("x", [SEQ, D_MODEL])
    wt_d = din("wt", [DEPTH, NSLAB, 128, 8, 512])
    wo_d = din("wo", [DEPTH, 4, 128, 4, 1024])
    cv_d = din("cv", [DEPTH, 128, CM.n])
    lora_d = din("lora", [DEPTH, 128, 512])
    lru_d = din("lru", [DEPTH, 128, 2, 4, 128])
    gpost_d = din("gpost", [DEPTH, 128, D_MODEL])
    ident_d = din("ident", [128, 128])
    dmat_d = din("d_dmat", [128, 8, 64])
    dxi_d = din("d_xi", [128, 8])
    dzeta_d = din("d_zeta", [128, 8])
    dgch_d = din("d_gch", [128, 4])
    rope_d = din("rope", [SEQ, 64])
    bones_d = din("b_ones", [128, 128])
    bi2_d = din("b_i2", [128, 64])
    bm5_d = din("b_m5", [128, 5, 64])
    amask_d = din("a_mask", [128, 512])
    asel_d = din("a_sel", [8, 644])
    y_d = nc.dram_tensor("y", [SEQ, D_MODEL], F32, kind="ExternalOutput").ap()

    ident = P.sb("ident", [128, 128])
    cv = [P.sb(f"cv{l}", [128, CM.n]) for l in range(DEPTH)]
    lru1 = P.sb("lru", [128, 2, 4, 128]); lru = [lru1, lru1]
    gpost1 = P.sb("gpost", [128, D_MODEL]); gpost = [gpost1, gpost1]
    lora_up = [P.sb(f"lora_up{l}", [128, 512]) for l in range(DEPTH)]
    b_ones = P.sb("b_ones", [128, 128]); b_i2 = P.sb("b_i2", [128, 1, 64]); b_m5 = P.sb("b_m5", [128, 5, 64])
    b_hist = [P.sb(f"b_hist{l}", [128, 13]) for l in range(DEPTH)]
    b_S = [P.sb(f"b_S{l}", [128, 4, 64]) for l in range(DEPTH)]
    ident16 = P.sb("ident16", [128, 128], BF16)
    AlT16 = P.sb("AlT16", [128, 4, T], BF16); RhT16 = P.sb("RhT16", [128, 4, T], BF16)
    bk16 = P.sb("bk16", [128, 2, T], BF16)
    bm16 = [P.sb(f"bm16_{i}", [128, 8, 64], BF16) for i in range(3)]
    b_S16 = P.sb("b_S16", [128, 4, 64], BF16)
    a_n16 = P.sb("a_n16", [128, 4, 1], BF16); wj16 = P.sb("wj16", [128, NTB, 8], BF16)
    ones16 = P.sb("ones16", [128, 64], BF16); eps12 = P.sb("eps12", [128, 1])
    gLt = P.sb("gLt", [128, 4, NCH]); PRall = [P.sb(f"PR{i}", [128, 5, 8, 64], BF16) for i in range(NTB)]; Y2 = P.sb("Y2", [128, 8, 64])
    K2g = P.sb("K2g", [128, 2, 4, 64])
    ones = P.sb("ones", [128, T]); zeros = P.sb("zeros", [128, T])
    xt = [P.sb(f"xt{i}", [128, NTB, D_MODEL]) for i in range(1)]
    hT = P.sb("hT", [128, 8, T], BF16)
    yT = P.sb("yT", [128, 16, T], BF16)
    slab = [P.sb(f"slab{i}", [128, 8, 512], BF16) for i in range(NSLABBUF)]
    sm = P.sb("small", [128, 64])
    scr = P.sb("scr", [128, D_MODEL])
    scr2 = P.sb("scr2", [128, D_MODEL])
    FS = [P.sb(f"fs{i}", [128, 4, T + 4]) for i in range(7)]
    c_hist = [P.sb(f"c_hist{l}", [128, 4, 3]) for l in range(DEPTH)]
    c_state = [P.sb(f"c_state{l}", [128, 4]) for l in range(DEPTH)]
    c_coef = [P.sb(f"c_coef{l}", [128, 8]) for l in range(DEPTH)]
    PB = [P.ps(f"pb{i}") for i in range(8)]
    TM = [P.sb(f"tm{i}", [128, NTB, 8, 64]) for i in range(6)]
    TM16 = [P.sb(f"tm16_{i}", [128, NTB, 8, 64], BF16) for i in range(3)]
    d_dmat = P.sb("d_dmat", [128, 8, 64]); d_xi = P.sb("d_xi", [128, 4, 2, 1]); d_zeta = P.sb("d_zeta", [128, 8, 1])
    d_gch = P.sb("d_gch", [128, 4, 1]); rope_sb = P.sb("rope_sb", [128, NTB, 1, 64])
    d_R = [P.sb(f"d_R{l}", [128, 4, 64]) for l in range(DEPTH)]
    a_mask = P.sb("a_mask", [128, 512]); a_sel = P.sb("a_sel", [8, 644])
    ga = P.sb("ga", [8, 12, T + 1]); scl = P.sb("scl", [8, NCH]); rhs_sc = P.sb("rhs_sc", [8, NCH, 4])
    scb = P.sb("scb", [128, NCH, 4, 1]); negPx = P.sb("negPx", [8, 2, 8, 64]); ET = P.sb("ET", [128, 8, 64])
    tmsc = P.sb("tmsc", [128, NTB, 3, 8]); hnum = P.sb("hnum", [128, 8, 64]); vw = P.sb("vw", [128, 8, 64])
    a_hist = [P.sb(f"a_hist{l}", [128, 8, 3]) for l in range(DEPTH)]
    a_C = [P.sb(f"a_C{l}", [128, 4, 64]) for l in range(DEPTH)]
    a_n = [P.sb(f"a_n{l}", [128, 4, 1]) for l in range(DEPTH)]
    a_car = [P.sb(f"a_car{l}", [8, 4]) for l in range(DEPTH)]
    AT = P.sb("AT", [128, 8, 64]); tmp4 = P.sb("tmp4", [128, 4, 2, 64]); lnst = P.sb("lnst", [128, 64])
    state = dict(slab_i=0, pb_i=0)

    import os
    cut = int(os.environ.get("KCUT", "9"))
    def dma(out, in_, eng="sp"):
        return P.op(eng, lambda e: e.dma_start(out=_a(out), in_=_a(in_)), [out], [in_], dma=True)

    def mm(out, lhsT, rhs, start=True, stop=True):
        P.op("pe", lambda e: e.matmul(_a(out), _a(lhsT), _a(rhs), start=start, stop=stop),
             [out], [lhsT, rhs] + ([] if start else [out]))

    def tr(out, in_, idn):
        P.op("pe", lambda e: e.transpose(_a(out), _a(in_), _a(idn)), [out], [in_, idn])

    def act(out, in_, func, bias=None, scale=1.0, accum=None, eng="act"):
        kw = {}
        if bias is not None:
            kw["bias"] = _a(bias)
        if accum is not None:
            kw["accum_out"] = _a(accum)
        P.op("act", lambda e: e.activation(_a(out), _a(in_), func, scale=_a(scale), **kw),
             [out, accum], [in_, bias, scale])

    def tt(out, a, b, op, eng="dve"):
        P.op(eng, lambda e: e.tensor_tensor(_a(out), _a(a), _a(b), op), [out], [a, b])

    def ts(out, a, s1, s2, op0, op1=ALU.bypass, eng="dve"):
        P.op(eng, lambda e: e.tensor_scalar(_a(out), _a(a), _a(s1), _a(s2), op0, op1), [out], [a, s1, s2])

    def stt(out, a, s, b, op0, op1):
        P.op("dve", lambda e: e.scalar_tensor_tensor(_a(out), _a(a), _a(s), _a(b), op0, op1), [out], [a, s, b])

    def scan(out, d0, d1, init, op0, op1):
        P.op("dve", lambda e: e.tensor_tensor_scan(_a(out), _a(d0), _a(d1), _a(init), op0, op1),
             [out], [d0, d1, init])

    def cp(out, in_, eng="dve"):
        if isinstance(in_, Ref) and in_.buf.psum and eng == "dve":
            return act(out, in_, AF.Copy)
        P.op(eng, lambda e: e.tensor_copy(_a(out), _a(in_)), [out], [in_])

    def recip(out, in_):
        P.op("dve", lambda e: e.reciprocal(_a(out), _a(in_)), [out], [in_])

    def memset(out, val, eng="dve"):
        P.op(eng, lambda e: e.memset(_a(out), val), [out], [])

    def reduce(out, in_, op, axis=AX.X):
        P.op("dve", lambda e: e.tensor_reduce(_a(out), _a(in_), axis, op), [out], [in_])

    def interleave(gens):
        gens = list(gens)
        while gens:
            for gz in list(gens):
                try:
                    next(gz)
                except StopIteration:
                    gens.remove(gz)

    def next_pb():
        state["pb_i"] = (state["pb_i"] + 1) % 2
        return PB[state["pb_i"]]

    def load_slab(l, name, ncols=512):
        s = slab[state["slab_i"]]
        state["slab_i"] = (state["slab_i"] + 1) % NSLABBUF
        dma(s[:, :, 0:ncols], wt_d[l, SLAB_ID[name], :, :, 0:ncols], eng="pool")
        return s

    def load_wo(l, si):
        s = slab[state["slab_i"]]
        state["slab_i"] = (state["slab_i"] + 1) % NSLABBUF
        dma(s[:, :, :], wo_d[l, si].rearrange("p f n -> p (f n)").rearrange("p (a b) -> p a b", a=8), eng="pool")
        return s

    def cvc(l, name, i=0):
        o, w = CM.off[name]
        return cv[l][:, o + i:o + i + 1]

    def fm_proj(s, gi, out_cols=T):
        pb = next_pb()
        for kc in range(8):
            mm(pb[:, 0:T], s[:, kc, gi * 128:(gi + 1) * 128], hT[:, kc, :], start=(kc == 0), stop=(kc == 7))
        return pb

    dma(ident[:, :], ident_d)
    for l in range(DEPTH):
        dma(cv[l][:, :], cv_d[l])
        dma(lora_up[l][:, :], lora_d[l])
        memset(b_hist[l][:, :], 0.0)
        memset(b_S[l][:, :, :], 0.0)
    dma(d_dmat[:, :, :], dmat_d)
    dma(d_xi[:, :, :, :], dxi_d.rearrange("p (g j o) -> p g j o", g=4, j=2))
    dma(d_zeta[:, :, :], dzeta_d.rearrange("p (h o) -> p h o", o=1))
    dma(d_gch[:, :, :], dgch_d.rearrange("p (g o) -> p g o", o=1))
    dma(a_mask[:, :], amask_d)
    dma(a_sel[:, :], asel_d)
    for l in range(DEPTH):
        memset(d_R[l][:, :, :], 0.0)
        memset(a_hist[l][:, :, :], 0.0)
        memset(a_C[l][:, :, :], 0.0)
        memset(a_n[l][:, :, :], 0.0)
        memset(a_car[l][:, :], 0.0)
        ts(a_car[l][0:8, 2:3], cv[l][0:8, CM.off["a_fb"][0]:CM.off["a_fb"][0] + 1], -1.0, None, ALU.mult)
    dma(b_ones[:, :], bones_d)
    dma(b_i2[:, 0, :], bi2_d)
    dma(b_m5[:, :, :], bm5_d)
    memset(ones[:, :], 1.0)
    cp(ident16[:, :], ident[:, :])
    memset(ones16[:, :], 1.0)
    memset(eps12[:, :], 1e-12)
    memset(zeros[:, :], 0.0)
    for l in range(DEPTH):
        memset(c_hist[l][:, :, :], 0.0)
        memset(c_state[l][:, :], 0.0)
        o, w = CM.off["c_lam"]
        act(sm[:, 0:4], cv[l][:, o:o + 4], AF.Exp, scale=-1.0)
        act(sm[:, 4:8], sm[:, 0:4], AF.Ln, bias=ones[:, 0:1])
        ts(c_coef[l][:, 0:4], sm[:, 4:8], -8.0, None, ALU.mult)
        ts(c_coef[l][:, 4:8], sm[:, 4:8], -16.0, None, ALU.mult)

    def rmsnorm_pre(l, X):
        for tb in range(NTB):
            act(scr[:, :], X[:, tb, :], AF.Square)
            reduce(sm[:, 8 + tb:9 + tb], scr[:, :], ALU.add)
        ts(sm[:, 12:12 + NTB], sm[:, 8:8 + NTB], 1.0 / D_MODEL, 1e-6, ALU.mult, ALU.add)
        act(sm[:, 16:16 + NTB], sm[:, 12:12 + NTB], AF.Sqrt)
        recip(sm[:, 20:20 + NTB], sm[:, 16:16 + NTB])
        for tb in range(NTB):
            act(scr[:, :], X[:, tb, :], AF.Copy, scale=sm[:, 20 + tb:21 + tb])
            for half in range(2):
                pb = next_pb()
                for q in range(4):
                    kc = half * 4 + q
                    tr(pb[:, q * 128:(q + 1) * 128], scr[:, kc * 128:(kc + 1) * 128], ident[:, :])
                for q in range(4):
                    kc = half * 4 + q
                    ts(hT[:, kc, tb * 128:(tb + 1) * 128], pb[:, q * 128:(q + 1) * 128],
                       cvc(l, "gpre", kc), None, ALU.mult)

    def mixer_C(l):
        cx, xc, rg, ig, aa, uu, hh = FS[0], FS[1], FS[2], FS[3], FS[4], FS[5], FS[6]
        dma(lru[l][:, :, :, :], lru_d[l])
        wx = load_slab(l, "Cx")

        def cstream(g):
            pb = fm_proj(wx, g)
            cp(cx[:, g, 0:3], c_hist[l][:, g, :])
            act(cx[:, g, 3:3 + T], pb[:, 0:T], AF.Copy)
            yield
            cp(c_hist[l][:, g, :], cx[:, g, T:T + 3])
            o, w = CM.off["c_cw"]
            ts(xc[:, g, 0:T], cx[:, g, 0:T], cv[l][:, o + g:o + g + 1], cvc(l, "c_cb", g), ALU.mult, ALU.add)
            yield
            for j in range(1, 4):
                stt(xc[:, g, 0:T], cx[:, g, j:j + T], cv[l][:, o + 4 * j + g:o + 4 * j + g + 1], xc[:, g, 0:T],
                    ALU.mult, ALU.add)
                yield
        interleave([cstream(g) for g in range(4)])
        for g in range(4):
            pb = next_pb()
            mm(pb[:, 0:T], lru[l][:, 0, g, :], xc[:, g, 0:T])
            act(rg[:, g, 0:T], pb[:, 0:T], AF.Sigmoid, bias=cvc(l, "c_br", g))
            pb = next_pb()
            mm(pb[:, 0:T], lru[l][:, 1, g, :], xc[:, g, 0:T])
            act(ig[:, g, 0:T], pb[:, 0:T], AF.Sigmoid, bias=cvc(l, "c_bi", g))
        for g in range(4):
            act(aa[:, g, 0:T], rg[:, g, 0:T], AF.Exp, scale=c_coef[l][:, g:g + 1])
            act(uu[:, g, 0:T], rg[:, g, 0:T], AF.Exp, scale=c_coef[l][:, 4 + g:5 + g])
        ts(uu[:, :, 0:T], uu[:, :, 0:T], -1.0, 1.0, ALU.mult, ALU.add)
        act(uu[:, :, 0:T], uu[:, :, 0:T], AF.Sqrt)
        tt(ig[:, :, 0:T], ig[:, :, 0:T], xc[:, :, 0:T], ALU.mult)
        tt(uu[:, :, 0:T], uu[:, :, 0:T], ig[:, :, 0:T], ALU.mult)
        for g in range(4):
            scan(hh[:, g, 0:T], aa[:, g, 0:T], uu[:, g, 0:T], c_state[l][:, g:g + 1], ALU.mult, ALU.add)
            cp(c_state[l][:, g:g + 1], hh[:, g, T - 1:T])
        wz = load_slab(l, "Cz")
        for g in range(4):
            pb = fm_proj(wz, g)
            act(rg[:, g, 0:T], pb[:, 0:T], AF.Silu)
            tt(yT[:, 8 + g, :], hh[:, g, 0:T], rg[:, g, 0:T], ALU.mult)

    def tm_proj(sl, tb):
        pb = next_pb()
        for kc in range(8):
            mm(pb[:, :], hT[:, kc, tb * 128:(tb + 1) * 128], sl[:, kc, :], start=(kc == 0), stop=(kc == 7))
        return pb

    def to_fm(dst, src, tb):
        pb = next_pb()
        for g in range(4):
            tr(pb[:, g * 128:(g + 1) * 128], src[:, tb, 2 * g:2 * g + 2, 0:64], ident[:, :])
        cp(dst[:, 0:4, tb * 128:(tb + 1) * 128], pb[:, :].rr("p (a b) -> p a b", a=4))

    def to_tm(dst, src, tb):
        pb = next_pb()
        for g in range(4):
            tr(pb[:, g * 128:(g + 1) * 128], src[:, g, tb * 128:(tb + 1) * 128], ident[:, :])
        cp(dst[:, tb, :, 0:64], pb[:, :].rr("p (h e) -> p h e", h=8))

    def head_ln(dst, src, tb):
        X3 = src[:, tb, :, 0:64]
        D3 = dst[:, tb, :, 0:64]
        reduce(lnst[:, 0:8], X3, ALU.add)
        stt(D3, lnst[:, 0:8].rr("p (h o) -> p h o", o=1).bc([128, 8, 64]), -1.0 / 64, X3, ALU.mult, ALU.add)
        tt(scr[:, 0:512].rr("p (h e) -> p h e", h=8), D3, D3, ALU.mult)
        reduce(lnst[:, 8:16], scr[:, 0:512].rr("p (h e) -> p h e", h=8), ALU.add)
        ts(lnst[:, 16:24], lnst[:, 8:16], 1.0 / 64, 1e-5, ALU.mult, ALU.add)
        act(lnst[:, 24:32], lnst[:, 16:24], AF.Sqrt)
        recip(lnst[:, 32:40], lnst[:, 24:32])
        tt(D3, D3, lnst[:, 32:40].rr("p (h o) -> p h o", o=1).bc([128, 8, 64]), ALU.mult)

    def mixer_D(l, ti):
        qr, kr, osb, xn = TM[0], TM[1], TM[4], TM[5]
        vv, kz = TM16[1], TM16[2]
        qT, kT, sz = AlT16, RhT16, FS[2]
        AT = bm16[0]
        d_R16 = b_S16
        cp(d_R16[:, :, :], d_R[l][:, :, :])
        S = [PB[3], PB[4]]; O = [PB[5], PB[6]]; ST = [PB[2], PB[7]]
        dma(rope_sb[:, :, :, :], rope_d.rearrange("(n tb p) (o e) -> n p tb o e", tb=NTB, p=128, o=1)[ti])
        for name, dst in (("Dq", qr), ("Dk", kr)):
            sl = load_slab(l, name)
            for tb in range(NTB):
                pb = tm_proj(sl, tb)
                act(scr[:, 0:512], pb[:, :], AF.Copy)
                raw = scr[:, 0:512].rr("p (h e) -> p h e", h=8)
                x1, x2 = raw[:, :, 0:32], raw[:, :, 32:64]
                cos = rope_sb[:, tb, :, 0:32].bc([128, 8, 32]); sin = rope_sb[:, tb, :, 32:64].bc([128, 8, 32])
                t1 = scr2[:, 0:256].rr("p (h e) -> p h e", h=8); t2 = scr2[:, 256:512].rr("p (h e) -> p h e", h=8)
                d1, d2 = dst[:, tb, :, 0:32], dst[:, tb, :, 32:64]
                tt(d1, x1, cos, ALU.mult); tt(t1, x2, sin, ALU.mult); tt(d1, d1, t1, ALU.subtract)
                tt(d2, x2, cos, ALU.mult); tt(t2, x1, sin, ALU.mult); tt(d2, d2, t2, ALU.add)
                to_fm(qT if name == "Dq" else kT, dst, tb)
        sl = load_slab(l, "Dv")
        for tb in range(NTB):
            pb = tm_proj(sl, tb)
            act(vv[:, tb, :, 0:64], pb[:, :].rr("p (h e) -> p h e", h=8), AF.Copy)
            tt(kz[:, tb, :, 0:64], kr[:, tb, :, 0:64], d_zeta[:, :, :].bc([128, 8, 64]), ALU.mult)
        sl = load_slab(l, "Dz")
        for g in range(4):
            pb = fm_proj(sl, g)
            act(sz[:, g, 0:T], pb[:, 0:T], AF.Silu)
        for c in range(NCH):
            tb, p = c // 2, c % 2
            rows = slice(64 * p, 64 * p + 64)
            cols = slice(c * 64, (c + 1) * 64)
            for j in range(2):
                jr = slice(64 * j, 64 * j + 64)
                for g in range(4):
                    mm(S[j][rows, g * 64:(g + 1) * 64], kT[jr, g, cols], qT[jr, g, cols])
                    mm(S[j][rows, 256 + g * 64:256 + (g + 1) * 64], qT[jr, g, cols], d_R16[jr, g, :])
            for j in range(2):
                tt(AT[rows, 4 * j:4 * j + 4, :], S[j][rows, 0:256].rr("p (a b) -> p a b", a=4),
                   d_dmat[rows, 4 * j:4 * j + 4, :], ALU.mult)
                tt(tmp4[rows, :, j, 0:64], S[j][rows, 256:512].rr("p (a b) -> p a b", a=4),
                   d_xi[rows, :, j, :].bc([64, 4, 64]), ALU.mult)
            for h in range(8):
                g, j = h // 2, h % 2
                mm(O[p][rows, h * 64:(h + 1) * 64], AT[rows, j * 4 + g, :], vv[rows, tb, h, 0:64])
            tt(osb[rows, tb, :, 0:64], O[p][rows, :].rr("p (h e) -> p h e", h=8),
               tmp4[rows, :, :, 0:64].rr("p g j e -> p (g j) e"), ALU.add)
            for h in range(8):
                g, j = h // 2, h % 2
                mm(ST[p][64 * j:64 * j + 64, g * 64:(g + 1) * 64], kz[rows, tb, h, 0:64], vv[rows, tb, h, 0:64])
            tt(d_R[l][:, :, :], d_R[l][:, :, :], d_gch[:, :, :].bc([128, 4, 64]), ALU.mult)
            tt(d_R[l][:, :, :], d_R[l][:, :, :], ST[p][:, 0:256].rr("p (a b) -> p a b", a=4), ALU.add)
            cp(d_R16[:, :, :], d_R[l][:, :, :])
        for tb in range(NTB):
            head_ln(xn, osb, tb)
            pb = next_pb()
            for g in range(4):
                tr(pb[:, g * 128:(g + 1) * 128], xn[:, tb, 2 * g:2 * g + 2, 0:64], ident[:, :])
            for g in range(4):
                stt(yT[:, 12 + g, tb * 128:(tb + 1) * 128], pb[:, g * 128:(g + 1) * 128], cvc(l, "d_nw", g),
                    sz[:, g, tb * 128:(tb + 1) * 128], ALU.mult, ALU.mult)

    def conv_silu(l, sl, g8, dst, gdst, cx, whichhist, cwname, cbname, ngroups_total, silu=True):
        pass

    def mixer_A(l, ti):
        cx, qT32, kT32, sz = FS[0], FS[1], FS[2], FS[3]
        qT, kT = AlT16, RhT16
        so, osb, xn = TM[2], TM[3], TM[4]
        ktm, vv = TM16[0], TM16[1]
        AT, vw = bm16[0], bm16[1]
        a_C16 = b_S16
        cp(a_C16[:, :, :], a_C[l][:, :, :])
        cp(a_n16[:, :, :], a_n[l][:, :, :])
        S = [PB[3], PB[4]]; O = [PB[5], PB[6]]; ST = [PB[2], PB[7]]; JX = [PB[0], PB[1]]
        R_I, R_SP, R_F, R_G, R_P, R_MU, R_PE, R_SI, R_WJ, R_NM, R_NP = range(11)
        ocw = CM.off["a_cw"][0]
        for which, (name, dstT, dst16) in enumerate((("Aq", qT32, qT), ("Ak", kT32, kT))):
            sl = load_slab(l, name)

            def astream(g, which=which, sl=sl, dstT=dstT, dst16=dst16):
                g8 = which * 4 + g
                pb = fm_proj(sl, g)
                cp(cx[:, g, 0:3], a_hist[l][:, g8, :])
                act(cx[:, g, 3:3 + T], pb[:, 0:T], AF.Copy)
                yield
                cp(a_hist[l][:, g8, :], cx[:, g, T:T + 3])
                ts(dstT[:, g, 0:T], cx[:, g, 0:T], cv[l][:, ocw + g8:ocw + g8 + 1], cvc(l, "a_cb", g8),
                   ALU.mult, ALU.add)
                yield
                for j in range(1, 4):
                    stt(dstT[:, g, 0:T], cx[:, g, j:j + T], cv[l][:, ocw + 8 * j + g8:ocw + 8 * j + g8 + 1],
                        dstT[:, g, 0:T], ALU.mult, ALU.add)
                    yield
                act(dst16[:, g, 0:T], dstT[:, g, 0:T], AF.Silu)
                yield
            interleave([astream(g) for g in range(4)])
        sl = load_slab(l, "Az")
        for g in range(4):
            pb = fm_proj(sl, g)
            act(sz[:, g, 0:T], pb[:, 0:T], AF.Silu)
        acut = int(os.environ.get("ACUT", "99"))
        if acut < 1:
            return
        sl = load_slab(l, "Ag", ncols=128)
        pbi = next_pb()
        for kc in range(8):
            mm(pbi[0:8, 0:T], sl[:, kc, 0:8], hT[:, kc, :], start=(kc == 0), stop=(kc == 7))
        act(ga[0:8, R_I, 0:T], pbi[0:8, 0:T], AF.Identity, bias=cv[l][0:8, CM.off["a_ib"][0]:CM.off["a_ib"][0] + 1])
        gcut = int(os.environ.get("GCUT", "99"))
        if gcut < 1:
            return
        pbf = next_pb()
        for kc in range(8):
            mm(pbf[0:8, 0:T], sl[:, kc, 8:16], hT[:, kc, :], start=(kc == 0), stop=(kc == 7))
        act(ga[0:8, R_SP, 0:T], pbf[0:8, 0:T], AF.Exp, bias=a_car[l][0:8, 2:3], scale=-1.0)
        act(ga[0:8, R_SP, 0:T], ga[0:8, R_SP, 0:T], AF.Ln, bias=ones[0:8, 0:1])
        if gcut < 2:
            return
        scan(ga[0:8, R_F, 0:T], ones[0:8, 0:T], ga[0:8, R_SP, 0:T], a_car[l][0:8, 0:1], ALU.mult, ALU.subtract)
        cp(a_car[l][0:8, 0:1], ga[0:8, R_F, T - 1:T])
        tt(ga[0:8, R_G, 0:T], ga[0:8, R_I, 0:T], ga[0:8, R_F, 0:T], ALU.subtract)
        if gcut < 3:
            return
        cp(ga[0:8, R_P, 0:1], a_car[l][0:8, 1:2])
        scan(ga[0:8, R_P, 1:T + 1], ones[0:8, 0:T], ga[0:8, R_G, 0:T], a_car[l][0:8, 1:2], ALU.mult, ALU.max)
        cp(a_car[l][0:8, 1:2], ga[0:8, R_P, T:T + 1])
        if gcut < 4:
            return
        for c in range(NCH):
            cols = slice(c * 64, (c + 1) * 64)
            ts(ga[0:8, R_MU, cols], zeros[0:8, 0:64], ga[0:8, R_P, c * 64:c * 64 + 1], None, ALU.add)
            ts(ga[0:8, R_PE, cols], zeros[0:8, 0:64], ga[0:8, R_P, c * 64 + 64:c * 64 + 65], None, ALU.add)
            tt(scl[0:8, c:c + 1], ga[0:8, R_P, c * 64:c * 64 + 1], ga[0:8, R_P, c * 64 + 64:c * 64 + 65], ALU.subtract)
        if gcut < 5:
            return
        Pv = ga[0:8, R_P, 1:T + 1]
        tt(ga[0:8, R_SI, 0:T], ga[0:8, R_MU, 0:T], Pv, ALU.subtract)
        tt(ga[0:8, R_WJ, 0:T], ga[0:8, R_G, 0:T], ga[0:8, R_PE, 0:T], ALU.subtract)
        stt(ga[0:8, R_NM, 0:T], ga[0:8, R_F, 0:T], -1.0, Pv, ALU.mult, ALU.subtract)
        ts(ga[0:8, R_NP, 0:T], Pv, -1.0, None, ALU.mult)
        if acut < 2:
            return
        for tb in range(NTB):
            pb = next_pb()
            for r, R in enumerate((R_SI, R_WJ, R_NM)):
                tr(pb[:, r * 8:(r + 1) * 8], ga[0:8, R, tb * 128:(tb + 1) * 128], ident[0:8, 0:8])
            act(tmsc[:, tb, :, :], pb[:, 0:24].rr("p (a b) -> p a b", a=3), AF.Exp)
        if acut < 3:
            return
        tt(rhs_sc[0:8, :, :], scl[0:8, :].rr("p (c o) -> p c o", o=1).bc([8, NCH, 4]),
           a_sel[0:8, 512:516].rr("p (o g) -> p o g", o=1).bc([8, NCH, 4]), ALU.mult)
        pb = next_pb()
        mm(pb[:, 0:NCH * 4], a_sel[0:8, 516:644], rhs_sc[0:8, :, :].rr("p c g -> p (c g)"))
        act(scb[:, :, :, :].rr("p c g o -> p (c g o)"), pb[:, 0:NCH * 4], AF.Exp)
        if acut < 4:
            return
        for tb in range(NTB):
            pbt = next_pb()
            for g in range(4):
                mm(pbt[:, g * 128:(g + 1) * 128], kT[:, g, tb * 128:(tb + 1) * 128], ident16[:, :])
            cp(ktm[:, tb, :, :], pbt[:, :].rr("p (h e) -> p h e", h=8))
            cp(wj16[:, tb, :], tmsc[:, tb, 1, :])
        sl = load_slab(l, "Av")
        for tb in range(NTB):
            pb = tm_proj(sl, tb)
            act(vv[:, tb, :, :], pb[:, :].rr("p (h e) -> p h e", h=8), AF.Copy)
        sl = load_slab(l, "Ao")
        for tb in range(NTB):
            pb = tm_proj(sl, tb)
            act(so[:, tb, :, :], pb[:, :].rr("p (h e) -> p h e", h=8), AF.Sigmoid)
        for tb in range(NTB):
            E = next_pb()
            mm(E[:, :], ident[:, :], a_mask[:, :], start=True, stop=False)
            mm(E[:, :], ga[0:8, R_G, tb * 128:(tb + 1) * 128], a_sel[0:8, 0:512], start=False, stop=False)
            for p in range(2):
                c = tb * 2 + p
                tt(negPx[0:8, p, :, :], ga[0:8, R_NP:R_NP + 1, c * 64:(c + 1) * 64].bc([8, 8, 64]),
                   a_sel[0:8, 0:512].rr("p (h e) -> p h e", h=8), ALU.mult)
                mm(E[64 * p:64 * p + 64, :], ones[0:8, 0:64], negPx[0:8, p, :, :].rr("p h e -> p (h e)"),
                   start=False, stop=True)
            act(ET[:, :, :].rr("p h e -> p (h e)"), E[:, :], AF.Exp)
            if acut < 5:
                continue
            for p in range(2):
                c = tb * 2 + p
                rows = slice(64 * p, 64 * p + 64)
                cols = slice(c * 64, (c + 1) * 64)
                sI4 = tmsc[:, tb, 0, :].rr("p (g j o) -> p g j o", g=4, j=2)
                for j in range(2):
                    jr = slice(64 * j, 64 * j + 64)
                    for g in range(4):
                        mm(S[j][rows, g * 64:(g + 1) * 64], kT[jr, g, cols], qT[jr, g, cols])
                        mm(JX[j][rows, g * 64:(g + 1) * 64], qT[jr, g, cols], a_C16[jr, g, :])
                        mm(JX[j][rows, 256 + g:257 + g], qT[jr, g, cols], a_n16[jr, g, :])
                for j in range(2):
                    stt(AT[rows, 4 * j:4 * j + 4, :], S[j][rows, 0:256].rr("p (a b) -> p a b", a=4), 0.125,
                        ET[rows, 4 * j:4 * j + 4, :], ALU.mult, ALU.mult)
                    tt(tmp4[rows, :, j, :], JX[j][rows, 0:256].rr("p (a b) -> p a b", a=4),
                       sI4[rows, :, j, :].bc([64, 4, 64]), ALU.mult)
                    tt(lnst[rows, 40:48].rr("p (g j) -> p g j", j=2)[:, :, j], JX[j][rows, 256:260],
                       tmsc[rows, tb, 0, :].rr("p (g j) -> p g j", j=2)[:, :, j], ALU.mult)
                for h in range(8):
                    g, j = h // 2, h % 2
                    mm(O[p][rows, h * 64:(h + 1) * 64], AT[rows, j * 4 + g, :], vv[rows, tb, h, :])
                    mm(ST[p][rows, 384 + h:385 + h], AT[rows, j * 4 + g, :], ones16[rows, 0:1])
                tt(hnum[rows, :, :], O[p][rows, :].rr("p (h e) -> p h e", h=8),
                   tmp4[rows, :, :, :].rr("p g j e -> p (g j) e"), ALU.add)
                tt(lnst[rows, 48:56], ST[p][rows, 384:392], lnst[rows, 40:48], ALU.add)
                stt(lnst[rows, 48:56], lnst[rows, 48:56], -1.0, lnst[rows, 48:56], ALU.mult, ALU.max)
                tt(lnst[rows, 48:56], lnst[rows, 48:56], tmsc[rows, tb, 2, :], ALU.max)
                recip(lnst[rows, 56:64], lnst[rows, 48:56])
                tt(hnum[rows, :, :], hnum[rows, :, :],
                   lnst[rows, 56:64].rr("p (h o) -> p h o", o=1).bc([64, 8, 64]), ALU.mult)
                tt(osb[rows, tb, :, :], hnum[rows, :, :], so[rows, tb, :, :], ALU.mult)
                tt(vw[rows, :, :], vv[rows, tb, :, :],
                   tmsc[rows, tb, 1, :].rr("p (h o) -> p h o", o=1).bc([64, 8, 64]), ALU.mult)
                for h in range(8):
                    g, j = h // 2, h % 2
                    jr = slice(64 * j, 64 * j + 64)
                    mm(ST[p][jr, g * 64:(g + 1) * 64], ktm[rows, tb, h, :], vw[rows, h, :])
                    mm(ST[p][jr, 256 + g:257 + g], ktm[rows, tb, h, :], wj16[rows, tb, h:h + 1])
                tt(a_C[l][:, :, :], a_C[l][:, :, :], scb[:, c, :, :].bc([128, 4, 64]), ALU.mult)
                stt(a_C[l][:, :, :], ST[p][:, 0:256].rr("p (a b) -> p a b", a=4), 0.125, a_C[l][:, :, :],
                    ALU.mult, ALU.add)
                tt(a_n[l][:, :, :], a_n[l][:, :, :], scb[:, c, :, :], ALU.mult)
                stt(a_n[l][:, :, :], ST[p][:, 256:260].rr("p (a b) -> p a b", b=1), 0.125, a_n[l][:, :, :],
                    ALU.mult, ALU.add)
                cp(a_C16[:, :, :], a_C[l][:, :, :])
                cp(a_n16[:, :, :], a_n[l][:, :, :])
        for tb in range(NTB):
            head_ln(xn, osb, tb)
            pb = next_pb()
            for g in range(4):
                tr(pb[:, g * 128:(g + 1) * 128], xn[:, tb, 2 * g:2 * g + 2, 0:64], ident[:, :])
            for g in range(4):
                stt(yT[:, g, tb * 128:(tb + 1) * 128], pb[:, g * 128:(g + 1) * 128], cvc(l, "a_nw", g),
                    sz[:, g, tb * 128:(tb + 1) * 128], ALU.mult, ALU.mult)

    def mixer_B(l, ti):
        rS, kS, vS, sz = FS[0], FS[1], FS[2], FS[3]
        RhT, AlT, bon = rS, kS, vS
        pools = (FS[4], FS[5], FS[6])

        def slot(i):
            return pools[i // 4][:, i % 4, :]
        raw, lora, aT, lw, lc, kt, kh, beta, eg, egi, egm, tA = [slot(i) for i in range(12)]
        wkv, xn = TM[3], TM[4]
        Vtm, Btm, Ktm = TM16[0], TM16[1], TM16[2]
        BJ = [PB[3], PB[4]]; PA = [PB[2], PB[7]]; PQ = [PB[5], PB[6]]; PX = [PB[0], PB[1]]
        Pn, Qn, Xm = bm16
        W2 = AT
        beta16, kt16 = bk16[:, 0, :], bk16[:, 1, :]
        omu = CM.off["b_mu"][0]

        def shift_mix(dst, sl, gi, hidx, mucol, raw_, tA_):
            pb = fm_proj(sl, gi)
            cp(raw_[:, 0:1], b_hist[l][:, hidx:hidx + 1])
            act(raw_[:, 1:T + 1], pb[:, 0:T], AF.Copy)
            yield
            cp(b_hist[l][:, hidx:hidx + 1], raw_[:, T:T + 1])
            tt(tA_[:, 0:T], raw_[:, 0:T], raw_[:, 1:T + 1], ALU.subtract)
            yield
            stt(dst, tA_[:, 0:T], cv[l][:, omu + mucol:omu + mucol + 1], raw_[:, 1:T + 1], ALU.mult, ALU.add)
            yield

        cp(b_S16[:, :, :], b_S[l][:, :, :])
        sl = load_slab(l, "Bl", ncols=128)
        interleave([shift_mix(lora[:, 0:T], sl, 0, 12, 12, raw, tA)])
        act(lora[0:64, 0:T], lora[0:64, 0:T], AF.Tanh)
        tslots = [(slot(2 + 2 * i), slot(3 + 2 * i)) for i in range(4)]
        for wi, (name, dst) in enumerate((("Br", rS), ("Bk", kS), ("Bv", vS))):
            sl = load_slab(l, name)
            interleave([shift_mix(dst[:, g, 0:T], sl, g, wi * 4 + g, wi * 4 + g, tslots[g][0], tslots[g][1])
                        for g in range(4)])
        sl = load_slab(l, "Bz")
        for g in range(4):
            pb = fm_proj(sl, g)
            act(sz[:, g, 0:T], pb[:, 0:T], AF.Silu)
        bcut = int(os.environ.get("BCUT", "99"))
        if bcut < 1:
            return
        for g in range(4):
            gc = slice(g * 128, (g + 1) * 128)

            def chainW(g=g, gc=gc):
                pb = next_pb()
                mm(pb[:, 0:T], lora_up[l][0:64, gc], lora[0:64, 0:T])
                act(lw[:, 0:T], pb[:, 0:T], AF.Sigmoid, bias=cvc(l, "b_w0", g))
                yield
                for c in range(NCH):
                    cols = slice(c * 64, (c + 1) * 64)
                    scan(lc[:, cols], ones[:, 0:64], lw[:, cols], 0.0, ALU.mult, ALU.add)
                    yield
                act(eg[:, 0:T], lc[:, 0:T], AF.Exp, scale=-0.606531)
                act(egi[:, 0:T], lc[:, 0:T], AF.Exp, scale=0.606531)
                tt(tA[:, 0:T], lc[:, 0:T], lw[:, 0:T], ALU.subtract)
                yield
                act(egm[:, 0:T], tA[:, 0:T], AF.Exp, scale=-0.606531)
                cp(gLt[:, g, :], eg[:, 63:T:64])
                yield

            def chainA(g=g, gc=gc):
                pb = next_pb()
                mm(pb[:, 0:T], lora_up[l][64:128, gc], lora[64:128, 0:T])
                act(aT[:, 0:T], pb[:, 0:T], AF.Sigmoid, bias=cvc(l, "b_a0", g))
                yield
                ts(beta[:, 0:T], aT[:, 0:T], -1.0, cvc(l, "b_ka", g), ALU.add, ALU.mult)
                yield
                stt(kt[:, 0:T], beta[:, 0:T], 1.0, kS[:, g, 0:T], ALU.add, ALU.mult)
                yield

            def chainK(g=g):
                ts(kh[:, 0:T], kS[:, g, 0:T], cvc(l, "b_kk", g), None, ALU.mult)
                yield
                tt(raw[:, 0:T], kh[:, 0:T], kh[:, 0:T], ALU.mult)
                yield
                pb = next_pb()
                mm(pb[:, 0:T], b_ones[:, :], raw[:, 0:T])
                act(raw[:, 0:T], pb[:, 0:T], AF.Sqrt, bias=eps12[:, 0:1])
                yield
                recip(raw[:, 0:T], raw[:, 0:T])
                yield
                tt(kh[:, 0:T], kh[:, 0:T], raw[:, 0:T], ALU.mult)
                yield

            def chainV(g=g):
                for tb in range(NTB):
                    pbt = next_pb()
                    tr(pbt[:, 0:128], vS[:, g, tb * 128:(tb + 1) * 128], ident[:, :])
                    cp(Vtm[:, tb, 2 * g:2 * g + 2, :], pbt[:, 0:128].rr("p (a b) -> p a b", a=2))
                    yield
            interleave([chainW(), chainA(), chainK(), chainV()])

            def chainR(g=g):
                stt(raw[:, 0:T], rS[:, g, 0:T], cvc(l, "b_rk", g), kt[:, 0:T], ALU.mult, ALU.mult)
                yield
                pbb = next_pb()
                mm(pbb[:, 0:T], b_ones[:, :], raw[:, 0:T])
                tt(bon[:, g, 0:T], pbb[:, 0:T], vS[:, g, 0:T], ALU.mult)
                yield

            def chainB(g=g):
                tt(beta[:, 0:T], aT[:, 0:T], kh[:, 0:T], ALU.mult)
                yield
                tt(beta16[:, 0:T], beta[:, 0:T], egi[:, 0:T], ALU.mult)
                yield

            def chainO(g=g):
                tt(AlT16[:, g, 0:T], kh[:, 0:T], egm[:, 0:T], ALU.mult)
                yield
                tt(RhT16[:, g, 0:T], rS[:, g, 0:T], eg[:, 0:T], ALU.mult)
                yield
            interleave([chainR(), chainB(), chainO()])
            tt(kt16[:, 0:T], kt[:, 0:T], egi[:, 0:T], ALU.mult)
            for tb in range(NTB):
                pbt = next_pb()
                mm(pbt[:, 0:128], beta16[:, tb * 128:(tb + 1) * 128], ident16[:, :])
                mm(pbt[:, 128:256], kt16[:, tb * 128:(tb + 1) * 128], ident16[:, :])
                cp(Btm[:, tb, 2 * g:2 * g + 2, :], pbt[:, 0:128].rr("p (a b) -> p a b", a=2))
                cp(Ktm[:, tb, 2 * g:2 * g + 2, :], pbt[:, 128:256].rr("p (a b) -> p a b", a=2))
            for tb in range(NTB):
                for j in range(2):
                    jr = slice(64 * j, 64 * j + 64)
                    for p in range(2):
                        c = tb * 2 + p
                        rows = slice(64 * p, 64 * p + 64)
                        cols = slice(c * 64, (c + 1) * 64)
                        A_, B_, K_, R_ = AlT16[jr, g, cols], beta16[jr, cols], kt16[jr, cols], RhT16[jr, g, cols]
                        mm(BJ[j][rows, 0:64], A_, B_)
                        mm(BJ[j][rows, 64:128], B_, A_)
                        mm(BJ[j][rows, 128:192], K_, A_)
                        mm(BJ[j][rows, 192:256], B_, R_)
                        mm(BJ[j][rows, 256:320], K_, R_)
                    tt(PRall[tb][:, :, 2 * g + j, :], BJ[j][:, 0:320].rr("p (k t) -> p k t", k=5), b_m5[:, :, :],
                       ALU.mult)
        if bcut < 2:
            return
        for tb in range(NTB):
            PRt = PRall[tb]
            P0, Q0, MkT, NbT, NkT = (PRt[:, k, :, :] for k in range(5))
            Wsb, Usb = PRt[:, 2, :, :], PRt[:, 4, :, :]
            for p in range(2):
                rows = slice(64 * p, 64 * p + 64)
                c = tb * 2 + p
                for h in range(8):
                    g, j = h // 2, h % 2
                    mm(PA[p][rows, h * 64:(h + 1) * 64], MkT[rows, h, :], Vtm[rows, tb, h, :])
                    mm(PQ[p][rows, h * 64:(h + 1) * 64], NkT[rows, h, :], Vtm[rows, tb, h, :])
                    mm(PX[p][64 * j:64 * j + 64, g * 64:(g + 1) * 64], Ktm[rows, tb, h, :], Vtm[rows, tb, h, :])
                act(W2[rows, :, :], PA[p][rows, :].rr("p (h e) -> p h e", h=8), AF.Copy)
                act(Y2[rows, :, :], PQ[p][rows, :].rr("p (h e) -> p h e", h=8), AF.Copy)
                tt(K2g[:, p, :, :], PX[p][:, 0:256].rr("p (a b) -> p a b", a=4),
                   gLt[:, :, c:c + 1].bc([128, 4, 64]), ALU.mult)
            tt(Xm[:, :, :], b_i2[:, :, :].bc([128, 8, 64]), Q0, ALU.subtract)
            Pc, Qc = P0, Q0
            nxt = [(Pn, Qn), (P0, Q0)]
            for i in range(1, 6):
                Pd, Qd = nxt[(i - 1) % 2]
                for p in range(2):
                    rows = slice(64 * p, 64 * p + 64)
                    for h in range(8):
                        hc = slice(h * 64, (h + 1) * 64)
                        mm(PA[p][rows, hc], Qc[rows, h, :], Pc[rows, h, :])
                        if i < 5:
                            mm(PQ[p][rows, hc], Pc[rows, h, :], Qc[rows, h, :])
                for p in range(2):
                    rows = slice(64 * p, 64 * p + 64)
                    act(Pd[rows, :, :], PA[p][rows, :].rr("p (h e) -> p h e", h=8), AF.Copy)
                    if i < 5:
                        cp(Qd[rows, :, :], PQ[p][rows, :].rr("p (h e) -> p h e", h=8))
                Pc, Qc = Pd, Qd
                for p in range(2):
                    rows = slice(64 * p, 64 * p + 64)
                    for h in range(8):
                        mm(PX[p][rows, h * 64:(h + 1) * 64], Pc[rows, h, :], Xm[rows, h, :])
                for p in range(2):
                    rows = slice(64 * p, 64 * p + 64)
                    tt(Xm[rows, :, :], Xm[rows, :, :], PX[p][rows, :].rr("p (h e) -> p h e", h=8), ALU.add)
            if bcut < 3:
                continue
            for p in range(2):
                c = tb * 2 + p
                rows = slice(64 * p, 64 * p + 64)
                cols = slice(c * 64, (c + 1) * 64)
                for j in range(2):
                    jr = slice(64 * j, 64 * j + 64)
                    for g in range(4):
                        mm(BJ[j][rows, g * 64:(g + 1) * 64], AlT16[jr, g, cols], b_S16[jr, g, :])
                        mm(BJ[j][rows, 256 + g * 64:256 + (g + 1) * 64], RhT16[jr, g, cols], b_S16[jr, g, :])
                W24 = W2[rows, :, :].rr("p (g j) e -> p g j e", j=2)
                Ws4 = Wsb[rows, :, :].rr("p (g j) e -> p g j e", j=2)
                Y24 = Y2[rows, :, :].rr("p (g j) e -> p g j e", j=2)
                for j in range(2):
                    tt(Ws4[:, :, j, :], BJ[j][rows, 0:256].rr("p (a b) -> p a b", a=4), W24[:, :, j, :], ALU.add)
                    tt(tmp4[rows, :, j, :], BJ[j][rows, 256:512].rr("p (a b) -> p a b", a=4), Y24[:, :, j, :],
                       ALU.add)
                for h in range(8):
                    mm(PX[p][rows, h * 64:(h + 1) * 64], Xm[rows, h, :], Wsb[rows, h, :])
                act(Usb[rows, :, :], PX[p][rows, :].rr("p (h e) -> p h e", h=8), AF.Copy)
                for h in range(8):
                    g, j = h // 2, h % 2
                    mm(PA[p][rows, h * 64:(h + 1) * 64], NbT[rows, h, :], Usb[rows, h, :])
                    mm(PQ[p][64 * j:64 * j + 64, g * 64:(g + 1) * 64], Btm[rows, tb, h, :], Usb[rows, h, :])
                tt(wkv[rows, tb, :, :], tmp4[rows, :, :, :].rr("p g j e -> p (g j) e"),
                   PA[p][rows, :].rr("p (h e) -> p h e", h=8), ALU.subtract)
                tt(b_S[l][:, :, :], b_S[l][:, :, :], PQ[p][:, 0:256].rr("p (a b) -> p a b", a=4), ALU.subtract)
                tt(b_S[l][:, :, :], b_S[l][:, :, :], gLt[:, :, c:c + 1].bc([128, 4, 64]), ALU.mult)
                tt(b_S[l][:, :, :], b_S[l][:, :, :], K2g[:, p, :, :], ALU.add)
                cp(b_S16[:, :, :], b_S[l][:, :, :])
        ogw, ogb = CM.off["b_gw"][0], CM.off["b_gb"][0]
        for tb in range(NTB):
            head_ln(xn, wkv, tb)
            pb = next_pb()
            for g in range(4):
                tr(pb[:, g * 128:(g + 1) * 128], xn[:, tb, 2 * g:2 * g + 2, 0:64], ident[:, :])
            for g in range(4):
                tc_ = slice(tb * 128, (tb + 1) * 128)
                ts(scr[:, 0:128], pb[:, g * 128:(g + 1) * 128], cv[l][:, ogw + g:ogw + g + 1],
                   cv[l][:, ogb + g:ogb + g + 1], ALU.mult, ALU.add)
                tt(scr[:, 0:128], scr[:, 0:128], bon[:, g, tc_], ALU.add)
                tt(yT[:, 4 + g, tc_], scr[:, 0:128], sz[:, g, tc_], ALU.mult)

    def out_proj_residual(l, X):
        dma(gpost[l][:, :], gpost_d[l])
        for si in range(4):
            s = load_wo(l, si)
            for tb in range(NTB):
                for half in range(2):
                    pb = PB[4 + tb * 2 + half]
                    for q in range(4):
                        fc = si * 4 + q
                        mm(pb[:, :], yT[:, fc, tb * 128:(tb + 1) * 128], s[:, q * 2 + half, :],
                           start=(fc == 0), stop=(fc == 15))
        if cut < 4:
            return
        for tb in range(NTB):
            for half in range(2):
                act(scr[:, half * 512:(half + 1) * 512], PB[4 + tb * 2 + half][:, :], AF.Copy)
            if cut < 5:
                continue
            act(scr2[:, :], scr[:, :], AF.Square)
            reduce(sm[:, 24:25], scr2[:, :], ALU.add)
            ts(sm[:, 25:26], sm[:, 24:25], 1.0 / D_MODEL, 1e-6, ALU.mult, ALU.add)
            act(sm[:, 26:27], sm[:, 25:26], AF.Sqrt)
            recip(sm[:, 27:28], sm[:, 26:27])
            if cut < 6:
                continue
            stt(scr2[:, :], scr[:, :], sm[:, 27:28], gpost[l][:, :], ALU.mult, ALU.mult)
            if cut < 7:
                continue
            tt(X[:, tb, :], X[:, tb, :], scr2[:, :], ALU.add)

    xv = x_d.rearrange("(n tb p) d -> n p tb d", tb=NTB, p=128)
    yv = y_d.rearrange("(n tb p) d -> n p tb d", tb=NTB, p=128)
    for ti in range(ntiles):
        X = xt[0]
        dma(X[:, :, :], xv[ti])
        for l in range(nlayers):
            if cut >= 1:
                rmsnorm_pre(l, X)
            memset(yT[:, :, :], 0.0)
            if "C" in mixers and cut >= 2:
                mixer_C(l)
            if "D" in mixers:
                mixer_D(l, ti)
            if "A" in mixers:
                mixer_A(l, ti)
            if "B" in mixers:
                mixer_B(l, ti)
            if cut >= 3:
                out_proj_residual(l, X)
        dma(yv[ti], X[:, :, :])

    P.emit()
    return nc, stack


def kernel(**inputs):
    inputs = {k: np.asarray(v) for k, v in inputs.items()}
    return run(inputs)


def make_in_map(xc, hp, tb):
    m = dict(x=xc, wt=hp["wt"], wo=hp["wo"], cv=hp["cv"], lora=hp["lora"], lru=hp["lru"], gpost=hp["gpost"])
    m.update(tb)
    return m


def run(inputs, ntiles=SEQ // T, nlayers=DEPTH, mixers="ABCD"):
    hp = host_prepare(inputs)
    tb = host_tables()
    nc, stack = build_program(ntiles, nlayers, mixers)
    x = np.ascontiguousarray(inputs["x"], dtype=np.float32)
    in_maps = [make_in_map(x[c], hp, tb) for c in range(NCORES)]
    with stack:
        res = run_bass_kernel_spmd(nc, in_maps, core_ids=list(range(NCORES)))
    out = np.stack([np.asarray(r["y"]) for r in res.results], axis=0)
    return out.astype(np.float32)
```

```python
import contextlib
import numpy as np
import concourse.bass as bass
import concourse.mybir as mybir
from concourse.bass_utils import run_bass_kernel_spmd

F32 = mybir.dt.float32
BF16 = mybir.dt.bfloat16
AF = mybir.ActivationFunctionType
ALU = mybir.AluOpType
AX = mybir.AxisListType

D_MODEL = 1024
SEQ = 4096
BATCH = 4
DEPTH = 2
G = 512
D_IN = 7824
T = 256
NTB = T // 128
NCH = T // 64
NCORES = 4
NSLABBUF = 3
import os as _os
INORDER = tuple(_os.environ.get("INORDER", "pe").split(","))


class Buf:
    def __init__(self, name, tile, psum=False):
        self.name = name
        self.tile = tile
        self.psum = psum
        self.acc = []
        self.dma_sem = None
        self.dma_ops = []

    def __getitem__(self, idx):
        return Ref(self, self.tile[idx])


def _box(ap):
    pat = ap.ap
    pstep = pat[0][0]
    off = int(ap.offset)
    p0 = off // pstep if pstep else 0
    f0 = off % pstep if pstep else off
    ext = 0
    for st, cnt in pat[1:]:
        ext += abs(st) * (cnt - 1)
    return (p0, p0 + pat[0][1], f0, f0 + ext + 1)


class Ref:
    def __init__(self, buf, ap, box=None):
        self.buf = buf
        self.ap = ap
        if box is None:
            box = _box(ap)
            if buf.psum:
                box = ((box[0] // 32) * 32, ((box[1] + 31) // 32) * 32, 0, 512)
        self.box = box

    def bc(self, shape):
        return Ref(self.buf, self.ap.to_broadcast(list(shape)), self.box)

    def __getitem__(self, idx):
        return Ref(self.buf, self.ap[idx])

    def rr(self, pat, **kw):
        return Ref(self.buf, self.ap.rearrange(pat, **kw), self.box)


def _ovl(a, b):
    return a[0] < b[1] and b[0] < a[1] and a[2] < b[3] and b[2] < a[3]


def _cov(a, b):
    return a[0] <= b[0] and a[1] >= b[1] and a[2] <= b[2] and a[3] >= b[3]


class Prog:
    ENG = ("pe", "act", "dve", "pool", "sp")

    def __init__(self, nc, stack):
        self.nc = nc
        self.stack = stack
        self.ops = []
        self.nbuf = 0

    def sb(self, name, shape, dtype=F32):
        t = self.stack.enter_context(self.nc.sbuf_tensor("s_" + name, list(shape), dtype))
        return Buf(name, t)

    def ps(self, name):
        t = self.stack.enter_context(self.nc.psum_tensor("p_" + name, [128, 512], F32))
        return Buf(name, t, psum=True)

    def op(self, eng, fn, outs, ins, dma=False):
        oid = len(self.ops)
        deps = set()
        for r, w in [(x, True) for x in outs] + [(x, False) for x in ins]:
            if r is None or not isinstance(r, Ref):
                continue
            b = r.buf
            for (bx, o2, w2) in b.acc:
                if (w or w2) and _ovl(bx, r.box):
                    deps.add(o2)
        for r, w in [(x, True) for x in outs] + [(x, False) for x in ins]:
            if r is None or not isinstance(r, Ref):
                continue
            b = r.buf
            if w:
                b.acc = [a for a in b.acc if not _cov(r.box, a[0])]
            b.acc.append((r.box, oid, w))
            if len(b.acc) > 48:
                merged = {}
                for (bx, o2, w2) in b.acc:
                    k = (self.ops[o2]["eng"] if o2 < oid else eng, w2)
                    if k in merged:
                        m = merged[k]
                        merged[k] = ((min(m[0][0], bx[0]), max(m[0][1], bx[1]), min(m[0][2], bx[2]),
                                      max(m[0][3], bx[3])), max(m[1], o2), w2)
                    else:
                        merged[k] = (bx, o2, w2)
                b.acc = list(merged.values())
        dbuf = None
        if dma:
            for r in list(outs) + list(ins):
                if isinstance(r, Ref):
                    dbuf = r.buf
            dbuf.dma_ops.append(oid)
        deps.discard(oid)
        self.ops.append(dict(eng=eng, fn=fn, deps=sorted(deps), dma=dma, dbuf=dbuf, sig=False))
        return oid

    def emit(self):
        nc = self.nc
        ops = self.ops
        for o in ops:
            for d in o["deps"]:
                od = ops[d]
                if od["dma"]:
                    continue
                if od["eng"] == o["eng"] and o["eng"] in INORDER and not o["dma"]:
                    continue
                od["sig"] = True
        sems = {e: self.stack.enter_context(nc.semaphore("sem_" + e)) for e in self.ENG}
        cnt = {e: 0 for e in self.ENG}
        for o in ops:
            if o["dma"]:
                b = o["dbuf"]
                if b.dma_sem is None:
                    b.dma_sem = self.stack.enter_context(nc.semaphore("dsem_" + b.name))
            elif o["sig"]:
                cnt[o["eng"]] += 1
                o["signo"] = cnt[o["eng"]]
        per_eng = {e: [] for e in self.ENG}
        for i, o in enumerate(ops):
            per_eng[o["eng"]].append(i)
        import bisect

        def gen(engname, e):
            waited = {}
            for i in per_eng[engname]:
                o = ops[i]
                need = {}
                for d in o["deps"]:
                    od = ops[d]
                    if od["dma"]:
                        b = od["dbuf"]
                        n = bisect.bisect_left(b.dma_ops, i)
                        key = ("d", id(b))
                        if need.get(key, (None, 0))[1] < 16 * n:
                            need[key] = (b.dma_sem, 16 * n)
                    else:
                        if od["eng"] == engname and engname in INORDER and not o["dma"]:
                            continue
                        key = ("e", od["eng"])
                        if need.get(key, (None, 0))[1] < od["signo"]:
                            need[key] = (sems[od["eng"]], od["signo"])
                for key, (sem, val) in need.items():
                    if waited.get(key, 0) >= val:
                        continue
                    e.wait_ge(sem, val)
                    waited[key] = val
                ins = o["fn"](e)
                if o["dma"]:
                    ins.then_inc(o["dbuf"].dma_sem, 16)
                elif o["sig"]:
                    ins.then_inc(sems[engname], 1)

        with nc.Block() as block:
            @block.tensor
            def _(e):
                gen("pe", e)

            @block.scalar
            def _(e):
                gen("act", e)

            @block.vector
            def _(e):
                gen("dve", e)

            @block.gpsimd
            def _(e):
                gen("pool", e)

            @block.sync
            def _(e):
                gen("sp", e)
                seen = set()
                for o in ops:
                    if o["dma"] and id(o["dbuf"]) not in seen:
                        seen.add(id(o["dbuf"]))
                        e.wait_ge(o["dbuf"].dma_sem, 16 * len(o["dbuf"].dma_ops))


def _a(x):
    return x.ap if isinstance(x, Ref) else x


def _fm4(v):
    return np.ascontiguousarray(v.reshape(-1, 128).T)


class CMap:
    def __init__(self):
        self.off = {}
        self.n = 0

    def add(self, name, w):
        self.off[name] = (self.n, w)
        self.n += w


def build_cmap():
    c = CMap()
    c.add("gpre", 8)
    c.add("a_cw", 32); c.add("a_cb", 8); c.add("a_ib", 1); c.add("a_fb", 1); c.add("a_nw", 4)
    c.add("b_mu", 13); c.add("b_w0", 4); c.add("b_a0", 4); c.add("b_kk", 4); c.add("b_ka", 4)
    c.add("b_rk", 4); c.add("b_gw", 4); c.add("b_gb", 4)
    c.add("c_cw", 16); c.add("c_cb", 4); c.add("c_br", 4); c.add("c_bi", 4); c.add("c_lam", 4)
    c.add("d_nw", 4)
    return c


CM = build_cmap()

def _cols(a, b):
    return list(range(a, b))


def build_slabs():
    slabs = []
    def fm(name, start):
        slabs.append((name, _cols(start, start + 512)))
    fm("Aq", 0); fm("Ak", 512); fm("Az", 2048)
    slabs.append(("Ag", _cols(2560, 2576) + [-1] * (512 - 16)))
    fm("Br", 2576); fm("Bk", 3088); fm("Bv", 3600)
    slabs.append(("Bl", _cols(4112, 4240) + [-1] * (512 - 128)))
    fm("Bz", 4240); fm("Cx", 4752); fm("Cz", 5264); fm("Dz", 7312)
    fm("Av", 1024); fm("Ao", 1536); fm("Dq", 5776); fm("Dk", 6288); fm("Dv", 6800)
    return slabs


SLABS = build_slabs()
SLAB_ID = {s[0]: i for i, s in enumerate(SLABS)}
NSLAB = len(SLABS)


def host_prepare(inp):
    f = np.float32
    w_in = inp["w_in"]
    w_in_p = np.concatenate([w_in, np.zeros((DEPTH, D_MODEL, 1), f)], axis=2)
    wt = np.empty((DEPTH, NSLAB, 128, 8, 512), f)
    for si, (name, cols) in enumerate(SLABS):
        blk = w_in_p[:, :, cols]
        wt[:, si] = blk.reshape(DEPTH, 8, 128, 512).transpose(0, 2, 1, 3)
    w_out = inp["w_out"]
    wo = np.ascontiguousarray(w_out.reshape(DEPTH, 4, 4, 128, 1024).transpose(0, 1, 3, 2, 4))
    cv = np.zeros((DEPTH, 128, CM.n), f)
    def put(name, arr):
        o, w = CM.off[name]
        cv[:, :, o:o + w] = arr
    put("gpre", inp["norm_pre"].reshape(DEPTH, 8, 128).transpose(0, 2, 1))
    cw = inp["mlstm_conv_w"]
    put("a_cw", cw.reshape(DEPTH, 4, 8, 128).transpose(0, 3, 1, 2).reshape(DEPTH, 128, 32))
    put("a_cb", inp["mlstm_conv_b"].reshape(DEPTH, 8, 128).transpose(0, 2, 1))
    ib = np.zeros((DEPTH, 128, 1), f); ib[:, 0:8, 0] = inp["mlstm_i_bias"]; put("a_ib", ib)
    fb = np.zeros((DEPTH, 128, 1), f); fb[:, 0:8, 0] = inp["mlstm_f_bias"]; put("a_fb", fb)
    def fm4(x):
        return x.reshape(DEPTH, -1, 128).transpose(0, 2, 1)
    put("a_nw", fm4(inp["mlstm_norm_w"]))
    mu = inp["rwkv_mu"]
    put("b_mu", fm4(mu))
    for k, nm in [("b_w0", "rwkv_w0"), ("b_a0", "rwkv_a0"), ("b_kk", "rwkv_k_k"), ("b_ka", "rwkv_k_a"),
                  ("b_rk", "rwkv_r_k"), ("b_gw", "rwkv_gn_w"), ("b_gb", "rwkv_gn_b"),
                  ("c_cb", "lru_conv_b"), ("c_br", "lru_b_r"), ("c_bi", "lru_b_i"), ("c_lam", "lru_lambda"),
                  ("d_nw", "ret_norm_w")]:
        put(k, fm4(inp[nm]))
    lw = inp["lru_conv_w"]
    put("c_cw", lw.reshape(DEPTH, 4, 4, 128).transpose(0, 3, 1, 2).reshape(DEPTH, 128, 16))
    lora = np.concatenate([inp["rwkv_w_up"], inp["rwkv_a_up"]], axis=1)
    lru = np.zeros((DEPTH, 128, 2, 4, 128), f)
    for which, nm in enumerate(["lru_w_r", "lru_w_i"]):
        w = inp[nm]
        for g in range(4):
            for j in range(2):
                lru[:, 64 * j:64 * j + 64, which, g, 64 * j:64 * j + 64] = w[:, 2 * g + j]
    gpost = np.ascontiguousarray(np.broadcast_to(inp["norm_post"][:, None, :], (DEPTH, 128, D_MODEL)))
    return dict(wt=wt, wo=wo, cv=cv, lora=np.ascontiguousarray(lora), lru=lru, gpost=gpost)


def host_tables():
    f = np.float32
    t = {}
    t["ident"] = np.eye(128, dtype=f)
    hd = 64
    log_g = np.log1p(-np.exp2(-5.0 - np.arange(8, dtype=np.float64)))
    idx = np.arange(64, dtype=np.float64)
    dmat = np.exp(log_g[:, None, None] * np.abs(idx[:, None] - idx[None, :])) * hd ** -0.5
    xi = np.exp(log_g[:, None] * (idx + 1.0))
    zeta = np.exp(log_g[:, None] * (63.0 - idx)) * hd ** -0.5
    gch = np.exp(log_g * 64.0)
    hs2h = [2 * (s % 4) + (s // 4) for s in range(8)]
    dm = np.zeros((128, 8, 64));
    for s in range(8):
        dm[0:64, s] = dmat[hs2h[s]]; dm[64:128, s] = dmat[hs2h[s]]
    t["d_dmat"] = dm.astype(f)
    t["d_xi"] = np.tile(xi.T, (2, 1)).astype(f)
    t["d_zeta"] = np.tile(zeta.T, (2, 1)).astype(f)
    gc = np.zeros((128, 4))
    for g in range(4):
        for j in range(2):
            gc[64 * j:64 * j + 64, g] = gch[2 * g + j]
    t["d_gch"] = gc.astype(f)
    mk = np.zeros((128, 8, 64))
    sidx = np.arange(128) % 64
    mk[:] = np.where(sidx[:, None, None] <= np.arange(64)[None, None, :], 0.0, -30000.0)
    t["a_mask"] = mk.reshape(128, 512).astype(f)
    sel = np.zeros((8, 512 + 4 + 128))
    for hp_ in range(8):
        for hs in range(8):
            if hp_ == hs2h[hs]:
                sel[hp_, hs * 64:(hs + 1) * 64] = 1.0
        sel[hp_, 512 + hp_ // 2] = 1.0
        jj = hp_ % 2
        sel[hp_, 516 + 64 * jj:516 + 64 * jj + 64] = 1.0
    t["a_sel"] = sel.astype(f)
    bo = np.zeros((128, 128)); bo[0:64, 0:64] = 1.0; bo[64:128, 64:128] = 1.0
    t["b_ones"] = bo.astype(f)
    i2 = np.zeros((128, 64)); i2[0:64] = np.eye(64); i2[64:128] = np.eye(64)
    t["b_i2"] = i2.astype(f)
    rr_ = (np.arange(128) % 64)[:, None]; cc_ = np.arange(64)[None, :]
    m5 = np.zeros((128, 5, 64))
    m5[:, 0] = rr_ > cc_; m5[:, 1] = cc_ > rr_; m5[:, 2] = cc_ > rr_; m5[:, 3] = cc_ >= rr_; m5[:, 4] = cc_ >= rr_
    t["b_m5"] = m5.astype(f)
    half = 32
    pos = np.arange(SEQ, dtype=np.float32)
    inv_freq = (np.float32(10000.0) ** (-np.arange(half, dtype=np.float32) / np.float32(half))).astype(np.float32)
    ang = (pos[:, None] * inv_freq[None, :]).astype(np.float32).astype(np.float64)
    t["rope"] = np.concatenate([np.cos(ang), np.sin(ang)], axis=1).astype(f)
    return t


def build_program(ntiles=SEQ // T, nlayers=DEPTH, mixers="ABCD"):
    nc = bass.Bass("TRN2", target_bir_lowering=False)
    stack = contextlib.ExitStack()
    P = Prog(nc, stack)
    dram = {}

    def din(name, shape):
        dram[name] = nc.dram_tensor(name, list(shape), F32, kind="ExternalInput").ap()
        return dram[name]

    x_d = din("x", [SEQ, D_MODEL])
    wt_d = din("wt", [DEPTH, NSLAB, 128, 8, 512])
    wo_d = din("wo", [DEPTH, 4, 128, 4, 1024])
    cv_d = din("cv", [DEPTH, 128, CM.n])
    lora_d = din("lora", [DEPTH, 128, 512])
    lru_d = din("lru", [DEPTH, 128, 2, 4, 128])
    gpost_d = din("gpost", [DEPTH, 128, D_MODEL])
    ident_d = din("ident", [128, 128])
    dmat_d = din("d_dmat", [128, 8, 64])
    dxi_d = din("d_xi", [128, 8])
    dzeta_d = din("d_zeta", [128, 8])
    dgch_d = din("d_gch", [128, 4])
    rope_d = din("rope", [SEQ, 64])
    bones_d = din("b_ones", [128, 128])
    bi2_d = din("b_i2", [128, 64])
    bm5_d = din("b_m5", [128, 5, 64])
    amask_d = din("a_mask", [128, 512])
    asel_d = din("a_sel", [8, 644])
    y_d = nc.dram_tensor("y", [SEQ, D_MODEL], F32, kind="ExternalOutput").ap()

    ident = P.sb("ident", [128, 128])
    cv = [P.sb(f"cv{l}", [128, CM.n]) for l in range(DEPTH)]
    lru1 = P.sb("lru", [128, 2, 4, 128]); lru = [lru1, lru1]
    gpost1 = P.sb("gpost", [128, D_MODEL]); gpost = [gpost1, gpost1]
    lora_up = [P.sb(f"lora_up{l}", [128, 512]) for l in range(DEPTH)]
    b_ones = P.sb("b_ones", [128, 128]); b_i2 = P.sb("b_i2", [128, 1, 64]); b_m5 = P.sb("b_m5", [128, 5, 64])
    b_hist = [P.sb(f"b_hist{l}", [128, 13]) for l in range(DEPTH)]
    b_S = [P.sb(f"b_S{l}", [128, 4, 64]) for l in range(DEPTH)]
    ident16 = P.sb("ident16", [128, 128], BF16)
    AlT16 = P.sb("AlT16", [128, 4, T], BF16); RhT16 = P.sb("RhT16", [128, 4, T], BF16)
    bk16 = P.sb("bk16", [128, 2, T], BF16)
    bm16 = [P.sb(f"bm16_{i}", [128, 8, 64], BF16) for i in range(3)]
    b_S16 = P.sb("b_S16", [128, 4, 64], BF16)
    a_n16 = P.sb("a_n16", [128, 4, 1], BF16); wj16 = P.sb("wj16", [128, NTB, 8], BF16)
    ones16 = P.sb("ones16", [128, 64], BF16); eps12 = P.sb("eps12", [128, 1])
    gLt = P.sb("gLt", [128, 4, NCH]); PRall = [P.sb(f"PR{i}", [128, 5, 8, 64], BF16) for i in range(NTB)]; Y2 = P.sb("Y2", [128, 8, 64])
    K2g = P.sb("K2g", [128, 2, 4, 64])
    ones = P.sb("ones", [128, T]); zeros = P.sb("zeros", [128, T])
    xt = [P.sb(f"xt{i}", [128, NTB, D_MODEL]) for i in range(1)]
    hT = P.sb("hT", [128, 8, T], BF16)
    yT = P.sb("yT", [128, 16, T], BF16)
    slab = [P.sb(f"slab{i}", [128, 8, 512], BF16) for i in range(NSLABBUF)]
    sm = P.sb("small", [128, 64])
    scr = P.sb("scr", [128, D_MODEL])
    scr2 = P.sb("scr2", [128, D_MODEL])
    FS = [P.sb(f"fs{i}", [128, 4, T + 4]) for i in range(7)]
    c_hist = [P.sb(f"c_hist{l}", [128, 4, 3]) for l in range(DEPTH)]
    c_state = [P.sb(f"c_state{l}", [128, 4]) for l in range(DEPTH)]
    c_coef = [P.sb(f"c_coef{l}", [128, 8]) for l in range(DEPTH)]
    PB = [P.ps(f"pb{i}") for i in range(8)]
    TM = [P.sb(f"tm{i}", [128, NTB, 8, 64]) for i in range(6)]
    TM16 = [P.sb(f"tm16_{i}", [128, NTB, 8, 64], BF16) for i in range(3)]
    d_dmat = P.sb("d_dmat", [128, 8, 64]); d_xi = P.sb("d_xi", [128, 4, 2, 1]); d_zeta = P.sb("d_zeta", [128, 8, 1])
    d_gch = P.sb("d_gch", [128, 4, 1]); rope_sb = P.sb("rope_sb", [128, NTB, 1, 64])
    d_R = [P.sb(f"d_R{l}", [128, 4, 64]) for l in range(DEPTH)]
    a_mask = P.sb("a_mask", [128, 512]); a_sel = P.sb("a_sel", [8, 644])
    ga = P.sb("ga", [8, 12, T + 1]); scl = P.sb("scl", [8, NCH]); rhs_sc = P.sb("rhs_sc", [8, NCH, 4])
    scb = P.sb("scb", [128, NCH, 4, 1]); negPx = P.sb("negPx", [8, 2, 8, 64]); ET = P.sb("ET", [128, 8, 64])
    ET1 = P.sb("ET1", [128, 8, 64])
    tmsc = P.sb("tmsc", [128, NTB, 3, 8]); hnum = P.sb("hnum", [128, 8, 64]); vw = P.sb("vw", [128, 8, 64])
    a_hist = [P.sb(f"a_hist{l}", [128, 8, 3]) for l in range(DEPTH)]
    a_C = [P.sb(f"a_C{l}", [128, 4, 64]) for l in range(DEPTH)]
    a_n = [P.sb(f"a_n{l}", [128, 4, 1]) for l in range(DEPTH)]
    a_car = [P.sb(f"a_car{l}", [8, 4]) for l in range(DEPTH)]
    AT = P.sb("AT", [128, 8, 64]); tmp4 = P.sb("tmp4", [128, 4, 2, 64]); lnst = P.sb("lnst", [128, 64])
    state = dict(slab_i=0, pb_i=0)

    import os
    cut = int(os.environ.get("KCUT", "9"))
    def dma(out, in_, eng="sp"):
        return P.op(eng, lambda e: e.dma_start(out=_a(out), in_=_a(in_)), [out], [in_], dma=True)

    def mm(out, lhsT, rhs, start=True, stop=True):
        P.op("pe", lambda e: e.matmul(_a(out), _a(lhsT), _a(rhs), start=start, stop=stop),
             [out], [lhsT, rhs] + ([] if start else [out]))

    def tr(out, in_, idn):
        P.op("pe", lambda e: e.transpose(_a(out), _a(in_), _a(idn)), [out], [in_, idn])

    def act(out, in_, func, bias=None, scale=1.0, accum=None, eng="act"):
        kw = {}
        if bias is not None:
            kw["bias"] = _a(bias)
        if accum is not None:
            kw["accum_out"] = _a(accum)
        P.op("act", lambda e: e.activation(_a(out), _a(in_), func, scale=_a(scale), **kw),
             [out, accum], [in_, bias, scale])

    def tt(out, a, b, op, eng="dve"):
        P.op(eng, lambda e: e.tensor_tensor(_a(out), _a(a), _a(b), op), [out], [a, b])

    def ts(out, a, s1, s2, op0, op1=ALU.bypass, eng="dve"):
        P.op(eng, lambda e: e.tensor_scalar(_a(out), _a(a), _a(s1), _a(s2), op0, op1), [out], [a, s1, s2])

    def stt(out, a, s, b, op0, op1):
        P.op("dve", lambda e: e.scalar_tensor_tensor(_a(out), _a(a), _a(s), _a(b), op0, op1), [out], [a, s, b])

    def scan(out, d0, d1, init, op0, op1):
        P.op("dve", lambda e: e.tensor_tensor_scan(_a(out), _a(d0), _a(d1), _a(init), op0, op1),
             [out], [d0, d1, init])

    def cp(out, in_, eng="dve"):
        if isinstance(in_, Ref) and in_.buf.psum and eng == "dve":
            return act(out, in_, AF.Copy)
        P.op(eng, lambda e: e.tensor_copy(_a(out), _a(in_)), [out], [in_])

    def recip(out, in_):
        P.op("dve", lambda e: e.reciprocal(_a(out), _a(in_)), [out], [in_])

    def memset(out, val, eng="dve"):
        P.op(eng, lambda e: e.memset(_a(out), val), [out], [])

    def reduce(out, in_, op, axis=AX.X):
        P.op("dve", lambda e: e.tensor_reduce(_a(out), _a(in_), axis, op), [out], [in_])

    def interleave(gens):
        gens = list(gens)
        while gens:
            for gz in list(gens):
                try:
                    next(gz)
                except StopIteration:
                    gens.remove(gz)

    def run_chunks(make_gen):
        gens = [make_gen(c) for c in range(NCH)]

        def step(c):
            try:
                next(gens[c])
            except StopIteration:
                pass
        pairs = NCH // 2
        step(0); step(1)
        for k in range(pairs):
            a, b = 2 * k, 2 * k + 1
            step(a); step(b)
            step(a); step(b)
            step(a); step(a)
            if k + 1 < pairs:
                step(2 * k + 2)
            step(b); step(b)
            if k + 1 < pairs:
                step(2 * k + 3)

    def next_pb():
        state["pb_i"] = (state["pb_i"] + 1) % 2
        return PB[state["pb_i"]]

    def load_slab(l, name, ncols=512):
        s = slab[state["slab_i"]]
        state["slab_i"] = (state["slab_i"] + 1) % NSLABBUF
        dma(s[:, :, 0:ncols], wt_d[l, SLAB_ID[name], :, :, 0:ncols], eng="pool")
        return s

    def load_wo(l, si):
        s = slab[state["slab_i"]]
        state["slab_i"] = (state["slab_i"] + 1) % NSLABBUF
        dma(s[:, :, :], wo_d[l, si].rearrange("p f n -> p (f n)").rearrange("p (a b) -> p a b", a=8), eng="pool")
        return s

    def cvc(l, name, i=0):
        o, w = CM.off[name]
        return cv[l][:, o + i:o + i + 1]

    def fm_proj(s, gi, out_cols=T):
        pb = next_pb()
        for kc in range(8):
            mm(pb[:, 0:T], s[:, kc, gi * 128:(gi + 1) * 128], hT[:, kc, :], start=(kc == 0), stop=(kc == 7))
        return pb

    dma(ident[:, :], ident_d)
    for l in range(DEPTH):
        dma(cv[l][:, :], cv_d[l])
        dma(lora_up[l][:, :], lora_d[l])
        memset(b_hist[l][:, :], 0.0)
        memset(b_S[l][:, :, :], 0.0)
    dma(d_dmat[:, :, :], dmat_d)
    dma(d_xi[:, :, :, :], dxi_d.rearrange("p (g j o) -> p g j o", g=4, j=2))
    dma(d_zeta[:, :, :], dzeta_d.rearrange("p (h o) -> p h o", o=1))
    dma(d_gch[:, :, :], dgch_d.rearrange("p (g o) -> p g o", o=1))
    dma(a_mask[:, :], amask_d)
    dma(a_sel[:, :], asel_d)
    for l in range(DEPTH):
        memset(d_R[l][:, :, :], 0.0)
        memset(a_hist[l][:, :, :], 0.0)
        memset(a_C[l][:, :, :], 0.0)
        memset(a_n[l][:, :, :], 0.0)
        memset(a_car[l][:, :], 0.0)
        ts(a_car[l][0:8, 2:3], cv[l][0:8, CM.off["a_fb"][0]:CM.off["a_fb"][0] + 1], -1.0, None, ALU.mult)
    dma(b_ones[:, :], bones_d)
    dma(b_i2[:, 0, :], bi2_d)
    dma(b_m5[:, :, :], bm5_d)
    memset(ones[:, :], 1.0)
    cp(ident16[:, :], ident[:, :])
    memset(ones16[:, :], 1.0)
    memset(eps12[:, :], 1e-12)
    memset(zeros[:, :], 0.0)
    for l in range(DEPTH):
        memset(c_hist[l][:, :, :], 0.0)
        memset(c_state[l][:, :], 0.0)
        o, w = CM.off["c_lam"]
        act(sm[:, 0:4], cv[l][:, o:o + 4], AF.Exp, scale=-1.0)
        act(sm[:, 4:8], sm[:, 0:4], AF.Ln, bias=ones[:, 0:1])
        ts(c_coef[l][:, 0:4], sm[:, 4:8], -8.0, None, ALU.mult)
        ts(c_coef[l][:, 4:8], sm[:, 4:8], -16.0, None, ALU.mult)

    def rmsnorm_pre(l, X):
        for tb in range(NTB):
            act(scr[:, :], X[:, tb, :], AF.Square)
            reduce(sm[:, 8 + tb:9 + tb], scr[:, :], ALU.add)
        ts(sm[:, 12:12 + NTB], sm[:, 8:8 + NTB], 1.0 / D_MODEL, 1e-6, ALU.mult, ALU.add)
        act(sm[:, 16:16 + NTB], sm[:, 12:12 + NTB], AF.Sqrt)
        recip(sm[:, 20:20 + NTB], sm[:, 16:16 + NTB])
        for tb in range(NTB):
            act(scr[:, :], X[:, tb, :], AF.Copy, scale=sm[:, 20 + tb:21 + tb])
            for half in range(2):
                pb = next_pb()
                for q in range(4):
                    kc = half * 4 + q
                    tr(pb[:, q * 128:(q + 1) * 128], scr[:, kc * 128:(kc + 1) * 128], ident[:, :])
                for q in range(4):
                    kc = half * 4 + q
                    ts(hT[:, kc, tb * 128:(tb + 1) * 128], pb[:, q * 128:(q + 1) * 128],
                       cvc(l, "gpre", kc), None, ALU.mult)

    def mixer_C(l):
        cx, xc, rg, ig, aa, uu, hh = FS[0], FS[1], FS[2], FS[3], FS[4], FS[5], FS[6]
        dma(lru[l][:, :, :, :], lru_d[l])
        wx = load_slab(l, "Cx")

        def cstream(g):
            pb = fm_proj(wx, g)
            cp(cx[:, g, 0:3], c_hist[l][:, g, :])
            act(cx[:, g, 3:3 + T], pb[:, 0:T], AF.Copy)
            yield
            cp(c_hist[l][:, g, :], cx[:, g, T:T + 3])
            o, w = CM.off["c_cw"]
            ts(xc[:, g, 0:T], cx[:, g, 0:T], cv[l][:, o + g:o + g + 1], cvc(l, "c_cb", g), ALU.mult, ALU.add)
            yield
            for j in range(1, 4):
                stt(xc[:, g, 0:T], cx[:, g, j:j + T], cv[l][:, o + 4 * j + g:o + 4 * j + g + 1], xc[:, g, 0:T],
                    ALU.mult, ALU.add)
                yield
        interleave([cstream(g) for g in range(4)])
        for g in range(4):
            pb = next_pb()
            mm(pb[:, 0:T], lru[l][:, 0, g, :], xc[:, g, 0:T])
            act(rg[:, g, 0:T], pb[:, 0:T], AF.Sigmoid, bias=cvc(l, "c_br", g))
            pb = next_pb()
            mm(pb[:, 0:T], lru[l][:, 1, g, :], xc[:, g, 0:T])
            act(ig[:, g, 0:T], pb[:, 0:T], AF.Sigmoid, bias=cvc(l, "c_bi", g))
        for g in range(4):
            act(aa[:, g, 0:T], rg[:, g, 0:T], AF.Exp, scale=c_coef[l][:, g:g + 1])
            act(uu[:, g, 0:T], rg[:, g, 0:T], AF.Exp, scale=c_coef[l][:, 4 + g:5 + g])
        ts(uu[:, :, 0:T], uu[:, :, 0:T], -1.0, 1.0, ALU.mult, ALU.add)
        act(uu[:, :, 0:T], uu[:, :, 0:T], AF.Sqrt)
        tt(ig[:, :, 0:T], ig[:, :, 0:T], xc[:, :, 0:T], ALU.mult)
        tt(uu[:, :, 0:T], uu[:, :, 0:T], ig[:, :, 0:T], ALU.mult)
        for g in range(4):
            scan(hh[:, g, 0:T], aa[:, g, 0:T], uu[:, g, 0:T], c_state[l][:, g:g + 1], ALU.mult, ALU.add)
            cp(c_state[l][:, g:g + 1], hh[:, g, T - 1:T])
        wz = load_slab(l, "Cz")
        for g in range(4):
            pb = fm_proj(wz, g)
            act(rg[:, g, 0:T], pb[:, 0:T], AF.Silu)
            tt(yT[:, 8 + g, :], hh[:, g, 0:T], rg[:, g, 0:T], ALU.mult)

    def tm_proj(sl, tb):
        pb = next_pb()
        for kc in range(8):
            mm(pb[:, :], hT[:, kc, tb * 128:(tb + 1) * 128], sl[:, kc, :], start=(kc == 0), stop=(kc == 7))
        return pb

    def to_fm(dst, src, tb):
        pb = next_pb()
        for g in range(4):
            tr(pb[:, g * 128:(g + 1) * 128], src[:, tb, 2 * g:2 * g + 2, 0:64], ident[:, :])
        cp(dst[:, 0:4, tb * 128:(tb + 1) * 128], pb[:, :].rr("p (a b) -> p a b", a=4))

    def to_tm(dst, src, tb):
        pb = next_pb()
        for g in range(4):
            tr(pb[:, g * 128:(g + 1) * 128], src[:, g, tb * 128:(tb + 1) * 128], ident[:, :])
        cp(dst[:, tb, :, 0:64], pb[:, :].rr("p (h e) -> p h e", h=8))

    def head_ln(dst, src, tb):
        X3 = src[:, tb, :, 0:64]
        D3 = dst[:, tb, :, 0:64]
        reduce(lnst[:, 0:8], X3, ALU.add)
        stt(D3, lnst[:, 0:8].rr("p (h o) -> p h o", o=1).bc([128, 8, 64]), -1.0 / 64, X3, ALU.mult, ALU.add)
        tt(scr[:, 0:512].rr("p (h e) -> p h e", h=8), D3, D3, ALU.mult)
        reduce(lnst[:, 8:16], scr[:, 0:512].rr("p (h e) -> p h e", h=8), ALU.add)
        ts(lnst[:, 16:24], lnst[:, 8:16], 1.0 / 64, 1e-5, ALU.mult, ALU.add)
        act(lnst[:, 24:32], lnst[:, 16:24], AF.Sqrt)
        recip(lnst[:, 32:40], lnst[:, 24:32])
        tt(D3, D3, lnst[:, 32:40].rr("p (h o) -> p h o", o=1).bc([128, 8, 64]), ALU.mult)

    def mixer_D(l, ti):
        qr, kr, osb, xn = TM[0], TM[1], TM[4], TM[5]
        vv, kz = TM16[1], TM16[2]
        qT, kT, sz = AlT16, RhT16, FS[2]
        AT = bm16[0]
        d_R16 = b_S16
        cp(d_R16[:, :, :], d_R[l][:, :, :])
        S = [PB[3], PB[4]]; O = [PB[5], PB[6]]; ST = [PB[2], PB[7]]
        dma(rope_sb[:, :, :, :], rope_d.rearrange("(n tb p) (o e) -> n p tb o e", tb=NTB, p=128, o=1)[ti])
        for name, dst in (("Dq", qr), ("Dk", kr)):
            sl = load_slab(l, name)
            for tb in range(NTB):
                pb = tm_proj(sl, tb)
                act(scr[:, 0:512], pb[:, :], AF.Copy)
                raw = scr[:, 0:512].rr("p (h e) -> p h e", h=8)
                x1, x2 = raw[:, :, 0:32], raw[:, :, 32:64]
                cos = rope_sb[:, tb, :, 0:32].bc([128, 8, 32]); sin = rope_sb[:, tb, :, 32:64].bc([128, 8, 32])
                t1 = scr2[:, 0:256].rr("p (h e) -> p h e", h=8); t2 = scr2[:, 256:512].rr("p (h e) -> p h e", h=8)
                d1, d2 = dst[:, tb, :, 0:32], dst[:, tb, :, 32:64]
                tt(d1, x1, cos, ALU.mult); tt(t1, x2, sin, ALU.mult); tt(d1, d1, t1, ALU.subtract)
                tt(d2, x2, cos, ALU.mult); tt(t2, x1, sin, ALU.mult); tt(d2, d2, t2, ALU.add)
                to_fm(qT if name == "Dq" else kT, dst, tb)
        sl = load_slab(l, "Dv")
        for tb in range(NTB):
            pb = tm_proj(sl, tb)
            act(vv[:, tb, :, 0:64], pb[:, :].rr("p (h e) -> p h e", h=8), AF.Copy)
            tt(kz[:, tb, :, 0:64], kr[:, tb, :, 0:64], d_zeta[:, :, :].bc([128, 8, 64]), ALU.mult)
        sl = load_slab(l, "Dz")
        for g in range(4):
            pb = fm_proj(sl, g)
            act(sz[:, g, 0:T], pb[:, 0:T], AF.Silu)
        def d_chunk(c):
            tb, p = c // 2, c % 2
            rows = slice(64 * p, 64 * p + 64)
            cols = slice(c * 64, (c + 1) * 64)
            for j in range(2):
                jr = slice(64 * j, 64 * j + 64)
                for g in range(4):
                    mm(S[j][rows, g * 64:(g + 1) * 64], kT[jr, g, cols], qT[jr, g, cols])
            for h in range(8):
                g, j = h // 2, h % 2
                mm(ST[p][64 * j:64 * j + 64, g * 64:(g + 1) * 64], kz[rows, tb, h, 0:64], vv[rows, tb, h, 0:64])
            yield
            for j in range(2):
                tt(AT[rows, 4 * j:4 * j + 4, :], S[j][rows, 0:256].rr("p (a b) -> p a b", a=4),
                   d_dmat[rows, 4 * j:4 * j + 4, :], ALU.mult)
            yield
            for h in range(8):
                g, j = h // 2, h % 2
                mm(O[p][rows, h * 64:(h + 1) * 64], AT[rows, j * 4 + g, :], vv[rows, tb, h, 0:64])
            yield
            for j in range(2):
                jr = slice(64 * j, 64 * j + 64)
                for g in range(4):
                    mm(S[j][rows, 256 + g * 64:256 + (g + 1) * 64], qT[jr, g, cols], d_R16[jr, g, :])
            yield
            tt(d_R[l][:, :, :], d_R[l][:, :, :], d_gch[:, :, :].bc([128, 4, 64]), ALU.mult)
            tt(d_R[l][:, :, :], d_R[l][:, :, :], ST[p][:, 0:256].rr("p (a b) -> p a b", a=4), ALU.add)
            for j in range(2):
                tt(tmp4[rows, :, j, 0:64], S[j][rows, 256:512].rr("p (a b) -> p a b", a=4),
                   d_xi[rows, :, j, :].bc([64, 4, 64]), ALU.mult)
            cp(d_R16[:, :, :], d_R[l][:, :, :])
            tt(osb[rows, tb, :, 0:64], O[p][rows, :].rr("p (h e) -> p h e", h=8),
               tmp4[rows, :, :, 0:64].rr("p g j e -> p (g j) e"), ALU.add)
            yield
        run_chunks(d_chunk)
        for tb in range(NTB):
            head_ln(xn, osb, tb)
            pb = next_pb()
            for g in range(4):
                tr(pb[:, g * 128:(g + 1) * 128], xn[:, tb, 2 * g:2 * g + 2, 0:64], ident[:, :])
            for g in range(4):
                stt(yT[:, 12 + g, tb * 128:(tb + 1) * 128], pb[:, g * 128:(g + 1) * 128], cvc(l, "d_nw", g),
                    sz[:, g, tb * 128:(tb + 1) * 128], ALU.mult, ALU.mult)

    def conv_silu(l, sl, g8, dst, gdst, cx, whichhist, cwname, cbname, ngroups_total, silu=True):
        pass

    def mixer_A(l, ti):
        cx, qT32, kT32, sz = FS[0], FS[1], FS[2], FS[3]
        qT, kT = AlT16, RhT16
        so, osb, xn = TM[2], TM[3], TM[4]
        ktm, vv = TM16[0], TM16[1]
        AT, vw = bm16[0], bm16[1]
        a_C16 = b_S16
        cp(a_C16[:, :, :], a_C[l][:, :, :])
        cp(a_n16[:, :, :], a_n[l][:, :, :])
        S = [PB[3], PB[4]]; O = [PB[5], PB[6]]; ST = [PB[2], PB[7]]; JX = [PB[0], PB[1]]
        R_I, R_SP, R_F, R_G, R_P, R_MU, R_PE, R_SI, R_WJ, R_NM, R_NP = range(11)
        ocw = CM.off["a_cw"][0]
        for which, (name, dstT, dst16) in enumerate((("Aq", qT32, qT), ("Ak", kT32, kT))):
            sl = load_slab(l, name)

            def astream(g, which=which, sl=sl, dstT=dstT, dst16=dst16):
                g8 = which * 4 + g
                pb = fm_proj(sl, g)
                cp(cx[:, g, 0:3], a_hist[l][:, g8, :])
                act(cx[:, g, 3:3 + T], pb[:, 0:T], AF.Copy)
                yield
                cp(a_hist[l][:, g8, :], cx[:, g, T:T + 3])
                ts(dstT[:, g, 0:T], cx[:, g, 0:T], cv[l][:, ocw + g8:ocw + g8 + 1], cvc(l, "a_cb", g8),
                   ALU.mult, ALU.add)
                yield
                for j in range(1, 4):
                    stt(dstT[:, g, 0:T], cx[:, g, j:j + T], cv[l][:, ocw + 8 * j + g8:ocw + 8 * j + g8 + 1],
                        dstT[:, g, 0:T], ALU.mult, ALU.add)
                    yield
                act(dst16[:, g, 0:T], dstT[:, g, 0:T], AF.Silu)
                yield
            interleave([astream(g) for g in range(4)])
        sl = load_slab(l, "Az")
        for g in range(4):
            pb = fm_proj(sl, g)
            act(sz[:, g, 0:T], pb[:, 0:T], AF.Silu)
        acut = int(os.environ.get("ACUT", "99"))
        if acut < 1:
            return
        sl = load_slab(l, "Ag", ncols=128)
        pbi = next_pb()
        for kc in range(8):
            mm(pbi[0:8, 0:T], sl[:, kc, 0:8], hT[:, kc, :], start=(kc == 0), stop=(kc == 7))
        act(ga[0:8, R_I, 0:T], pbi[0:8, 0:T], AF.Identity, bias=cv[l][0:8, CM.off["a_ib"][0]:CM.off["a_ib"][0] + 1])
        gcut = int(os.environ.get("GCUT", "99"))
        if gcut < 1:
            return
        pbf = next_pb()
        for kc in range(8):
            mm(pbf[0:8, 0:T], sl[:, kc, 8:16], hT[:, kc, :], start=(kc == 0), stop=(kc == 7))
        act(ga[0:8, R_SP, 0:T], pbf[0:8, 0:T], AF.Exp, bias=a_car[l][0:8, 2:3], scale=-1.0)
        act(ga[0:8, R_SP, 0:T], ga[0:8, R_SP, 0:T], AF.Ln, bias=ones[0:8, 0:1])
        if gcut < 2:
            return
        scan(ga[0:8, R_F, 0:T], ones[0:8, 0:T], ga[0:8, R_SP, 0:T], a_car[l][0:8, 0:1], ALU.mult, ALU.subtract)
        cp(a_car[l][0:8, 0:1], ga[0:8, R_F, T - 1:T])
        tt(ga[0:8, R_G, 0:T], ga[0:8, R_I, 0:T], ga[0:8, R_F, 0:T], ALU.subtract)
        if gcut < 3:
            return
        cp(ga[0:8, R_P, 0:1], a_car[l][0:8, 1:2])
        scan(ga[0:8, R_P, 1:T + 1], ones[0:8, 0:T], ga[0:8, R_G, 0:T], a_car[l][0:8, 1:2], ALU.mult, ALU.max)
        cp(a_car[l][0:8, 1:2], ga[0:8, R_P, T:T + 1])
        if gcut < 4:
            return
        for c in range(NCH):
            cols = slice(c * 64, (c + 1) * 64)
            ts(ga[0:8, R_MU, cols], zeros[0:8, 0:64], ga[0:8, R_P, c * 64:c * 64 + 1], None, ALU.add)
            ts(ga[0:8, R_PE, cols], zeros[0:8, 0:64], ga[0:8, R_P, c * 64 + 64:c * 64 + 65], None, ALU.add)
            tt(scl[0:8, c:c + 1], ga[0:8, R_P, c * 64:c * 64 + 1], ga[0:8, R_P, c * 64 + 64:c * 64 + 65], ALU.subtract)
        if gcut < 5:
            return
        Pv = ga[0:8, R_P, 1:T + 1]
        tt(ga[0:8, R_SI, 0:T], ga[0:8, R_MU, 0:T], Pv, ALU.subtract)
        tt(ga[0:8, R_WJ, 0:T], ga[0:8, R_G, 0:T], ga[0:8, R_PE, 0:T], ALU.subtract)
        stt(ga[0:8, R_NM, 0:T], ga[0:8, R_F, 0:T], -1.0, Pv, ALU.mult, ALU.subtract)
        ts(ga[0:8, R_NP, 0:T], Pv, -1.0, None, ALU.mult)
        if acut < 2:
            return
        for tb in range(NTB):
            pb = next_pb()
            for r, R in enumerate((R_SI, R_WJ, R_NM)):
                tr(pb[:, r * 8:(r + 1) * 8], ga[0:8, R, tb * 128:(tb + 1) * 128], ident[0:8, 0:8])
            act(tmsc[:, tb, :, :], pb[:, 0:24].rr("p (a b) -> p a b", a=3), AF.Exp)
        if acut < 3:
            return
        tt(rhs_sc[0:8, :, :], scl[0:8, :].rr("p (c o) -> p c o", o=1).bc([8, NCH, 4]),
           a_sel[0:8, 512:516].rr("p (o g) -> p o g", o=1).bc([8, NCH, 4]), ALU.mult)
        pb = next_pb()
        mm(pb[:, 0:NCH * 4], a_sel[0:8, 516:644], rhs_sc[0:8, :, :].rr("p c g -> p (c g)"))
        act(scb[:, :, :, :].rr("p c g o -> p (c g o)"), pb[:, 0:NCH * 4], AF.Exp)
        if acut < 4:
            return
        for tb in range(NTB):
            pbt = next_pb()
            for g in range(4):
                mm(pbt[:, g * 128:(g + 1) * 128], kT[:, g, tb * 128:(tb + 1) * 128], ident16[:, :])
            cp(ktm[:, tb, :, :], pbt[:, :].rr("p (h e) -> p h e", h=8))
            cp(wj16[:, tb, :], tmsc[:, tb, 1, :])
        sl = load_slab(l, "Av")
        for tb in range(NTB):
            pb = tm_proj(sl, tb)
            act(vv[:, tb, :, :], pb[:, :].rr("p (h e) -> p h e", h=8), AF.Copy)
        sl = load_slab(l, "Ao")
        for tb in range(NTB):
            pb = tm_proj(sl, tb)
            act(so[:, tb, :, :], pb[:, :].rr("p (h e) -> p h e", h=8), AF.Sigmoid)
        ETs = [ET, ET1]
        for tb in range(NTB):
            E = next_pb()
            mm(E[:, :], ident[:, :], a_mask[:, :], start=True, stop=False)
            mm(E[:, :], ga[0:8, R_G, tb * 128:(tb + 1) * 128], a_sel[0:8, 0:512], start=False, stop=False)
            for p in range(2):
                c = tb * 2 + p
                tt(negPx[0:8, p, :, :], ga[0:8, R_NP:R_NP + 1, c * 64:(c + 1) * 64].bc([8, 8, 64]),
                   a_sel[0:8, 0:512].rr("p (h e) -> p h e", h=8), ALU.mult)
                mm(E[64 * p:64 * p + 64, :], ones[0:8, 0:64], negPx[0:8, p, :, :].rr("p h e -> p (h e)"),
                   start=False, stop=True)
            act(ETs[tb][:, :, :].rr("p h e -> p (h e)"), E[:, :], AF.Exp)

        def a_chunk(c):
            tb, p = c // 2, c % 2
            rows = slice(64 * p, 64 * p + 64)
            cols = slice(c * 64, (c + 1) * 64)
            ETt = ETs[tb]
            sI4 = tmsc[:, tb, 0, :].rr("p (g j o) -> p g j o", g=4, j=2)
            for j in range(2):
                jr = slice(64 * j, 64 * j + 64)
                for g in range(4):
                    mm(S[j][rows, g * 64:(g + 1) * 64], kT[jr, g, cols], qT[jr, g, cols])
            tt(vw[rows, :, :], vv[rows, tb, :, :],
               tmsc[rows, tb, 1, :].rr("p (h o) -> p h o", o=1).bc([64, 8, 64]), ALU.mult)
            yield
            for j in range(2):
                stt(AT[rows, 4 * j:4 * j + 4, :], S[j][rows, 0:256].rr("p (a b) -> p a b", a=4), 0.125,
                    ETt[rows, 4 * j:4 * j + 4, :], ALU.mult, ALU.mult)
            yield
            for h in range(8):
                g, j = h // 2, h % 2
                mm(O[p][rows, h * 64:(h + 1) * 64], AT[rows, j * 4 + g, :], vv[rows, tb, h, :])
                mm(ST[p][rows, 384 + h:385 + h], AT[rows, j * 4 + g, :], ones16[rows, 0:1])
            for h in range(8):
                g, j = h // 2, h % 2
                jr = slice(64 * j, 64 * j + 64)
                mm(ST[p][jr, g * 64:(g + 1) * 64], ktm[rows, tb, h, :], vw[rows, h, :])
                mm(ST[p][jr, 256 + g:257 + g], ktm[rows, tb, h, :], wj16[rows, tb, h:h + 1])
            yield
            for j in range(2):
                jr = slice(64 * j, 64 * j + 64)
                for g in range(4):
                    mm(JX[j][rows, g * 64:(g + 1) * 64], qT[jr, g, cols], a_C16[jr, g, :])
                    mm(JX[j][rows, 256 + g:257 + g], qT[jr, g, cols], a_n16[jr, g, :])
            yield
            for j in range(2):
                tt(tmp4[rows, :, j, :], JX[j][rows, 0:256].rr("p (a b) -> p a b", a=4),
                   sI4[rows, :, j, :].bc([64, 4, 64]), ALU.mult)
                tt(lnst[rows, 40:48].rr("p (g j) -> p g j", j=2)[:, :, j], JX[j][rows, 256:260],
                   tmsc[rows, tb, 0, :].rr("p (g j) -> p g j", j=2)[:, :, j], ALU.mult)
            tt(a_C[l][:, :, :], a_C[l][:, :, :], scb[:, c, :, :].bc([128, 4, 64]), ALU.mult)
            stt(a_C[l][:, :, :], ST[p][:, 0:256].rr("p (a b) -> p a b", a=4), 0.125, a_C[l][:, :, :],
                ALU.mult, ALU.add)
            tt(a_n[l][:, :, :], a_n[l][:, :, :], scb[:, c, :, :], ALU.mult)
            stt(a_n[l][:, :, :], ST[p][:, 256:260].rr("p (a b) -> p a b", b=1), 0.125, a_n[l][:, :, :],
                ALU.mult, ALU.add)
            cp(a_C16[:, :, :], a_C[l][:, :, :])
            cp(a_n16[:, :, :], a_n[l][:, :, :])
            tt(hnum[rows, :, :], O[p][rows, :].rr("p (h e) -> p h e", h=8),
               tmp4[rows, :, :, :].rr("p g j e -> p (g j) e"), ALU.add)
            tt(lnst[rows, 48:56], ST[p][rows, 384:392], lnst[rows, 40:48], ALU.add)
            stt(lnst[rows, 48:56], lnst[rows, 48:56], -1.0, lnst[rows, 48:56], ALU.mult, ALU.max)
            tt(lnst[rows, 48:56], lnst[rows, 48:56], tmsc[rows, tb, 2, :], ALU.max)
            recip(lnst[rows, 56:64], lnst[rows, 48:56])
            tt(hnum[rows, :, :], hnum[rows, :, :],
               lnst[rows, 56:64].rr("p (h o) -> p h o", o=1).bc([64, 8, 64]), ALU.mult)
            tt(osb[rows, tb, :, :], hnum[rows, :, :], so[rows, tb, :, :], ALU.mult)
            yield
        run_chunks(a_chunk)
        for tb in range(NTB):
            head_ln(xn, osb, tb)
            pb = next_pb()
            for g in range(4):
                tr(pb[:, g * 128:(g + 1) * 128], xn[:, tb, 2 * g:2 * g + 2, 0:64], ident[:, :])
            for g in range(4):
                stt(yT[:, g, tb * 128:(tb + 1) * 128], pb[:, g * 128:(g + 1) * 128], cvc(l, "a_nw", g),
                    sz[:, g, tb * 128:(tb + 1) * 128], ALU.mult, ALU.mult)

    def mixer_B(l, ti):
        rS, kS, vS, sz = FS[0], FS[1], FS[2], FS[3]
        RhT, AlT, bon = rS, kS, vS
        pools = (FS[4], FS[5], FS[6])

        def slot(i):
            return pools[i // 4][:, i % 4, :]
        raw, lora, aT, lw, lc, kt, kh, beta, eg, egi, egm, tA = [slot(i) for i in range(12)]
        wkv, xn = TM[3], TM[4]
        Vtm, Btm, Ktm = TM16[0], TM16[1], TM16[2]
        BJ = [PB[3], PB[4]]; PA = [PB[2], PB[7]]; PQ = [PB[5], PB[6]]; PX = [PB[0], PB[1]]
        Pn, Qn, Xm = bm16
        W2 = AT
        beta16, kt16 = bk16[:, 0, :], bk16[:, 1, :]
        omu = CM.off["b_mu"][0]

        def shift_mix(dst, sl, gi, hidx, mucol, raw_, tA_):
            pb = fm_proj(sl, gi)
            cp(raw_[:, 0:1], b_hist[l][:, hidx:hidx + 1])
            act(raw_[:, 1:T + 1], pb[:, 0:T], AF.Copy)
            yield
            cp(b_hist[l][:, hidx:hidx + 1], raw_[:, T:T + 1])
            tt(tA_[:, 0:T], raw_[:, 0:T], raw_[:, 1:T + 1], ALU.subtract)
            yield
            stt(dst, tA_[:, 0:T], cv[l][:, omu + mucol:omu + mucol + 1], raw_[:, 1:T + 1], ALU.mult, ALU.add)
            yield

        cp(b_S16[:, :, :], b_S[l][:, :, :])
        sl = load_slab(l, "Bl", ncols=128)
        interleave([shift_mix(lora[:, 0:T], sl, 0, 12, 12, raw, tA)])
        act(lora[0:64, 0:T], lora[0:64, 0:T], AF.Tanh)
        tslots = [(slot(2 + 2 * i), slot(3 + 2 * i)) for i in range(4)]
        for wi, (name, dst) in enumerate((("Br", rS), ("Bk", kS), ("Bv", vS))):
            sl = load_slab(l, name)
            interleave([shift_mix(dst[:, g, 0:T], sl, g, wi * 4 + g, wi * 4 + g, tslots[g][0], tslots[g][1])
                        for g in range(4)])
        sl = load_slab(l, "Bz")
        for g in range(4):
            pb = fm_proj(sl, g)
            act(sz[:, g, 0:T], pb[:, 0:T], AF.Silu)
        bcut = int(os.environ.get("BCUT", "99"))
        if bcut < 1:
            return
        for g in range(4):
            gc = slice(g * 128, (g + 1) * 128)

            def chainW(g=g, gc=gc):
                pb = next_pb()
                mm(pb[:, 0:T], lora_up[l][0:64, gc], lora[0:64, 0:T])
                act(lw[:, 0:T], pb[:, 0:T], AF.Sigmoid, bias=cvc(l, "b_w0", g))
                yield
                for c in range(NCH):
                    cols = slice(c * 64, (c + 1) * 64)
                    scan(lc[:, cols], ones[:, 0:64], lw[:, cols], 0.0, ALU.mult, ALU.add)
                    yield
                act(eg[:, 0:T], lc[:, 0:T], AF.Exp, scale=-0.606531)
                act(egi[:, 0:T], lc[:, 0:T], AF.Exp, scale=0.606531)
                tt(tA[:, 0:T], lc[:, 0:T], lw[:, 0:T], ALU.subtract)
                yield
                act(egm[:, 0:T], tA[:, 0:T], AF.Exp, scale=-0.606531)
                cp(gLt[:, g, :], eg[:, 63:T:64])
                yield

            def chainA(g=g, gc=gc):
                pb = next_pb()
                mm(pb[:, 0:T], lora_up[l][64:128, gc], lora[64:128, 0:T])
                act(aT[:, 0:T], pb[:, 0:T], AF.Sigmoid, bias=cvc(l, "b_a0", g))
                yield
                ts(beta[:, 0:T], aT[:, 0:T], -1.0, cvc(l, "b_ka", g), ALU.add, ALU.mult)
                yield
                stt(kt[:, 0:T], beta[:, 0:T], 1.0, kS[:, g, 0:T], ALU.add, ALU.mult)
                yield

            def chainK(g=g):
                ts(kh[:, 0:T], kS[:, g, 0:T], cvc(l, "b_kk", g), None, ALU.mult)
                yield
                tt(raw[:, 0:T], kh[:, 0:T], kh[:, 0:T], ALU.mult)
                yield
                pb = next_pb()
                mm(pb[:, 0:T], b_ones[:, :], raw[:, 0:T])
                act(raw[:, 0:T], pb[:, 0:T], AF.Sqrt, bias=eps12[:, 0:1])
                yield
                recip(raw[:, 0:T], raw[:, 0:T])
                yield
                tt(kh[:, 0:T], kh[:, 0:T], raw[:, 0:T], ALU.mult)
                yield

            def chainV(g=g):
                for tb in range(NTB):
                    pbt = next_pb()
                    tr(pbt[:, 0:128], vS[:, g, tb * 128:(tb + 1) * 128], ident[:, :])
                    cp(Vtm[:, tb, 2 * g:2 * g + 2, :], pbt[:, 0:128].rr("p (a b) -> p a b", a=2))
                    yield
            interleave([chainW(), chainA(), chainK(), chainV()])

            def chainR(g=g):
                stt(raw[:, 0:T], rS[:, g, 0:T], cvc(l, "b_rk", g), kt[:, 0:T], ALU.mult, ALU.mult)
                yield
                pbb = next_pb()
                mm(pbb[:, 0:T], b_ones[:, :], raw[:, 0:T])
                tt(bon[:, g, 0:T], pbb[:, 0:T], vS[:, g, 0:T], ALU.mult)
                yield

            def chainB(g=g):
                tt(beta[:, 0:T], aT[:, 0:T], kh[:, 0:T], ALU.mult)
                yield
                tt(beta16[:, 0:T], beta[:, 0:T], egi[:, 0:T], ALU.mult)
                yield

            def chainO(g=g):
                tt(AlT16[:, g, 0:T], kh[:, 0:T], egm[:, 0:T], ALU.mult)
                yield
                tt(RhT16[:, g, 0:T], rS[:, g, 0:T], eg[:, 0:T], ALU.mult)
                yield
            interleave([chainR(), chainB(), chainO()])
            tt(kt16[:, 0:T], kt[:, 0:T], egi[:, 0:T], ALU.mult)
            for tb in range(NTB):
                pbt = next_pb()
                mm(pbt[:, 0:128], beta16[:, tb * 128:(tb + 1) * 128], ident16[:, :])
                mm(pbt[:, 128:256], kt16[:, tb * 128:(tb + 1) * 128], ident16[:, :])
                cp(Btm[:, tb, 2 * g:2 * g + 2, :], pbt[:, 0:128].rr("p (a b) -> p a b", a=2))
                cp(Ktm[:, tb, 2 * g:2 * g + 2, :], pbt[:, 128:256].rr("p (a b) -> p a b", a=2))
            for tb in range(NTB):
                for j in range(2):
                    jr = slice(64 * j, 64 * j + 64)
                    for p in range(2):
                        c = tb * 2 + p
                        rows = slice(64 * p, 64 * p + 64)
                        cols = slice(c * 64, (c + 1) * 64)
                        A_, B_, K_, R_ = AlT16[jr, g, cols], beta16[jr, cols], kt16[jr, cols], RhT16[jr, g, cols]
                        mm(BJ[j][rows, 0:64], A_, B_)
                        mm(BJ[j][rows, 64:128], B_, A_)
                        mm(BJ[j][rows, 128:192], K_, A_)
                        mm(BJ[j][rows, 192:256], B_, R_)
                        mm(BJ[j][rows, 256:320], K_, R_)
                    tt(PRall[tb][:, :, 2 * g + j, :], BJ[j][:, 0:320].rr("p (k t) -> p k t", k=5), b_m5[:, :, :],
                       ALU.mult)
        if bcut < 2:
            return
        for tb in range(NTB):
            PRt = PRall[tb]
            P0, Q0, MkT, NbT, NkT = (PRt[:, k, :, :] for k in range(5))
            Wsb, Usb = PRt[:, 2, :, :], PRt[:, 4, :, :]
            for p in range(2):
                rows = slice(64 * p, 64 * p + 64)
                c = tb * 2 + p
                for h in range(8):
                    g, j = h // 2, h % 2
                    mm(PA[p][rows, h * 64:(h + 1) * 64], MkT[rows, h, :], Vtm[rows, tb, h, :])
                    mm(PQ[p][rows, h * 64:(h + 1) * 64], NkT[rows, h, :], Vtm[rows, tb, h, :])
                    mm(PX[p][64 * j:64 * j + 64, g * 64:(g + 1) * 64], Ktm[rows, tb, h, :], Vtm[rows, tb, h, :])
                act(W2[rows, :, :], PA[p][rows, :].rr("p (h e) -> p h e", h=8), AF.Copy)
                act(Y2[rows, :, :], PQ[p][rows, :].rr("p (h e) -> p h e", h=8), AF.Copy)
                tt(K2g[:, p, :, :], PX[p][:, 0:256].rr("p (a b) -> p a b", a=4),
                   gLt[:, :, c:c + 1].bc([128, 4, 64]), ALU.mult)
            tt(Xm[:, :, :], b_i2[:, :, :].bc([128, 8, 64]), Q0, ALU.subtract)
            Pc, Qc = P0, Q0
            nxt = [(Pn, Qn), (P0, Q0)]
            for i in range(1, 6):
                Pd, Qd = nxt[(i - 1) % 2]
                for p in range(2):
                    rows = slice(64 * p, 64 * p + 64)
                    for h in range(8):
                        hc = slice(h * 64, (h + 1) * 64)
                        mm(PA[p][rows, hc], Qc[rows, h, :], Pc[rows, h, :])
                        if i < 5:
                            mm(PQ[p][rows, hc], Pc[rows, h, :], Qc[rows, h, :])
                for p in range(2):
                    rows = slice(64 * p, 64 * p + 64)
                    act(Pd[rows, :, :], PA[p][rows, :].rr("p (h e) -> p h e", h=8), AF.Copy)
                    if i < 5:
                        cp(Qd[rows, :, :], PQ[p][rows, :].rr("p (h e) -> p h e", h=8))
                Pc, Qc = Pd, Qd
                for p in range(2):
                    rows = slice(64 * p, 64 * p + 64)
                    for h in range(8):
                        mm(PX[p][rows, h * 64:(h + 1) * 64], Pc[rows, h, :], Xm[rows, h, :])
                for p in range(2):
                    rows = slice(64 * p, 64 * p + 64)
                    tt(Xm[rows, :, :], Xm[rows, :, :], PX[p][rows, :].rr("p (h e) -> p h e", h=8), ALU.add)
            if bcut < 3:
                continue
            for p in range(2):
                c = tb * 2 + p
                rows = slice(64 * p, 64 * p + 64)
                cols = slice(c * 64, (c + 1) * 64)
                for j in range(2):
                    jr = slice(64 * j, 64 * j + 64)
                    for g in range(4):
                        mm(BJ[j][rows, g * 64:(g + 1) * 64], AlT16[jr, g, cols], b_S16[jr, g, :])
                        mm(BJ[j][rows, 256 + g * 64:256 + (g + 1) * 64], RhT16[jr, g, cols], b_S16[jr, g, :])
                W24 = W2[rows, :, :].rr("p (g j) e -> p g j e", j=2)
                Ws4 = Wsb[rows, :, :].rr("p (g j) e -> p g j e", j=2)
                Y24 = Y2[rows, :, :].rr("p (g j) e -> p g j e", j=2)
                for j in range(2):
                    tt(Ws4[:, :, j, :], BJ[j][rows, 0:256].rr("p (a b) -> p a b", a=4), W24[:, :, j, :], ALU.add)
                    tt(tmp4[rows, :, j, :], BJ[j][rows, 256:512].rr("p (a b) -> p a b", a=4), Y24[:, :, j, :],
                       ALU.add)
                for h in range(8):
                    mm(PX[p][rows, h * 64:(h + 1) * 64], Xm[rows, h, :], Wsb[rows, h, :])
                act(Usb[rows, :, :], PX[p][rows, :].rr("p (h e) -> p h e", h=8), AF.Copy)
                for h in range(8):
                    g, j = h // 2, h % 2
                    mm(PA[p][rows, h * 64:(h + 1) * 64], NbT[rows, h, :], Usb[rows, h, :])
                    mm(PQ[p][64 * j:64 * j + 64, g * 64:(g + 1) * 64], Btm[rows, tb, h, :], Usb[rows, h, :])
                tt(wkv[rows, tb, :, :], tmp4[rows, :, :, :].rr("p g j e -> p (g j) e"),
                   PA[p][rows, :].rr("p (h e) -> p h e", h=8), ALU.subtract)
                tt(b_S[l][:, :, :], b_S[l][:, :, :], PQ[p][:, 0:256].rr("p (a b) -> p a b", a=4), ALU.subtract)
                tt(b_S[l][:, :, :], b_S[l][:, :, :], gLt[:, :, c:c + 1].bc([128, 4, 64]), ALU.mult)
                tt(b_S[l][:, :, :], b_S[l][:, :, :], K2g[:, p, :, :], ALU.add)
                cp(b_S16[:, :, :], b_S[l][:, :, :])
        ogw, ogb = CM.off["b_gw"][0], CM.off["b_gb"][0]
        for tb in range(NTB):
            head_ln(xn, wkv, tb)
            pb = next_pb()
            for g in range(4):
                tr(pb[:, g * 128:(g + 1) * 128], xn[:, tb, 2 * g:2 * g + 2, 0:64], ident[:, :])
            for g in range(4):
                tc_ = slice(tb * 128, (tb + 1) * 128)
                ts(scr[:, 0:128], pb[:, g * 128:(g + 1) * 128], cv[l][:, ogw + g:ogw + g + 1],
                   cv[l][:, ogb + g:ogb + g + 1], ALU.mult, ALU.add)
                tt(scr[:, 0:128], scr[:, 0:128], bon[:, g, tc_], ALU.add)
                tt(yT[:, 4 + g, tc_], scr[:, 0:128], sz[:, g, tc_], ALU.mult)

    def out_proj_residual(l, X):
        dma(gpost[l][:, :], gpost_d[l])
        for si in range(4):
            s = load_wo(l, si)
            for tb in range(NTB):
                for half in range(2):
                    pb = PB[4 + tb * 2 + half]
                    for q in range(4):
                        fc = si * 4 + q
                        mm(pb[:, :], yT[:, fc, tb * 128:(tb + 1) * 128], s[:, q * 2 + half, :],
                           start=(fc == 0), stop=(fc == 15))
        if cut < 4:
            return
        for tb in range(NTB):
            for half in range(2):
                act(scr[:, half * 512:(half + 1) * 512], PB[4 + tb * 2 + half][:, :], AF.Copy)
            if cut < 5:
                continue
            act(scr2[:, :], scr[:, :], AF.Square)
            reduce(sm[:, 24:25], scr2[:, :], ALU.add)
            ts(sm[:, 25:26], sm[:, 24:25], 1.0 / D_MODEL, 1e-6, ALU.mult, ALU.add)
            act(sm[:, 26:27], sm[:, 25:26], AF.Sqrt)
            recip(sm[:, 27:28], sm[:, 26:27])
            if cut < 6:
                continue
            stt(scr2[:, :], scr[:, :], sm[:, 27:28], gpost[l][:, :], ALU.mult, ALU.mult)
            if cut < 7:
                continue
            tt(X[:, tb, :], X[:, tb, :], scr2[:, :], ALU.add)

    xv = x_d.rearrange("(n tb p) d -> n p tb d", tb=NTB, p=128)
    yv = y_d.rearrange("(n tb p) d -> n p tb d", tb=NTB, p=128)
    for ti in range(ntiles):
        X = xt[0]
        dma(X[:, :, :], xv[ti])
        for l in range(nlayers):
            if cut >= 1:
                rmsnorm_pre(l, X)
            memset(yT[:, :, :], 0.0)
            if "C" in mixers and cut >= 2:
                mixer_C(l)
            if "D" in mixers:
                mixer_D(l, ti)
            if "A" in mixers:
                mixer_A(l, ti)
            if "B" in mixers:
                mixer_B(l, ti)
            if cut >= 3:
                out_proj_residual(l, X)
        dma(yv[ti], X[:, :, :])

    P.emit()
    return nc, stack


def kernel(**inputs):
    inputs = {k: np.asarray(v) for k, v in inputs.items()}
    return run(inputs)


def make_in_map(xc, hp, tb):
    m = dict(x=xc, wt=hp["wt"], wo=hp["wo"], cv=hp["cv"], lora=hp["lora"], lru=hp["lru"], gpost=hp["gpost"])
    m.update(tb)
    return m


def run(inputs, ntiles=SEQ // T, nlayers=DEPTH, mixers="ABCD"):
    hp = host_prepare(inputs)
    tb = host_tables()
    nc, stack = build_program(ntiles, nlayers, mixers)
    x = np.ascontiguousarray(inputs["x"], dtype=np.float32)
    in_maps = [make_in_map(x[c], hp, tb) for c in range(NCORES)]
    with stack:
        res = run_bass_kernel_spmd(nc, in_maps, core_ids=list(range(NCORES)))
    out = np.stack([np.asarray(r["y"]) for r in res.results], axis=0)
    return out.astype(np.float32)
```

```python
import contextlib
import numpy as np
import concourse.bass as bass
import concourse.mybir as mybir
from concourse.bass_utils import run_bass_kernel_spmd

F32 = mybir.dt.float32
BF16 = mybir.dt.bfloat16
AF = mybir.ActivationFunctionType
ALU = mybir.AluOpType
AX = mybir.AxisListType

D_MODEL = 1024
SEQ = 4096
BATCH = 4
DEPTH = 2
G = 512
D_IN = 7824
T = 256
NTB = T // 128
NCH = T // 64
NCORES = 4
NSLABBUF = 3
import os as _os
INORDER = tuple(_os.environ.get("INORDER", "pe").split(","))


class Buf:
    def __init__(self, name, tile, psum=False):
        self.name = name
        self.tile = tile
        self.psum = psum
        self.acc = []
        self.dma_sem = None
        self.dma_ops = []

    def __getitem__(self, idx):
        return Ref(self, self.tile[idx])


def _box(ap):
    pat = ap.ap
    pstep = pat[0][0]
    off = int(ap.offset)
    p0 = off // pstep if pstep else 0
    f0 = off % pstep if pstep else off
    ext = 0
    for st, cnt in pat[1:]:
        ext += abs(st) * (cnt - 1)
    return (p0, p0 + pat[0][1], f0, f0 + ext + 1)


class Ref:
    def __init__(self, buf, ap, box=None):
        self.buf = buf
        self.ap = ap
        if box is None:
            box = _box(ap)
            if buf.psum:
                box = ((box[0] // 32) * 32, ((box[1] + 31) // 32) * 32, 0, 512)
        self.box = box

    def bc(self, shape):
        return Ref(self.buf, self.ap.to_broadcast(list(shape)), self.box)

    def __getitem__(self, idx):
        return Ref(self.buf, self.ap[idx])

    def rr(self, pat, **kw):
        return Ref(self.buf, self.ap.rearrange(pat, **kw), self.box)


def _ovl(a, b):
    return a[0] < b[1] and b[0] < a[1] and a[2] < b[3] and b[2] < a[3]


def _cov(a, b):
    return a[0] <= b[0] and a[1] >= b[1] and a[2] <= b[2] and a[3] >= b[3]


class Prog:
    ENG = ("pe", "act", "dve", "pool", "sp")

    def __init__(self, nc, stack):
        self.nc = nc
        self.stack = stack
        self.ops = []
        self.nbuf = 0

    def sb(self, name, shape, dtype=F32):
        t = self.stack.enter_context(self.nc.sbuf_tensor("s_" + name, list(shape), dtype))
        return Buf(name, t)

    def ps(self, name):
        t = self.stack.enter_context(self.nc.psum_tensor("p_" + name, [128, 512], F32))
        return Buf(name, t, psum=True)

    def op(self, eng, fn, outs, ins, dma=False):
        oid = len(self.ops)
        deps = set()
        for r, w in [(x, True) for x in outs] + [(x, False) for x in ins]:
            if r is None or not isinstance(r, Ref):
                continue
            b = r.buf
            for (bx, o2, w2) in b.acc:
                if (w or w2) and _ovl(bx, r.box):
                    deps.add(o2)
        for r, w in [(x, True) for x in outs] + [(x, False) for x in ins]:
            if r is None or not isinstance(r, Ref):
                continue
            b = r.buf
            if w:
                b.acc = [a for a in b.acc if not _cov(r.box, a[0])]
            b.acc.append((r.box, oid, w))
            if len(b.acc) > 48:
                merged = {}
                for (bx, o2, w2) in b.acc:
                    k = (self.ops[o2]["eng"] if o2 < oid else eng, w2)
                    if k in merged:
                        m = merged[k]
                        merged[k] = ((min(m[0][0], bx[0]), max(m[0][1], bx[1]), min(m[0][2], bx[2]),
                                      max(m[0][3], bx[3])), max(m[1], o2), w2)
                    else:
                        merged[k] = (bx, o2, w2)
                b.acc = list(merged.values())
        dbuf = None
        if dma:
            for r in list(outs) + list(ins):
                if isinstance(r, Ref):
                    dbuf = r.buf
            dbuf.dma_ops.append(oid)
        deps.discard(oid)
        self.ops.append(dict(eng=eng, fn=fn, deps=sorted(deps), dma=dma, dbuf=dbuf, sig=False))
        return oid

    def emit(self):
        nc = self.nc
        ops = self.ops
        for o in ops:
            for d in o["deps"]:
                od = ops[d]
                if od["dma"]:
                    continue
                if od["eng"] == o["eng"] and o["eng"] in INORDER and not o["dma"]:
                    continue
                od["sig"] = True
        sems = {e: self.stack.enter_context(nc.semaphore("sem_" + e)) for e in self.ENG}
        cnt = {e: 0 for e in self.ENG}
        for o in ops:
            if o["dma"]:
                b = o["dbuf"]
                if b.dma_sem is None:
                    b.dma_sem = self.stack.enter_context(nc.semaphore("dsem_" + b.name))
            elif o["sig"]:
                cnt[o["eng"]] += 1
                o["signo"] = cnt[o["eng"]]
        per_eng = {e: [] for e in self.ENG}
        for i, o in enumerate(ops):
            per_eng[o["eng"]].append(i)
        import bisect

        def gen(engname, e):
            waited = {}
            for i in per_eng[engname]:
                o = ops[i]
                need = {}
                for d in o["deps"]:
                    od = ops[d]
                    if od["dma"]:
                        b = od["dbuf"]
                        n = bisect.bisect_left(b.dma_ops, i)
                        key = ("d", id(b))
                        if need.get(key, (None, 0))[1] < 16 * n:
                            need[key] = (b.dma_sem, 16 * n)
                    else:
                        if od["eng"] == engname and engname in INORDER and not o["dma"]:
                            continue
                        key = ("e", od["eng"])
                        if need.get(key, (None, 0))[1] < od["signo"]:
                            need[key] = (sems[od["eng"]], od["signo"])
                for key, (sem, val) in need.items():
                    if waited.get(key, 0) >= val:
                        continue
                    e.wait_ge(sem, val)
                    waited[key] = val
                ins = o["fn"](e)
                if o["dma"]:
                    ins.then_inc(o["dbuf"].dma_sem, 16)
                elif o["sig"]:
                    ins.then_inc(sems[engname], 1)

        with nc.Block() as block:
            @block.tensor
            def _(e):
                gen("pe", e)

            @block.scalar
            def _(e):
                gen("act", e)

            @block.vector
            def _(e):
                gen("dve", e)

            @block.gpsimd
            def _(e):
                gen("pool", e)

            @block.sync
            def _(e):
                gen("sp", e)
                seen = set()
                for o in ops:
                    if o["dma"] and id(o["dbuf"]) not in seen:
                        seen.add(id(o["dbuf"]))
                        e.wait_ge(o["dbuf"].dma_sem, 16 * len(o["dbuf"].dma_ops))


def _a(x):
    return x.ap if isinstance(x, Ref) else x


def _fm4(v):
    return np.ascontiguousarray(v.reshape(-1, 128).T)


class CMap:
    def __init__(self):
        self.off = {}
        self.n = 0

    def add(self, name, w):
        self.off[name] = (self.n, w)
        self.n += w


def build_cmap():
    c = CMap()
    c.add("gpre", 8)
    c.add("a_cw", 32); c.add("a_cb", 8); c.add("a_ib", 1); c.add("a_fb", 1); c.add("a_nw", 4)
    c.add("b_mu", 13); c.add("b_w0", 4); c.add("b_a0", 4); c.add("b_kk", 4); c.add("b_ka", 4)
    c.add("b_rk", 4); c.add("b_gw", 4); c.add("b_gb", 4)
    c.add("c_cw", 16); c.add("c_cb", 4); c.add("c_br", 4); c.add("c_bi", 4); c.add("c_lam", 4)
    c.add("d_nw", 4)
    return c


CM = build_cmap()

def _cols(a, b):
    return list(range(a, b))


def build_slabs():
    slabs = []
    def fm(name, start):
        slabs.append((name, _cols(start, start + 512)))
    fm("Aq", 0); fm("Ak", 512); fm("Az", 2048)
    slabs.append(("Ag", _cols(2560, 2576) + [-1] * (512 - 16)))
    fm("Br", 2576); fm("Bk", 3088); fm("Bv", 3600)
    slabs.append(("Bl", _cols(4112, 4240) + [-1] * (512 - 128)))
    fm("Bz", 4240); fm("Cx", 4752); fm("Cz", 5264); fm("Dz", 7312)
    fm("Av", 1024); fm("Ao", 1536); fm("Dq", 5776); fm("Dk", 6288); fm("Dv", 6800)
    return slabs


SLABS = build_slabs()
SLAB_ID = {s[0]: i for i, s in enumerate(SLABS)}
NSLAB = len(SLABS)


def host_prepare(inp):
    f = np.float32
    w_in = inp["w_in"]
    w_in_p = np.concatenate([w_in, np.zeros((DEPTH, D_MODEL, 1), f)], axis=2)
    wt = np.empty((DEPTH, NSLAB, 128, 8, 512), f)
    for si, (name, cols) in enumerate(SLABS):
        blk = w_in_p[:, :, cols]
        wt[:, si] = blk.reshape(DEPTH, 8, 128, 512).transpose(0, 2, 1, 3)
    w_out = inp["w_out"]
    wo = np.ascontiguousarray(w_out.reshape(DEPTH, 4, 4, 128, 1024).transpose(0, 1, 3, 2, 4))
    cv = np.zeros((DEPTH, 128, CM.n), f)
    def put(name, arr):
        o, w = CM.off[name]
        cv[:, :, o:o + w] = arr
    put("gpre", inp["norm_pre"].reshape(DEPTH, 8, 128).transpose(0, 2, 1))
    cw = inp["mlstm_conv_w"]
    put("a_cw", cw.reshape(DEPTH, 4, 8, 128).transpose(0, 3, 1, 2).reshape(DEPTH, 128, 32))
    put("a_cb", inp["mlstm_conv_b"].reshape(DEPTH, 8, 128).transpose(0, 2, 1))
    ib = np.zeros((DEPTH, 128, 1), f); ib[:, 0:8, 0] = inp["mlstm_i_bias"]; put("a_ib", ib)
    fb = np.zeros((DEPTH, 128, 1), f); fb[:, 0:8, 0] = inp["mlstm_f_bias"]; put("a_fb", fb)
    def fm4(x):
        return x.reshape(DEPTH, -1, 128).transpose(0, 2, 1)
    put("a_nw", fm4(inp["mlstm_norm_w"]))
    mu = inp["rwkv_mu"]
    put("b_mu", fm4(mu))
    for k, nm in [("b_w0", "rwkv_w0"), ("b_a0", "rwkv_a0"), ("b_kk", "rwkv_k_k"), ("b_ka", "rwkv_k_a"),
                  ("b_rk", "rwkv_r_k"), ("b_gw", "rwkv_gn_w"), ("b_gb", "rwkv_gn_b"),
                  ("c_cb", "lru_conv_b"), ("c_br", "lru_b_r"), ("c_bi", "lru_b_i"), ("c_lam", "lru_lambda"),
                  ("d_nw", "ret_norm_w")]:
        put(k, fm4(inp[nm]))
    lw = inp["lru_conv_w"]
    put("c_cw", lw.reshape(DEPTH, 4, 4, 128).transpose(0, 3, 1, 2).reshape(DEPTH, 128, 16))
    lora = np.concatenate([inp["rwkv_w_up"], inp["rwkv_a_up"]], axis=1)
    lru = np.zeros((DEPTH, 128, 2, 4, 128), f)
    for which, nm in enumerate(["lru_w_r", "lru_w_i"]):
        w = inp[nm]
        for g in range(4):
            for j in range(2):
                lru[:, 64 * j:64 * j + 64, which, g, 64 * j:64 * j + 64] = w[:, 2 * g + j]
    gpost = np.ascontiguousarray(np.broadcast_to(inp["norm_post"][:, None, :], (DEPTH, 128, D_MODEL)))
    return dict(wt=wt, wo=wo, cv=cv, lora=np.ascontiguousarray(lora), lru=lru, gpost=gpost)


def host_tables():
    f = np.float32
    t = {}
    t["ident"] = np.eye(128, dtype=f)
    hd = 64
    log_g = np.log1p(-np.exp2(-5.0 - np.arange(8, dtype=np.float64)))
    idx = np.arange(64, dtype=np.float64)
    dmat = np.exp(log_g[:, None, None] * np.abs(idx[:, None] - idx[None, :])) * hd ** -0.5
    xi = np.exp(log_g[:, None] * (idx + 1.0))
    zeta = np.exp(log_g[:, None] * (63.0 - idx)) * hd ** -0.5
    gch = np.exp(log_g * 64.0)
    hs2h = [2 * (s % 4) + (s // 4) for s in range(8)]
    dm = np.zeros((128, 8, 64));
    for s in range(8):
        dm[0:64, s] = dmat[hs2h[s]]; dm[64:128, s] = dmat[hs2h[s]]
    t["d_dmat"] = dm.astype(f)
    t["d_xi"] = np.tile(xi.T, (2, 1)).astype(f)
    t["d_zeta"] = np.tile(zeta.T, (2, 1)).astype(f)
    gc = np.zeros((128, 4))
    for g in range(4):
        for j in range(2):
            gc[64 * j:64 * j + 64, g] = gch[2 * g + j]
    t["d_gch"] = gc.astype(f)
    mk = np.zeros((128, 8, 64))
    sidx = np.arange(128) % 64
    mk[:] = np.where(sidx[:, None, None] <= np.arange(64)[None, None, :], 0.0, -30000.0)
    t["a_mask"] = mk.reshape(128, 512).astype(f)
    sel = np.zeros((8, 512 + 4 + 128))
    for hp_ in range(8):
        for hs in range(8):
            if hp_ == hs2h[hs]:
                sel[hp_, hs * 64:(hs + 1) * 64] = 1.0
        sel[hp_, 512 + hp_ // 2] = 1.0
        jj = hp_ % 2
        sel[hp_, 516 + 64 * jj:516 + 64 * jj + 64] = 1.0
    t["a_sel"] = sel.astype(f)
    bo = np.zeros((128, 128)); bo[0:64, 0:64] = 1.0; bo[64:128, 64:128] = 1.0
    t["b_ones"] = bo.astype(f)
    i2 = np.zeros((128, 64)); i2[0:64] = np.eye(64); i2[64:128] = np.eye(64)
    t["b_i2"] = i2.astype(f)
    rr_ = (np.arange(128) % 64)[:, None]; cc_ = np.arange(64)[None, :]
    m5 = np.zeros((128, 5, 64))
    m5[:, 0] = rr_ > cc_; m5[:, 1] = cc_ > rr_; m5[:, 2] = cc_ > rr_; m5[:, 3] = cc_ >= rr_; m5[:, 4] = cc_ >= rr_
    t["b_m5"] = m5.astype(f)
    half = 32
    pos = np.arange(SEQ, dtype=np.float32)
    inv_freq = (np.float32(10000.0) ** (-np.arange(half, dtype=np.float32) / np.float32(half))).astype(np.float32)
    ang = (pos[:, None] * inv_freq[None, :]).astype(np.float32).astype(np.float64)
    t["rope"] = np.concatenate([np.cos(ang), np.sin(ang)], axis=1).astype(f)
    return t


def build_program(ntiles=SEQ // T, nlayers=DEPTH, mixers="ABCD"):
    nc = bass.Bass("TRN2", target_bir_lowering=False)
    stack = contextlib.ExitStack()
    P = Prog(nc, stack)
    dram = {}

    def din(name, shape):
        dram[name] = nc.dram_tensor(name, list(shape), F32, kind="ExternalInput").ap()
        return dram[name]

    x_d = din("x", [SEQ, D_MODEL])
    wt_d = din("wt", [DEPTH, NSLAB, 128, 8, 512])
    wo_d = din("wo", [DEPTH, 4, 128, 4, 1024])
    cv_d = din("cv", [DEPTH, 128, CM.n])
    lora_d = din("lora", [DEPTH, 128, 512])
    lru_d = din("lru", [DEPTH, 128, 2, 4, 128])
    gpost_d = din("gpost", [DEPTH, 128, D_MODEL])
    ident_d = din("ident", [128, 128])
    dmat_d = din("d_dmat", [128, 8, 64])
    dxi_d = din("d_xi", [128, 8])
    dzeta_d = din("d_zeta", [128, 8])
    dgch_d = din("d_gch", [128, 4])
    rope_d = din("rope", [SEQ, 64])
    bones_d = din("b_ones", [128, 128])
    bi2_d = din("b_i2", [128, 64])
    bm5_d = din("b_m5", [128, 5, 64])
    amask_d = din("a_mask", [128, 512])
    asel_d = din("a_sel", [8, 644])
    y_d = nc.dram_tensor("y", [SEQ, D_MODEL], F32, kind="ExternalOutput").ap()

    ident = P.sb("ident", [128, 128])
    cv = [P.sb(f"cv{l}", [128, CM.n]) for l in range(DEPTH)]
    lru1 = P.sb("lru", [128, 2, 4, 128]); lru = [lru1, lru1]
    gpost1 = P.sb("gpost", [128, D_MODEL]); gpost = [gpost1, gpost1]
    lora_up = [P.sb(f"lora_up{l}", [128, 512]) for l in range(DEPTH)]
    b_ones = P.sb("b_ones", [128, 128]); b_i2 = P.sb("b_i2", [128, 1, 64]); b_m5 = P.sb("b_m5", [128, 5, 64])
    b_hist = [P.sb(f"b_hist{l}", [128, 13]) for l in range(DEPTH)]
    b_S = [P.sb(f"b_S{l}", [128, 4, 64]) for l in range(DEPTH)]
    ident16 = P.sb("ident16", [128, 128], BF16)
    AlT16 = P.sb("AlT16", [128, 4, T], BF16); RhT16 = P.sb("RhT16", [128, 4, T], BF16)
    bk16 = P.sb("bk16", [128, 4, T], BF16)
    bm16 = [P.sb(f"bm16_{i}", [128, 8, 64], BF16) for i in range(3)]
    b_S16 = P.sb("b_S16", [128, 4, 64], BF16)
    a_n16 = P.sb("a_n16", [128, 4, 1], BF16); wj16 = P.sb("wj16", [128, NTB, 8], BF16)
    ones16 = P.sb("ones16", [128, 64], BF16); eps12 = P.sb("eps12", [128, 1])
    gLt = P.sb("gLt", [128, 4, NCH]); PRall = [P.sb(f"PR{i}", [128, 5, 8, 64], BF16) for i in range(NTB)]; Y2 = P.sb("Y2", [128, 8, 64])
    K2g = P.sb("K2g", [128, 2, 4, 64])
    ones = P.sb("ones", [128, T]); zeros = P.sb("zeros", [128, T])
    xt = [P.sb(f"xt{i}", [128, NTB, D_MODEL]) for i in range(1)]
    hT = P.sb("hT", [128, 8, T], BF16)
    yT = P.sb("yT", [128, 16, T], BF16)
    slab = [P.sb(f"slab{i}", [128, 8, 512], BF16) for i in range(NSLABBUF)]
    sm = P.sb("small", [128, 64])
    scr = P.sb("scr", [128, D_MODEL])
    scr2 = P.sb("scr2", [128, D_MODEL])
    FS = [P.sb(f"fs{i}", [128, 4, T + 4]) for i in range(10)]
    c_hist = [P.sb(f"c_hist{l}", [128, 4, 3]) for l in range(DEPTH)]
    c_state = [P.sb(f"c_state{l}", [128, 4]) for l in range(DEPTH)]
    c_coef = [P.sb(f"c_coef{l}", [128, 8]) for l in range(DEPTH)]
    PB = [P.ps(f"pb{i}") for i in range(8)]
    TM = [P.sb(f"tm{i}", [128, NTB, 8, 64]) for i in range(6)]
    TM16 = [P.sb(f"tm16_{i}", [128, NTB, 8, 64], BF16) for i in range(3)]
    d_dmat = P.sb("d_dmat", [128, 8, 64]); d_xi = P.sb("d_xi", [128, 4, 2, 1]); d_zeta = P.sb("d_zeta", [128, 8, 1])
    d_gch = P.sb("d_gch", [128, 4, 1]); rope_sb = P.sb("rope_sb", [128, NTB, 1, 64])
    d_R = [P.sb(f"d_R{l}", [128, 4, 64]) for l in range(DEPTH)]
    a_mask = P.sb("a_mask", [128, 512]); a_sel = P.sb("a_sel", [8, 644])
    ga = P.sb("ga", [8, 12, T + 1]); scl = P.sb("scl", [8, NCH]); rhs_sc = P.sb("rhs_sc", [8, NCH, 4])
    scb = P.sb("scb", [128, NCH, 4, 1]); negPx = P.sb("negPx", [8, 2, 8, 64]); ET = P.sb("ET", [128, 8, 64])
    ET1 = P.sb("ET1", [128, 8, 64])
    tmsc = P.sb("tmsc", [128, NTB, 3, 8]); hnum = P.sb("hnum", [128, 8, 64]); vw = P.sb("vw", [128, 8, 64])
    a_hist = [P.sb(f"a_hist{l}", [128, 8, 3]) for l in range(DEPTH)]
    a_C = [P.sb(f"a_C{l}", [128, 4, 64]) for l in range(DEPTH)]
    a_n = [P.sb(f"a_n{l}", [128, 4, 1]) for l in range(DEPTH)]
    a_car = [P.sb(f"a_car{l}", [8, 4]) for l in range(DEPTH)]
    AT = P.sb("AT", [128, 8, 64]); tmp4 = P.sb("tmp4", [128, 4, 2, 64]); lnst = P.sb("lnst", [128, 64]); lnst_b = P.sb("lnst_b", [128, 40]); lnsts = [lnst, lnst_b]
    state = dict(slab_i=0, pb_i=0)

    import os
    cut = int(os.environ.get("KCUT", "9"))
    def dma(out, in_, eng="sp"):
        return P.op(eng, lambda e: e.dma_start(out=_a(out), in_=_a(in_)), [out], [in_], dma=True)

    def mm(out, lhsT, rhs, start=True, stop=True):
        P.op("pe", lambda e: e.matmul(_a(out), _a(lhsT), _a(rhs), start=start, stop=stop),
             [out], [lhsT, rhs] + ([] if start else [out]))

    def tr(out, in_, idn):
        P.op("pe", lambda e: e.transpose(_a(out), _a(in_), _a(idn)), [out], [in_, idn])

    def act(out, in_, func, bias=None, scale=1.0, accum=None, eng="act"):
        kw = {}
        if bias is not None:
            kw["bias"] = _a(bias)
        if accum is not None:
            kw["accum_out"] = _a(accum)
        P.op("act", lambda e: e.activation(_a(out), _a(in_), func, scale=_a(scale), **kw),
             [out, accum], [in_, bias, scale])

    def tt(out, a, b, op, eng="dve"):
        P.op(eng, lambda e: e.tensor_tensor(_a(out), _a(a), _a(b), op), [out], [a, b])

    def ts(out, a, s1, s2, op0, op1=ALU.bypass, eng="dve"):
        P.op(eng, lambda e: e.tensor_scalar(_a(out), _a(a), _a(s1), _a(s2), op0, op1), [out], [a, s1, s2])

    def stt(out, a, s, b, op0, op1):
        P.op("dve", lambda e: e.scalar_tensor_tensor(_a(out), _a(a), _a(s), _a(b), op0, op1), [out], [a, s, b])

    def scan(out, d0, d1, init, op0, op1):
        P.op("dve", lambda e: e.tensor_tensor_scan(_a(out), _a(d0), _a(d1), _a(init), op0, op1),
             [out], [d0, d1, init])

    def cp(out, in_, eng="dve"):
        if isinstance(in_, Ref) and in_.buf.psum and eng == "dve":
            return act(out, in_, AF.Copy)
        P.op(eng, lambda e: e.tensor_copy(_a(out), _a(in_)), [out], [in_])

    def recip(out, in_):
        P.op("dve", lambda e: e.reciprocal(_a(out), _a(in_)), [out], [in_])

    def memset(out, val, eng="dve"):
        P.op(eng, lambda e: e.memset(_a(out), val), [out], [])

    def reduce(out, in_, op, axis=AX.X):
        P.op("dve", lambda e: e.tensor_reduce(_a(out), _a(in_), axis, op), [out], [in_])

    def interleave(gens):
        gens = list(gens)
        while gens:
            for gz in list(gens):
                try:
                    next(gz)
                except StopIteration:
                    gens.remove(gz)

    def run_chunks(make_gen):
        gens = [make_gen(c) for c in range(NCH)]

        def step(c):
            try:
                next(gens[c])
            except StopIteration:
                pass
        pairs = NCH // 2
        step(0); step(1)
        for k in range(pairs):
            a, b = 2 * k, 2 * k + 1
            step(a); step(b)
            step(a); step(b)
            step(a); step(a)
            if k + 1 < pairs:
                step(2 * k + 2)
            step(b); step(b)
            if k + 1 < pairs:
                step(2 * k + 3)

    def next_pb():
        state["pb_i"] = (state["pb_i"] + 1) % 2
        return PB[state["pb_i"]]

    def load_slab(l, name, ncols=512):
        s = slab[state["slab_i"]]
        state["slab_i"] = (state["slab_i"] + 1) % NSLABBUF
        dma(s[:, :, 0:ncols], wt_d[l, SLAB_ID[name], :, :, 0:ncols], eng="pool")
        return s

    def load_wo(l, si):
        s = slab[state["slab_i"]]
        state["slab_i"] = (state["slab_i"] + 1) % NSLABBUF
        dma(s[:, :, :], wo_d[l, si].rearrange("p f n -> p (f n)").rearrange("p (a b) -> p a b", a=8), eng="pool")
        return s

    def cvc(l, name, i=0):
        o, w = CM.off[name]
        return cv[l][:, o + i:o + i + 1]

    def fm_proj(s, gi, out_cols=T):
        pb = next_pb()
        for kc in range(8):
            mm(pb[:, 0:T], s[:, kc, gi * 128:(gi + 1) * 128], hT[:, kc, :], start=(kc == 0), stop=(kc == 7))
        return pb

    dma(ident[:, :], ident_d)
    for l in range(DEPTH):
        dma(cv[l][:, :], cv_d[l])
        dma(lora_up[l][:, :], lora_d[l])
        memset(b_hist[l][:, :], 0.0)
        memset(b_S[l][:, :, :], 0.0)
    dma(d_dmat[:, :, :], dmat_d)
    dma(d_xi[:, :, :, :], dxi_d.rearrange("p (g j o) -> p g j o", g=4, j=2))
    dma(d_zeta[:, :, :], dzeta_d.rearrange("p (h o) -> p h o", o=1))
    dma(d_gch[:, :, :], dgch_d.rearrange("p (g o) -> p g o", o=1))
    dma(a_mask[:, :], amask_d)
    dma(a_sel[:, :], asel_d)
    for l in range(DEPTH):
        memset(d_R[l][:, :, :], 0.0)
        memset(a_hist[l][:, :, :], 0.0)
        memset(a_C[l][:, :, :], 0.0)
        memset(a_n[l][:, :, :], 0.0)
        memset(a_car[l][:, :], 0.0)
        ts(a_car[l][0:8, 2:3], cv[l][0:8, CM.off["a_fb"][0]:CM.off["a_fb"][0] + 1], -1.0, None, ALU.mult)
    dma(b_ones[:, :], bones_d)
    dma(b_i2[:, 0, :], bi2_d)
    dma(b_m5[:, :, :], bm5_d)
    memset(ones[:, :], 1.0)
    cp(ident16[:, :], ident[:, :])
    memset(ones16[:, :], 1.0)
    memset(eps12[:, :], 1e-12)
    memset(zeros[:, :], 0.0)
    for l in range(DEPTH):
        memset(c_hist[l][:, :, :], 0.0)
        memset(c_state[l][:, :], 0.0)
        o, w = CM.off["c_lam"]
        act(sm[:, 0:4], cv[l][:, o:o + 4], AF.Exp, scale=-1.0)
        act(sm[:, 4:8], sm[:, 0:4], AF.Ln, bias=ones[:, 0:1])
        ts(c_coef[l][:, 0:4], sm[:, 4:8], -8.0, None, ALU.mult)
        ts(c_coef[l][:, 4:8], sm[:, 4:8], -16.0, None, ALU.mult)

    def rmsnorm_pre(l, X):
        scrs = [scr, scr2]

        def stream(tb):
            sc = scrs[tb % 2]
            act(sc[:, :], X[:, tb, :], AF.Square)
            yield
            reduce(sm[:, 8 + tb:9 + tb], sc[:, :], ALU.add)
            yield
            ts(sm[:, 12 + tb:13 + tb], sm[:, 8 + tb:9 + tb], 1.0 / D_MODEL, 1e-6, ALU.mult, ALU.add)
            yield
            act(sm[:, 16 + tb:17 + tb], sm[:, 12 + tb:13 + tb], AF.Sqrt)
            yield
            recip(sm[:, 20 + tb:21 + tb], sm[:, 16 + tb:17 + tb])
            yield
            act(sc[:, :], X[:, tb, :], AF.Copy, scale=sm[:, 20 + tb:21 + tb])
            yield
            for half in range(2):
                pb = next_pb()
                for q in range(4):
                    kc = half * 4 + q
                    tr(pb[:, q * 128:(q + 1) * 128], sc[:, kc * 128:(kc + 1) * 128], ident[:, :])
                for q in range(4):
                    kc = half * 4 + q
                    ts(hT[:, kc, tb * 128:(tb + 1) * 128], pb[:, q * 128:(q + 1) * 128],
                       cvc(l, "gpre", kc), None, ALU.mult)
                yield
        interleave([stream(tb) for tb in range(NTB)])

    def mixer_C(l):
        cx, xc, rg, ig, aa, uu, hh = FS[0], FS[1], FS[2], FS[3], FS[4], FS[5], FS[6]
        dma(lru[l][:, :, :, :], lru_d[l])
        wx = load_slab(l, "Cx")

        def cstream(g):
            pb = fm_proj(wx, g)
            cp(cx[:, g, 0:3], c_hist[l][:, g, :])
            act(cx[:, g, 3:3 + T], pb[:, 0:T], AF.Copy)
            yield
            cp(c_hist[l][:, g, :], cx[:, g, T:T + 3])
            o, w = CM.off["c_cw"]
            ts(xc[:, g, 0:T], cx[:, g, 0:T], cv[l][:, o + g:o + g + 1], cvc(l, "c_cb", g), ALU.mult, ALU.add)
            yield
            for j in range(1, 4):
                stt(xc[:, g, 0:T], cx[:, g, j:j + T], cv[l][:, o + 4 * j + g:o + 4 * j + g + 1], xc[:, g, 0:T],
                    ALU.mult, ALU.add)
                yield
        interleave([cstream(g) for g in range(4)])
        for g in range(4):
            pb = next_pb()
            mm(pb[:, 0:T], lru[l][:, 0, g, :], xc[:, g, 0:T])
            act(rg[:, g, 0:T], pb[:, 0:T], AF.Sigmoid, bias=cvc(l, "c_br", g))
            pb = next_pb()
            mm(pb[:, 0:T], lru[l][:, 1, g, :], xc[:, g, 0:T])
            act(ig[:, g, 0:T], pb[:, 0:T], AF.Sigmoid, bias=cvc(l, "c_bi", g))
        for g in range(4):
            act(aa[:, g, 0:T], rg[:, g, 0:T], AF.Exp, scale=c_coef[l][:, g:g + 1])
            act(uu[:, g, 0:T], rg[:, g, 0:T], AF.Exp, scale=c_coef[l][:, 4 + g:5 + g])
        ts(uu[:, :, 0:T], uu[:, :, 0:T], -1.0, 1.0, ALU.mult, ALU.add)
        act(uu[:, :, 0:T], uu[:, :, 0:T], AF.Sqrt)
        tt(ig[:, :, 0:T], ig[:, :, 0:T], xc[:, :, 0:T], ALU.mult)
        tt(uu[:, :, 0:T], uu[:, :, 0:T], ig[:, :, 0:T], ALU.mult)
        for g in range(4):
            scan(hh[:, g, 0:T], aa[:, g, 0:T], uu[:, g, 0:T], c_state[l][:, g:g + 1], ALU.mult, ALU.add)
            cp(c_state[l][:, g:g + 1], hh[:, g, T - 1:T])
        wz = load_slab(l, "Cz")
        for g in range(4):
            pb = fm_proj(wz, g)
            act(rg[:, g, 0:T], pb[:, 0:T], AF.Silu)
            tt(yT[:, 8 + g, :], hh[:, g, 0:T], rg[:, g, 0:T], ALU.mult)

    def tm_proj(sl, tb):
        pb = next_pb()
        for kc in range(8):
            mm(pb[:, :], hT[:, kc, tb * 128:(tb + 1) * 128], sl[:, kc, :], start=(kc == 0), stop=(kc == 7))
        return pb

    def to_fm(dst, src, tb):
        pb = next_pb()
        for g in range(4):
            tr(pb[:, g * 128:(g + 1) * 128], src[:, tb, 2 * g:2 * g + 2, 0:64], ident[:, :])
        cp(dst[:, 0:4, tb * 128:(tb + 1) * 128], pb[:, :].rr("p (a b) -> p a b", a=4))

    def to_tm(dst, src, tb):
        pb = next_pb()
        for g in range(4):
            tr(pb[:, g * 128:(g + 1) * 128], src[:, g, tb * 128:(tb + 1) * 128], ident[:, :])
        cp(dst[:, tb, :, 0:64], pb[:, :].rr("p (h e) -> p h e", h=8))

    def head_ln_stream(dst, src, tb):
        X3 = src[:, tb, :, 0:64]
        D3 = dst[:, tb, :, 0:64]
        st_ = lnsts[tb % 2]
        sq = scr[:, (tb % 2) * 512:(tb % 2) * 512 + 512].rr("p (h e) -> p h e", h=8)
        reduce(st_[:, 0:8], X3, ALU.add)
        yield
        stt(D3, st_[:, 0:8].rr("p (h o) -> p h o", o=1).bc([128, 8, 64]), -1.0 / 64, X3, ALU.mult, ALU.add)
        yield
        tt(sq, D3, D3, ALU.mult)
        yield
        reduce(st_[:, 8:16], sq, ALU.add)
        yield
        ts(st_[:, 16:24], st_[:, 8:16], 1.0 / 64, 1e-5, ALU.mult, ALU.add)
        yield
        act(st_[:, 24:32], st_[:, 16:24], AF.Sqrt)
        yield
        recip(st_[:, 32:40], st_[:, 24:32])
        yield
        tt(D3, D3, st_[:, 32:40].rr("p (h o) -> p h o", o=1).bc([128, 8, 64]), ALU.mult)
        yield

    def head_ln_all(dst, src):
        interleave([head_ln_stream(dst, src, tb) for tb in range(NTB)])

    def mixer_D(l, ti):
        qr, kr, osb, xn = TM[0], TM[1], TM[4], TM[5]
        vv, kz = TM16[1], TM16[2]
        qT, kT, sz = AlT16, RhT16, FS[2]
        AT = bm16[0]
        d_R16 = b_S16
        cp(d_R16[:, :, :], d_R[l][:, :, :])
        S = [PB[3], PB[4]]; O = [PB[5], PB[6]]; ST = [PB[2], PB[7]]
        dma(rope_sb[:, :, :, :], rope_d.rearrange("(n tb p) (o e) -> n p tb o e", tb=NTB, p=128, o=1)[ti])
        for name, dst in (("Dq", qr), ("Dk", kr)):
            sl = load_slab(l, name)
            for tb in range(NTB):
                pb = tm_proj(sl, tb)
                act(scr[:, 0:512], pb[:, :], AF.Copy)
                raw = scr[:, 0:512].rr("p (h e) -> p h e", h=8)
                x1, x2 = raw[:, :, 0:32], raw[:, :, 32:64]
                cos = rope_sb[:, tb, :, 0:32].bc([128, 8, 32]); sin = rope_sb[:, tb, :, 32:64].bc([128, 8, 32])
                t1 = scr2[:, 0:256].rr("p (h e) -> p h e", h=8); t2 = scr2[:, 256:512].rr("p (h e) -> p h e", h=8)
                d1, d2 = dst[:, tb, :, 0:32], dst[:, tb, :, 32:64]
                tt(d1, x1, cos, ALU.mult); tt(t1, x2, sin, ALU.mult); tt(d1, d1, t1, ALU.subtract)
                tt(d2, x2, cos, ALU.mult); tt(t2, x1, sin, ALU.mult); tt(d2, d2, t2, ALU.add)
                to_fm(qT if name == "Dq" else kT, dst, tb)
        sl = load_slab(l, "Dv")
        for tb in range(NTB):
            pb = tm_proj(sl, tb)
            act(vv[:, tb, :, 0:64], pb[:, :].rr("p (h e) -> p h e", h=8), AF.Copy)
            tt(kz[:, tb, :, 0:64], kr[:, tb, :, 0:64], d_zeta[:, :, :].bc([128, 8, 64]), ALU.mult)
        sl = load_slab(l, "Dz")
        for g in range(4):
            pb = fm_proj(sl, g)
            act(sz[:, g, 0:T], pb[:, 0:T], AF.Silu)
        def d_chunk(c):
            tb, p = c // 2, c % 2
            rows = slice(64 * p, 64 * p + 64)
            cols = slice(c * 64, (c + 1) * 64)
            for j in range(2):
                jr = slice(64 * j, 64 * j + 64)
                for g in range(4):
                    mm(S[j][rows, g * 64:(g + 1) * 64], kT[jr, g, cols], qT[jr, g, cols])
            for h in range(8):
                g, j = h // 2, h % 2
                mm(ST[p][64 * j:64 * j + 64, g * 64:(g + 1) * 64], kz[rows, tb, h, 0:64], vv[rows, tb, h, 0:64])
            yield
            for j in range(2):
                tt(AT[rows, 4 * j:4 * j + 4, :], S[j][rows, 0:256].rr("p (a b) -> p a b", a=4),
                   d_dmat[rows, 4 * j:4 * j + 4, :], ALU.mult)
            yield
            for h in range(8):
                g, j = h // 2, h % 2
                mm(O[p][rows, h * 64:(h + 1) * 64], AT[rows, j * 4 + g, :], vv[rows, tb, h, 0:64])
            yield
            for j in range(2):
                jr = slice(64 * j, 64 * j + 64)
                for g in range(4):
                    mm(S[j][rows, 256 + g * 64:256 + (g + 1) * 64], qT[jr, g, cols], d_R16[jr, g, :])
            yield
            tt(d_R[l][:, :, :], d_R[l][:, :, :], d_gch[:, :, :].bc([128, 4, 64]), ALU.mult)
            tt(d_R[l][:, :, :], d_R[l][:, :, :], ST[p][:, 0:256].rr("p (a b) -> p a b", a=4), ALU.add)
            for j in range(2):
                tt(tmp4[rows, :, j, 0:64], S[j][rows, 256:512].rr("p (a b) -> p a b", a=4),
                   d_xi[rows, :, j, :].bc([64, 4, 64]), ALU.mult)
            cp(d_R16[:, :, :], d_R[l][:, :, :])
            tt(osb[rows, tb, :, 0:64], O[p][rows, :].rr("p (h e) -> p h e", h=8),
               tmp4[rows, :, :, 0:64].rr("p g j e -> p (g j) e"), ALU.add)
            yield
        run_chunks(d_chunk)
        head_ln_all(xn, osb)
        for tb in range(NTB):
            pb = next_pb()
            for g in range(4):
                tr(pb[:, g * 128:(g + 1) * 128], xn[:, tb, 2 * g:2 * g + 2, 0:64], ident[:, :])
            for g in range(4):
                stt(yT[:, 12 + g, tb * 128:(tb + 1) * 128], pb[:, g * 128:(g + 1) * 128], cvc(l, "d_nw", g),
                    sz[:, g, tb * 128:(tb + 1) * 128], ALU.mult, ALU.mult)

    def conv_silu(l, sl, g8, dst, gdst, cx, whichhist, cwname, cbname, ngroups_total, silu=True):
        pass

    def mixer_A(l, ti):
        cx, qT32, kT32, sz = FS[0], FS[1], FS[2], FS[3]
        qT, kT = AlT16, RhT16
        so, osb, xn = TM[2], TM[3], TM[4]
        ktm, vv = TM16[0], TM16[1]
        AT, vw = bm16[0], bm16[1]
        a_C16 = b_S16
        cp(a_C16[:, :, :], a_C[l][:, :, :])
        cp(a_n16[:, :, :], a_n[l][:, :, :])
        S = [PB[3], PB[4]]; O = [PB[5], PB[6]]; ST = [PB[2], PB[7]]; JX = [PB[0], PB[1]]
        R_I, R_SP, R_F, R_G, R_P, R_MU, R_PE, R_SI, R_WJ, R_NM, R_NP = range(11)
        ocw = CM.off["a_cw"][0]
        for which, (name, dstT, dst16) in enumerate((("Aq", qT32, qT), ("Ak", kT32, kT))):
            sl = load_slab(l, name)

            def astream(g, which=which, sl=sl, dstT=dstT, dst16=dst16):
                g8 = which * 4 + g
                pb = fm_proj(sl, g)
                cp(cx[:, g, 0:3], a_hist[l][:, g8, :])
                act(cx[:, g, 3:3 + T], pb[:, 0:T], AF.Copy)
                yield
                cp(a_hist[l][:, g8, :], cx[:, g, T:T + 3])
                ts(dstT[:, g, 0:T], cx[:, g, 0:T], cv[l][:, ocw + g8:ocw + g8 + 1], cvc(l, "a_cb", g8),
                   ALU.mult, ALU.add)
                yield
                for j in range(1, 4):
                    stt(dstT[:, g, 0:T], cx[:, g, j:j + T], cv[l][:, ocw + 8 * j + g8:ocw + 8 * j + g8 + 1],
                        dstT[:, g, 0:T], ALU.mult, ALU.add)
                    yield
                act(dst16[:, g, 0:T], dstT[:, g, 0:T], AF.Silu)
                yield
            interleave([astream(g) for g in range(4)])
        sl = load_slab(l, "Az")
        for g in range(4):
            pb = fm_proj(sl, g)
            act(sz[:, g, 0:T], pb[:, 0:T], AF.Silu)
        acut = int(os.environ.get("ACUT", "99"))
        if acut < 1:
            return
        sl = load_slab(l, "Ag", ncols=128)
        pbi = next_pb()
        for kc in range(8):
            mm(pbi[0:8, 0:T], sl[:, kc, 0:8], hT[:, kc, :], start=(kc == 0), stop=(kc == 7))
        act(ga[0:8, R_I, 0:T], pbi[0:8, 0:T], AF.Identity, bias=cv[l][0:8, CM.off["a_ib"][0]:CM.off["a_ib"][0] + 1])
        gcut = int(os.environ.get("GCUT", "99"))
        if gcut < 1:
            return
        pbf = next_pb()
        for kc in range(8):
            mm(pbf[0:8, 0:T], sl[:, kc, 8:16], hT[:, kc, :], start=(kc == 0), stop=(kc == 7))
        act(ga[0:8, R_SP, 0:T], pbf[0:8, 0:T], AF.Exp, bias=a_car[l][0:8, 2:3], scale=-1.0)
        act(ga[0:8, R_SP, 0:T], ga[0:8, R_SP, 0:T], AF.Ln, bias=ones[0:8, 0:1])
        if gcut < 2:
            return
        scan(ga[0:8, R_F, 0:T], ones[0:8, 0:T], ga[0:8, R_SP, 0:T], a_car[l][0:8, 0:1], ALU.mult, ALU.subtract)
        cp(a_car[l][0:8, 0:1], ga[0:8, R_F, T - 1:T])
        tt(ga[0:8, R_G, 0:T], ga[0:8, R_I, 0:T], ga[0:8, R_F, 0:T], ALU.subtract)
        if gcut < 3:
            return
        cp(ga[0:8, R_P, 0:1], a_car[l][0:8, 1:2])
        scan(ga[0:8, R_P, 1:T + 1], ones[0:8, 0:T], ga[0:8, R_G, 0:T], a_car[l][0:8, 1:2], ALU.mult, ALU.max)
        cp(a_car[l][0:8, 1:2], ga[0:8, R_P, T:T + 1])
        if gcut < 4:
            return
        for c in range(NCH):
            cols = slice(c * 64, (c + 1) * 64)
            ts(ga[0:8, R_MU, cols], zeros[0:8, 0:64], ga[0:8, R_P, c * 64:c * 64 + 1], None, ALU.add)
            ts(ga[0:8, R_PE, cols], zeros[0:8, 0:64], ga[0:8, R_P, c * 64 + 64:c * 64 + 65], None, ALU.add)
            tt(scl[0:8, c:c + 1], ga[0:8, R_P, c * 64:c * 64 + 1], ga[0:8, R_P, c * 64 + 64:c * 64 + 65], ALU.subtract)
        if gcut < 5:
            return
        Pv = ga[0:8, R_P, 1:T + 1]
        tt(ga[0:8, R_SI, 0:T], ga[0:8, R_MU, 0:T], Pv, ALU.subtract)
        tt(ga[0:8, R_WJ, 0:T], ga[0:8, R_G, 0:T], ga[0:8, R_PE, 0:T], ALU.subtract)
        stt(ga[0:8, R_NM, 0:T], ga[0:8, R_F, 0:T], -1.0, Pv, ALU.mult, ALU.subtract)
        ts(ga[0:8, R_NP, 0:T], Pv, -1.0, None, ALU.mult)
        if acut < 2:
            return
        for tb in range(NTB):
            pb = next_pb()
            for r, R in enumerate((R_SI, R_WJ, R_NM)):
                tr(pb[:, r * 8:(r + 1) * 8], ga[0:8, R, tb * 128:(tb + 1) * 128], ident[0:8, 0:8])
            act(tmsc[:, tb, :, :], pb[:, 0:24].rr("p (a b) -> p a b", a=3), AF.Exp)
        if acut < 3:
            return
        tt(rhs_sc[0:8, :, :], scl[0:8, :].rr("p (c o) -> p c o", o=1).bc([8, NCH, 4]),
           a_sel[0:8, 512:516].rr("p (o g) -> p o g", o=1).bc([8, NCH, 4]), ALU.mult)
        pb = next_pb()
        mm(pb[:, 0:NCH * 4], a_sel[0:8, 516:644], rhs_sc[0:8, :, :].rr("p c g -> p (c g)"))
        act(scb[:, :, :, :].rr("p c g o -> p (c g o)"), pb[:, 0:NCH * 4], AF.Exp)
        if acut < 4:
            return
        for tb in range(NTB):
            pbt = next_pb()
            for g in range(4):
                mm(pbt[:, g * 128:(g + 1) * 128], kT[:, g, tb * 128:(tb + 1) * 128], ident16[:, :])
            cp(ktm[:, tb, :, :], pbt[:, :].rr("p (h e) -> p h e", h=8))
            cp(wj16[:, tb, :], tmsc[:, tb, 1, :])
        sl = load_slab(l, "Av")
        for tb in range(NTB):
            pb = tm_proj(sl, tb)
            act(vv[:, tb, :, :], pb[:, :].rr("p (h e) -> p h e", h=8), AF.Copy)
        sl = load_slab(l, "Ao")
        for tb in range(NTB):
            pb = tm_proj(sl, tb)
            act(so[:, tb, :, :], pb[:, :].rr("p (h e) -> p h e", h=8), AF.Sigmoid)
        ETs = [ET, ET1]
        for tb in range(NTB):
            E = next_pb()
            mm(E[:, :], ident[:, :], a_mask[:, :], start=True, stop=False)
            mm(E[:, :], ga[0:8, R_G, tb * 128:(tb + 1) * 128], a_sel[0:8, 0:512], start=False, stop=False)
            for p in range(2):
                c = tb * 2 + p
                tt(negPx[0:8, p, :, :], ga[0:8, R_NP:R_NP + 1, c * 64:(c + 1) * 64].bc([8, 8, 64]),
                   a_sel[0:8, 0:512].rr("p (h e) -> p h e", h=8), ALU.mult)
                mm(E[64 * p:64 * p + 64, :], ones[0:8, 0:64], negPx[0:8, p, :, :].rr("p h e -> p (h e)"),
                   start=False, stop=True)
            act(ETs[tb][:, :, :].rr("p h e -> p (h e)"), E[:, :], AF.Exp)

        def a_chunk(c):
            tb, p = c // 2, c % 2
            rows = slice(64 * p, 64 * p + 64)
            cols = slice(c * 64, (c + 1) * 64)
            ETt = ETs[tb]
            sI4 = tmsc[:, tb, 0, :].rr("p (g j o) -> p g j o", g=4, j=2)
            for j in range(2):
                jr = slice(64 * j, 64 * j + 64)
                for g in range(4):
                    mm(S[j][rows, g * 64:(g + 1) * 64], kT[jr, g, cols], qT[jr, g, cols])
            tt(vw[rows, :, :], vv[rows, tb, :, :],
               tmsc[rows, tb, 1, :].rr("p (h o) -> p h o", o=1).bc([64, 8, 64]), ALU.mult)
            yield
            for j in range(2):
                stt(AT[rows, 4 * j:4 * j + 4, :], S[j][rows, 0:256].rr("p (a b) -> p a b", a=4), 0.125,
                    ETt[rows, 4 * j:4 * j + 4, :], ALU.mult, ALU.mult)
            yield
            for h in range(8):
                g, j = h // 2, h % 2
                mm(O[p][rows, h * 64:(h + 1) * 64], AT[rows, j * 4 + g, :], vv[rows, tb, h, :])
                mm(ST[p][rows, 384 + h:385 + h], AT[rows, j * 4 + g, :], ones16[rows, 0:1])
            for h in range(8):
                g, j = h // 2, h % 2
                jr = slice(64 * j, 64 * j + 64)
                mm(ST[p][jr, g * 64:(g + 1) * 64], ktm[rows, tb, h, :], vw[rows, h, :])
                mm(ST[p][jr, 256 + g:257 + g], ktm[rows, tb, h, :], wj16[rows, tb, h:h + 1])
            yield
            for j in range(2):
                jr = slice(64 * j, 64 * j + 64)
                for g in range(4):
                    mm(JX[j][rows, g * 64:(g + 1) * 64], qT[jr, g, cols], a_C16[jr, g, :])
                    mm(JX[j][rows, 256 + g:257 + g], qT[jr, g, cols], a_n16[jr, g, :])
            yield
            for j in range(2):
                tt(tmp4[rows, :, j, :], JX[j][rows, 0:256].rr("p (a b) -> p a b", a=4),
                   sI4[rows, :, j, :].bc([64, 4, 64]), ALU.mult)
                tt(lnst[rows, 40:48].rr("p (g j) -> p g j", j=2)[:, :, j], JX[j][rows, 256:260],
                   tmsc[rows, tb, 0, :].rr("p (g j) -> p g j", j=2)[:, :, j], ALU.mult)
            tt(a_C[l][:, :, :], a_C[l][:, :, :], scb[:, c, :, :].bc([128, 4, 64]), ALU.mult)
            stt(a_C[l][:, :, :], ST[p][:, 0:256].rr("p (a b) -> p a b", a=4), 0.125, a_C[l][:, :, :],
                ALU.mult, ALU.add)
            tt(a_n[l][:, :, :], a_n[l][:, :, :], scb[:, c, :, :], ALU.mult)
            stt(a_n[l][:, :, :], ST[p][:, 256:260].rr("p (a b) -> p a b", b=1), 0.125, a_n[l][:, :, :],
                ALU.mult, ALU.add)
            cp(a_C16[:, :, :], a_C[l][:, :, :])
            cp(a_n16[:, :, :], a_n[l][:, :, :])
            tt(hnum[rows, :, :], O[p][rows, :].rr("p (h e) -> p h e", h=8),
               tmp4[rows, :, :, :].rr("p g j e -> p (g j) e"), ALU.add)
            tt(lnst[rows, 48:56], ST[p][rows, 384:392], lnst[rows, 40:48], ALU.add)
            stt(lnst[rows, 48:56], lnst[rows, 48:56], -1.0, lnst[rows, 48:56], ALU.mult, ALU.max)
            tt(lnst[rows, 48:56], lnst[rows, 48:56], tmsc[rows, tb, 2, :], ALU.max)
            recip(lnst[rows, 56:64], lnst[rows, 48:56])
            tt(hnum[rows, :, :], hnum[rows, :, :],
               lnst[rows, 56:64].rr("p (h o) -> p h o", o=1).bc([64, 8, 64]), ALU.mult)
            tt(osb[rows, tb, :, :], hnum[rows, :, :], so[rows, tb, :, :], ALU.mult)
            yield
        run_chunks(a_chunk)
        head_ln_all(xn, osb)
        for tb in range(NTB):
            pb = next_pb()
            for g in range(4):
                tr(pb[:, g * 128:(g + 1) * 128], xn[:, tb, 2 * g:2 * g + 2, 0:64], ident[:, :])
            for g in range(4):
                stt(yT[:, g, tb * 128:(tb + 1) * 128], pb[:, g * 128:(g + 1) * 128], cvc(l, "a_nw", g),
                    sz[:, g, tb * 128:(tb + 1) * 128], ALU.mult, ALU.mult)

    def mixer_B(l, ti):
        rS, kS, vS, sz = FS[0], FS[1], FS[2], FS[3]
        RhT, AlT, bon = rS, kS, vS
        pools = (FS[4], FS[5], FS[6])
        pools2 = (FS[7], FS[8], FS[9])

        def slot(i):
            return pools[i // 4][:, i % 4, :]

        def slot2(i):
            return pools2[i // 4][:, i % 4, :]
        raw, lora, aT, lw, lc, kt, kh, beta, eg, egi, egm, tA = [slot(i) for i in range(12)]
        SETS = [dict(raw=raw, aT=aT, lw=lw, lc=lc, kt=kt, kh=kh, beta=beta, eg=eg, egi=egi, egm=egm, tA=tA,
                     beta16=bk16[:, 0, :], kt16=bk16[:, 1, :]),
                dict(raw=slot2(0), aT=slot2(2), lw=slot2(3), lc=slot2(4), kt=slot2(5), kh=slot2(6), beta=slot2(7),
                     eg=slot2(8), egi=slot2(9), egm=slot2(10), tA=slot2(11),
                     beta16=bk16[:, 2, :], kt16=bk16[:, 3, :])]
        wkv, xn = TM[3], TM[4]
        Vtm, Btm, Ktm = TM16[0], TM16[1], TM16[2]
        BJ = [PB[3], PB[4]]; PA = [PB[2], PB[7]]; PQ = [PB[5], PB[6]]; PX = [PB[0], PB[1]]
        Pn, Qn, Xm = bm16
        W2 = AT
        beta16, kt16 = bk16[:, 0, :], bk16[:, 1, :]
        omu = CM.off["b_mu"][0]

        def shift_mix(dst, sl, gi, hidx, mucol, raw_, tA_):
            pb = fm_proj(sl, gi)
            cp(raw_[:, 0:1], b_hist[l][:, hidx:hidx + 1])
            act(raw_[:, 1:T + 1], pb[:, 0:T], AF.Copy)
            yield
            cp(b_hist[l][:, hidx:hidx + 1], raw_[:, T:T + 1])
            tt(tA_[:, 0:T], raw_[:, 0:T], raw_[:, 1:T + 1], ALU.subtract)
            yield
            stt(dst, tA_[:, 0:T], cv[l][:, omu + mucol:omu + mucol + 1], raw_[:, 1:T + 1], ALU.mult, ALU.add)
            yield

        cp(b_S16[:, :, :], b_S[l][:, :, :])
        sl = load_slab(l, "Bl", ncols=128)
        interleave([shift_mix(lora[:, 0:T], sl, 0, 12, 12, raw, tA)])
        act(lora[0:64, 0:T], lora[0:64, 0:T], AF.Tanh)
        tslots = [(slot(2 + 2 * i), slot(3 + 2 * i)) for i in range(4)]
        for wi, (name, dst) in enumerate((("Br", rS), ("Bk", kS), ("Bv", vS))):
            sl = load_slab(l, name)
            interleave([shift_mix(dst[:, g, 0:T], sl, g, wi * 4 + g, wi * 4 + g, tslots[g][0], tslots[g][1])
                        for g in range(4)])
        sl = load_slab(l, "Bz")
        for g in range(4):
            pb = fm_proj(sl, g)
            act(sz[:, g, 0:T], pb[:, 0:T], AF.Silu)
        bcut = int(os.environ.get("BCUT", "99"))
        if bcut < 1:
            return
        def group_chains(g, Z):
            gc = slice(g * 128, (g + 1) * 128)
            raw, aT, lw, lc, kt, kh, beta = Z["raw"], Z["aT"], Z["lw"], Z["lc"], Z["kt"], Z["kh"], Z["beta"]
            eg, egi, egm, tA, beta16, kt16 = Z["eg"], Z["egi"], Z["egm"], Z["tA"], Z["beta16"], Z["kt16"]

            def chainW():
                pb = next_pb()
                mm(pb[:, 0:T], lora_up[l][0:64, gc], lora[0:64, 0:T])
                act(lw[:, 0:T], pb[:, 0:T], AF.Sigmoid, bias=cvc(l, "b_w0", g))
                yield
                for c in range(NCH):
                    cols = slice(c * 64, (c + 1) * 64)
                    scan(lc[:, cols], ones[:, 0:64], lw[:, cols], 0.0, ALU.mult, ALU.add)
                    yield
                act(eg[:, 0:T], lc[:, 0:T], AF.Exp, scale=-0.606531)
                act(egi[:, 0:T], lc[:, 0:T], AF.Exp, scale=0.606531)
                tt(tA[:, 0:T], lc[:, 0:T], lw[:, 0:T], ALU.subtract)
                yield
                act(egm[:, 0:T], tA[:, 0:T], AF.Exp, scale=-0.606531)
                cp(gLt[:, g, :], eg[:, 63:T:64])
                yield

            def chainA():
                pb = next_pb()
                mm(pb[:, 0:T], lora_up[l][64:128, gc], lora[64:128, 0:T])
                act(aT[:, 0:T], pb[:, 0:T], AF.Sigmoid, bias=cvc(l, "b_a0", g))
                yield
                ts(beta[:, 0:T], aT[:, 0:T], -1.0, cvc(l, "b_ka", g), ALU.add, ALU.mult)
                yield
                stt(kt[:, 0:T], beta[:, 0:T], 1.0, kS[:, g, 0:T], ALU.add, ALU.mult)
                yield

            def chainK():
                ts(kh[:, 0:T], kS[:, g, 0:T], cvc(l, "b_kk", g), None, ALU.mult)
                yield
                tt(raw[:, 0:T], kh[:, 0:T], kh[:, 0:T], ALU.mult)
                yield
                pb = next_pb()
                mm(pb[:, 0:T], b_ones[:, :], raw[:, 0:T])
                act(raw[:, 0:T], pb[:, 0:T], AF.Sqrt, bias=eps12[:, 0:1])
                yield
                recip(raw[:, 0:T], raw[:, 0:T])
                yield
                tt(kh[:, 0:T], kh[:, 0:T], raw[:, 0:T], ALU.mult)
                yield

            def chainV():
                for tb in range(NTB):
                    pbt = next_pb()
                    tr(pbt[:, 0:128], vS[:, g, tb * 128:(tb + 1) * 128], ident[:, :])
                    cp(Vtm[:, tb, 2 * g:2 * g + 2, :], pbt[:, 0:128].rr("p (a b) -> p a b", a=2))
                    yield

            def chainR():
                stt(raw[:, 0:T], rS[:, g, 0:T], cvc(l, "b_rk", g), kt[:, 0:T], ALU.mult, ALU.mult)
                yield
                pbb = next_pb()
                mm(pbb[:, 0:T], b_ones[:, :], raw[:, 0:T])
                tt(bon[:, g, 0:T], pbb[:, 0:T], vS[:, g, 0:T], ALU.mult)
                yield

            def chainB():
                tt(beta[:, 0:T], aT[:, 0:T], kh[:, 0:T], ALU.mult)
                yield
                tt(beta16[:, 0:T], beta[:, 0:T], egi[:, 0:T], ALU.mult)
                yield

            def chainO():
                tt(AlT16[:, g, 0:T], kh[:, 0:T], egm[:, 0:T], ALU.mult)
                yield
                tt(RhT16[:, g, 0:T], rS[:, g, 0:T], eg[:, 0:T], ALU.mult)
                yield
                tt(kt16[:, 0:T], kt[:, 0:T], egi[:, 0:T], ALU.mult)
                yield

            def tail():
                for tb in range(NTB):
                    pbt = next_pb()
                    mm(pbt[:, 0:128], beta16[:, tb * 128:(tb + 1) * 128], ident16[:, :])
                    mm(pbt[:, 128:256], kt16[:, tb * 128:(tb + 1) * 128], ident16[:, :])
                    cp(Btm[:, tb, 2 * g:2 * g + 2, :], pbt[:, 0:128].rr("p (a b) -> p a b", a=2))
                    cp(Ktm[:, tb, 2 * g:2 * g + 2, :], pbt[:, 128:256].rr("p (a b) -> p a b", a=2))
                    yield
            return [chainW(), chainA(), chainK(), chainV()], [chainR(), chainB(), chainO()], tail

        def products(g, Z):
            beta16, kt16 = Z["beta16"], Z["kt16"]
            for tb in range(NTB):
                for j in range(2):
                    jr = slice(64 * j, 64 * j + 64)
                    for p in range(2):
                        c = tb * 2 + p
                        rows = slice(64 * p, 64 * p + 64)
                        cols = slice(c * 64, (c + 1) * 64)
                        A_, B_, K_, R_ = AlT16[jr, g, cols], beta16[jr, cols], kt16[jr, cols], RhT16[jr, g, cols]
                        mm(BJ[j][rows, 0:64], A_, B_)
                        mm(BJ[j][rows, 64:128], B_, A_)
                        mm(BJ[j][rows, 128:192], K_, A_)
                        mm(BJ[j][rows, 192:256], B_, R_)
                        mm(BJ[j][rows, 256:320], K_, R_)
                    tt(PRall[tb][:, :, 2 * g + j, :], BJ[j][:, 0:320].rr("p (k t) -> p k t", k=5), b_m5[:, :, :],
                       ALU.mult)

        for g0 in (0, 2):
            ph1, ph2, tails = [], [], []
            for gi, g in enumerate((g0, g0 + 1)):
                a1, a2, tl = group_chains(g, SETS[gi])
                ph1 += a1; ph2 += a2; tails.append(tl())
            interleave(ph1)
            interleave(ph2)
            interleave(tails)
            for gi, g in enumerate((g0, g0 + 1)):
                products(g, SETS[gi])
        if bcut < 2:
            return
        for tb in range(NTB):
            PRt = PRall[tb]
            P0, Q0, MkT, NbT, NkT = (PRt[:, k, :, :] for k in range(5))
            Wsb, Usb = PRt[:, 2, :, :], PRt[:, 4, :, :]
            for p in range(2):
                rows = slice(64 * p, 64 * p + 64)
                c = tb * 2 + p
                for h in range(8):
                    g, j = h // 2, h % 2
                    mm(PA[p][rows, h * 64:(h + 1) * 64], MkT[rows, h, :], Vtm[rows, tb, h, :])
                    mm(PQ[p][rows, h * 64:(h + 1) * 64], NkT[rows, h, :], Vtm[rows, tb, h, :])
                    mm(PX[p][64 * j:64 * j + 64, g * 64:(g + 1) * 64], Ktm[rows, tb, h, :], Vtm[rows, tb, h, :])
                act(W2[rows, :, :], PA[p][rows, :].rr("p (h e) -> p h e", h=8), AF.Copy)
                act(Y2[rows, :, :], PQ[p][rows, :].rr("p (h e) -> p h e", h=8), AF.Copy)
                tt(K2g[:, p, :, :], PX[p][:, 0:256].rr("p (a b) -> p a b", a=4),
                   gLt[:, :, c:c + 1].bc([128, 4, 64]), ALU.mult)
            tt(Xm[:, :, :], b_i2[:, :, :].bc([128, 8, 64]), Q0, ALU.subtract)
            Pc, Qc = P0, Q0
            nxt = [(Pn, Qn), (P0, Q0)]
            for i in range(1, 6):
                Pd, Qd = nxt[(i - 1) % 2]
                for p in range(2):
                    rows = slice(64 * p, 64 * p + 64)
                    for h in range(8):
                        hc = slice(h * 64, (h + 1) * 64)
                        mm(PA[p][rows, hc], Qc[rows, h, :], Pc[rows, h, :])
                        if i < 5:
                            mm(PQ[p][rows, hc], Pc[rows, h, :], Qc[rows, h, :])
                for p in range(2):
                    rows = slice(64 * p, 64 * p + 64)
                    act(Pd[rows, :, :], PA[p][rows, :].rr("p (h e) -> p h e", h=8), AF.Copy)
                    if i < 5:
                        cp(Qd[rows, :, :], PQ[p][rows, :].rr("p (h e) -> p h e", h=8))
                Pc, Qc = Pd, Qd
                for p in range(2):
                    rows = slice(64 * p, 64 * p + 64)
                    for h in range(8):
                        mm(PX[p][rows, h * 64:(h + 1) * 64], Pc[rows, h, :], Xm[rows, h, :])
                for p in range(2):
                    rows = slice(64 * p, 64 * p + 64)
                    tt(Xm[rows, :, :], Xm[rows, :, :], PX[p][rows, :].rr("p (h e) -> p h e", h=8), ALU.add)
            if bcut < 3:
                continue
            for p in range(2):
                c = tb * 2 + p
                rows = slice(64 * p, 64 * p + 64)
                cols = slice(c * 64, (c + 1) * 64)
                for j in range(2):
                    jr = slice(64 * j, 64 * j + 64)
                    for g in range(4):
                        mm(BJ[j][rows, g * 64:(g + 1) * 64], AlT16[jr, g, cols], b_S16[jr, g, :])
                        mm(BJ[j][rows, 256 + g * 64:256 + (g + 1) * 64], RhT16[jr, g, cols], b_S16[jr, g, :])
                W24 = W2[rows, :, :].rr("p (g j) e -> p g j e", j=2)
                Ws4 = Wsb[rows, :, :].rr("p (g j) e -> p g j e", j=2)
                Y24 = Y2[rows, :, :].rr("p (g j) e -> p g j e", j=2)
                for j in range(2):
                    tt(Ws4[:, :, j, :], BJ[j][rows, 0:256].rr("p (a b) -> p a b", a=4), W24[:, :, j, :], ALU.add)
                    tt(tmp4[rows, :, j, :], BJ[j][rows, 256:512].rr("p (a b) -> p a b", a=4), Y24[:, :, j, :],
                       ALU.add)
                for h in range(8):
                    mm(PX[p][rows, h * 64:(h + 1) * 64], Xm[rows, h, :], Wsb[rows, h, :])
                act(Usb[rows, :, :], PX[p][rows, :].rr("p (h e) -> p h e", h=8), AF.Copy)
                for h in range(8):
                    g, j = h // 2, h % 2
                    mm(PA[p][rows, h * 64:(h + 1) * 64], NbT[rows, h, :], Usb[rows, h, :])
                    mm(PQ[p][64 * j:64 * j + 64, g * 64:(g + 1) * 64], Btm[rows, tb, h, :], Usb[rows, h, :])
                tt(wkv[rows, tb, :, :], tmp4[rows, :, :, :].rr("p g j e -> p (g j) e"),
                   PA[p][rows, :].rr("p (h e) -> p h e", h=8), ALU.subtract)
                tt(b_S[l][:, :, :], b_S[l][:, :, :], PQ[p][:, 0:256].rr("p (a b) -> p a b", a=4), ALU.subtract)
                tt(b_S[l][:, :, :], b_S[l][:, :, :], gLt[:, :, c:c + 1].bc([128, 4, 64]), ALU.mult)
                tt(b_S[l][:, :, :], b_S[l][:, :, :], K2g[:, p, :, :], ALU.add)
                cp(b_S16[:, :, :], b_S[l][:, :, :])
        ogw, ogb = CM.off["b_gw"][0], CM.off["b_gb"][0]
        head_ln_all(xn, wkv)
        for tb in range(NTB):
            pb = next_pb()
            for g in range(4):
                tr(pb[:, g * 128:(g + 1) * 128], xn[:, tb, 2 * g:2 * g + 2, 0:64], ident[:, :])
            for g in range(4):
                tc_ = slice(tb * 128, (tb + 1) * 128)
                ts(scr[:, 0:128], pb[:, g * 128:(g + 1) * 128], cv[l][:, ogw + g:ogw + g + 1],
                   cv[l][:, ogb + g:ogb + g + 1], ALU.mult, ALU.add)
                tt(scr[:, 0:128], scr[:, 0:128], bon[:, g, tc_], ALU.add)
                tt(yT[:, 4 + g, tc_], scr[:, 0:128], sz[:, g, tc_], ALU.mult)

    def out_proj_residual(l, X):
        dma(gpost[l][:, :], gpost_d[l])
        for si in range(4):
            s = load_wo(l, si)
            for tb in range(NTB):
                for half in range(2):
                    pb = PB[4 + tb * 2 + half]
                    for q in range(4):
                        fc = si * 4 + q
                        mm(pb[:, :], yT[:, fc, tb * 128:(tb + 1) * 128], s[:, q * 2 + half, :],
                           start=(fc == 0), stop=(fc == 15))
        if cut < 4:
            return
        for tb in range(NTB):
            for half in range(2):
                act(scr[:, half * 512:(half + 1) * 512], PB[4 + tb * 2 + half][:, :], AF.Copy)
            if cut < 5:
                continue
            act(scr2[:, :], scr[:, :], AF.Square)
            reduce(sm[:, 24:25], scr2[:, :], ALU.add)
            ts(sm[:, 25:26], sm[:, 24:25], 1.0 / D_MODEL, 1e-6, ALU.mult, ALU.add)
            act(sm[:, 26:27], sm[:, 25:26], AF.Sqrt)
            recip(sm[:, 27:28], sm[:, 26:27])
            if cut < 6:
                continue
            stt(scr2[:, :], scr[:, :], sm[:, 27:28], gpost[l][:, :], ALU.mult, ALU.mult)
            if cut < 7:
                continue
            tt(X[:, tb, :], X[:, tb, :], scr2[:, :], ALU.add)

    xv = x_d.rearrange("(n tb p) d -> n p tb d", tb=NTB, p=128)
    yv = y_d.rearrange("(n tb p) d -> n p tb d", tb=NTB, p=128)
    for ti in range(ntiles):
        X = xt[0]
        dma(X[:, :, :], xv[ti])
        for l in range(nlayers):
            if cut >= 1:
                rmsnorm_pre(l, X)
            memset(yT[:, :, :], 0.0)
            if "C" in mixers and cut >= 2:
                mixer_C(l)
            if "D" in mixers:
                mixer_D(l, ti)
            if "A" in mixers:
                mixer_A(l, ti)
            if "B" in mixers:
                mixer_B(l, ti)
            if cut >= 3:
                out_proj_residual(l, X)
        dma(yv[ti], X[:, :, :])

    P.emit()
    return nc, stack


def kernel(**inputs):
    inputs = {k: np.asarray(v) for k, v in inputs.items()}
    return run(inputs)


def make_in_map(xc, hp, tb):
    m = dict(x=xc, wt=hp["wt"], wo=hp["wo"], cv=hp["cv"], lora=hp["lora"], lru=hp["lru"], gpost=hp["gpost"])
    m.update(tb)
    return m


def run(inputs, ntiles=SEQ // T, nlayers=DEPTH, mixers="ABCD"):
    hp = host_prepare(inputs)
    tb = host_tables()
    nc, stack = build_program(ntiles, nlayers, mixers)
    x = np.ascontiguousarray(inputs["x"], dtype=np.float32)
    in_maps = [make_in_map(x[c], hp, tb) for c in range(NCORES)]
    with stack:
        res = run_bass_kernel_spmd(nc, in_maps, core_ids=list(range(NCORES)))
    out = np.stack([np.asarray(r["y"]) for r in res.results], axis=0)
    return out.astype(np.float32)
```

```python
import contextlib
import numpy as np
import concourse.bass as bass
import concourse.mybir as mybir
from concourse.bass_utils import run_bass_kernel_spmd

F32 = mybir.dt.float32
BF16 = mybir.dt.bfloat16
AF = mybir.ActivationFunctionType
ALU = mybir.AluOpType
AX = mybir.AxisListType

D_MODEL = 1024
SEQ = 4096
BATCH = 4
DEPTH = 2
G = 512
D_IN = 7824
T = 256
NTB = T // 128
NCH = T // 64
NCORES = 4
NSLABBUF = 3
import os as _os
INORDER = tuple(_os.environ.get("INORDER", "pe").split(","))


class Buf:
    def __init__(self, name, tile, psum=False):
        self.name = name
        self.tile = tile
        self.psum = psum
        self.acc = []
        self.dma_sem = None
        self.dma_ops = []

    def __getitem__(self, idx):
        return Ref(self, self.tile[idx])


def _box(ap):
    pat = ap.ap
    pstep = pat[0][0]
    off = int(ap.offset)
    p0 = off // pstep if pstep else 0
    f0 = off % pstep if pstep else off
    ext = 0
    for st, cnt in pat[1:]:
        ext += abs(st) * (cnt - 1)
    return (p0, p0 + pat[0][1], f0, f0 + ext + 1)


class Ref:
    def __init__(self, buf, ap, box=None):
        self.buf = buf
        self.ap = ap
        if box is None:
            box = _box(ap)
            if buf.psum:
                box = ((box[0] // 32) * 32, ((box[1] + 31) // 32) * 32, 0, 512)
        self.box = box

    def bc(self, shape):
        return Ref(self.buf, self.ap.to_broadcast(list(shape)), self.box)

    def __getitem__(self, idx):
        return Ref(self.buf, self.ap[idx])

    def rr(self, pat, **kw):
        return Ref(self.buf, self.ap.rearrange(pat, **kw), self.box)


def _ovl(a, b):
    return a[0] < b[1] and b[0] < a[1] and a[2] < b[3] and b[2] < a[3]


def _cov(a, b):
    return a[0] <= b[0] and a[1] >= b[1] and a[2] <= b[2] and a[3] >= b[3]


class Prog:
    ENG = ("pe", "act", "dve", "pool", "sp")

    def __init__(self, nc, stack):
        self.nc = nc
        self.stack = stack
        self.ops = []
        self.nbuf = 0

    def sb(self, name, shape, dtype=F32):
        t = self.stack.enter_context(self.nc.sbuf_tensor("s_" + name, list(shape), dtype))
        return Buf(name, t)

    def ps(self, name):
        t = self.stack.enter_context(self.nc.psum_tensor("p_" + name, [128, 512], F32))
        return Buf(name, t, psum=True)

    def op(self, eng, fn, outs, ins, dma=False):
        oid = len(self.ops)
        deps = set()
        for r, w in [(x, True) for x in outs] + [(x, False) for x in ins]:
            if r is None or not isinstance(r, Ref):
                continue
            b = r.buf
            for (bx, o2, w2) in b.acc:
                if (w or w2) and _ovl(bx, r.box):
                    deps.add(o2)
        for r, w in [(x, True) for x in outs] + [(x, False) for x in ins]:
            if r is None or not isinstance(r, Ref):
                continue
            b = r.buf
            if w:
                b.acc = [a for a in b.acc if not _cov(r.box, a[0])]
            b.acc.append((r.box, oid, w))
            if len(b.acc) > 48:
                merged = {}
                for (bx, o2, w2) in b.acc:
                    k = (self.ops[o2]["eng"] if o2 < oid else eng, w2)
                    if k in merged:
                        m = merged[k]
                        merged[k] = ((min(m[0][0], bx[0]), max(m[0][1], bx[1]), min(m[0][2], bx[2]),
                                      max(m[0][3], bx[3])), max(m[1], o2), w2)
                    else:
                        merged[k] = (bx, o2, w2)
                b.acc = list(merged.values())
        dbuf = None
        if dma:
            for r in list(outs) + list(ins):
                if isinstance(r, Ref):
                    dbuf = r.buf
            dbuf.dma_ops.append(oid)
        deps.discard(oid)
        self.ops.append(dict(eng=eng, fn=fn, deps=sorted(deps), dma=dma, dbuf=dbuf, sig=False))
        return oid

    def emit(self):
        nc = self.nc
        ops = self.ops
        for o in ops:
            for d in o["deps"]:
                od = ops[d]
                if od["dma"]:
                    continue
                if od["eng"] == o["eng"] and o["eng"] in INORDER and not o["dma"]:
                    continue
                od["sig"] = True
        sems = {e: self.stack.enter_context(nc.semaphore("sem_" + e)) for e in self.ENG}
        cnt = {e: 0 for e in self.ENG}
        for o in ops:
            if o["dma"]:
                b = o["dbuf"]
                if b.dma_sem is None:
                    b.dma_sem = self.stack.enter_context(nc.semaphore("dsem_" + b.name))
            elif o["sig"]:
                cnt[o["eng"]] += 1
                o["signo"] = cnt[o["eng"]]
        per_eng = {e: [] for e in self.ENG}
        for i, o in enumerate(ops):
            per_eng[o["eng"]].append(i)
        import bisect

        def gen(engname, e):
            waited = {}
            for i in per_eng[engname]:
                o = ops[i]
                need = {}
                for d in o["deps"]:
                    od = ops[d]
                    if od["dma"]:
                        b = od["dbuf"]
                        n = bisect.bisect_left(b.dma_ops, i)
                        key = ("d", id(b))
                        if need.get(key, (None, 0))[1] < 16 * n:
                            need[key] = (b.dma_sem, 16 * n)
                    else:
                        if od["eng"] == engname and engname in INORDER and not o["dma"]:
                            continue
                        key = ("e", od["eng"])
                        if need.get(key, (None, 0))[1] < od["signo"]:
                            need[key] = (sems[od["eng"]], od["signo"])
                for key, (sem, val) in need.items():
                    if waited.get(key, 0) >= val:
                        continue
                    e.wait_ge(sem, val)
                    waited[key] = val
                ins = o["fn"](e)
                if o["dma"]:
                    ins.then_inc(o["dbuf"].dma_sem, 16)
                elif o["sig"]:
                    ins.then_inc(sems[engname], 1)

        with nc.Block() as block:
            @block.tensor
            def _(e):
                gen("pe", e)

            @block.scalar
            def _(e):
                gen("act", e)

            @block.vector
            def _(e):
                gen("dve", e)

            @block.gpsimd
            def _(e):
                gen("pool", e)

            @block.sync
            def _(e):
                gen("sp", e)
                seen = set()
                for o in ops:
                    if o["dma"] and id(o["dbuf"]) not in seen:
                        seen.add(id(o["dbuf"]))
                        e.wait_ge(o["dbuf"].dma_sem, 16 * len(o["dbuf"].dma_ops))


def _a(x):
    return x.ap if isinstance(x, Ref) else x


def _fm4(v):
    return np.ascontiguousarray(v.reshape(-1, 128).T)


class CMap:
    def __init__(self):
        self.off = {}
        self.n = 0

    def add(self, name, w):
        self.off[name] = (self.n, w)
        self.n += w


def build_cmap():
    c = CMap()
    c.add("gpre", 8)
    c.add("a_cw", 32); c.add("a_cb", 8); c.add("a_ib", 1); c.add("a_fb", 1); c.add("a_nw", 4)
    c.add("b_mu", 13); c.add("b_w0", 4); c.add("b_a0", 4); c.add("b_kk", 4); c.add("b_ka", 4)
    c.add("b_rk", 4); c.add("b_gw", 4); c.add("b_gb", 4)
    c.add("c_cw", 16); c.add("c_cb", 4); c.add("c_br", 4); c.add("c_bi", 4); c.add("c_lam", 4)
    c.add("d_nw", 4)
    return c


CM = build_cmap()

def _cols(a, b):
    return list(range(a, b))


def build_slabs():
    slabs = []
    def fm(name, start):
        slabs.append((name, _cols(start, start + 512)))
    fm("Aq", 0); fm("Ak", 512); fm("Az", 2048)
    slabs.append(("Ag", _cols(2560, 2576) + [-1] * (512 - 16)))
    fm("Br", 2576); fm("Bk", 3088); fm("Bv", 3600)
    slabs.append(("Bl", _cols(4112, 4240) + [-1] * (512 - 128)))
    fm("Bz", 4240); fm("Cx", 4752); fm("Cz", 5264); fm("Dz", 7312)
    fm("Av", 1024); fm("Ao", 1536); fm("Dq", 5776); fm("Dk", 6288); fm("Dv", 6800)
    return slabs


SLABS = build_slabs()
SLAB_ID = {s[0]: i for i, s in enumerate(SLABS)}
NSLAB = len(SLABS)


def host_prepare(inp):
    f = np.float32
    w_in = inp["w_in"]
    w_in_p = np.concatenate([w_in, np.zeros((DEPTH, D_MODEL, 1), f)], axis=2)
    wt = np.empty((DEPTH, NSLAB, 128, 8, 512), f)
    for si, (name, cols) in enumerate(SLABS):
        blk = w_in_p[:, :, cols]
        wt[:, si] = blk.reshape(DEPTH, 8, 128, 512).transpose(0, 2, 1, 3)
    w_out = inp["w_out"]
    wo = np.ascontiguousarray(w_out.reshape(DEPTH, 4, 4, 128, 1024).transpose(0, 1, 3, 2, 4))
    cv = np.zeros((DEPTH, 128, CM.n), f)
    def put(name, arr):
        o, w = CM.off[name]
        cv[:, :, o:o + w] = arr
    put("gpre", inp["norm_pre"].reshape(DEPTH, 8, 128).transpose(0, 2, 1))
    cw = inp["mlstm_conv_w"]
    put("a_cw", cw.reshape(DEPTH, 4, 8, 128).transpose(0, 3, 1, 2).reshape(DEPTH, 128, 32))
    put("a_cb", inp["mlstm_conv_b"].reshape(DEPTH, 8, 128).transpose(0, 2, 1))
    ib = np.zeros((DEPTH, 128, 1), f); ib[:, 0:8, 0] = inp["mlstm_i_bias"]; put("a_ib", ib)
    fb = np.zeros((DEPTH, 128, 1), f); fb[:, 0:8, 0] = inp["mlstm_f_bias"]; put("a_fb", fb)
    def fm4(x):
        return x.reshape(DEPTH, -1, 128).transpose(0, 2, 1)
    put("a_nw", fm4(inp["mlstm_norm_w"]))
    mu = inp["rwkv_mu"]
    put("b_mu", fm4(mu))
    for k, nm in [("b_w0", "rwkv_w0"), ("b_a0", "rwkv_a0"), ("b_kk", "rwkv_k_k"), ("b_ka", "rwkv_k_a"),
                  ("b_rk", "rwkv_r_k"), ("b_gw", "rwkv_gn_w"), ("b_gb", "rwkv_gn_b"),
                  ("c_cb", "lru_conv_b"), ("c_br", "lru_b_r"), ("c_bi", "lru_b_i"), ("c_lam", "lru_lambda"),
                  ("d_nw", "ret_norm_w")]:
        put(k, fm4(inp[nm]))
    lw = inp["lru_conv_w"]
    put("c_cw", lw.reshape(DEPTH, 4, 4, 128).transpose(0, 3, 1, 2).reshape(DEPTH, 128, 16))
    lora = np.concatenate([inp["rwkv_w_up"], inp["rwkv_a_up"]], axis=1)
    lru = np.zeros((DEPTH, 128, 2, 4, 128), f)
    for which, nm in enumerate(["lru_w_r", "lru_w_i"]):
        w = inp[nm]
        for g in range(4):
            for j in range(2):
                lru[:, 64 * j:64 * j + 64, which, g, 64 * j:64 * j + 64] = w[:, 2 * g + j]
    gpost = np.ascontiguousarray(np.broadcast_to(inp["norm_post"][:, None, :], (DEPTH, 128, D_MODEL)))
    return dict(wt=wt, wo=wo, cv=cv, lora=np.ascontiguousarray(lora), lru=lru, gpost=gpost)


def host_tables():
    f = np.float32
    t = {}
    t["ident"] = np.eye(128, dtype=f)
    hd = 64
    log_g = np.log1p(-np.exp2(-5.0 - np.arange(8, dtype=np.float64)))
    idx = np.arange(64, dtype=np.float64)
    dmat = np.exp(log_g[:, None, None] * np.abs(idx[:, None] - idx[None, :])) * hd ** -0.5
    xi = np.exp(log_g[:, None] * (idx + 1.0))
    zeta = np.exp(log_g[:, None] * (63.0 - idx)) * hd ** -0.5
    gch = np.exp(log_g * 64.0)
    hs2h = [2 * (s % 4) + (s // 4) for s in range(8)]
    dm = np.zeros((128, 8, 64));
    for s in range(8):
        dm[0:64, s] = dmat[hs2h[s]]; dm[64:128, s] = dmat[hs2h[s]]
    t["d_dmat"] = dm.astype(f)
    t["d_xi"] = np.tile(xi.T, (2, 1)).astype(f)
    t["d_zeta"] = np.tile(zeta.T, (2, 1)).astype(f)
    gc = np.zeros((128, 4))
    for g in range(4):
        for j in range(2):
            gc[64 * j:64 * j + 64, g] = gch[2 * g + j]
    t["d_gch"] = gc.astype(f)
    mk = np.zeros((128, 8, 64))
    sidx = np.arange(128) % 64
    mk[:] = np.where(sidx[:, None, None] <= np.arange(64)[None, None, :], 0.0, -30000.0)
    t["a_mask"] = mk.reshape(128, 512).astype(f)
    sel = np.zeros((8, 512 + 4 + 128))
    for hp_ in range(8):
        for hs in range(8):
            if hp_ == hs2h[hs]:
                sel[hp_, hs * 64:(hs + 1) * 64] = 1.0
        sel[hp_, 512 + hp_ // 2] = 1.0
        jj = hp_ % 2
        sel[hp_, 516 + 64 * jj:516 + 64 * jj + 64] = 1.0
    t["a_sel"] = sel.astype(f)
    bo = np.zeros((128, 128)); bo[0:64, 0:64] = 1.0; bo[64:128, 64:128] = 1.0
    t["b_ones"] = bo.astype(f)
    i2 = np.zeros((128, 64)); i2[0:64] = np.eye(64); i2[64:128] = np.eye(64)
    t["b_i2"] = i2.astype(f)
    rr_ = (np.arange(128) % 64)[:, None]; cc_ = np.arange(64)[None, :]
    m5 = np.zeros((128, 5, 64))
    m5[:, 0] = rr_ > cc_; m5[:, 1] = cc_ > rr_; m5[:, 2] = cc_ > rr_; m5[:, 3] = cc_ >= rr_; m5[:, 4] = cc_ >= rr_
    t["b_m5"] = m5.astype(f)
    half = 32
    pos = np.arange(SEQ, dtype=np.float32)
    inv_freq = (np.float32(10000.0) ** (-np.arange(half, dtype=np.float32) / np.float32(half))).astype(np.float32)
    ang = (pos[:, None] * inv_freq[None, :]).astype(np.float32).astype(np.float64)
    t["rope"] = np.concatenate([np.cos(ang), np.sin(ang)], axis=1).astype(f)
    return t


def build_program(ntiles=SEQ // T, nlayers=DEPTH, mixers="ABCD"):
    nc = bass.Bass("TRN2", target_bir_lowering=False)
    stack = contextlib.ExitStack()
    P = Prog(nc, stack)
    dram = {}

    def din(name, shape):
        dram[name] = nc.dram_tensor(name, list(shape), F32, kind="ExternalInput").ap()
        return dram[name]

    x_d = din("x", [SEQ, D_MODEL])
    wt_d = din("wt", [DEPTH, NSLAB, 128, 8, 512])
    wo_d = din("wo", [DEPTH, 4, 128, 4, 1024])
    cv_d = din("cv", [DEPTH, 128, CM.n])
    lora_d = din("lora", [DEPTH, 128, 512])
    lru_d = din("lru", [DEPTH, 128, 2, 4, 128])
    gpost_d = din("gpost", [DEPTH, 128, D_MODEL])
    ident_d = din("ident", [128, 128])
    dmat_d = din("d_dmat", [128, 8, 64])
    dxi_d = din("d_xi", [128, 8])
    dzeta_d = din("d_zeta", [128, 8])
    dgch_d = din("d_gch", [128, 4])
    rope_d = din("rope", [SEQ, 64])
    bones_d = din("b_ones", [128, 128])
    bi2_d = din("b_i2", [128, 64])
    bm5_d = din("b_m5", [128, 5, 64])
    amask_d = din("a_mask", [128, 512])
    asel_d = din("a_sel", [8, 644])
    y_d = nc.dram_tensor("y", [SEQ, D_MODEL], F32, kind="ExternalOutput").ap()

    ident = P.sb("ident", [128, 128])
    cv = [P.sb(f"cv{l}", [128, CM.n]) for l in range(DEPTH)]
    lru1 = P.sb("lru", [128, 2, 4, 128]); lru = [lru1, lru1]
    gpost1 = P.sb("gpost", [128, D_MODEL]); gpost = [gpost1, gpost1]
    lora_up = [P.sb(f"lora_up{l}", [128, 512]) for l in range(DEPTH)]
    b_ones = P.sb("b_ones", [128, 128]); b_i2 = P.sb("b_i2", [128, 1, 64]); b_m5 = P.sb("b_m5", [128, 5, 64])
    b_hist = [P.sb(f"b_hist{l}", [128, 13]) for l in range(DEPTH)]
    b_S = [P.sb(f"b_S{l}", [128, 4, 64]) for l in range(DEPTH)]
    ident16 = P.sb("ident16", [128, 128], BF16)
    AlT16 = P.sb("AlT16", [128, 4, T], BF16); RhT16 = P.sb("RhT16", [128, 4, T], BF16)
    bk16 = P.sb("bk16", [128, 4, T], BF16)
    bm16 = [P.sb(f"bm16_{i}", [128, 8, 64], BF16) for i in range(3)]
    b_S16 = P.sb("b_S16", [128, 4, 64], BF16)
    a_n16 = P.sb("a_n16", [128, 4, 1], BF16); wj16 = P.sb("wj16", [128, NTB, 8], BF16)
    ones16 = P.sb("ones16", [128, 64], BF16); eps12 = P.sb("eps12", [128, 1])
    gLt = P.sb("gLt", [128, 4, NCH]); PRall = [P.sb(f"PR{i}", [128, 5, 8, 64], BF16) for i in range(NTB)]; Y2 = P.sb("Y2", [128, 8, 64])
    K2g = P.sb("K2g", [128, 2, 4, 64])
    ones = P.sb("ones", [128, T]); zeros = P.sb("zeros", [128, T])
    xt = [P.sb(f"xt{i}", [128, NTB, D_MODEL]) for i in range(1)]
    hT = P.sb("hT", [128, 8, T], BF16)
    yT = P.sb("yT", [128, 16, T], BF16)
    slab = [P.sb(f"slab{i}", [128, 8, 512], BF16) for i in range(NSLABBUF)]
    sm = P.sb("small", [128, 64])
    scr = P.sb("scr", [128, D_MODEL])
    scr2 = P.sb("scr2", [128, D_MODEL])
    FS = [P.sb(f"fs{i}", [128, 4, T + 4]) for i in range(10)]
    c_hist = [P.sb(f"c_hist{l}", [128, 4, 3]) for l in range(DEPTH)]
    c_state = [P.sb(f"c_state{l}", [128, 4]) for l in range(DEPTH)]
    c_coef = [P.sb(f"c_coef{l}", [128, 8]) for l in range(DEPTH)]
    PB = [P.ps(f"pb{i}") for i in range(8)]
    TM = [P.sb(f"tm{i}", [128, NTB, 8, 64]) for i in range(6)]
    TM16 = [P.sb(f"tm16_{i}", [128, NTB, 8, 64], BF16) for i in range(3)]
    d_dmat = P.sb("d_dmat", [128, 8, 64]); d_xi = P.sb("d_xi", [128, 4, 2, 1]); d_zeta = P.sb("d_zeta", [128, 8, 1])
    d_gch = P.sb("d_gch", [128, 4, 1]); rope_sb = P.sb("rope_sb", [128, NTB, 1, 64])
    d_R = [P.sb(f"d_R{l}", [128, 4, 64]) for l in range(DEPTH)]
    a_mask = P.sb("a_mask", [128, 512]); a_sel = P.sb("a_sel", [8, 644])
    ga = P.sb("ga", [8, 12, T + 1]); scl = P.sb("scl", [8, NCH]); rhs_sc = P.sb("rhs_sc", [8, NCH, 4])
    scb = P.sb("scb", [128, NCH, 4, 1]); negPx = P.sb("negPx", [8, 2, 8, 64]); ET = P.sb("ET", [128, 8, 64])
    ET1 = P.sb("ET1", [128, 8, 64])
    tmsc = P.sb("tmsc", [128, NTB, 3, 8]); hnum = P.sb("hnum", [128, 8, 64]); vw = P.sb("vw", [128, 8, 64])
    a_hist = [P.sb(f"a_hist{l}", [128, 8, 3]) for l in range(DEPTH)]
    a_C = [P.sb(f"a_C{l}", [128, 4, 64]) for l in range(DEPTH)]
    a_n = [P.sb(f"a_n{l}", [128, 4, 1]) for l in range(DEPTH)]
    a_car = [P.sb(f"a_car{l}", [8, 4]) for l in range(DEPTH)]
    AT = P.sb("AT", [128, 8, 64]); tmp4 = P.sb("tmp4", [128, 4, 2, 64]); lnst = P.sb("lnst", [128, 64]); lnst_b = P.sb("lnst_b", [128, 40]); lnsts = [lnst, lnst_b]
    state = dict(slab_i=0, pb_i=0)

    import os
    cut = int(os.environ.get("KCUT", "9"))
    def dma(out, in_, eng="sp"):
        return P.op(eng, lambda e: e.dma_start(out=_a(out), in_=_a(in_)), [out], [in_], dma=True)

    def mm(out, lhsT, rhs, start=True, stop=True):
        P.op("pe", lambda e: e.matmul(_a(out), _a(lhsT), _a(rhs), start=start, stop=stop),
             [out], [lhsT, rhs] + ([] if start else [out]))

    def tr(out, in_, idn):
        P.op("pe", lambda e: e.transpose(_a(out), _a(in_), _a(idn)), [out], [in_, idn])

    def act(out, in_, func, bias=None, scale=1.0, accum=None, eng="act"):
        kw = {}
        if bias is not None:
            kw["bias"] = _a(bias)
        if accum is not None:
            kw["accum_out"] = _a(accum)
        P.op("act", lambda e: e.activation(_a(out), _a(in_), func, scale=_a(scale), **kw),
             [out, accum], [in_, bias, scale])

    def tt(out, a, b, op, eng="dve"):
        P.op(eng, lambda e: e.tensor_tensor(_a(out), _a(a), _a(b), op), [out], [a, b])

    def ts(out, a, s1, s2, op0, op1=ALU.bypass, eng="dve"):
        P.op(eng, lambda e: e.tensor_scalar(_a(out), _a(a), _a(s1), _a(s2), op0, op1), [out], [a, s1, s2])

    def stt(out, a, s, b, op0, op1):
        P.op("dve", lambda e: e.scalar_tensor_tensor(_a(out), _a(a), _a(s), _a(b), op0, op1), [out], [a, s, b])

    def scan(out, d0, d1, init, op0, op1):
        P.op("dve", lambda e: e.tensor_tensor_scan(_a(out), _a(d0), _a(d1), _a(init), op0, op1),
             [out], [d0, d1, init])

    def cp(out, in_, eng="dve"):
        if isinstance(in_, Ref) and in_.buf.psum and eng == "dve":
            return act(out, in_, AF.Copy)
        P.op(eng, lambda e: e.tensor_copy(_a(out), _a(in_)), [out], [in_])

    def recip(out, in_):
        P.op("dve", lambda e: e.reciprocal(_a(out), _a(in_)), [out], [in_])

    def memset(out, val, eng="dve"):
        P.op(eng, lambda e: e.memset(_a(out), val), [out], [])

    def reduce(out, in_, op, axis=AX.X):
        P.op("dve", lambda e: e.tensor_reduce(_a(out), _a(in_), axis, op), [out], [in_])

    def interleave(gens):
        gens = list(gens)
        while gens:
            for gz in list(gens):
                try:
                    next(gz)
                except StopIteration:
                    gens.remove(gz)

    def run_chunks(make_gen):
        gens = [make_gen(c) for c in range(NCH)]

        def step(c):
            try:
                next(gens[c])
            except StopIteration:
                pass
        pairs = NCH // 2
        step(0); step(1)
        for k in range(pairs):
            a, b = 2 * k, 2 * k + 1
            step(a); step(b)
            step(a); step(b)
            step(a); step(a)
            if k + 1 < pairs:
                step(2 * k + 2)
            step(b); step(b)
            if k + 1 < pairs:
                step(2 * k + 3)

    def next_pb():
        state["pb_i"] = (state["pb_i"] + 1) % 2
        return PB[state["pb_i"]]

    def load_slab(l, name, ncols=512):
        s = slab[state["slab_i"]]
        state["slab_i"] = (state["slab_i"] + 1) % NSLABBUF
        dma(s[:, :, 0:ncols], wt_d[l, SLAB_ID[name], :, :, 0:ncols], eng="pool")
        return s

    def load_wo(l, si):
        s = slab[state["slab_i"]]
        state["slab_i"] = (state["slab_i"] + 1) % NSLABBUF
        dma(s[:, :, :], wo_d[l, si].rearrange("p f n -> p (f n)").rearrange("p (a b) -> p a b", a=8), eng="pool")
        return s

    def cvc(l, name, i=0):
        o, w = CM.off[name]
        return cv[l][:, o + i:o + i + 1]

    def fm_proj(s, gi, out_cols=T):
        pb = next_pb()
        for kc in range(8):
            mm(pb[:, 0:T], s[:, kc, gi * 128:(gi + 1) * 128], hT[:, kc, :], start=(kc == 0), stop=(kc == 7))
        return pb

    dma(ident[:, :], ident_d)
    for l in range(DEPTH):
        dma(cv[l][:, :], cv_d[l])
        dma(lora_up[l][:, :], lora_d[l])
        memset(b_hist[l][:, :], 0.0)
        memset(b_S[l][:, :, :], 0.0)
    dma(d_dmat[:, :, :], dmat_d)
    dma(d_xi[:, :, :, :], dxi_d.rearrange("p (g j o) -> p g j o", g=4, j=2))
    dma(d_zeta[:, :, :], dzeta_d.rearrange("p (h o) -> p h o", o=1))
    dma(d_gch[:, :, :], dgch_d.rearrange("p (g o) -> p g o", o=1))
    dma(a_mask[:, :], amask_d)
    dma(a_sel[:, :], asel_d)
    for l in range(DEPTH):
        memset(d_R[l][:, :, :], 0.0)
        memset(a_hist[l][:, :, :], 0.0)
        memset(a_C[l][:, :, :], 0.0)
        memset(a_n[l][:, :, :], 0.0)
        memset(a_car[l][:, :], 0.0)
        ts(a_car[l][0:8, 2:3], cv[l][0:8, CM.off["a_fb"][0]:CM.off["a_fb"][0] + 1], -1.0, None, ALU.mult)
    dma(b_ones[:, :], bones_d)
    dma(b_i2[:, 0, :], bi2_d)
    dma(b_m5[:, :, :], bm5_d)
    memset(ones[:, :], 1.0)
    cp(ident16[:, :], ident[:, :])
    memset(ones16[:, :], 1.0)
    memset(eps12[:, :], 1e-12)
    memset(zeros[:, :], 0.0)
    for l in range(DEPTH):
        memset(c_hist[l][:, :, :], 0.0)
        memset(c_state[l][:, :], 0.0)
        o, w = CM.off["c_lam"]
        act(sm[:, 0:4], cv[l][:, o:o + 4], AF.Exp, scale=-1.0)
        act(sm[:, 4:8], sm[:, 0:4], AF.Ln, bias=ones[:, 0:1])
        ts(c_coef[l][:, 0:4], sm[:, 4:8], -8.0, None, ALU.mult)
        ts(c_coef[l][:, 4:8], sm[:, 4:8], -16.0, None, ALU.mult)

    def rmsnorm_pre(l, X):
        scrs = [scr, scr2]

        def stream(tb):
            sc = scrs[tb % 2]
            act(sc[:, :], X[:, tb, :], AF.Square)
            yield
            reduce(sm[:, 8 + tb:9 + tb], sc[:, :], ALU.add)
            yield
            ts(sm[:, 12 + tb:13 + tb], sm[:, 8 + tb:9 + tb], 1.0 / D_MODEL, 1e-6, ALU.mult, ALU.add)
            yield
            act(sm[:, 16 + tb:17 + tb], sm[:, 12 + tb:13 + tb], AF.Sqrt)
            yield
            recip(sm[:, 20 + tb:21 + tb], sm[:, 16 + tb:17 + tb])
            yield
            act(sc[:, :], X[:, tb, :], AF.Copy, scale=sm[:, 20 + tb:21 + tb])
            yield
            for half in range(2):
                pb = next_pb()
                for q in range(4):
                    kc = half * 4 + q
                    tr(pb[:, q * 128:(q + 1) * 128], sc[:, kc * 128:(kc + 1) * 128], ident[:, :])
                for q in range(4):
                    kc = half * 4 + q
                    ts(hT[:, kc, tb * 128:(tb + 1) * 128], pb[:, q * 128:(q + 1) * 128],
                       cvc(l, "gpre", kc), None, ALU.mult)
                yield
        interleave([stream(tb) for tb in range(NTB)])

    def mixer_C(l):
        cx, xc, rg, ig, aa, uu, hh = FS[0], FS[1], FS[2], FS[3], FS[4], FS[5], FS[6]
        dma(lru[l][:, :, :, :], lru_d[l])
        wx = load_slab(l, "Cx")

        def cstream(g):
            pb = fm_proj(wx, g)
            cp(cx[:, g, 0:3], c_hist[l][:, g, :])
            act(cx[:, g, 3:3 + T], pb[:, 0:T], AF.Copy)
            yield
            cp(c_hist[l][:, g, :], cx[:, g, T:T + 3])
            o, w = CM.off["c_cw"]
            ts(xc[:, g, 0:T], cx[:, g, 0:T], cv[l][:, o + g:o + g + 1], cvc(l, "c_cb", g), ALU.mult, ALU.add)
            yield
            for j in range(1, 4):
                stt(xc[:, g, 0:T], cx[:, g, j:j + T], cv[l][:, o + 4 * j + g:o + 4 * j + g + 1], xc[:, g, 0:T],
                    ALU.mult, ALU.add)
                yield
        interleave([cstream(g) for g in range(4)])
        for g in range(4):
            pb = next_pb()
            mm(pb[:, 0:T], lru[l][:, 0, g, :], xc[:, g, 0:T])
            act(rg[:, g, 0:T], pb[:, 0:T], AF.Sigmoid, bias=cvc(l, "c_br", g))
            pb = next_pb()
            mm(pb[:, 0:T], lru[l][:, 1, g, :], xc[:, g, 0:T])
            act(ig[:, g, 0:T], pb[:, 0:T], AF.Sigmoid, bias=cvc(l, "c_bi", g))
        for g in range(4):
            act(aa[:, g, 0:T], rg[:, g, 0:T], AF.Exp, scale=c_coef[l][:, g:g + 1])
            act(uu[:, g, 0:T], rg[:, g, 0:T], AF.Exp, scale=c_coef[l][:, 4 + g:5 + g])
        ts(uu[:, :, 0:T], uu[:, :, 0:T], -1.0, 1.0, ALU.mult, ALU.add)
        act(uu[:, :, 0:T], uu[:, :, 0:T], AF.Sqrt)
        tt(ig[:, :, 0:T], ig[:, :, 0:T], xc[:, :, 0:T], ALU.mult)
        tt(uu[:, :, 0:T], uu[:, :, 0:T], ig[:, :, 0:T], ALU.mult)
        for g in range(4):
            scan(hh[:, g, 0:T], aa[:, g, 0:T], uu[:, g, 0:T], c_state[l][:, g:g + 1], ALU.mult, ALU.add)
            cp(c_state[l][:, g:g + 1], hh[:, g, T - 1:T])
        wz = load_slab(l, "Cz")
        for g in range(4):
            pb = fm_proj(wz, g)
            act(rg[:, g, 0:T], pb[:, 0:T], AF.Silu)
            tt(yT[:, 8 + g, :], hh[:, g, 0:T], rg[:, g, 0:T], ALU.mult)

    def tm_proj(sl, tb):
        pb = next_pb()
        for kc in range(8):
            mm(pb[:, :], hT[:, kc, tb * 128:(tb + 1) * 128], sl[:, kc, :], start=(kc == 0), stop=(kc == 7))
        return pb

    def to_fm(dst, src, tb):
        pb = next_pb()
        for g in range(4):
            tr(pb[:, g * 128:(g + 1) * 128], src[:, tb, 2 * g:2 * g + 2, 0:64], ident[:, :])
        cp(dst[:, 0:4, tb * 128:(tb + 1) * 128], pb[:, :].rr("p (a b) -> p a b", a=4))

    def to_tm(dst, src, tb):
        pb = next_pb()
        for g in range(4):
            tr(pb[:, g * 128:(g + 1) * 128], src[:, g, tb * 128:(tb + 1) * 128], ident[:, :])
        cp(dst[:, tb, :, 0:64], pb[:, :].rr("p (h e) -> p h e", h=8))

    def head_ln_stream(dst, src, tb):
        X3 = src[:, tb, :, 0:64]
        D3 = dst[:, tb, :, 0:64]
        st_ = lnsts[tb % 2]
        sq = scr[:, (tb % 2) * 512:(tb % 2) * 512 + 512].rr("p (h e) -> p h e", h=8)
        reduce(st_[:, 0:8], X3, ALU.add)
        yield
        stt(D3, st_[:, 0:8].rr("p (h o) -> p h o", o=1).bc([128, 8, 64]), -1.0 / 64, X3, ALU.mult, ALU.add)
        yield
        tt(sq, D3, D3, ALU.mult)
        yield
        reduce(st_[:, 8:16], sq, ALU.add)
        yield
        ts(st_[:, 16:24], st_[:, 8:16], 1.0 / 64, 1e-5, ALU.mult, ALU.add)
        yield
        act(st_[:, 24:32], st_[:, 16:24], AF.Sqrt)
        yield
        recip(st_[:, 32:40], st_[:, 24:32])
        yield
        tt(D3, D3, st_[:, 32:40].rr("p (h o) -> p h o", o=1).bc([128, 8, 64]), ALU.mult)
        yield

    def head_ln_all(dst, src):
        interleave([head_ln_stream(dst, src, tb) for tb in range(NTB)])

    def mixer_D(l, ti):
        qr, kr, osb, xn = TM[0], TM[1], TM[4], TM[5]
        vv, kz = TM16[1], TM16[2]
        qT, kT, sz = AlT16, RhT16, FS[2]
        AT = bm16[0]
        d_R16 = b_S16
        cp(d_R16[:, :, :], d_R[l][:, :, :])
        S = [PB[3], PB[4]]; O = [PB[5], PB[6]]; ST = [PB[2], PB[7]]
        dma(rope_sb[:, :, :, :], rope_d.rearrange("(n tb p) (o e) -> n p tb o e", tb=NTB, p=128, o=1)[ti])
        for name, dst in (("Dq", qr), ("Dk", kr)):
            sl = load_slab(l, name)

            def rstream(tb, name=name, dst=dst, sl=sl):
                o5 = (tb % 2) * 512
                pb = tm_proj(sl, tb)
                act(scr[:, o5:o5 + 512], pb[:, :], AF.Copy)
                yield
                raw = scr[:, o5:o5 + 512].rr("p (h e) -> p h e", h=8)
                x1, x2 = raw[:, :, 0:32], raw[:, :, 32:64]
                cos = rope_sb[:, tb, :, 0:32].bc([128, 8, 32]); sin = rope_sb[:, tb, :, 32:64].bc([128, 8, 32])
                t1 = scr2[:, o5:o5 + 256].rr("p (h e) -> p h e", h=8)
                t2 = scr2[:, o5 + 256:o5 + 512].rr("p (h e) -> p h e", h=8)
                d1, d2 = dst[:, tb, :, 0:32], dst[:, tb, :, 32:64]
                tt(d1, x1, cos, ALU.mult); tt(t1, x2, sin, ALU.mult)
                yield
                tt(d2, x2, cos, ALU.mult); tt(t2, x1, sin, ALU.mult)
                yield
                tt(d1, d1, t1, ALU.subtract)
                yield
                tt(d2, d2, t2, ALU.add)
                yield
                to_fm(qT if name == "Dq" else kT, dst, tb)
                yield
            interleave([rstream(tb) for tb in range(NTB)])
        sl = load_slab(l, "Dv")
        for tb in range(NTB):
            pb = tm_proj(sl, tb)
            act(vv[:, tb, :, 0:64], pb[:, :].rr("p (h e) -> p h e", h=8), AF.Copy)
            tt(kz[:, tb, :, 0:64], kr[:, tb, :, 0:64], d_zeta[:, :, :].bc([128, 8, 64]), ALU.mult)
        sl = load_slab(l, "Dz")
        for g in range(4):
            pb = fm_proj(sl, g)
            act(sz[:, g, 0:T], pb[:, 0:T], AF.Silu)
        def d_chunk(c):
            tb, p = c // 2, c % 2
            rows = slice(64 * p, 64 * p + 64)
            cols = slice(c * 64, (c + 1) * 64)
            for j in range(2):
                jr = slice(64 * j, 64 * j + 64)
                for g in range(4):
                    mm(S[j][rows, g * 64:(g + 1) * 64], kT[jr, g, cols], qT[jr, g, cols])
            for h in range(8):
                g, j = h // 2, h % 2
                mm(ST[p][64 * j:64 * j + 64, g * 64:(g + 1) * 64], kz[rows, tb, h, 0:64], vv[rows, tb, h, 0:64])
            yield
            for j in range(2):
                tt(AT[rows, 4 * j:4 * j + 4, :], S[j][rows, 0:256].rr("p (a b) -> p a b", a=4),
                   d_dmat[rows, 4 * j:4 * j + 4, :], ALU.mult)
            yield
            for h in range(8):
                g, j = h // 2, h % 2
                mm(O[p][rows, h * 64:(h + 1) * 64], AT[rows, j * 4 + g, :], vv[rows, tb, h, 0:64])
            yield
            for j in range(2):
                jr = slice(64 * j, 64 * j + 64)
                for g in range(4):
                    mm(S[j][rows, 256 + g * 64:256 + (g + 1) * 64], qT[jr, g, cols], d_R16[jr, g, :])
            yield
            tt(d_R[l][:, :, :], d_R[l][:, :, :], d_gch[:, :, :].bc([128, 4, 64]), ALU.mult)
            tt(d_R[l][:, :, :], d_R[l][:, :, :], ST[p][:, 0:256].rr("p (a b) -> p a b", a=4), ALU.add)
            for j in range(2):
                tt(tmp4[rows, :, j, 0:64], S[j][rows, 256:512].rr("p (a b) -> p a b", a=4),
                   d_xi[rows, :, j, :].bc([64, 4, 64]), ALU.mult)
            cp(d_R16[:, :, :], d_R[l][:, :, :])
            tt(osb[rows, tb, :, 0:64], O[p][rows, :].rr("p (h e) -> p h e", h=8),
               tmp4[rows, :, :, 0:64].rr("p g j e -> p (g j) e"), ALU.add)
            yield
        run_chunks(d_chunk)
        head_ln_all(xn, osb)
        for tb in range(NTB):
            pb = next_pb()
            for g in range(4):
                tr(pb[:, g * 128:(g + 1) * 128], xn[:, tb, 2 * g:2 * g + 2, 0:64], ident[:, :])
            for g in range(4):
                stt(yT[:, 12 + g, tb * 128:(tb + 1) * 128], pb[:, g * 128:(g + 1) * 128], cvc(l, "d_nw", g),
                    sz[:, g, tb * 128:(tb + 1) * 128], ALU.mult, ALU.mult)

    def conv_silu(l, sl, g8, dst, gdst, cx, whichhist, cwname, cbname, ngroups_total, silu=True):
        pass

    def mixer_A(l, ti):
        cx, qT32, kT32, sz = FS[0], FS[1], FS[2], FS[3]
        qT, kT = AlT16, RhT16
        so, osb, xn = TM[2], TM[3], TM[4]
        ktm, vv = TM16[0], TM16[1]
        AT, vw = bm16[0], bm16[1]
        a_C16 = b_S16
        cp(a_C16[:, :, :], a_C[l][:, :, :])
        cp(a_n16[:, :, :], a_n[l][:, :, :])
        S = [PB[3], PB[4]]; O = [PB[5], PB[6]]; ST = [PB[2], PB[7]]; JX = [PB[0], PB[1]]
        R_I, R_SP, R_F, R_G, R_P, R_MU, R_PE, R_SI, R_WJ, R_NM, R_NP = range(11)
        ocw = CM.off["a_cw"][0]
        for which, (name, dstT, dst16) in enumerate((("Aq", qT32, qT), ("Ak", kT32, kT))):
            sl = load_slab(l, name)

            def astream(g, which=which, sl=sl, dstT=dstT, dst16=dst16):
                g8 = which * 4 + g
                pb = fm_proj(sl, g)
                cp(cx[:, g, 0:3], a_hist[l][:, g8, :])
                act(cx[:, g, 3:3 + T], pb[:, 0:T], AF.Copy)
                yield
                cp(a_hist[l][:, g8, :], cx[:, g, T:T + 3])
                ts(dstT[:, g, 0:T], cx[:, g, 0:T], cv[l][:, ocw + g8:ocw + g8 + 1], cvc(l, "a_cb", g8),
                   ALU.mult, ALU.add)
                yield
                for j in range(1, 4):
                    stt(dstT[:, g, 0:T], cx[:, g, j:j + T], cv[l][:, ocw + 8 * j + g8:ocw + 8 * j + g8 + 1],
                        dstT[:, g, 0:T], ALU.mult, ALU.add)
                    yield
                act(dst16[:, g, 0:T], dstT[:, g, 0:T], AF.Silu)
                yield
            interleave([astream(g) for g in range(4)])
        sl = load_slab(l, "Az")
        for g in range(4):
            pb = fm_proj(sl, g)
            act(sz[:, g, 0:T], pb[:, 0:T], AF.Silu)
        acut = int(os.environ.get("ACUT", "99"))
        if acut < 1:
            return
        sl = load_slab(l, "Ag", ncols=128)
        pbi = next_pb()
        for kc in range(8):
            mm(pbi[0:8, 0:T], sl[:, kc, 0:8], hT[:, kc, :], start=(kc == 0), stop=(kc == 7))
        act(ga[0:8, R_I, 0:T], pbi[0:8, 0:T], AF.Identity, bias=cv[l][0:8, CM.off["a_ib"][0]:CM.off["a_ib"][0] + 1])
        gcut = int(os.environ.get("GCUT", "99"))
        if gcut < 1:
            return
        pbf = next_pb()
        for kc in range(8):
            mm(pbf[0:8, 0:T], sl[:, kc, 8:16], hT[:, kc, :], start=(kc == 0), stop=(kc == 7))
        act(ga[0:8, R_SP, 0:T], pbf[0:8, 0:T], AF.Exp, bias=a_car[l][0:8, 2:3], scale=-1.0)
        act(ga[0:8, R_SP, 0:T], ga[0:8, R_SP, 0:T], AF.Ln, bias=ones[0:8, 0:1])
        if gcut < 2:
            return
        scan(ga[0:8, R_F, 0:T], ones[0:8, 0:T], ga[0:8, R_SP, 0:T], a_car[l][0:8, 0:1], ALU.mult, ALU.subtract)
        cp(a_car[l][0:8, 0:1], ga[0:8, R_F, T - 1:T])
        tt(ga[0:8, R_G, 0:T], ga[0:8, R_I, 0:T], ga[0:8, R_F, 0:T], ALU.subtract)
        if gcut < 3:
            return
        cp(ga[0:8, R_P, 0:1], a_car[l][0:8, 1:2])
        scan(ga[0:8, R_P, 1:T + 1], ones[0:8, 0:T], ga[0:8, R_G, 0:T], a_car[l][0:8, 1:2], ALU.mult, ALU.max)
        cp(a_car[l][0:8, 1:2], ga[0:8, R_P, T:T + 1])
        if gcut < 4:
            return
        for c in range(NCH):
            cols = slice(c * 64, (c + 1) * 64)
            ts(ga[0:8, R_MU, cols], zeros[0:8, 0:64], ga[0:8, R_P, c * 64:c * 64 + 1], None, ALU.add)
            ts(ga[0:8, R_PE, cols], zeros[0:8, 0:64], ga[0:8, R_P, c * 64 + 64:c * 64 + 65], None, ALU.add)
            tt(scl[0:8, c:c + 1], ga[0:8, R_P, c * 64:c * 64 + 1], ga[0:8, R_P, c * 64 + 64:c * 64 + 65], ALU.subtract)
        if gcut < 5:
            return
        Pv = ga[0:8, R_P, 1:T + 1]
        tt(ga[0:8, R_SI, 0:T], ga[0:8, R_MU, 0:T], Pv, ALU.subtract)
        tt(ga[0:8, R_WJ, 0:T], ga[0:8, R_G, 0:T], ga[0:8, R_PE, 0:T], ALU.subtract)
        stt(ga[0:8, R_NM, 0:T], ga[0:8, R_F, 0:T], -1.0, Pv, ALU.mult, ALU.subtract)
        ts(ga[0:8, R_NP, 0:T], Pv, -1.0, None, ALU.mult)
        if acut < 2:
            return
        for tb in range(NTB):
            pb = next_pb()
            for r, R in enumerate((R_SI, R_WJ, R_NM)):
                tr(pb[:, r * 8:(r + 1) * 8], ga[0:8, R, tb * 128:(tb + 1) * 128], ident[0:8, 0:8])
            act(tmsc[:, tb, :, :], pb[:, 0:24].rr("p (a b) -> p a b", a=3), AF.Exp)
        if acut < 3:
            return
        tt(rhs_sc[0:8, :, :], scl[0:8, :].rr("p (c o) -> p c o", o=1).bc([8, NCH, 4]),
           a_sel[0:8, 512:516].rr("p (o g) -> p o g", o=1).bc([8, NCH, 4]), ALU.mult)
        pb = next_pb()
        mm(pb[:, 0:NCH * 4], a_sel[0:8, 516:644], rhs_sc[0:8, :, :].rr("p c g -> p (c g)"))
        act(scb[:, :, :, :].rr("p c g o -> p (c g o)"), pb[:, 0:NCH * 4], AF.Exp)
        if acut < 4:
            return
        for tb in range(NTB):
            pbt = next_pb()
            for g in range(4):
                mm(pbt[:, g * 128:(g + 1) * 128], kT[:, g, tb * 128:(tb + 1) * 128], ident16[:, :])
            cp(ktm[:, tb, :, :], pbt[:, :].rr("p (h e) -> p h e", h=8))
            cp(wj16[:, tb, :], tmsc[:, tb, 1, :])
        sl = load_slab(l, "Av")
        for tb in range(NTB):
            pb = tm_proj(sl, tb)
            act(vv[:, tb, :, :], pb[:, :].rr("p (h e) -> p h e", h=8), AF.Copy)
        sl = load_slab(l, "Ao")
        for tb in range(NTB):
            pb = tm_proj(sl, tb)
            act(so[:, tb, :, :], pb[:, :].rr("p (h e) -> p h e", h=8), AF.Sigmoid)
        ETs = [ET, ET1]
        for tb in range(NTB):
            E = next_pb()
            mm(E[:, :], ident[:, :], a_mask[:, :], start=True, stop=False)
            mm(E[:, :], ga[0:8, R_G, tb * 128:(tb + 1) * 128], a_sel[0:8, 0:512], start=False, stop=False)
            for p in range(2):
                c = tb * 2 + p
                tt(negPx[0:8, p, :, :], ga[0:8, R_NP:R_NP + 1, c * 64:(c + 1) * 64].bc([8, 8, 64]),
                   a_sel[0:8, 0:512].rr("p (h e) -> p h e", h=8), ALU.mult)
                mm(E[64 * p:64 * p + 64, :], ones[0:8, 0:64], negPx[0:8, p, :, :].rr("p h e -> p (h e)"),
                   start=False, stop=True)
            act(ETs[tb][:, :, :].rr("p h e -> p (h e)"), E[:, :], AF.Exp)

        def a_chunk(c):
            tb, p = c // 2, c % 2
            rows = slice(64 * p, 64 * p + 64)
            cols = slice(c * 64, (c + 1) * 64)
            ETt = ETs[tb]
            sI4 = tmsc[:, tb, 0, :].rr("p (g j o) -> p g j o", g=4, j=2)
            for j in range(2):
                jr = slice(64 * j, 64 * j + 64)
                for g in range(4):
                    mm(S[j][rows, g * 64:(g + 1) * 64], kT[jr, g, cols], qT[jr, g, cols])
            tt(vw[rows, :, :], vv[rows, tb, :, :],
               tmsc[rows, tb, 1, :].rr("p (h o) -> p h o", o=1).bc([64, 8, 64]), ALU.mult)
            yield
            for j in range(2):
                stt(AT[rows, 4 * j:4 * j + 4, :], S[j][rows, 0:256].rr("p (a b) -> p a b", a=4), 0.125,
                    ETt[rows, 4 * j:4 * j + 4, :], ALU.mult, ALU.mult)
            yield
            for h in range(8):
                g, j = h // 2, h % 2
                mm(O[p][rows, h * 64:(h + 1) * 64], AT[rows, j * 4 + g, :], vv[rows, tb, h, :])
                mm(ST[p][rows, 384 + h:385 + h], AT[rows, j * 4 + g, :], ones16[rows, 0:1])
            for h in range(8):
                g, j = h // 2, h % 2
                jr = slice(64 * j, 64 * j + 64)
                mm(ST[p][jr, g * 64:(g + 1) * 64], ktm[rows, tb, h, :], vw[rows, h, :])
                mm(ST[p][jr, 256 + g:257 + g], ktm[rows, tb, h, :], wj16[rows, tb, h:h + 1])
            yield
            for j in range(2):
                jr = slice(64 * j, 64 * j + 64)
                for g in range(4):
                    mm(JX[j][rows, g * 64:(g + 1) * 64], qT[jr, g, cols], a_C16[jr, g, :])
                    mm(JX[j][rows, 256 + g:257 + g], qT[jr, g, cols], a_n16[jr, g, :])
            yield
            for j in range(2):
                tt(tmp4[rows, :, j, :], JX[j][rows, 0:256].rr("p (a b) -> p a b", a=4),
                   sI4[rows, :, j, :].bc([64, 4, 64]), ALU.mult)
                tt(lnst[rows, 40:48].rr("p (g j) -> p g j", j=2)[:, :, j], JX[j][rows, 256:260],
                   tmsc[rows, tb, 0, :].rr("p (g j) -> p g j", j=2)[:, :, j], ALU.mult)
            tt(a_C[l][:, :, :], a_C[l][:, :, :], scb[:, c, :, :].bc([128, 4, 64]), ALU.mult)
            stt(a_C[l][:, :, :], ST[p][:, 0:256].rr("p (a b) -> p a b", a=4), 0.125, a_C[l][:, :, :],
                ALU.mult, ALU.add)
            tt(a_n[l][:, :, :], a_n[l][:, :, :], scb[:, c, :, :], ALU.mult)
            stt(a_n[l][:, :, :], ST[p][:, 256:260].rr("p (a b) -> p a b", b=1), 0.125, a_n[l][:, :, :],
                ALU.mult, ALU.add)
            cp(a_C16[:, :, :], a_C[l][:, :, :])
            cp(a_n16[:, :, :], a_n[l][:, :, :])
            tt(hnum[rows, :, :], O[p][rows, :].rr("p (h e) -> p h e", h=8),
               tmp4[rows, :, :, :].rr("p g j e -> p (g j) e"), ALU.add)
            tt(lnst[rows, 48:56], ST[p][rows, 384:392], lnst[rows, 40:48], ALU.add)
            stt(lnst[rows, 48:56], lnst[rows, 48:56], -1.0, lnst[rows, 48:56], ALU.mult, ALU.max)
            tt(lnst[rows, 48:56], lnst[rows, 48:56], tmsc[rows, tb, 2, :], ALU.max)
            recip(lnst[rows, 56:64], lnst[rows, 48:56])
            tt(hnum[rows, :, :], hnum[rows, :, :],
               lnst[rows, 56:64].rr("p (h o) -> p h o", o=1).bc([64, 8, 64]), ALU.mult)
            tt(osb[rows, tb, :, :], hnum[rows, :, :], so[rows, tb, :, :], ALU.mult)
            yield
        run_chunks(a_chunk)
        head_ln_all(xn, osb)
        for tb in range(NTB):
            pb = next_pb()
            for g in range(4):
                tr(pb[:, g * 128:(g + 1) * 128], xn[:, tb, 2 * g:2 * g + 2, 0:64], ident[:, :])
            for g in range(4):
                stt(yT[:, g, tb * 128:(tb + 1) * 128], pb[:, g * 128:(g + 1) * 128], cvc(l, "a_nw", g),
                    sz[:, g, tb * 128:(tb + 1) * 128], ALU.mult, ALU.mult)

    def mixer_B(l, ti):
        rS, kS, vS, sz = FS[0], FS[1], FS[2], FS[3]
        RhT, AlT, bon = rS, kS, vS
        pools = (FS[4], FS[5], FS[6])
        pools2 = (FS[7], FS[8], FS[9])

        def slot(i):
            return pools[i // 4][:, i % 4, :]

        def slot2(i):
            return pools2[i // 4][:, i % 4, :]
        raw, lora, aT, lw, lc, kt, kh, beta, eg, egi, egm, tA = [slot(i) for i in range(12)]
        SETS = [dict(raw=raw, aT=aT, lw=lw, lc=lc, kt=kt, kh=kh, beta=beta, eg=eg, egi=egi, egm=egm, tA=tA,
                     beta16=bk16[:, 0, :], kt16=bk16[:, 1, :]),
                dict(raw=slot2(0), aT=slot2(2), lw=slot2(3), lc=slot2(4), kt=slot2(5), kh=slot2(6), beta=slot2(7),
                     eg=slot2(8), egi=slot2(9), egm=slot2(10), tA=slot2(11),
                     beta16=bk16[:, 2, :], kt16=bk16[:, 3, :])]
        wkv, xn = TM[3], TM[4]
        Vtm, Btm, Ktm = TM16[0], TM16[1], TM16[2]
        BJ = [PB[3], PB[4]]; PA = [PB[2], PB[7]]; PQ = [PB[5], PB[6]]; PX = [PB[0], PB[1]]
        Pn, Qn, Xm = bm16
        W2 = AT
        beta16, kt16 = bk16[:, 0, :], bk16[:, 1, :]
        omu = CM.off["b_mu"][0]

        def shift_mix(dst, sl, gi, hidx, mucol, raw_, tA_):
            pb = fm_proj(sl, gi)
            cp(raw_[:, 0:1], b_hist[l][:, hidx:hidx + 1])
            act(raw_[:, 1:T + 1], pb[:, 0:T], AF.Copy)
            yield
            cp(b_hist[l][:, hidx:hidx + 1], raw_[:, T:T + 1])
            tt(tA_[:, 0:T], raw_[:, 0:T], raw_[:, 1:T + 1], ALU.subtract)
            yield
            stt(dst, tA_[:, 0:T], cv[l][:, omu + mucol:omu + mucol + 1], raw_[:, 1:T + 1], ALU.mult, ALU.add)
            yield

        cp(b_S16[:, :, :], b_S[l][:, :, :])
        sl = load_slab(l, "Bl", ncols=128)
        interleave([shift_mix(lora[:, 0:T], sl, 0, 12, 12, raw, tA)])
        act(lora[0:64, 0:T], lora[0:64, 0:T], AF.Tanh)
        tslots = [(slot(2 + 2 * i), slot(3 + 2 * i)) for i in range(4)]
        for wi, (name, dst) in enumerate((("Br", rS), ("Bk", kS), ("Bv", vS))):
            sl = load_slab(l, name)
            interleave([shift_mix(dst[:, g, 0:T], sl, g, wi * 4 + g, wi * 4 + g, tslots[g][0], tslots[g][1])
                        for g in range(4)])
        sl = load_slab(l, "Bz")
        for g in range(4):
            pb = fm_proj(sl, g)
            act(sz[:, g, 0:T], pb[:, 0:T], AF.Silu)
        bcut = int(os.environ.get("BCUT", "99"))
        if bcut < 1:
            return
        def group_chains(g, Z):
            gc = slice(g * 128, (g + 1) * 128)
            raw, aT, lw, lc, kt, kh, beta = Z["raw"], Z["aT"], Z["lw"], Z["lc"], Z["kt"], Z["kh"], Z["beta"]
            eg, egi, egm, tA, beta16, kt16 = Z["eg"], Z["egi"], Z["egm"], Z["tA"], Z["beta16"], Z["kt16"]

            def chainW():
                pb = next_pb()
                mm(pb[:, 0:T], lora_up[l][0:64, gc], lora[0:64, 0:T])
                act(lw[:, 0:T], pb[:, 0:T], AF.Sigmoid, bias=cvc(l, "b_w0", g))
                yield
                for c in range(NCH):
                    cols = slice(c * 64, (c + 1) * 64)
                    scan(lc[:, cols], ones[:, 0:64], lw[:, cols], 0.0, ALU.mult, ALU.add)
                    yield
                act(eg[:, 0:T], lc[:, 0:T], AF.Exp, scale=-0.606531)
                act(egi[:, 0:T], lc[:, 0:T], AF.Exp, scale=0.606531)
                tt(tA[:, 0:T], lc[:, 0:T], lw[:, 0:T], ALU.subtract)
                yield
                act(egm[:, 0:T], tA[:, 0:T], AF.Exp, scale=-0.606531)
                cp(gLt[:, g, :], eg[:, 63:T:64])
                yield

            def chainA():
                pb = next_pb()
                mm(pb[:, 0:T], lora_up[l][64:128, gc], lora[64:128, 0:T])
                act(aT[:, 0:T], pb[:, 0:T], AF.Sigmoid, bias=cvc(l, "b_a0", g))
                yield
                ts(beta[:, 0:T], aT[:, 0:T], -1.0, cvc(l, "b_ka", g), ALU.add, ALU.mult)
                yield
                stt(kt[:, 0:T], beta[:, 0:T], 1.0, kS[:, g, 0:T], ALU.add, ALU.mult)
                yield

            def chainK():
                ts(kh[:, 0:T], kS[:, g, 0:T], cvc(l, "b_kk", g), None, ALU.mult)
                yield
                tt(raw[:, 0:T], kh[:, 0:T], kh[:, 0:T], ALU.mult)
                yield
                pb = next_pb()
                mm(pb[:, 0:T], b_ones[:, :], raw[:, 0:T])
                act(raw[:, 0:T], pb[:, 0:T], AF.Sqrt, bias=eps12[:, 0:1])
                yield
                recip(raw[:, 0:T], raw[:, 0:T])
                yield
                tt(kh[:, 0:T], kh[:, 0:T], raw[:, 0:T], ALU.mult)
                yield

            def chainV():
                for tb in range(NTB):
                    pbt = next_pb()
                    tr(pbt[:, 0:128], vS[:, g, tb * 128:(tb + 1) * 128], ident[:, :])
                    cp(Vtm[:, tb, 2 * g:2 * g + 2, :], pbt[:, 0:128].rr("p (a b) -> p a b", a=2))
                    yield

            def chainR():
                stt(raw[:, 0:T], rS[:, g, 0:T], cvc(l, "b_rk", g), kt[:, 0:T], ALU.mult, ALU.mult)
                yield
                pbb = next_pb()
                mm(pbb[:, 0:T], b_ones[:, :], raw[:, 0:T])
                tt(bon[:, g, 0:T], pbb[:, 0:T], vS[:, g, 0:T], ALU.mult)
                yield

            def chainB():
                tt(beta[:, 0:T], aT[:, 0:T], kh[:, 0:T], ALU.mult)
                yield
                tt(beta16[:, 0:T], beta[:, 0:T], egi[:, 0:T], ALU.mult)
                yield

            def chainO():
                tt(AlT16[:, g, 0:T], kh[:, 0:T], egm[:, 0:T], ALU.mult)
                yield
                tt(RhT16[:, g, 0:T], rS[:, g, 0:T], eg[:, 0:T], ALU.mult)
                yield
                tt(kt16[:, 0:T], kt[:, 0:T], egi[:, 0:T], ALU.mult)
                yield

            def tail():
                for tb in range(NTB):
                    pbt = next_pb()
                    mm(pbt[:, 0:128], beta16[:, tb * 128:(tb + 1) * 128], ident16[:, :])
                    mm(pbt[:, 128:256], kt16[:, tb * 128:(tb + 1) * 128], ident16[:, :])
                    cp(Btm[:, tb, 2 * g:2 * g + 2, :], pbt[:, 0:128].rr("p (a b) -> p a b", a=2))
                    cp(Ktm[:, tb, 2 * g:2 * g + 2, :], pbt[:, 128:256].rr("p (a b) -> p a b", a=2))
                    yield
            return [chainW(), chainA(), chainK(), chainV()], [chainR(), chainB(), chainO()], tail

        def products(g, Z):
            beta16, kt16 = Z["beta16"], Z["kt16"]
            for tb in range(NTB):
                for j in range(2):
                    jr = slice(64 * j, 64 * j + 64)
                    for p in range(2):
                        c = tb * 2 + p
                        rows = slice(64 * p, 64 * p + 64)
                        cols = slice(c * 64, (c + 1) * 64)
                        A_, B_, K_, R_ = AlT16[jr, g, cols], beta16[jr, cols], kt16[jr, cols], RhT16[jr, g, cols]
                        mm(BJ[j][rows, 0:64], A_, B_)
                        mm(BJ[j][rows, 64:128], B_, A_)
                        mm(BJ[j][rows, 128:192], K_, A_)
                        mm(BJ[j][rows, 192:256], B_, R_)
                        mm(BJ[j][rows, 256:320], K_, R_)
                    tt(PRall[tb][:, :, 2 * g + j, :], BJ[j][:, 0:320].rr("p (k t) -> p k t", k=5), b_m5[:, :, :],
                       ALU.mult)

        for g0 in (0, 2):
            ph1, ph2, tails = [], [], []
            for gi, g in enumerate((g0, g0 + 1)):
                a1, a2, tl = group_chains(g, SETS[gi])
                ph1 += a1; ph2 += a2; tails.append(tl())
            interleave(ph1)
            interleave(ph2)
            interleave(tails)
            for gi, g in enumerate((g0, g0 + 1)):
                products(g, SETS[gi])
        if bcut < 2:
            return
        for tb in range(NTB):
            PRt = PRall[tb]
            P0, Q0, MkT, NbT, NkT = (PRt[:, k, :, :] for k in range(5))
            Wsb, Usb = PRt[:, 2, :, :], PRt[:, 4, :, :]
            for p in range(2):
                rows = slice(64 * p, 64 * p + 64)
                c = tb * 2 + p
                for h in range(8):
                    g, j = h // 2, h % 2
                    mm(PA[p][rows, h * 64:(h + 1) * 64], MkT[rows, h, :], Vtm[rows, tb, h, :])
                    mm(PQ[p][rows, h * 64:(h + 1) * 64], NkT[rows, h, :], Vtm[rows, tb, h, :])
                    mm(PX[p][64 * j:64 * j + 64, g * 64:(g + 1) * 64], Ktm[rows, tb, h, :], Vtm[rows, tb, h, :])
                act(W2[rows, :, :], PA[p][rows, :].rr("p (h e) -> p h e", h=8), AF.Copy)
                act(Y2[rows, :, :], PQ[p][rows, :].rr("p (h e) -> p h e", h=8), AF.Copy)
                tt(K2g[:, p, :, :], PX[p][:, 0:256].rr("p (a b) -> p a b", a=4),
                   gLt[:, :, c:c + 1].bc([128, 4, 64]), ALU.mult)
            tt(Xm[:, :, :], b_i2[:, :, :].bc([128, 8, 64]), Q0, ALU.subtract)
            Pc, Qc = P0, Q0
            nxt = [(Pn, Qn), (P0, Q0)]
            for i in range(1, 6):
                Pd, Qd = nxt[(i - 1) % 2]
                for p in range(2):
                    rows = slice(64 * p, 64 * p + 64)
                    for h in range(8):
                        hc = slice(h * 64, (h + 1) * 64)
                        mm(PA[p][rows, hc], Qc[rows, h, :], Pc[rows, h, :])
                        if i < 5:
                            mm(PQ[p][rows, hc], Pc[rows, h, :], Qc[rows, h, :])
                for p in range(2):
                    rows = slice(64 * p, 64 * p + 64)
                    act(Pd[rows, :, :], PA[p][rows, :].rr("p (h e) -> p h e", h=8), AF.Copy)
                    if i < 5:
                        cp(Qd[rows, :, :], PQ[p][rows, :].rr("p (h e) -> p h e", h=8))
                Pc, Qc = Pd, Qd
                for p in range(2):
                    rows = slice(64 * p, 64 * p + 64)
                    for h in range(8):
                        mm(PX[p][rows, h * 64:(h + 1) * 64], Pc[rows, h, :], Xm[rows, h, :])
                for p in range(2):
                    rows = slice(64 * p, 64 * p + 64)
                    tt(Xm[rows, :, :], Xm[rows, :, :], PX[p][rows, :].rr("p (h e) -> p h e", h=8), ALU.add)
            if bcut < 3:
                continue
            for p in range(2):
                c = tb * 2 + p
                rows = slice(64 * p, 64 * p + 64)
                cols = slice(c * 64, (c + 1) * 64)
                for j in range(2):
                    jr = slice(64 * j, 64 * j + 64)
                    for g in range(4):
                        mm(BJ[j][rows, g * 64:(g + 1) * 64], AlT16[jr, g, cols], b_S16[jr, g, :])
                        mm(BJ[j][rows, 256 + g * 64:256 + (g + 1) * 64], RhT16[jr, g, cols], b_S16[jr, g, :])
                W24 = W2[rows, :, :].rr("p (g j) e -> p g j e", j=2)
                Ws4 = Wsb[rows, :, :].rr("p (g j) e -> p g j e", j=2)
                Y24 = Y2[rows, :, :].rr("p (g j) e -> p g j e", j=2)
                for j in range(2):
                    tt(Ws4[:, :, j, :], BJ[j][rows, 0:256].rr("p (a b) -> p a b", a=4), W24[:, :, j, :], ALU.add)
                    tt(tmp4[rows, :, j, :], BJ[j][rows, 256:512].rr("p (a b) -> p a b", a=4), Y24[:, :, j, :],
                       ALU.add)
                for h in range(8):
                    mm(PX[p][rows, h * 64:(h + 1) * 64], Xm[rows, h, :], Wsb[rows, h, :])
                act(Usb[rows, :, :], PX[p][rows, :].rr("p (h e) -> p h e", h=8), AF.Copy)
                for h in range(8):
                    g, j = h // 2, h % 2
                    mm(PA[p][rows, h * 64:(h + 1) * 64], NbT[rows, h, :], Usb[rows, h, :])
                    mm(PQ[p][64 * j:64 * j + 64, g * 64:(g + 1) * 64], Btm[rows, tb, h, :], Usb[rows, h, :])
                tt(wkv[rows, tb, :, :], tmp4[rows, :, :, :].rr("p g j e -> p (g j) e"),
                   PA[p][rows, :].rr("p (h e) -> p h e", h=8), ALU.subtract)
                tt(b_S[l][:, :, :], b_S[l][:, :, :], PQ[p][:, 0:256].rr("p (a b) -> p a b", a=4), ALU.subtract)
                tt(b_S[l][:, :, :], b_S[l][:, :, :], gLt[:, :, c:c + 1].bc([128, 4, 64]), ALU.mult)
                tt(b_S[l][:, :, :], b_S[l][:, :, :], K2g[:, p, :, :], ALU.add)
                cp(b_S16[:, :, :], b_S[l][:, :, :])
        ogw, ogb = CM.off["b_gw"][0], CM.off["b_gb"][0]
        head_ln_all(xn, wkv)
        for tb in range(NTB):
            pb = next_pb()
            for g in range(4):
                tr(pb[:, g * 128:(g + 1) * 128], xn[:, tb, 2 * g:2 * g + 2, 0:64], ident[:, :])
            for g in range(4):
                tc_ = slice(tb * 128, (tb + 1) * 128)
                ts(scr[:, 0:128], pb[:, g * 128:(g + 1) * 128], cv[l][:, ogw + g:ogw + g + 1],
                   cv[l][:, ogb + g:ogb + g + 1], ALU.mult, ALU.add)
                tt(scr[:, 0:128], scr[:, 0:128], bon[:, g, tc_], ALU.add)
                tt(yT[:, 4 + g, tc_], scr[:, 0:128], sz[:, g, tc_], ALU.mult)

    def out_proj_residual(l, X):
        dma(gpost[l][:, :], gpost_d[l])
        for si in range(4):
            s = load_wo(l, si)
            for tb in range(NTB):
                for half in range(2):
                    pb = PB[4 + tb * 2 + half]
                    for q in range(4):
                        fc = si * 4 + q
                        mm(pb[:, :], yT[:, fc, tb * 128:(tb + 1) * 128], s[:, q * 2 + half, :],
                           start=(fc == 0), stop=(fc == 15))
        if cut < 4:
            return
        for tb in range(NTB):
            for half in range(2):
                act(scr[:, half * 512:(half + 1) * 512], PB[4 + tb * 2 + half][:, :], AF.Copy)
            if cut < 5:
                continue
            act(scr2[:, :], scr[:, :], AF.Square)
            reduce(sm[:, 24:25], scr2[:, :], ALU.add)
            ts(sm[:, 25:26], sm[:, 24:25], 1.0 / D_MODEL, 1e-6, ALU.mult, ALU.add)
            act(sm[:, 26:27], sm[:, 25:26], AF.Sqrt)
            recip(sm[:, 27:28], sm[:, 26:27])
            if cut < 6:
                continue
            stt(scr2[:, :], scr[:, :], sm[:, 27:28], gpost[l][:, :], ALU.mult, ALU.mult)
            if cut < 7:
                continue
            tt(X[:, tb, :], X[:, tb, :], scr2[:, :], ALU.add)

    xv = x_d.rearrange("(n tb p) d -> n p tb d", tb=NTB, p=128)
    yv = y_d.rearrange("(n tb p) d -> n p tb d", tb=NTB, p=128)
    for ti in range(ntiles):
        X = xt[0]
        dma(X[:, :, :], xv[ti])
        for l in range(nlayers):
            if cut >= 1:
                rmsnorm_pre(l, X)
            memset(yT[:, :, :], 0.0)
            if "C" in mixers and cut >= 2:
                mixer_C(l)
            if "D" in mixers:
                mixer_D(l, ti)
            if "A" in mixers:
                mixer_A(l, ti)
            if "B" in mixers:
                mixer_B(l, ti)
            if cut >= 3:
                out_proj_residual(l, X)
        dma(yv[ti], X[:, :, :])

    P.emit()
    return nc, stack


def kernel(**inputs):
    inputs = {k: np.asarray(v) for k, v in inputs.items()}
    return run(inputs)


def make_in_map(xc, hp, tb):
    m = dict(x=xc, wt=hp["wt"], wo=hp["wo"], cv=hp["cv"], lora=hp["lora"], lru=hp["lru"], gpost=hp["gpost"])
    m.update(tb)
    return m


def run(inputs, ntiles=SEQ // T, nlayers=DEPTH, mixers="ABCD"):
    hp = host_prepare(inputs)
    tb = host_tables()
    nc, stack = build_program(ntiles, nlayers, mixers)
    x = np.ascontiguousarray(inputs["x"], dtype=np.float32)
    in_maps = [make_in_map(x[c], hp, tb) for c in range(NCORES)]
    with stack:
        res = run_bass_kernel_spmd(nc, in_maps, core_ids=list(range(NCORES)))
    out = np.stack([np.asarray(r["y"]) for r in res.results], axis=0)
    return out.astype(np.float32)
```

```python
import contextlib
import numpy as np
import concourse.bass as bass
import concourse.mybir as mybir
from concourse.bass_utils import run_bass_kernel_spmd

F32 = mybir.dt.float32
BF16 = mybir.dt.bfloat16
AF = mybir.ActivationFunctionType
ALU = mybir.AluOpType
AX = mybir.AxisListType

D_MODEL = 1024
SEQ = 4096
BATCH = 4
DEPTH = 2
G = 512
D_IN = 7824
T = 256
NTB = T // 128
NCH = T // 64
NCORES = 4
NSLABBUF = 3
import os as _os
INORDER = tuple(_os.environ.get("INORDER", "pe").split(","))


class Buf:
    def __init__(self, name, tile, psum=False):
        self.name = name
        self.tile = tile
        self.psum = psum
        self.acc = []
        self.dma_sem = None
        self.dma_ops = []

    def __getitem__(self, idx):
        return Ref(self, self.tile[idx])


def _box(ap):
    pat = ap.ap
    pstep = pat[0][0]
    off = int(ap.offset)
    p0 = off // pstep if pstep else 0
    f0 = off % pstep if pstep else off
    ext = 0
    for st, cnt in pat[1:]:
        ext += abs(st) * (cnt - 1)
    return (p0, p0 + pat[0][1], f0, f0 + ext + 1)


class Ref:
    def __init__(self, buf, ap, box=None):
        self.buf = buf
        self.ap = ap
        if box is None:
            box = _box(ap)
            if buf.psum:
                box = ((box[0] // 32) * 32, ((box[1] + 31) // 32) * 32, 0, 512)
        self.box = box

    def bc(self, shape):
        return Ref(self.buf, self.ap.to_broadcast(list(shape)), self.box)

    def __getitem__(self, idx):
        return Ref(self.buf, self.ap[idx])

    def rr(self, pat, **kw):
        return Ref(self.buf, self.ap.rearrange(pat, **kw), self.box)


def _ovl(a, b):
    return a[0] < b[1] and b[0] < a[1] and a[2] < b[3] and b[2] < a[3]


def _cov(a, b):
    return a[0] <= b[0] and a[1] >= b[1] and a[2] <= b[2] and a[3] >= b[3]


class Prog:
    ENG = ("pe", "act", "dve", "pool", "sp")

    def __init__(self, nc, stack):
        self.nc = nc
        self.stack = stack
        self.ops = []
        self.nbuf = 0

    def sb(self, name, shape, dtype=F32):
        t = self.stack.enter_context(self.nc.sbuf_tensor("s_" + name, list(shape), dtype))
        return Buf(name, t)

    def ps(self, name):
        t = self.stack.enter_context(self.nc.psum_tensor("p_" + name, [128, 512], F32))
        return Buf(name, t, psum=True)

    def op(self, eng, fn, outs, ins, dma=False):
        oid = len(self.ops)
        deps = set()
        for r, w in [(x, True) for x in outs] + [(x, False) for x in ins]:
            if r is None or not isinstance(r, Ref):
                continue
            b = r.buf
            for (bx, o2, w2) in b.acc:
                if (w or w2) and _ovl(bx, r.box):
                    deps.add(o2)
        for r, w in [(x, True) for x in outs] + [(x, False) for x in ins]:
            if r is None or not isinstance(r, Ref):
                continue
            b = r.buf
            if w:
                b.acc = [a for a in b.acc if not _cov(r.box, a[0])]
            b.acc.append((r.box, oid, w))
            if len(b.acc) > 48:
                merged = {}
                for (bx, o2, w2) in b.acc:
                    k = (self.ops[o2]["eng"] if o2 < oid else eng, w2)
                    if k in merged:
                        m = merged[k]
                        merged[k] = ((min(m[0][0], bx[0]), max(m[0][1], bx[1]), min(m[0][2], bx[2]),
                                      max(m[0][3], bx[3])), max(m[1], o2), w2)
                    else:
                        merged[k] = (bx, o2, w2)
                b.acc = list(merged.values())
        dbuf = None
        if dma:
            for r in list(outs) + list(ins):
                if isinstance(r, Ref):
                    dbuf = r.buf
            dbuf.dma_ops.append(oid)
        deps.discard(oid)
        self.ops.append(dict(eng=eng, fn=fn, deps=sorted(deps), dma=dma, dbuf=dbuf, sig=False))
        return oid

    def emit(self):
        nc = self.nc
        ops = self.ops
        for o in ops:
            for d in o["deps"]:
                od = ops[d]
                if od["dma"]:
                    continue
                if od["eng"] == o["eng"] and o["eng"] in INORDER and not o["dma"]:
                    continue
                od["sig"] = True
        sems = {e: self.stack.enter_context(nc.semaphore("sem_" + e)) for e in self.ENG}
        cnt = {e: 0 for e in self.ENG}
        for o in ops:
            if o["dma"]:
                b = o["dbuf"]
                if b.dma_sem is None:
                    b.dma_sem = self.stack.enter_context(nc.semaphore("dsem_" + b.name))
            elif o["sig"]:
                cnt[o["eng"]] += 1
                o["signo"] = cnt[o["eng"]]
        per_eng = {e: [] for e in self.ENG}
        for i, o in enumerate(ops):
            per_eng[o["eng"]].append(i)
        import bisect

        def gen(engname, e):
            waited = {}
            for i in per_eng[engname]:
                o = ops[i]
                need = {}
                for d in o["deps"]:
                    od = ops[d]
                    if od["dma"]:
                        b = od["dbuf"]
                        n = bisect.bisect_left(b.dma_ops, i)
                        key = ("d", id(b))
                        if need.get(key, (None, 0))[1] < 16 * n:
                            need[key] = (b.dma_sem, 16 * n)
                    else:
                        if od["eng"] == engname and engname in INORDER and not o["dma"]:
                            continue
                        key = ("e", od["eng"])
                        if need.get(key, (None, 0))[1] < od["signo"]:
                            need[key] = (sems[od["eng"]], od["signo"])
                for key, (sem, val) in need.items():
                    if waited.get(key, 0) >= val:
                        continue
                    e.wait_ge(sem, val)
                    waited[key] = val
                ins = o["fn"](e)
                if o["dma"]:
                    ins.then_inc(o["dbuf"].dma_sem, 16)
                elif o["sig"]:
                    ins.then_inc(sems[engname], 1)

        with nc.Block() as block:
            @block.tensor
            def _(e):
                gen("pe", e)

            @block.scalar
            def _(e):
                gen("act", e)

            @block.vector
            def _(e):
                gen("dve", e)

            @block.gpsimd
            def _(e):
                gen("pool", e)

            @block.sync
            def _(e):
                gen("sp", e)
                seen = set()
                for o in ops:
                    if o["dma"] and id(o["dbuf"]) not in seen:
                        seen.add(id(o["dbuf"]))
                        e.wait_ge(o["dbuf"].dma_sem, 16 * len(o["dbuf"].dma_ops))


def _a(x):
    return x.ap if isinstance(x, Ref) else x


def _fm4(v):
    return np.ascontiguousarray(v.reshape(-1, 128).T)


class CMap:
    def __init__(self):
        self.off = {}
        self.n = 0

    def add(self, name, w):
        self.off[name] = (self.n, w)
        self.n += w


def build_cmap():
    c = CMap()
    c.add("gpre", 8)
    c.add("a_cw", 32); c.add("a_cb", 8); c.add("a_ib", 1); c.add("a_fb", 1); c.add("a_nw", 4)
    c.add("b_mu", 13); c.add("b_w0", 4); c.add("b_a0", 4); c.add("b_kk", 4); c.add("b_ka", 4)
    c.add("b_rk", 4); c.add("b_gw", 4); c.add("b_gb", 4)
    c.add("c_cw", 16); c.add("c_cb", 4); c.add("c_br", 4); c.add("c_bi", 4); c.add("c_lam", 4)
    c.add("d_nw", 4)
    return c


CM = build_cmap()

def _cols(a, b):
    return list(range(a, b))


def build_slabs():
    slabs = []
    def fm(name, start):
        slabs.append((name, _cols(start, start + 512)))
    fm("Aq", 0); fm("Ak", 512); fm("Az", 2048)
    slabs.append(("Ag", _cols(2560, 2576) + [-1] * (512 - 16)))
    fm("Br", 2576); fm("Bk", 3088); fm("Bv", 3600)
    slabs.append(("Bl", _cols(4112, 4240) + [-1] * (512 - 128)))
    fm("Bz", 4240); fm("Cx", 4752); fm("Cz", 5264); fm("Dz", 7312)
    fm("Av", 1024); fm("Ao", 1536); fm("Dq", 5776); fm("Dk", 6288); fm("Dv", 6800)
    return slabs


SLABS = build_slabs()
SLAB_ID = {s[0]: i for i, s in enumerate(SLABS)}
NSLAB = len(SLABS)


def host_prepare(inp):
    f = np.float32
    w_in = inp["w_in"]
    w_in_p = np.concatenate([w_in, np.zeros((DEPTH, D_MODEL, 1), f)], axis=2)
    wt = np.empty((DEPTH, NSLAB, 128, 8, 512), f)
    for si, (name, cols) in enumerate(SLABS):
        blk = w_in_p[:, :, cols]
        wt[:, si] = blk.reshape(DEPTH, 8, 128, 512).transpose(0, 2, 1, 3)
    w_out = inp["w_out"]
    wo = np.ascontiguousarray(w_out.reshape(DEPTH, 4, 4, 128, 1024).transpose(0, 1, 3, 2, 4))
    cv = np.zeros((DEPTH, 128, CM.n), f)
    def put(name, arr):
        o, w = CM.off[name]
        cv[:, :, o:o + w] = arr
    put("gpre", inp["norm_pre"].reshape(DEPTH, 8, 128).transpose(0, 2, 1))
    cw = inp["mlstm_conv_w"]
    put("a_cw", cw.reshape(DEPTH, 4, 8, 128).transpose(0, 3, 1, 2).reshape(DEPTH, 128, 32))
    put("a_cb", inp["mlstm_conv_b"].reshape(DEPTH, 8, 128).transpose(0, 2, 1))
    ib = np.zeros((DEPTH, 128, 1), f); ib[:, 0:8, 0] = inp["mlstm_i_bias"]; put("a_ib", ib)
    fb = np.zeros((DEPTH, 128, 1), f); fb[:, 0:8, 0] = inp["mlstm_f_bias"]; put("a_fb", fb)
    def fm4(x):
        return x.reshape(DEPTH, -1, 128).transpose(0, 2, 1)
    put("a_nw", fm4(inp["mlstm_norm_w"]))
    mu = inp["rwkv_mu"]
    put("b_mu", fm4(mu))
    for k, nm in [("b_w0", "rwkv_w0"), ("b_a0", "rwkv_a0"), ("b_kk", "rwkv_k_k"), ("b_ka", "rwkv_k_a"),
                  ("b_rk", "rwkv_r_k"), ("b_gw", "rwkv_gn_w"), ("b_gb", "rwkv_gn_b"),
                  ("c_cb", "lru_conv_b"), ("c_br", "lru_b_r"), ("c_bi", "lru_b_i"), ("c_lam", "lru_lambda"),
                  ("d_nw", "ret_norm_w")]:
        put(k, fm4(inp[nm]))
    lw = inp["lru_conv_w"]
    put("c_cw", lw.reshape(DEPTH, 4, 4, 128).transpose(0, 3, 1, 2).reshape(DEPTH, 128, 16))
    lora = np.concatenate([inp["rwkv_w_up"], inp["rwkv_a_up"]], axis=1)
    lru = np.zeros((DEPTH, 128, 2, 4, 128), f)
    for which, nm in enumerate(["lru_w_r", "lru_w_i"]):
        w = inp[nm]
        for g in range(4):
            for j in range(2):
                lru[:, 64 * j:64 * j + 64, which, g, 64 * j:64 * j + 64] = w[:, 2 * g + j]
    gpost = np.ascontiguousarray(np.broadcast_to(inp["norm_post"][:, None, :], (DEPTH, 128, D_MODEL)))
    return dict(wt=wt, wo=wo, cv=cv, lora=np.ascontiguousarray(lora), lru=lru, gpost=gpost)


def host_tables():
    f = np.float32
    t = {}
    t["ident"] = np.eye(128, dtype=f)
    hd = 64
    log_g = np.log1p(-np.exp2(-5.0 - np.arange(8, dtype=np.float64)))
    idx = np.arange(64, dtype=np.float64)
    dmat = np.exp(log_g[:, None, None] * np.abs(idx[:, None] - idx[None, :])) * hd ** -0.5
    xi = np.exp(log_g[:, None] * (idx + 1.0))
    zeta = np.exp(log_g[:, None] * (63.0 - idx)) * hd ** -0.5
    gch = np.exp(log_g * 64.0)
    hs2h = [2 * (s % 4) + (s // 4) for s in range(8)]
    dm = np.zeros((128, 8, 64));
    for s in range(8):
        dm[0:64, s] = dmat[hs2h[s]]; dm[64:128, s] = dmat[hs2h[s]]
    t["d_dmat"] = dm.astype(f)
    t["d_xi"] = np.tile(xi.T, (2, 1)).astype(f)
    t["d_zeta"] = np.tile(zeta.T, (2, 1)).astype(f)
    gc = np.zeros((128, 4))
    for g in range(4):
        for j in range(2):
            gc[64 * j:64 * j + 64, g] = gch[2 * g + j]
    t["d_gch"] = gc.astype(f)
    mk = np.zeros((128, 8, 64))
    sidx = np.arange(128) % 64
    mk[:] = np.where(sidx[:, None, None] <= np.arange(64)[None, None, :], 0.0, -30000.0)
    t["a_mask"] = mk.reshape(128, 512).astype(f)
    sel = np.zeros((8, 512 + 4 + 128))
    for hp_ in range(8):
        for hs in range(8):
            if hp_ == hs2h[hs]:
                sel[hp_, hs * 64:(hs + 1) * 64] = 1.0
        sel[hp_, 512 + hp_ // 2] = 1.0
        jj = hp_ % 2
        sel[hp_, 516 + 64 * jj:516 + 64 * jj + 64] = 1.0
    t["a_sel"] = sel.astype(f)
    bo = np.zeros((128, 128)); bo[0:64, 0:64] = 1.0; bo[64:128, 64:128] = 1.0
    t["b_ones"] = bo.astype(f)
    i2 = np.zeros((128, 64)); i2[0:64] = np.eye(64); i2[64:128] = np.eye(64)
    t["b_i2"] = i2.astype(f)
    rr_ = (np.arange(128) % 64)[:, None]; cc_ = np.arange(64)[None, :]
    m5 = np.zeros((128, 5, 64))
    m5[:, 0] = rr_ > cc_; m5[:, 1] = cc_ > rr_; m5[:, 2] = cc_ > rr_; m5[:, 3] = cc_ >= rr_; m5[:, 4] = cc_ >= rr_
    t["b_m5"] = m5.astype(f)
    half = 32
    pos = np.arange(SEQ, dtype=np.float32)
    inv_freq = (np.float32(10000.0) ** (-np.arange(half, dtype=np.float32) / np.float32(half))).astype(np.float32)
    ang = (pos[:, None] * inv_freq[None, :]).astype(np.float32).astype(np.float64)
    t["rope"] = np.concatenate([np.cos(ang), np.sin(ang)], axis=1).astype(f)
    return t


def build_program(ntiles=SEQ // T, nlayers=DEPTH, mixers="ABCD"):
    nc = bass.Bass("TRN2", target_bir_lowering=False)
    stack = contextlib.ExitStack()
    P = Prog(nc, stack)
    dram = {}

    def din(name, shape):
        dram[name] = nc.dram_tensor(name, list(shape), F32, kind="ExternalInput").ap()
        return dram[name]

    x_d = din("x", [SEQ, D_MODEL])
    wt_d = din("wt", [DEPTH, NSLAB, 128, 8, 512])
    wo_d = din("wo", [DEPTH, 4, 128, 4, 1024])
    cv_d = din("cv", [DEPTH, 128, CM.n])
    lora_d = din("lora", [DEPTH, 128, 512])
    lru_d = din("lru", [DEPTH, 128, 2, 4, 128])
    gpost_d = din("gpost", [DEPTH, 128, D_MODEL])
    ident_d = din("ident", [128, 128])
    dmat_d = din("d_dmat", [128, 8, 64])
    dxi_d = din("d_xi", [128, 8])
    dzeta_d = din("d_zeta", [128, 8])
    dgch_d = din("d_gch", [128, 4])
    rope_d = din("rope", [SEQ, 64])
    bones_d = din("b_ones", [128, 128])
    bi2_d = din("b_i2", [128, 64])
    bm5_d = din("b_m5", [128, 5, 64])
    amask_d = din("a_mask", [128, 512])
    asel_d = din("a_sel", [8, 644])
    y_d = nc.dram_tensor("y", [SEQ, D_MODEL], F32, kind="ExternalOutput").ap()

    ident = P.sb("ident", [128, 128])
    cv = [P.sb(f"cv{l}", [128, CM.n]) for l in range(DEPTH)]
    lru1 = P.sb("lru", [128, 2, 4, 128]); lru = [lru1, lru1]
    gpost1 = P.sb("gpost", [128, D_MODEL]); gpost = [gpost1, gpost1]
    lora_up = [P.sb(f"lora_up{l}", [128, 512]) for l in range(DEPTH)]
    b_ones = P.sb("b_ones", [128, 128]); b_i2 = P.sb("b_i2", [128, 1, 64]); b_m5 = P.sb("b_m5", [128, 5, 64])
    b_hist = [P.sb(f"b_hist{l}", [128, 13]) for l in range(DEPTH)]
    b_S = [P.sb(f"b_S{l}", [128, 4, 64]) for l in range(DEPTH)]
    ident16 = P.sb("ident16", [128, 128], BF16)
    AlT16 = P.sb("AlT16", [128, 4, T], BF16); RhT16 = P.sb("RhT16", [128, 4, T], BF16)
    bk16 = P.sb("bk16", [128, 4, T], BF16)
    bm16 = [P.sb(f"bm16_{i}", [128, 8, 64], BF16) for i in range(3)]
    b_S16 = P.sb("b_S16", [128, 4, 64], BF16)
    a_n16 = P.sb("a_n16", [128, 4, 1], BF16); wj16 = P.sb("wj16", [128, NTB, 8], BF16)
    ones16 = P.sb("ones16", [128, 64], BF16); eps12 = P.sb("eps12", [128, 1])
    gLt = P.sb("gLt", [128, 4, NCH]); PRall = [P.sb(f"PR{i}", [128, 5, 8, 64], BF16) for i in range(NTB)]; Y2 = P.sb("Y2", [128, 8, 64])
    K2g = P.sb("K2g", [128, 2, 4, 64])
    ones = P.sb("ones", [128, T]); zeros = P.sb("zeros", [128, T])
    xt = [P.sb(f"xt{i}", [128, NTB, D_MODEL]) for i in range(1)]
    hT = P.sb("hT", [128, 8, T], BF16)
    yT = P.sb("yT", [128, 16, T], BF16)
    slab = [P.sb(f"slab{i}", [128, 8, 512], BF16) for i in range(NSLABBUF)]
    sm = P.sb("small", [128, 64])
    scr = P.sb("scr", [128, D_MODEL])
    scr2 = P.sb("scr2", [128, D_MODEL])
    FS = [P.sb(f"fs{i}", [128, 4, T + 4]) for i in range(10)]
    c_hist = [P.sb(f"c_hist{l}", [128, 4, 3]) for l in range(DEPTH)]
    c_state = [P.sb(f"c_state{l}", [128, 4]) for l in range(DEPTH)]
    c_coef = [P.sb(f"c_coef{l}", [128, 8]) for l in range(DEPTH)]
    PB = [P.ps(f"pb{i}") for i in range(8)]
    TM = [P.sb(f"tm{i}", [128, NTB, 8, 64]) for i in range(6)]
    TM16 = [P.sb(f"tm16_{i}", [128, NTB, 8, 64], BF16) for i in range(3)]
    d_dmat = P.sb("d_dmat", [128, 8, 64]); d_xi = P.sb("d_xi", [128, 4, 2, 1]); d_zeta = P.sb("d_zeta", [128, 8, 1])
    d_gch = P.sb("d_gch", [128, 4, 1]); rope_sb = P.sb("rope_sb", [128, NTB, 1, 64])
    d_R = [P.sb(f"d_R{l}", [128, 4, 64]) for l in range(DEPTH)]
    a_mask = P.sb("a_mask", [128, 512]); a_sel = P.sb("a_sel", [8, 644])
    ga = P.sb("ga", [8, 12, T + 1]); scl = P.sb("scl", [8, NCH]); rhs_sc = P.sb("rhs_sc", [8, NCH, 4])
    scb = P.sb("scb", [128, NCH, 4, 1]); negPx = P.sb("negPx", [8, 2, 8, 64]); ET = P.sb("ET", [128, 8, 64])
    ET1 = P.sb("ET1", [128, 8, 64])
    tmsc = P.sb("tmsc", [128, NTB, 3, 8]); hnum = P.sb("hnum", [128, 8, 64]); vw = P.sb("vw", [128, 8, 64])
    a_hist = [P.sb(f"a_hist{l}", [128, 8, 3]) for l in range(DEPTH)]
    a_C = [P.sb(f"a_C{l}", [128, 4, 64]) for l in range(DEPTH)]
    a_n = [P.sb(f"a_n{l}", [128, 4, 1]) for l in range(DEPTH)]
    a_car = [P.sb(f"a_car{l}", [8, 4]) for l in range(DEPTH)]
    AT = P.sb("AT", [128, 8, 64]); tmp4 = P.sb("tmp4", [128, 4, 2, 64]); lnst = P.sb("lnst", [128, 64]); lnst_b = P.sb("lnst_b", [128, 40]); lnsts = [lnst, lnst_b]
    state = dict(slab_i=0, pb_i=0)

    import os
    cut = int(os.environ.get("KCUT", "9"))
    def dma(out, in_, eng="sp"):
        return P.op(eng, lambda e: e.dma_start(out=_a(out), in_=_a(in_)), [out], [in_], dma=True)

    def mm(out, lhsT, rhs, start=True, stop=True):
        P.op("pe", lambda e: e.matmul(_a(out), _a(lhsT), _a(rhs), start=start, stop=stop),
             [out], [lhsT, rhs] + ([] if start else [out]))

    def tr(out, in_, idn):
        P.op("pe", lambda e: e.transpose(_a(out), _a(in_), _a(idn)), [out], [in_, idn])

    def act(out, in_, func, bias=None, scale=1.0, accum=None, eng="act"):
        kw = {}
        if bias is not None:
            kw["bias"] = _a(bias)
        if accum is not None:
            kw["accum_out"] = _a(accum)
        P.op("act", lambda e: e.activation(_a(out), _a(in_), func, scale=_a(scale), **kw),
             [out, accum], [in_, bias, scale])

    def tt(out, a, b, op, eng="dve"):
        P.op(eng, lambda e: e.tensor_tensor(_a(out), _a(a), _a(b), op), [out], [a, b])

    def ts(out, a, s1, s2, op0, op1=ALU.bypass, eng="dve"):
        P.op(eng, lambda e: e.tensor_scalar(_a(out), _a(a), _a(s1), _a(s2), op0, op1), [out], [a, s1, s2])

    def stt(out, a, s, b, op0, op1):
        P.op("dve", lambda e: e.scalar_tensor_tensor(_a(out), _a(a), _a(s), _a(b), op0, op1), [out], [a, s, b])

    def scan(out, d0, d1, init, op0, op1):
        P.op("dve", lambda e: e.tensor_tensor_scan(_a(out), _a(d0), _a(d1), _a(init), op0, op1),
             [out], [d0, d1, init])

    def cp(out, in_, eng="dve"):
        if isinstance(in_, Ref) and in_.buf.psum and eng == "dve":
            return act(out, in_, AF.Copy)
        P.op(eng, lambda e: e.tensor_copy(_a(out), _a(in_)), [out], [in_])

    def recip(out, in_):
        P.op("dve", lambda e: e.reciprocal(_a(out), _a(in_)), [out], [in_])

    def memset(out, val, eng="dve"):
        P.op(eng, lambda e: e.memset(_a(out), val), [out], [])

    def reduce(out, in_, op, axis=AX.X):
        P.op("dve", lambda e: e.tensor_reduce(_a(out), _a(in_), axis, op), [out], [in_])

    def interleave(gens):
        gens = list(gens)
        while gens:
            for gz in list(gens):
                try:
                    next(gz)
                except StopIteration:
                    gens.remove(gz)

    def run_chunks(make_gen):
        gens = [make_gen(c) for c in range(NCH)]

        def step(c):
            try:
                next(gens[c])
            except StopIteration:
                pass
        pairs = NCH // 2
        step(0); step(1)
        for k in range(pairs):
            a, b = 2 * k, 2 * k + 1
            step(a); step(b)
            step(a); step(b)
            step(a); step(a)
            if k + 1 < pairs:
                step(2 * k + 2)
            step(b); step(b)
            if k + 1 < pairs:
                step(2 * k + 3)

    def next_pb():
        state["pb_i"] = (state["pb_i"] + 1) % 8
        return PB[(0, 1, 2, 7, 3, 5, 4, 6)[state["pb_i"]]]

    def load_slab(l, name, ncols=512):
        s = slab[state["slab_i"]]
        state["slab_i"] = (state["slab_i"] + 1) % NSLABBUF
        dma(s[:, :, 0:ncols], wt_d[l, SLAB_ID[name], :, :, 0:ncols], eng="pool")
        return s

    def load_wo(l, si):
        s = slab[state["slab_i"]]
        state["slab_i"] = (state["slab_i"] + 1) % NSLABBUF
        dma(s[:, :, :], wo_d[l, si].rearrange("p f n -> p (f n)").rearrange("p (a b) -> p a b", a=8), eng="pool")
        return s

    def cvc(l, name, i=0):
        o, w = CM.off[name]
        return cv[l][:, o + i:o + i + 1]

    def fm_proj(s, gi, out_cols=T):
        pb = next_pb()
        for kc in range(8):
            mm(pb[:, 0:T], s[:, kc, gi * 128:(gi + 1) * 128], hT[:, kc, :], start=(kc == 0), stop=(kc == 7))
        return pb

    dma(ident[:, :], ident_d)
    for l in range(DEPTH):
        dma(cv[l][:, :], cv_d[l])
        dma(lora_up[l][:, :], lora_d[l])
        memset(b_hist[l][:, :], 0.0)
        memset(b_S[l][:, :, :], 0.0)
    dma(d_dmat[:, :, :], dmat_d)
    dma(d_xi[:, :, :, :], dxi_d.rearrange("p (g j o) -> p g j o", g=4, j=2))
    dma(d_zeta[:, :, :], dzeta_d.rearrange("p (h o) -> p h o", o=1))
    dma(d_gch[:, :, :], dgch_d.rearrange("p (g o) -> p g o", o=1))
    dma(a_mask[:, :], amask_d)
    dma(a_sel[:, :], asel_d)
    for l in range(DEPTH):
        memset(d_R[l][:, :, :], 0.0)
        memset(a_hist[l][:, :, :], 0.0)
        memset(a_C[l][:, :, :], 0.0)
        memset(a_n[l][:, :, :], 0.0)
        memset(a_car[l][:, :], 0.0)
        ts(a_car[l][0:8, 2:3], cv[l][0:8, CM.off["a_fb"][0]:CM.off["a_fb"][0] + 1], -1.0, None, ALU.mult)
    dma(b_ones[:, :], bones_d)
    dma(b_i2[:, 0, :], bi2_d)
    dma(b_m5[:, :, :], bm5_d)
    memset(ones[:, :], 1.0)
    cp(ident16[:, :], ident[:, :])
    memset(ones16[:, :], 1.0)
    memset(eps12[:, :], 1e-12)
    memset(zeros[:, :], 0.0)
    for l in range(DEPTH):
        memset(c_hist[l][:, :, :], 0.0)
        memset(c_state[l][:, :], 0.0)
        o, w = CM.off["c_lam"]
        act(sm[:, 0:4], cv[l][:, o:o + 4], AF.Exp, scale=-1.0)
        act(sm[:, 4:8], sm[:, 0:4], AF.Ln, bias=ones[:, 0:1])
        ts(c_coef[l][:, 0:4], sm[:, 4:8], -8.0, None, ALU.mult)
        ts(c_coef[l][:, 4:8], sm[:, 4:8], -16.0, None, ALU.mult)

    def rmsnorm_pre(l, X):
        scrs = [scr, scr2]

        def stream(tb):
            sc = scrs[tb % 2]
            act(sc[:, :], X[:, tb, :], AF.Square)
            yield
            reduce(sm[:, 8 + tb:9 + tb], sc[:, :], ALU.add)
            yield
            ts(sm[:, 12 + tb:13 + tb], sm[:, 8 + tb:9 + tb], 1.0 / D_MODEL, 1e-6, ALU.mult, ALU.add)
            yield
            act(sm[:, 16 + tb:17 + tb], sm[:, 12 + tb:13 + tb], AF.Sqrt)
            yield
            recip(sm[:, 20 + tb:21 + tb], sm[:, 16 + tb:17 + tb])
            yield
            act(sc[:, :], X[:, tb, :], AF.Copy, scale=sm[:, 20 + tb:21 + tb])
            yield
            for half in range(2):
                pb = next_pb()
                for q in range(4):
                    kc = half * 4 + q
                    tr(pb[:, q * 128:(q + 1) * 128], sc[:, kc * 128:(kc + 1) * 128], ident[:, :])
                for q in range(4):
                    kc = half * 4 + q
                    ts(hT[:, kc, tb * 128:(tb + 1) * 128], pb[:, q * 128:(q + 1) * 128],
                       cvc(l, "gpre", kc), None, ALU.mult)
                yield
        interleave([stream(tb) for tb in range(NTB)])

    def mixer_C(l):
        cx, xc, rg, ig, aa, uu, hh = FS[0], FS[1], FS[2], FS[3], FS[4], FS[5], FS[6]
        dma(lru[l][:, :, :, :], lru_d[l])
        wx = load_slab(l, "Cx")

        def cstream(g):
            pb = fm_proj(wx, g)
            cp(cx[:, g, 0:3], c_hist[l][:, g, :])
            act(cx[:, g, 3:3 + T], pb[:, 0:T], AF.Copy)
            yield
            cp(c_hist[l][:, g, :], cx[:, g, T:T + 3])
            o, w = CM.off["c_cw"]
            ts(xc[:, g, 0:T], cx[:, g, 0:T], cv[l][:, o + g:o + g + 1], cvc(l, "c_cb", g), ALU.mult, ALU.add)
            yield
            for j in range(1, 4):
                stt(xc[:, g, 0:T], cx[:, g, j:j + T], cv[l][:, o + 4 * j + g:o + 4 * j + g + 1], xc[:, g, 0:T],
                    ALU.mult, ALU.add)
                yield
        interleave([cstream(g) for g in range(4)])
        for g in range(4):
            pb = next_pb()
            mm(pb[:, 0:T], lru[l][:, 0, g, :], xc[:, g, 0:T])
            act(rg[:, g, 0:T], pb[:, 0:T], AF.Sigmoid, bias=cvc(l, "c_br", g))
            pb = next_pb()
            mm(pb[:, 0:T], lru[l][:, 1, g, :], xc[:, g, 0:T])
            act(ig[:, g, 0:T], pb[:, 0:T], AF.Sigmoid, bias=cvc(l, "c_bi", g))
        for g in range(4):
            act(aa[:, g, 0:T], rg[:, g, 0:T], AF.Exp, scale=c_coef[l][:, g:g + 1])
            act(uu[:, g, 0:T], rg[:, g, 0:T], AF.Exp, scale=c_coef[l][:, 4 + g:5 + g])
        ts(uu[:, :, 0:T], uu[:, :, 0:T], -1.0, 1.0, ALU.mult, ALU.add)
        act(uu[:, :, 0:T], uu[:, :, 0:T], AF.Sqrt)
        tt(ig[:, :, 0:T], ig[:, :, 0:T], xc[:, :, 0:T], ALU.mult)
        tt(uu[:, :, 0:T], uu[:, :, 0:T], ig[:, :, 0:T], ALU.mult)
        for g in range(4):
            scan(hh[:, g, 0:T], aa[:, g, 0:T], uu[:, g, 0:T], c_state[l][:, g:g + 1], ALU.mult, ALU.add)
            cp(c_state[l][:, g:g + 1], hh[:, g, T - 1:T])
        wz = load_slab(l, "Cz")
        for g in range(4):
            pb = fm_proj(wz, g)
            act(rg[:, g, 0:T], pb[:, 0:T], AF.Silu)
            tt(yT[:, 8 + g, :], hh[:, g, 0:T], rg[:, g, 0:T], ALU.mult)

    def tm_proj(sl, tb):
        pb = next_pb()
        for kc in range(8):
            mm(pb[:, :], hT[:, kc, tb * 128:(tb + 1) * 128], sl[:, kc, :], start=(kc == 0), stop=(kc == 7))
        return pb

    def to_fm(dst, src, tb):
        pb = next_pb()
        for g in range(4):
            tr(pb[:, g * 128:(g + 1) * 128], src[:, tb, 2 * g:2 * g + 2, 0:64], ident[:, :])
        cp(dst[:, 0:4, tb * 128:(tb + 1) * 128], pb[:, :].rr("p (a b) -> p a b", a=4))

    def to_tm(dst, src, tb):
        pb = next_pb()
        for g in range(4):
            tr(pb[:, g * 128:(g + 1) * 128], src[:, g, tb * 128:(tb + 1) * 128], ident[:, :])
        cp(dst[:, tb, :, 0:64], pb[:, :].rr("p (h e) -> p h e", h=8))

    def head_ln_stream(dst, src, tb):
        X3 = src[:, tb, :, 0:64]
        D3 = dst[:, tb, :, 0:64]
        st_ = lnsts[tb % 2]
        sq = scr[:, (tb % 2) * 512:(tb % 2) * 512 + 512].rr("p (h e) -> p h e", h=8)
        reduce(st_[:, 0:8], X3, ALU.add)
        yield
        stt(D3, st_[:, 0:8].rr("p (h o) -> p h o", o=1).bc([128, 8, 64]), -1.0 / 64, X3, ALU.mult, ALU.add)
        yield
        tt(sq, D3, D3, ALU.mult)
        yield
        reduce(st_[:, 8:16], sq, ALU.add)
        yield
        ts(st_[:, 16:24], st_[:, 8:16], 1.0 / 64, 1e-5, ALU.mult, ALU.add)
        yield
        act(st_[:, 24:32], st_[:, 16:24], AF.Sqrt)
        yield
        recip(st_[:, 32:40], st_[:, 24:32])
        yield
        tt(D3, D3, st_[:, 32:40].rr("p (h o) -> p h o", o=1).bc([128, 8, 64]), ALU.mult)
        yield

    def head_ln_all(dst, src):
        interleave([head_ln_stream(dst, src, tb) for tb in range(NTB)])

    def mixer_D(l, ti):
        qr, kr, osb, xn = TM[0], TM[1], TM[4], TM[5]
        vv, kz = TM16[1], TM16[2]
        qT, kT, sz = AlT16, RhT16, FS[2]
        AT = bm16[0]
        d_R16 = b_S16
        cp(d_R16[:, :, :], d_R[l][:, :, :])
        S = [PB[3], PB[4]]; O = [PB[5], PB[6]]; ST = [PB[2], PB[7]]
        dma(rope_sb[:, :, :, :], rope_d.rearrange("(n tb p) (o e) -> n p tb o e", tb=NTB, p=128, o=1)[ti])
        for name, dst in (("Dq", qr), ("Dk", kr)):
            sl = load_slab(l, name)

            def rstream(tb, name=name, dst=dst, sl=sl):
                o5 = (tb % 2) * 512
                pb = tm_proj(sl, tb)
                act(scr[:, o5:o5 + 512], pb[:, :], AF.Copy)
                yield
                raw = scr[:, o5:o5 + 512].rr("p (h e) -> p h e", h=8)
                x1, x2 = raw[:, :, 0:32], raw[:, :, 32:64]
                cos = rope_sb[:, tb, :, 0:32].bc([128, 8, 32]); sin = rope_sb[:, tb, :, 32:64].bc([128, 8, 32])
                t1 = scr2[:, o5:o5 + 256].rr("p (h e) -> p h e", h=8)
                t2 = scr2[:, o5 + 256:o5 + 512].rr("p (h e) -> p h e", h=8)
                d1, d2 = dst[:, tb, :, 0:32], dst[:, tb, :, 32:64]
                tt(d1, x1, cos, ALU.mult); tt(t1, x2, sin, ALU.mult)
                yield
                tt(d2, x2, cos, ALU.mult); tt(t2, x1, sin, ALU.mult)
                yield
                tt(d1, d1, t1, ALU.subtract)
                yield
                tt(d2, d2, t2, ALU.add)
                yield
                to_fm(qT if name == "Dq" else kT, dst, tb)
                yield
            interleave([rstream(tb) for tb in range(NTB)])
        sl = load_slab(l, "Dv")
        for tb in range(NTB):
            pb = tm_proj(sl, tb)
            act(vv[:, tb, :, 0:64], pb[:, :].rr("p (h e) -> p h e", h=8), AF.Copy)
            tt(kz[:, tb, :, 0:64], kr[:, tb, :, 0:64], d_zeta[:, :, :].bc([128, 8, 64]), ALU.mult)
        sl = load_slab(l, "Dz")
        for g in range(4):
            pb = fm_proj(sl, g)
            act(sz[:, g, 0:T], pb[:, 0:T], AF.Silu)
        def d_chunk(c):
            tb, p = c // 2, c % 2
            rows = slice(64 * p, 64 * p + 64)
            cols = slice(c * 64, (c + 1) * 64)
            for j in range(2):
                jr = slice(64 * j, 64 * j + 64)
                for g in range(4):
                    mm(S[j][rows, g * 64:(g + 1) * 64], kT[jr, g, cols], qT[jr, g, cols])
            for h in range(8):
                g, j = h // 2, h % 2
                mm(ST[p][64 * j:64 * j + 64, g * 64:(g + 1) * 64], kz[rows, tb, h, 0:64], vv[rows, tb, h, 0:64])
            yield
            for j in range(2):
                tt(AT[rows, 4 * j:4 * j + 4, :], S[j][rows, 0:256].rr("p (a b) -> p a b", a=4),
                   d_dmat[rows, 4 * j:4 * j + 4, :], ALU.mult)
            yield
            for h in range(8):
                g, j = h // 2, h % 2
                mm(O[p][rows, h * 64:(h + 1) * 64], AT[rows, j * 4 + g, :], vv[rows, tb, h, 0:64])
            yield
            for j in range(2):
                jr = slice(64 * j, 64 * j + 64)
                for g in range(4):
                    mm(S[j][rows, 256 + g * 64:256 + (g + 1) * 64], qT[jr, g, cols], d_R16[jr, g, :])
            yield
            tt(d_R[l][:, :, :], d_R[l][:, :, :], d_gch[:, :, :].bc([128, 4, 64]), ALU.mult)
            tt(d_R[l][:, :, :], d_R[l][:, :, :], ST[p][:, 0:256].rr("p (a b) -> p a b", a=4), ALU.add)
            for j in range(2):
                tt(tmp4[rows, :, j, 0:64], S[j][rows, 256:512].rr("p (a b) -> p a b", a=4),
                   d_xi[rows, :, j, :].bc([64, 4, 64]), ALU.mult)
            cp(d_R16[:, :, :], d_R[l][:, :, :])
            tt(osb[rows, tb, :, 0:64], O[p][rows, :].rr("p (h e) -> p h e", h=8),
               tmp4[rows, :, :, 0:64].rr("p g j e -> p (g j) e"), ALU.add)
            yield
        run_chunks(d_chunk)
        head_ln_all(xn, osb)
        for tb in range(NTB):
            pb = next_pb()
            for g in range(4):
                tr(pb[:, g * 128:(g + 1) * 128], xn[:, tb, 2 * g:2 * g + 2, 0:64], ident[:, :])
            for g in range(4):
                stt(yT[:, 12 + g, tb * 128:(tb + 1) * 128], pb[:, g * 128:(g + 1) * 128], cvc(l, "d_nw", g),
                    sz[:, g, tb * 128:(tb + 1) * 128], ALU.mult, ALU.mult)

    def conv_silu(l, sl, g8, dst, gdst, cx, whichhist, cwname, cbname, ngroups_total, silu=True):
        pass

    def mixer_A(l, ti):
        cx, qT32, kT32, sz = FS[0], FS[1], FS[2], FS[3]
        qT, kT = AlT16, RhT16
        so, osb, xn = TM[2], TM[3], TM[4]
        ktm, vv = TM16[0], TM16[1]
        AT, vw = bm16[0], bm16[1]
        a_C16 = b_S16
        cp(a_C16[:, :, :], a_C[l][:, :, :])
        cp(a_n16[:, :, :], a_n[l][:, :, :])
        S = [PB[3], PB[4]]; O = [PB[5], PB[6]]; ST = [PB[2], PB[7]]; JX = [PB[0], PB[1]]
        R_I, R_SP, R_F, R_G, R_P, R_MU, R_PE, R_SI, R_WJ, R_NM, R_NP = range(11)
        ocw = CM.off["a_cw"][0]
        for which, (name, dstT, dst16) in enumerate((("Aq", qT32, qT), ("Ak", kT32, kT))):
            sl = load_slab(l, name)

            def astream(g, which=which, sl=sl, dstT=dstT, dst16=dst16):
                g8 = which * 4 + g
                pb = fm_proj(sl, g)
                cp(cx[:, g, 0:3], a_hist[l][:, g8, :])
                act(cx[:, g, 3:3 + T], pb[:, 0:T], AF.Copy)
                yield
                cp(a_hist[l][:, g8, :], cx[:, g, T:T + 3])
                ts(dstT[:, g, 0:T], cx[:, g, 0:T], cv[l][:, ocw + g8:ocw + g8 + 1], cvc(l, "a_cb", g8),
                   ALU.mult, ALU.add)
                yield
                for j in range(1, 4):
                    stt(dstT[:, g, 0:T], cx[:, g, j:j + T], cv[l][:, ocw + 8 * j + g8:ocw + 8 * j + g8 + 1],
                        dstT[:, g, 0:T], ALU.mult, ALU.add)
                    yield
                act(dst16[:, g, 0:T], dstT[:, g, 0:T], AF.Silu)
                yield
            interleave([astream(g) for g in range(4)])
        sl = load_slab(l, "Az")
        for g in range(4):
            pb = fm_proj(sl, g)
            act(sz[:, g, 0:T], pb[:, 0:T], AF.Silu)
        acut = int(os.environ.get("ACUT", "99"))
        if acut < 1:
            return
        sl = load_slab(l, "Ag", ncols=128)
        pbi = next_pb()
        for kc in range(8):
            mm(pbi[0:8, 0:T], sl[:, kc, 0:8], hT[:, kc, :], start=(kc == 0), stop=(kc == 7))
        act(ga[0:8, R_I, 0:T], pbi[0:8, 0:T], AF.Identity, bias=cv[l][0:8, CM.off["a_ib"][0]:CM.off["a_ib"][0] + 1])
        gcut = int(os.environ.get("GCUT", "99"))
        if gcut < 1:
            return
        pbf = next_pb()
        for kc in range(8):
            mm(pbf[0:8, 0:T], sl[:, kc, 8:16], hT[:, kc, :], start=(kc == 0), stop=(kc == 7))
        act(ga[0:8, R_SP, 0:T], pbf[0:8, 0:T], AF.Exp, bias=a_car[l][0:8, 2:3], scale=-1.0)
        act(ga[0:8, R_SP, 0:T], ga[0:8, R_SP, 0:T], AF.Ln, bias=ones[0:8, 0:1])
        if gcut < 2:
            return
        scan(ga[0:8, R_F, 0:T], ones[0:8, 0:T], ga[0:8, R_SP, 0:T], a_car[l][0:8, 0:1], ALU.mult, ALU.subtract)
        cp(a_car[l][0:8, 0:1], ga[0:8, R_F, T - 1:T])
        tt(ga[0:8, R_G, 0:T], ga[0:8, R_I, 0:T], ga[0:8, R_F, 0:T], ALU.subtract)
        if gcut < 3:
            return
        cp(ga[0:8, R_P, 0:1], a_car[l][0:8, 1:2])
        scan(ga[0:8, R_P, 1:T + 1], ones[0:8, 0:T], ga[0:8, R_G, 0:T], a_car[l][0:8, 1:2], ALU.mult, ALU.max)
        cp(a_car[l][0:8, 1:2], ga[0:8, R_P, T:T + 1])
        if gcut < 4:
            return
        for c in range(NCH):
            cols = slice(c * 64, (c + 1) * 64)
            ts(ga[0:8, R_MU, cols], zeros[0:8, 0:64], ga[0:8, R_P, c * 64:c * 64 + 1], None, ALU.add)
            ts(ga[0:8, R_PE, cols], zeros[0:8, 0:64], ga[0:8, R_P, c * 64 + 64:c * 64 + 65], None, ALU.add)
            tt(scl[0:8, c:c + 1], ga[0:8, R_P, c * 64:c * 64 + 1], ga[0:8, R_P, c * 64 + 64:c * 64 + 65], ALU.subtract)
        if gcut < 5:
            return
        Pv = ga[0:8, R_P, 1:T + 1]
        tt(ga[0:8, R_SI, 0:T], ga[0:8, R_MU, 0:T], Pv, ALU.subtract)
        tt(ga[0:8, R_WJ, 0:T], ga[0:8, R_G, 0:T], ga[0:8, R_PE, 0:T], ALU.subtract)
        stt(ga[0:8, R_NM, 0:T], ga[0:8, R_F, 0:T], -1.0, Pv, ALU.mult, ALU.subtract)
        ts(ga[0:8, R_NP, 0:T], Pv, -1.0, None, ALU.mult)
        if acut < 2:
            return
        for tb in range(NTB):
            pb = next_pb()
            for r, R in enumerate((R_SI, R_WJ, R_NM)):
                tr(pb[:, r * 8:(r + 1) * 8], ga[0:8, R, tb * 128:(tb + 1) * 128], ident[0:8, 0:8])
            act(tmsc[:, tb, :, :], pb[:, 0:24].rr("p (a b) -> p a b", a=3), AF.Exp)
        if acut < 3:
            return
        tt(rhs_sc[0:8, :, :], scl[0:8, :].rr("p (c o) -> p c o", o=1).bc([8, NCH, 4]),
           a_sel[0:8, 512:516].rr("p (o g) -> p o g", o=1).bc([8, NCH, 4]), ALU.mult)
        pb = next_pb()
        mm(pb[:, 0:NCH * 4], a_sel[0:8, 516:644], rhs_sc[0:8, :, :].rr("p c g -> p (c g)"))
        act(scb[:, :, :, :].rr("p c g o -> p (c g o)"), pb[:, 0:NCH * 4], AF.Exp)
        if acut < 4:
            return
        for tb in range(NTB):
            pbt = next_pb()
            for g in range(4):
                mm(pbt[:, g * 128:(g + 1) * 128], kT[:, g, tb * 128:(tb + 1) * 128], ident16[:, :])
            cp(ktm[:, tb, :, :], pbt[:, :].rr("p (h e) -> p h e", h=8))
            cp(wj16[:, tb, :], tmsc[:, tb, 1, :])
        sl = load_slab(l, "Av")
        for tb in range(NTB):
            pb = tm_proj(sl, tb)
            act(vv[:, tb, :, :], pb[:, :].rr("p (h e) -> p h e", h=8), AF.Copy)
        sl = load_slab(l, "Ao")
        for tb in range(NTB):
            pb = tm_proj(sl, tb)
            act(so[:, tb, :, :], pb[:, :].rr("p (h e) -> p h e", h=8), AF.Sigmoid)
        ETs = [ET, ET1]
        for tb in range(NTB):
            E = next_pb()
            mm(E[:, :], ident[:, :], a_mask[:, :], start=True, stop=False)
            mm(E[:, :], ga[0:8, R_G, tb * 128:(tb + 1) * 128], a_sel[0:8, 0:512], start=False, stop=False)
            for p in range(2):
                c = tb * 2 + p
                tt(negPx[0:8, p, :, :], ga[0:8, R_NP:R_NP + 1, c * 64:(c + 1) * 64].bc([8, 8, 64]),
                   a_sel[0:8, 0:512].rr("p (h e) -> p h e", h=8), ALU.mult)
                mm(E[64 * p:64 * p + 64, :], ones[0:8, 0:64], negPx[0:8, p, :, :].rr("p h e -> p (h e)"),
                   start=False, stop=True)
            act(ETs[tb][:, :, :].rr("p h e -> p (h e)"), E[:, :], AF.Exp)

        def a_chunk(c):
            tb, p = c // 2, c % 2
            rows = slice(64 * p, 64 * p + 64)
            cols = slice(c * 64, (c + 1) * 64)
            ETt = ETs[tb]
            sI4 = tmsc[:, tb, 0, :].rr("p (g j o) -> p g j o", g=4, j=2)
            for j in range(2):
                jr = slice(64 * j, 64 * j + 64)
                for g in range(4):
                    mm(S[j][rows, g * 64:(g + 1) * 64], kT[jr, g, cols], qT[jr, g, cols])
            tt(vw[rows, :, :], vv[rows, tb, :, :],
               tmsc[rows, tb, 1, :].rr("p (h o) -> p h o", o=1).bc([64, 8, 64]), ALU.mult)
            yield
            for j in range(2):
                stt(AT[rows, 4 * j:4 * j + 4, :], S[j][rows, 0:256].rr("p (a b) -> p a b", a=4), 0.125,
                    ETt[rows, 4 * j:4 * j + 4, :], ALU.mult, ALU.mult)
            yield
            for h in range(8):
                g, j = h // 2, h % 2
                mm(O[p][rows, h * 64:(h + 1) * 64], AT[rows, j * 4 + g, :], vv[rows, tb, h, :])
                mm(ST[p][rows, 384 + h:385 + h], AT[rows, j * 4 + g, :], ones16[rows, 0:1])
            for h in range(8):
                g, j = h // 2, h % 2
                jr = slice(64 * j, 64 * j + 64)
                mm(ST[p][jr, g * 64:(g + 1) * 64], ktm[rows, tb, h, :], vw[rows, h, :])
                mm(ST[p][jr, 256 + g:257 + g], ktm[rows, tb, h, :], wj16[rows, tb, h:h + 1])
            yield
            for j in range(2):
                jr = slice(64 * j, 64 * j + 64)
                for g in range(4):
                    mm(JX[j][rows, g * 64:(g + 1) * 64], qT[jr, g, cols], a_C16[jr, g, :])
                    mm(JX[j][rows, 256 + g:257 + g], qT[jr, g, cols], a_n16[jr, g, :])
            yield
            for j in range(2):
                tt(tmp4[rows, :, j, :], JX[j][rows, 0:256].rr("p (a b) -> p a b", a=4),
                   sI4[rows, :, j, :].bc([64, 4, 64]), ALU.mult)
                tt(lnst[rows, 40:48].rr("p (g j) -> p g j", j=2)[:, :, j], JX[j][rows, 256:260],
                   tmsc[rows, tb, 0, :].rr("p (g j) -> p g j", j=2)[:, :, j], ALU.mult)
            tt(a_C[l][:, :, :], a_C[l][:, :, :], scb[:, c, :, :].bc([128, 4, 64]), ALU.mult)
            stt(a_C[l][:, :, :], ST[p][:, 0:256].rr("p (a b) -> p a b", a=4), 0.125, a_C[l][:, :, :],
                ALU.mult, ALU.add)
            tt(a_n[l][:, :, :], a_n[l][:, :, :], scb[:, c, :, :], ALU.mult)
            stt(a_n[l][:, :, :], ST[p][:, 256:260].rr("p (a b) -> p a b", b=1), 0.125, a_n[l][:, :, :],
                ALU.mult, ALU.add)
            cp(a_C16[:, :, :], a_C[l][:, :, :])
            cp(a_n16[:, :, :], a_n[l][:, :, :])
            tt(hnum[rows, :, :], O[p][rows, :].rr("p (h e) -> p h e", h=8),
               tmp4[rows, :, :, :].rr("p g j e -> p (g j) e"), ALU.add)
            tt(lnst[rows, 48:56], ST[p][rows, 384:392], lnst[rows, 40:48], ALU.add)
            stt(lnst[rows, 48:56], lnst[rows, 48:56], -1.0, lnst[rows, 48:56], ALU.mult, ALU.max)
            tt(lnst[rows, 48:56], lnst[rows, 48:56], tmsc[rows, tb, 2, :], ALU.max)
            recip(lnst[rows, 56:64], lnst[rows, 48:56])
            tt(hnum[rows, :, :], hnum[rows, :, :],
               lnst[rows, 56:64].rr("p (h o) -> p h o", o=1).bc([64, 8, 64]), ALU.mult)
            tt(osb[rows, tb, :, :], hnum[rows, :, :], so[rows, tb, :, :], ALU.mult)
            yield
        run_chunks(a_chunk)
        head_ln_all(xn, osb)
        for tb in range(NTB):
            pb = next_pb()
            for g in range(4):
                tr(pb[:, g * 128:(g + 1) * 128], xn[:, tb, 2 * g:2 * g + 2, 0:64], ident[:, :])
            for g in range(4):
                stt(yT[:, g, tb * 128:(tb + 1) * 128], pb[:, g * 128:(g + 1) * 128], cvc(l, "a_nw", g),
                    sz[:, g, tb * 128:(tb + 1) * 128], ALU.mult, ALU.mult)

    def mixer_B(l, ti):
        rS, kS, vS, sz = FS[0], FS[1], FS[2], FS[3]
        RhT, AlT, bon = rS, kS, vS
        pools = (FS[4], FS[5], FS[6])
        pools2 = (FS[7], FS[8], FS[9])

        def slot(i):
            return pools[i // 4][:, i % 4, :]

        def slot2(i):
            return pools2[i // 4][:, i % 4, :]
        raw, lora, aT, lw, lc, kt, kh, beta, eg, egi, egm, tA = [slot(i) for i in range(12)]
        SETS = [dict(raw=raw, aT=aT, lw=lw, lc=lc, kt=kt, kh=kh, beta=beta, eg=eg, egi=egi, egm=egm, tA=tA,
                     beta16=bk16[:, 0, :], kt16=bk16[:, 1, :]),
                dict(raw=slot2(0), aT=slot2(2), lw=slot2(3), lc=slot2(4), kt=slot2(5), kh=slot2(6), beta=slot2(7),
                     eg=slot2(8), egi=slot2(9), egm=slot2(10), tA=slot2(11),
                     beta16=bk16[:, 2, :], kt16=bk16[:, 3, :])]
        wkv, xn = TM[3], TM[4]
        Vtm, Btm, Ktm = TM16[0], TM16[1], TM16[2]
        BJ = [PB[3], PB[4]]; PA = [PB[2], PB[7]]; PQ = [PB[5], PB[6]]; PX = [PB[0], PB[1]]
        Pn, Qn, Xm = bm16
        W2 = AT
        beta16, kt16 = bk16[:, 0, :], bk16[:, 1, :]
        omu = CM.off["b_mu"][0]

        def shift_mix(dst, sl, gi, hidx, mucol, raw_, tA_):
            pb = fm_proj(sl, gi)
            cp(raw_[:, 0:1], b_hist[l][:, hidx:hidx + 1])
            act(raw_[:, 1:T + 1], pb[:, 0:T], AF.Copy)
            yield
            cp(b_hist[l][:, hidx:hidx + 1], raw_[:, T:T + 1])
            tt(tA_[:, 0:T], raw_[:, 0:T], raw_[:, 1:T + 1], ALU.subtract)
            yield
            stt(dst, tA_[:, 0:T], cv[l][:, omu + mucol:omu + mucol + 1], raw_[:, 1:T + 1], ALU.mult, ALU.add)
            yield

        cp(b_S16[:, :, :], b_S[l][:, :, :])
        sl = load_slab(l, "Bl", ncols=128)
        interleave([shift_mix(lora[:, 0:T], sl, 0, 12, 12, raw, tA)])
        act(lora[0:64, 0:T], lora[0:64, 0:T], AF.Tanh)
        tslots = [(slot(2 + 2 * i), slot(3 + 2 * i)) for i in range(4)]
        for wi, (name, dst) in enumerate((("Br", rS), ("Bk", kS), ("Bv", vS))):
            sl = load_slab(l, name)
            interleave([shift_mix(dst[:, g, 0:T], sl, g, wi * 4 + g, wi * 4 + g, tslots[g][0], tslots[g][1])
                        for g in range(4)])
        sl = load_slab(l, "Bz")
        for g in range(4):
            pb = fm_proj(sl, g)
            act(sz[:, g, 0:T], pb[:, 0:T], AF.Silu)
        bcut = int(os.environ.get("BCUT", "99"))
        if bcut < 1:
            return
        def group_chains(g, Z):
            gc = slice(g * 128, (g + 1) * 128)
            raw, aT, lw, lc, kt, kh, beta = Z["raw"], Z["aT"], Z["lw"], Z["lc"], Z["kt"], Z["kh"], Z["beta"]
            eg, egi, egm, tA, beta16, kt16 = Z["eg"], Z["egi"], Z["egm"], Z["tA"], Z["beta16"], Z["kt16"]

            def chainW():
                pb = next_pb()
                mm(pb[:, 0:T], lora_up[l][0:64, gc], lora[0:64, 0:T])
                act(lw[:, 0:T], pb[:, 0:T], AF.Sigmoid, bias=cvc(l, "b_w0", g))
                yield
                for c in range(NCH):
                    cols = slice(c * 64, (c + 1) * 64)
                    scan(lc[:, cols], ones[:, 0:64], lw[:, cols], 0.0, ALU.mult, ALU.add)
                    yield
                act(eg[:, 0:T], lc[:, 0:T], AF.Exp, scale=-0.606531)
                act(egi[:, 0:T], lc[:, 0:T], AF.Exp, scale=0.606531)
                tt(tA[:, 0:T], lc[:, 0:T], lw[:, 0:T], ALU.subtract)
                yield
                act(egm[:, 0:T], tA[:, 0:T], AF.Exp, scale=-0.606531)
                cp(gLt[:, g, :], eg[:, 63:T:64])
                yield

            def chainA():
                pb = next_pb()
                mm(pb[:, 0:T], lora_up[l][64:128, gc], lora[64:128, 0:T])
                act(aT[:, 0:T], pb[:, 0:T], AF.Sigmoid, bias=cvc(l, "b_a0", g))
                yield
                ts(beta[:, 0:T], aT[:, 0:T], -1.0, cvc(l, "b_ka", g), ALU.add, ALU.mult)
                yield
                stt(kt[:, 0:T], beta[:, 0:T], 1.0, kS[:, g, 0:T], ALU.add, ALU.mult)
                yield

            def chainK():
                ts(kh[:, 0:T], kS[:, g, 0:T], cvc(l, "b_kk", g), None, ALU.mult)
                yield
                tt(raw[:, 0:T], kh[:, 0:T], kh[:, 0:T], ALU.mult)
                yield
                pb = next_pb()
                mm(pb[:, 0:T], b_ones[:, :], raw[:, 0:T])
                act(raw[:, 0:T], pb[:, 0:T], AF.Sqrt, bias=eps12[:, 0:1])
                yield
                recip(raw[:, 0:T], raw[:, 0:T])
                yield
                tt(kh[:, 0:T], kh[:, 0:T], raw[:, 0:T], ALU.mult)
                yield

            def chainV():
                for tb in range(NTB):
                    pbt = next_pb()
                    tr(pbt[:, 0:128], vS[:, g, tb * 128:(tb + 1) * 128], ident[:, :])
                    cp(Vtm[:, tb, 2 * g:2 * g + 2, :], pbt[:, 0:128].rr("p (a b) -> p a b", a=2))
                    yield

            def chainR():
                stt(raw[:, 0:T], rS[:, g, 0:T], cvc(l, "b_rk", g), kt[:, 0:T], ALU.mult, ALU.mult)
                yield
                pbb = next_pb()
                mm(pbb[:, 0:T], b_ones[:, :], raw[:, 0:T])
                tt(bon[:, g, 0:T], pbb[:, 0:T], vS[:, g, 0:T], ALU.mult)
                yield

            def chainB():
                tt(beta[:, 0:T], aT[:, 0:T], kh[:, 0:T], ALU.mult)
                yield
                tt(beta16[:, 0:T], beta[:, 0:T], egi[:, 0:T], ALU.mult)
                yield

            def chainO():
                tt(AlT16[:, g, 0:T], kh[:, 0:T], egm[:, 0:T], ALU.mult)
                yield
                tt(RhT16[:, g, 0:T], rS[:, g, 0:T], eg[:, 0:T], ALU.mult)
                yield
                tt(kt16[:, 0:T], kt[:, 0:T], egi[:, 0:T], ALU.mult)
                yield

            def tail():
                for tb in range(NTB):
                    pbt = next_pb()
                    mm(pbt[:, 0:128], beta16[:, tb * 128:(tb + 1) * 128], ident16[:, :])
                    mm(pbt[:, 128:256], kt16[:, tb * 128:(tb + 1) * 128], ident16[:, :])
                    cp(Btm[:, tb, 2 * g:2 * g + 2, :], pbt[:, 0:128].rr("p (a b) -> p a b", a=2))
                    cp(Ktm[:, tb, 2 * g:2 * g + 2, :], pbt[:, 128:256].rr("p (a b) -> p a b", a=2))
                    yield
            return [chainW(), chainA(), chainK(), chainV()], [chainR(), chainB(), chainO()], tail

        def products(g, Z):
            beta16, kt16 = Z["beta16"], Z["kt16"]
            for tb in range(NTB):
                for j in range(2):
                    jr = slice(64 * j, 64 * j + 64)
                    for p in range(2):
                        c = tb * 2 + p
                        rows = slice(64 * p, 64 * p + 64)
                        cols = slice(c * 64, (c + 1) * 64)
                        A_, B_, K_, R_ = AlT16[jr, g, cols], beta16[jr, cols], kt16[jr, cols], RhT16[jr, g, cols]
                        mm(BJ[j][rows, 0:64], A_, B_)
                        mm(BJ[j][rows, 64:128], B_, A_)
                        mm(BJ[j][rows, 128:192], K_, A_)
                        mm(BJ[j][rows, 192:256], B_, R_)
                        mm(BJ[j][rows, 256:320], K_, R_)
                    tt(PRall[tb][:, :, 2 * g + j, :], BJ[j][:, 0:320].rr("p (k t) -> p k t", k=5), b_m5[:, :, :],
                       ALU.mult)

        for g0 in (0, 2):
            ph1, ph2, tails = [], [], []
            for gi, g in enumerate((g0, g0 + 1)):
                a1, a2, tl = group_chains(g, SETS[gi])
                ph1 += a1; ph2 += a2; tails.append(tl())
            interleave(ph1)
            interleave(ph2)
            interleave(tails)
            for gi, g in enumerate((g0, g0 + 1)):
                products(g, SETS[gi])
        if bcut < 2:
            return
        for tb in range(NTB):
            PRt = PRall[tb]
            P0, Q0, MkT, NbT, NkT = (PRt[:, k, :, :] for k in range(5))
            Wsb, Usb = PRt[:, 2, :, :], PRt[:, 4, :, :]
            for p in range(2):
                rows = slice(64 * p, 64 * p + 64)
                c = tb * 2 + p
                for h in range(8):
                    g, j = h // 2, h % 2
                    mm(PA[p][rows, h * 64:(h + 1) * 64], MkT[rows, h, :], Vtm[rows, tb, h, :])
                    mm(PQ[p][rows, h * 64:(h + 1) * 64], NkT[rows, h, :], Vtm[rows, tb, h, :])
                    mm(PX[p][64 * j:64 * j + 64, g * 64:(g + 1) * 64], Ktm[rows, tb, h, :], Vtm[rows, tb, h, :])
                act(W2[rows, :, :], PA[p][rows, :].rr("p (h e) -> p h e", h=8), AF.Copy)
                act(Y2[rows, :, :], PQ[p][rows, :].rr("p (h e) -> p h e", h=8), AF.Copy)
                tt(K2g[:, p, :, :], PX[p][:, 0:256].rr("p (a b) -> p a b", a=4),
                   gLt[:, :, c:c + 1].bc([128, 4, 64]), ALU.mult)
            tt(Xm[:, :, :], b_i2[:, :, :].bc([128, 8, 64]), Q0, ALU.subtract)
            Pc, Qc = P0, Q0
            nxt = [(Pn, Qn), (P0, Q0)]
            for i in range(1, 6):
                Pd, Qd = nxt[(i - 1) % 2]
                for p in range(2):
                    rows = slice(64 * p, 64 * p + 64)
                    for h in range(8):
                        hc = slice(h * 64, (h + 1) * 64)
                        mm(PA[p][rows, hc], Qc[rows, h, :], Pc[rows, h, :])
                        if i < 5:
                            mm(PQ[p][rows, hc], Pc[rows, h, :], Qc[rows, h, :])
                for p in range(2):
                    rows = slice(64 * p, 64 * p + 64)
                    act(Pd[rows, :, :], PA[p][rows, :].rr("p (h e) -> p h e", h=8), AF.Copy)
                    if i < 5:
                        cp(Qd[rows, :, :], PQ[p][rows, :].rr("p (h e) -> p h e", h=8))
                Pc, Qc = Pd, Qd
                for p in range(2):
                    rows = slice(64 * p, 64 * p + 64)
                    for h in range(8):
                        mm(PX[p][rows, h * 64:(h + 1) * 64], Pc[rows, h, :], Xm[rows, h, :])
                for p in range(2):
                    rows = slice(64 * p, 64 * p + 64)
                    tt(Xm[rows, :, :], Xm[rows, :, :], PX[p][rows, :].rr("p (h e) -> p h e", h=8), ALU.add)
            if bcut < 3:
                continue
            for p in range(2):
                c = tb * 2 + p
                rows = slice(64 * p, 64 * p + 64)
                cols = slice(c * 64, (c + 1) * 64)
                for j in range(2):
                    jr = slice(64 * j, 64 * j + 64)
                    for g in range(4):
                        mm(BJ[j][rows, g * 64:(g + 1) * 64], AlT16[jr, g, cols], b_S16[jr, g, :])
                        mm(BJ[j][rows, 256 + g * 64:256 + (g + 1) * 64], RhT16[jr, g, cols], b_S16[jr, g, :])
                W24 = W2[rows, :, :].rr("p (g j) e -> p g j e", j=2)
                Ws4 = Wsb[rows, :, :].rr("p (g j) e -> p g j e", j=2)
                Y24 = Y2[rows, :, :].rr("p (g j) e -> p g j e", j=2)
                for j in range(2):
                    tt(Ws4[:, :, j, :], BJ[j][rows, 0:256].rr("p (a b) -> p a b", a=4), W24[:, :, j, :], ALU.add)
                    tt(tmp4[rows, :, j, :], BJ[j][rows, 256:512].rr("p (a b) -> p a b", a=4), Y24[:, :, j, :],
                       ALU.add)
                for h in range(8):
                    mm(PX[p][rows, h * 64:(h + 1) * 64], Xm[rows, h, :], Wsb[rows, h, :])
                act(Usb[rows, :, :], PX[p][rows, :].rr("p (h e) -> p h e", h=8), AF.Copy)
                for h in range(8):
                    g, j = h // 2, h % 2
                    mm(PA[p][rows, h * 64:(h + 1) * 64], NbT[rows, h, :], Usb[rows, h, :])
                    mm(PQ[p][64 * j:64 * j + 64, g * 64:(g + 1) * 64], Btm[rows, tb, h, :], Usb[rows, h, :])
                tt(wkv[rows, tb, :, :], tmp4[rows, :, :, :].rr("p g j e -> p (g j) e"),
                   PA[p][rows, :].rr("p (h e) -> p h e", h=8), ALU.subtract)
                tt(b_S[l][:, :, :], b_S[l][:, :, :], PQ[p][:, 0:256].rr("p (a b) -> p a b", a=4), ALU.subtract)
                tt(b_S[l][:, :, :], b_S[l][:, :, :], gLt[:, :, c:c + 1].bc([128, 4, 64]), ALU.mult)
                tt(b_S[l][:, :, :], b_S[l][:, :, :], K2g[:, p, :, :], ALU.add)
                cp(b_S16[:, :, :], b_S[l][:, :, :])
        ogw, ogb = CM.off["b_gw"][0], CM.off["b_gb"][0]
        head_ln_all(xn, wkv)
        for tb in range(NTB):
            pb = next_pb()
            for g in range(4):
                tr(pb[:, g * 128:(g + 1) * 128], xn[:, tb, 2 * g:2 * g + 2, 0:64], ident[:, :])
            for g in range(4):
                tc_ = slice(tb * 128, (tb + 1) * 128)
                ts(scr[:, 0:128], pb[:, g * 128:(g + 1) * 128], cv[l][:, ogw + g:ogw + g + 1],
                   cv[l][:, ogb + g:ogb + g + 1], ALU.mult, ALU.add)
                tt(scr[:, 0:128], scr[:, 0:128], bon[:, g, tc_], ALU.add)
                tt(yT[:, 4 + g, tc_], scr[:, 0:128], sz[:, g, tc_], ALU.mult)

    def out_proj_residual(l, X):
        dma(gpost[l][:, :], gpost_d[l])
        for si in range(4):
            s = load_wo(l, si)
            for tb in range(NTB):
                for half in range(2):
                    pb = PB[4 + tb * 2 + half]
                    for q in range(4):
                        fc = si * 4 + q
                        mm(pb[:, :], yT[:, fc, tb * 128:(tb + 1) * 128], s[:, q * 2 + half, :],
                           start=(fc == 0), stop=(fc == 15))
        if cut < 4:
            return
        for tb in range(NTB):
            for half in range(2):
                act(scr[:, half * 512:(half + 1) * 512], PB[4 + tb * 2 + half][:, :], AF.Copy)
            if cut < 5:
                continue
            act(scr2[:, :], scr[:, :], AF.Square)
            reduce(sm[:, 24:25], scr2[:, :], ALU.add)
            ts(sm[:, 25:26], sm[:, 24:25], 1.0 / D_MODEL, 1e-6, ALU.mult, ALU.add)
            act(sm[:, 26:27], sm[:, 25:26], AF.Sqrt)
            recip(sm[:, 27:28], sm[:, 26:27])
            if cut < 6:
                continue
            stt(scr2[:, :], scr[:, :], sm[:, 27:28], gpost[l][:, :], ALU.mult, ALU.mult)
            if cut < 7:
                continue
            tt(X[:, tb, :], X[:, tb, :], scr2[:, :], ALU.add)

    xv = x_d.rearrange("(n tb p) d -> n p tb d", tb=NTB, p=128)
    yv = y_d.rearrange("(n tb p) d -> n p tb d", tb=NTB, p=128)
    for ti in range(ntiles):
        X = xt[0]
        dma(X[:, :, :], xv[ti])
        for l in range(nlayers):
            if cut >= 1:
                rmsnorm_pre(l, X)
            memset(yT[:, :, :], 0.0)
            if "C" in mixers and cut >= 2:
                mixer_C(l)
            if "D" in mixers:
                mixer_D(l, ti)
            if "A" in mixers:
                mixer_A(l, ti)
            if "B" in mixers:
                mixer_B(l, ti)
            if cut >= 3:
                out_proj_residual(l, X)
        dma(yv[ti], X[:, :, :])

    P.emit()
    return nc, stack


def kernel(**inputs):
    inputs = {k: np.asarray(v) for k, v in inputs.items()}
    return run(inputs)


def make_in_map(xc, hp, tb):
    m = dict(x=xc, wt=hp["wt"], wo=hp["wo"], cv=hp["cv"], lora=hp["lora"], lru=hp["lru"], gpost=hp["gpost"])
    m.update(tb)
    return m


def run(inputs, ntiles=SEQ // T, nlayers=DEPTH, mixers="ABCD"):
    hp = host_prepare(inputs)
    tb = host_tables()
    nc, stack = build_program(ntiles, nlayers, mixers)
    x = np.ascontiguousarray(inputs["x"], dtype=np.float32)
    in_maps = [make_in_map(x[c], hp, tb) for c in range(NCORES)]
    with stack:
        res = run_bass_kernel_spmd(nc, in_maps, core_ids=list(range(NCORES)))
    out = np.stack([np.asarray(r["y"]) for r in res.results], axis=0)
    return out.astype(np.float32)
```

```python
import contextlib
import numpy as np
import concourse.bass as bass
import concourse.mybir as mybir
from concourse.bass_utils import run_bass_kernel_spmd

F32 = mybir.dt.float32
BF16 = mybir.dt.bfloat16
AF = mybir.ActivationFunctionType
ALU = mybir.AluOpType
AX = mybir.AxisListType

D_MODEL = 1024
SEQ = 4096
BATCH = 4
DEPTH = 2
G = 512
D_IN = 7824
T = 256
NTB = T // 128
NCH = T // 64
NCORES = 4
NSLABBUF = 3
import os as _os
INORDER = tuple(_os.environ.get("INORDER", "pe").split(","))


class Buf:
    def __init__(self, name, tile, psum=False):
        self.name = name
        self.tile = tile
        self.psum = psum
        self.acc = []
        self.dma_sem = None
        self.dma_ops = []

    def __getitem__(self, idx):
        return Ref(self, self.tile[idx])


def _box(ap):
    pat = ap.ap
    pstep = pat[0][0]
    off = int(ap.offset)
    p0 = off // pstep if pstep else 0
    f0 = off % pstep if pstep else off
    ext = 0
    for st, cnt in pat[1:]:
        ext += abs(st) * (cnt - 1)
    return (p0, p0 + pat[0][1], f0, f0 + ext + 1)


class Ref:
    def __init__(self, buf, ap, box=None):
        self.buf = buf
        self.ap = ap
        if box is None:
            box = _box(ap)
            if buf.psum:
                box = ((box[0] // 32) * 32, ((box[1] + 31) // 32) * 32, 0, 512)
        self.box = box

    def bc(self, shape):
        return Ref(self.buf, self.ap.to_broadcast(list(shape)), self.box)

    def __getitem__(self, idx):
        return Ref(self.buf, self.ap[idx])

    def rr(self, pat, **kw):
        return Ref(self.buf, self.ap.rearrange(pat, **kw), self.box)


def _ovl(a, b):
    return a[0] < b[1] and b[0] < a[1] and a[2] < b[3] and b[2] < a[3]


def _cov(a, b):
    return a[0] <= b[0] and a[1] >= b[1] and a[2] <= b[2] and a[3] >= b[3]


class Prog:
    ENG = ("pe", "act", "dve", "pool", "sp")

    def __init__(self, nc, stack):
        self.nc = nc
        self.stack = stack
        self.ops = []
        self.nbuf = 0

    def sb(self, name, shape, dtype=F32):
        t = self.stack.enter_context(self.nc.sbuf_tensor("s_" + name, list(shape), dtype))
        return Buf(name, t)

    def ps(self, name):
        t = self.stack.enter_context(self.nc.psum_tensor("p_" + name, [128, 512], F32))
        return Buf(name, t, psum=True)

    def op(self, eng, fn, outs, ins, dma=False):
        oid = len(self.ops)
        deps = set()
        for r, w in [(x, True) for x in outs] + [(x, False) for x in ins]:
            if r is None or not isinstance(r, Ref):
                continue
            b = r.buf
            for (bx, o2, w2) in b.acc:
                if (w or w2) and _ovl(bx, r.box):
                    deps.add(o2)
        for r, w in [(x, True) for x in outs] + [(x, False) for x in ins]:
            if r is None or not isinstance(r, Ref):
                continue
            b = r.buf
            if w:
                b.acc = [a for a in b.acc if not _cov(r.box, a[0])]
            b.acc.append((r.box, oid, w))
            if len(b.acc) > 48:
                merged = {}
                for (bx, o2, w2) in b.acc:
                    k = (self.ops[o2]["eng"] if o2 < oid else eng, w2)
                    if k in merged:
                        m = merged[k]
                        merged[k] = ((min(m[0][0], bx[0]), max(m[0][1], bx[1]), min(m[0][2], bx[2]),
                                      max(m[0][3], bx[3])), max(m[1], o2), w2)
                    else:
                        merged[k] = (bx, o2, w2)
                b.acc = list(merged.values())
        dbuf = None
        if dma:
            for r in list(outs) + list(ins):
                if isinstance(r, Ref):
                    dbuf = r.buf
            dbuf.dma_ops.append(oid)
        deps.discard(oid)
        self.ops.append(dict(eng=eng, fn=fn, deps=sorted(deps), dma=dma, dbuf=dbuf, sig=False))
        return oid

    def emit(self):
        nc = self.nc
        ops = self.ops
        for o in ops:
            for d in o["deps"]:
                od = ops[d]
                if od["dma"]:
                    continue
                if od["eng"] == o["eng"] and o["eng"] in INORDER and not o["dma"]:
                    continue
                od["sig"] = True
        sems = {e: self.stack.enter_context(nc.semaphore("sem_" + e)) for e in self.ENG}
        cnt = {e: 0 for e in self.ENG}
        for o in ops:
            if o["dma"]:
                b = o["dbuf"]
                if b.dma_sem is None:
                    b.dma_sem = self.stack.enter_context(nc.semaphore("dsem_" + b.name))
            elif o["sig"]:
                cnt[o["eng"]] += 1
                o["signo"] = cnt[o["eng"]]
        per_eng = {e: [] for e in self.ENG}
        for i, o in enumerate(ops):
            per_eng[o["eng"]].append(i)
        import bisect

        def gen(engname, e):
            waited = {}
            for i in per_eng[engname]:
                o = ops[i]
                need = {}
                for d in o["deps"]:
                    od = ops[d]
                    if od["dma"]:
                        b = od["dbuf"]
                        n = bisect.bisect_left(b.dma_ops, i)
                        key = ("d", id(b))
                        if need.get(key, (None, 0))[1] < 16 * n:
                            need[key] = (b.dma_sem, 16 * n)
                    else:
                        if od["eng"] == engname and engname in INORDER and not o["dma"]:
                            continue
                        key = ("e", od["eng"])
                        if need.get(key, (None, 0))[1] < od["signo"]:
                            need[key] = (sems[od["eng"]], od["signo"])
                for key, (sem, val) in need.items():
                    if waited.get(key, 0) >= val:
                        continue
                    e.wait_ge(sem, val)
                    waited[key] = val
                ins = o["fn"](e)
                if o["dma"]:
                    ins.then_inc(o["dbuf"].dma_sem, 16)
                elif o["sig"]:
                    ins.then_inc(sems[engname], 1)

        with nc.Block() as block:
            @block.tensor
            def _(e):
                gen("pe", e)

            @block.scalar
            def _(e):
                gen("act", e)

            @block.vector
            def _(e):
                gen("dve", e)

            @block.gpsimd
            def _(e):
                gen("pool", e)

            @block.sync
            def _(e):
                gen("sp", e)
                seen = set()
                for o in ops:
                    if o["dma"] and id(o["dbuf"]) not in seen:
                        seen.add(id(o["dbuf"]))
                        e.wait_ge(o["dbuf"].dma_sem, 16 * len(o["dbuf"].dma_ops))


def _a(x):
    return x.ap if isinstance(x, Ref) else x


def _fm4(v):
    return np.ascontiguousarray(v.reshape(-1, 128).T)


class CMap:
    def __init__(self):
        self.off = {}
        self.n = 0

    def add(self, name, w):
        self.off[name] = (self.n, w)
        self.n += w


def build_cmap():
    c = CMap()
    c.add("gpre", 8)
    c.add("a_cw", 32); c.add("a_cb", 8); c.add("a_ib", 1); c.add("a_fb", 1); c.add("a_nw", 4)
    c.add("b_mu", 13); c.add("b_w0", 4); c.add("b_a0", 4); c.add("b_kk", 4); c.add("b_ka", 4)
    c.add("b_rk", 4); c.add("b_gw", 4); c.add("b_gb", 4)
    c.add("c_cw", 16); c.add("c_cb", 4); c.add("c_br", 4); c.add("c_bi", 4); c.add("c_lam", 4)
    c.add("d_nw", 4)
    return c


CM = build_cmap()

def _cols(a, b):
    return list(range(a, b))


def build_slabs():
    slabs = []
    def fm(name, start):
        slabs.append((name, _cols(start, start + 512)))
    fm("Aq", 0); fm("Ak", 512); fm("Az", 2048)
    slabs.append(("Ag", _cols(2560, 2576) + [-1] * (512 - 16)))
    fm("Br", 2576); fm("Bk", 3088); fm("Bv", 3600)
    slabs.append(("Bl", _cols(4112, 4240) + [-1] * (512 - 128)))
    fm("Bz", 4240); fm("Cx", 4752); fm("Cz", 5264); fm("Dz", 7312)
    fm("Av", 1024); fm("Ao", 1536); fm("Dq", 5776); fm("Dk", 6288); fm("Dv", 6800)
    return slabs


SLABS = build_slabs()
SLAB_ID = {s[0]: i for i, s in enumerate(SLABS)}
NSLAB = len(SLABS)


def host_prepare(inp):
    f = np.float32
    w_in = inp["w_in"]
    w_in_p = np.concatenate([w_in, np.zeros((DEPTH, D_MODEL, 1), f)], axis=2)
    wt = np.empty((DEPTH, NSLAB, 128, 8, 512), f)
    for si, (name, cols) in enumerate(SLABS):
        blk = w_in_p[:, :, cols]
        wt[:, si] = blk.reshape(DEPTH, 8, 128, 512).transpose(0, 2, 1, 3)
    w_out = inp["w_out"]
    wo = np.ascontiguousarray(w_out.reshape(DEPTH, 4, 4, 128, 1024).transpose(0, 1, 3, 2, 4))
    cv = np.zeros((DEPTH, 128, CM.n), f)
    def put(name, arr):
        o, w = CM.off[name]
        cv[:, :, o:o + w] = arr
    put("gpre", inp["norm_pre"].reshape(DEPTH, 8, 128).transpose(0, 2, 1))
    cw = inp["mlstm_conv_w"]
    put("a_cw", cw.reshape(DEPTH, 4, 8, 128).transpose(0, 3, 1, 2).reshape(DEPTH, 128, 32))
    put("a_cb", inp["mlstm_conv_b"].reshape(DEPTH, 8, 128).transpose(0, 2, 1))
    ib = np.zeros((DEPTH, 128, 1), f); ib[:, 0:8, 0] = inp["mlstm_i_bias"]; put("a_ib", ib)
    fb = np.zeros((DEPTH, 128, 1), f); fb[:, 0:8, 0] = inp["mlstm_f_bias"]; put("a_fb", fb)
    def fm4(x):
        return x.reshape(DEPTH, -1, 128).transpose(0, 2, 1)
    put("a_nw", fm4(inp["mlstm_norm_w"]))
    mu = inp["rwkv_mu"]
    put("b_mu", fm4(mu))
    for k, nm in [("b_w0", "rwkv_w0"), ("b_a0", "rwkv_a0"), ("b_kk", "rwkv_k_k"), ("b_ka", "rwkv_k_a"),
                  ("b_rk", "rwkv_r_k"), ("b_gw", "rwkv_gn_w"), ("b_gb", "rwkv_gn_b"),
                  ("c_cb", "lru_conv_b"), ("c_br", "lru_b_r"), ("c_bi", "lru_b_i"), ("c_lam", "lru_lambda"),
                  ("d_nw", "ret_norm_w")]:
        put(k, fm4(inp[nm]))
    lw = inp["lru_conv_w"]
    put("c_cw", lw.reshape(DEPTH, 4, 4, 128).transpose(0, 3, 1, 2).reshape(DEPTH, 128, 16))
    lora = np.concatenate([inp["rwkv_w_up"], inp["rwkv_a_up"]], axis=1)
    lru = np.zeros((DEPTH, 128, 2, 4, 128), f)
    for which, nm in enumerate(["lru_w_r", "lru_w_i"]):
        w = inp[nm]
        for g in range(4):
            for j in range(2):
                lru[:, 64 * j:64 * j + 64, which, g, 64 * j:64 * j + 64] = w[:, 2 * g + j]
    gpost = np.ascontiguousarray(np.broadcast_to(inp["norm_post"][:, None, :], (DEPTH, 128, D_MODEL)))
    return dict(wt=wt, wo=wo, cv=cv, lora=np.ascontiguousarray(lora), lru=lru, gpost=gpost)


def host_tables():
    f = np.float32
    t = {}
    t["ident"] = np.eye(128, dtype=f)
    hd = 64
    log_g = np.log1p(-np.exp2(-5.0 - np.arange(8, dtype=np.float64)))
    idx = np.arange(64, dtype=np.float64)
    dmat = np.exp(log_g[:, None, None] * np.abs(idx[:, None] - idx[None, :])) * hd ** -0.5
    xi = np.exp(log_g[:, None] * (idx + 1.0))
    zeta = np.exp(log_g[:, None] * (63.0 - idx)) * hd ** -0.5
    gch = np.exp(log_g * 64.0)
    hs2h = [2 * (s % 4) + (s // 4) for s in range(8)]
    dm = np.zeros((128, 8, 64));
    for s in range(8):
        dm[0:64, s] = dmat[hs2h[s]]; dm[64:128, s] = dmat[hs2h[s]]
    t["d_dmat"] = dm.astype(f)
    t["d_xi"] = np.tile(xi.T, (2, 1)).astype(f)
    t["d_zeta"] = np.tile(zeta.T, (2, 1)).astype(f)
    gc = np.zeros((128, 4))
    for g in range(4):
        for j in range(2):
            gc[64 * j:64 * j + 64, g] = gch[2 * g + j]
    t["d_gch"] = gc.astype(f)
    mk = np.zeros((128, 8, 64))
    sidx = np.arange(128) % 64
    mk[:] = np.where(sidx[:, None, None] <= np.arange(64)[None, None, :], 0.0, -30000.0)
    t["a_mask"] = mk.reshape(128, 512).astype(f)
    sel = np.zeros((8, 512 + 4 + 128))
    for hp_ in range(8):
        for hs in range(8):
            if hp_ == hs2h[hs]:
                sel[hp_, hs * 64:(hs + 1) * 64] = 1.0
        sel[hp_, 512 + hp_ // 2] = 1.0
        jj = hp_ % 2
        sel[hp_, 516 + 64 * jj:516 + 64 * jj + 64] = 1.0
    t["a_sel"] = sel.astype(f)
    bo = np.zeros((128, 128)); bo[0:64, 0:64] = 1.0; bo[64:128, 64:128] = 1.0
    t["b_ones"] = bo.astype(f)
    i2 = np.zeros((128, 64)); i2[0:64] = np.eye(64); i2[64:128] = np.eye(64)
    t["b_i2"] = i2.astype(f)
    rr_ = (np.arange(128) % 64)[:, None]; cc_ = np.arange(64)[None, :]
    m5 = np.zeros((128, 5, 64))
    m5[:, 0] = rr_ > cc_; m5[:, 1] = cc_ > rr_; m5[:, 2] = cc_ > rr_; m5[:, 3] = cc_ >= rr_; m5[:, 4] = cc_ >= rr_
    t["b_m5"] = m5.astype(f)
    half = 32
    pos = np.arange(SEQ, dtype=np.float32)
    inv_freq = (np.float32(10000.0) ** (-np.arange(half, dtype=np.float32) / np.float32(half))).astype(np.float32)
    ang = (pos[:, None] * inv_freq[None, :]).astype(np.float32).astype(np.float64)
    t["rope"] = np.concatenate([np.cos(ang), np.sin(ang)], axis=1).astype(f)
    return t


def build_program(ntiles=SEQ // T, nlayers=DEPTH, mixers="ABCD"):
    nc = bass.Bass("TRN2", target_bir_lowering=False)
    stack = contextlib.ExitStack()
    P = Prog(nc, stack)
    dram = {}

    def din(name, shape):
        dram[name] = nc.dram_tensor(name, list(shape), F32, kind="ExternalInput").ap()
        return dram[name]

    x_d = din("x", [SEQ, D_MODEL])
    wt_d = din("wt", [DEPTH, NSLAB, 128, 8, 512])
    wo_d = din("wo", [DEPTH, 4, 128, 4, 1024])
    cv_d = din("cv", [DEPTH, 128, CM.n])
    lora_d = din("lora", [DEPTH, 128, 512])
    lru_d = din("lru", [DEPTH, 128, 2, 4, 128])
    gpost_d = din("gpost", [DEPTH, 128, D_MODEL])
    ident_d = din("ident", [128, 128])
    dmat_d = din("d_dmat", [128, 8, 64])
    dxi_d = din("d_xi", [128, 8])
    dzeta_d = din("d_zeta", [128, 8])
    dgch_d = din("d_gch", [128, 4])
    rope_d = din("rope", [SEQ, 64])
    bones_d = din("b_ones", [128, 128])
    bi2_d = din("b_i2", [128, 64])
    bm5_d = din("b_m5", [128, 5, 64])
    amask_d = din("a_mask", [128, 512])
    asel_d = din("a_sel", [8, 644])
    y_d = nc.dram_tensor("y", [SEQ, D_MODEL], F32, kind="ExternalOutput").ap()

    ident = P.sb("ident", [128, 128])
    cv = [P.sb(f"cv{l}", [128, CM.n]) for l in range(DEPTH)]
    lru1 = P.sb("lru", [128, 2, 4, 128]); lru = [lru1, lru1]
    gpost1 = P.sb("gpost", [128, D_MODEL]); gpost = [gpost1, gpost1]
    lora_up = [P.sb(f"lora_up{l}", [128, 512]) for l in range(DEPTH)]
    b_ones = P.sb("b_ones", [128, 128]); b_i2 = P.sb("b_i2", [128, 1, 64]); b_m5 = P.sb("b_m5", [128, 5, 64])
    b_hist = [P.sb(f"b_hist{l}", [128, 13]) for l in range(DEPTH)]
    b_S = [P.sb(f"b_S{l}", [128, 4, 64]) for l in range(DEPTH)]
    ident16 = P.sb("ident16", [128, 128], BF16)
    AlT16 = P.sb("AlT16", [128, 4, T], BF16); RhT16 = P.sb("RhT16", [128, 4, T], BF16)
    bk16 = P.sb("bk16", [128, 4, T], BF16)
    bm16 = [P.sb(f"bm16_{i}", [128, 8, 64], BF16) for i in range(3)]
    b_S16 = P.sb("b_S16", [128, 4, 64], BF16)
    a_n16 = P.sb("a_n16", [128, 4, 1], BF16); wj16 = P.sb("wj16", [128, NTB, 8], BF16)
    ones16 = P.sb("ones16", [128, 64], BF16); eps12 = P.sb("eps12", [128, 1])
    gLt = P.sb("gLt", [128, 4, NCH]); PRall = [P.sb(f"PR{i}", [128, 5, 8, 64], BF16) for i in range(NTB)]; Y2 = P.sb("Y2", [128, 8, 64])
    K2g = P.sb("K2g", [128, 2, 4, 64])
    ones = P.sb("ones", [128, T]); zeros = P.sb("zeros", [128, T])
    xt = [P.sb(f"xt{i}", [128, NTB, D_MODEL]) for i in range(1)]
    hT = P.sb("hT", [128, 8, T], BF16)
    yT = P.sb("yT", [128, 16, T], BF16)
    slab = [P.sb(f"slab{i}", [128, 8, 512], BF16) for i in range(NSLABBUF)]
    sm = P.sb("small", [128, 64])
    scr = P.sb("scr", [128, D_MODEL])
    scr2 = P.sb("scr2", [128, D_MODEL])
    FS = [P.sb(f"fs{i}", [128, 4, T + 4]) for i in range(10)]
    c_hist = [P.sb(f"c_hist{l}", [128, 4, 3]) for l in range(DEPTH)]
    c_state = [P.sb(f"c_state{l}", [128, 4]) for l in range(DEPTH)]
    c_coef = [P.sb(f"c_coef{l}", [128, 8]) for l in range(DEPTH)]
    PB = [P.ps(f"pb{i}") for i in range(8)]
    TM = [P.sb(f"tm{i}", [128, NTB, 8, 64]) for i in range(6)]
    TM16 = [P.sb(f"tm16_{i}", [128, NTB, 8, 64], BF16) for i in range(3)]
    d_dmat = P.sb("d_dmat", [128, 8, 64]); d_xi = P.sb("d_xi", [128, 4, 2, 1]); d_zeta = P.sb("d_zeta", [128, 8, 1])
    d_gch = P.sb("d_gch", [128, 4, 1]); rope_sb = P.sb("rope_sb", [128, NTB, 1, 64])
    d_R = [P.sb(f"d_R{l}", [128, 4, 64]) for l in range(DEPTH)]
    a_mask = P.sb("a_mask", [128, 512]); a_sel = P.sb("a_sel", [8, 644])
    ga = P.sb("ga", [8, 12, T + 1]); scl = P.sb("scl", [8, NCH]); rhs_sc = P.sb("rhs_sc", [8, NCH, 4])
    scb = P.sb("scb", [128, NCH, 4, 1]); negPx = P.sb("negPx", [8, 2, 8, 64]); ET = P.sb("ET", [128, 8, 64])
    ET1 = P.sb("ET1", [128, 8, 64])
    tmsc = P.sb("tmsc", [128, NTB, 3, 8]); hnum = P.sb("hnum", [128, 8, 64]); vw = P.sb("vw", [128, 8, 64])
    a_hist = [P.sb(f"a_hist{l}", [128, 8, 3]) for l in range(DEPTH)]
    a_C = [P.sb(f"a_C{l}", [128, 4, 64]) for l in range(DEPTH)]
    a_n = [P.sb(f"a_n{l}", [128, 4, 1]) for l in range(DEPTH)]
    a_car = [P.sb(f"a_car{l}", [8, 4]) for l in range(DEPTH)]
    AT = P.sb("AT", [128, 8, 64]); tmp4 = P.sb("tmp4", [128, 4, 2, 64]); lnst = P.sb("lnst", [128, 64]); lnst_b = P.sb("lnst_b", [128, 40]); lnsts = [lnst, lnst_b]
    state = dict(slab_i=0, pb_i=0)

    import os
    cut = int(os.environ.get("KCUT", "9"))
    def dma(out, in_, eng="sp"):
        return P.op(eng, lambda e: e.dma_start(out=_a(out), in_=_a(in_)), [out], [in_], dma=True)

    def mm(out, lhsT, rhs, start=True, stop=True):
        P.op("pe", lambda e: e.matmul(_a(out), _a(lhsT), _a(rhs), start=start, stop=stop),
             [out], [lhsT, rhs] + ([] if start else [out]))

    def tr(out, in_, idn):
        P.op("pe", lambda e: e.transpose(_a(out), _a(in_), _a(idn)), [out], [in_, idn])

    def act(out, in_, func, bias=None, scale=1.0, accum=None, eng="act"):
        kw = {}
        if bias is not None:
            kw["bias"] = _a(bias)
        if accum is not None:
            kw["accum_out"] = _a(accum)
        P.op("act", lambda e: e.activation(_a(out), _a(in_), func, scale=_a(scale), **kw),
             [out, accum], [in_, bias, scale])

    def tt(out, a, b, op, eng="dve"):
        P.op(eng, lambda e: e.tensor_tensor(_a(out), _a(a), _a(b), op), [out], [a, b])

    def ts(out, a, s1, s2, op0, op1=ALU.bypass, eng="dve"):
        P.op(eng, lambda e: e.tensor_scalar(_a(out), _a(a), _a(s1), _a(s2), op0, op1), [out], [a, s1, s2])

    def stt(out, a, s, b, op0, op1):
        P.op("dve", lambda e: e.scalar_tensor_tensor(_a(out), _a(a), _a(s), _a(b), op0, op1), [out], [a, s, b])

    def scan(out, d0, d1, init, op0, op1):
        P.op("dve", lambda e: e.tensor_tensor_scan(_a(out), _a(d0), _a(d1), _a(init), op0, op1),
             [out], [d0, d1, init])

    def cp(out, in_, eng="dve", force=False):
        if isinstance(in_, Ref) and in_.buf.psum and eng == "dve" and not force:
            return act(out, in_, AF.Copy)
        P.op(eng, lambda e: e.tensor_copy(_a(out), _a(in_)), [out], [in_])

    def recip(out, in_):
        P.op("dve", lambda e: e.reciprocal(_a(out), _a(in_)), [out], [in_])

    def memset(out, val, eng="dve"):
        P.op(eng, lambda e: e.memset(_a(out), val), [out], [])

    def reduce(out, in_, op, axis=AX.X):
        P.op("dve", lambda e: e.tensor_reduce(_a(out), _a(in_), axis, op), [out], [in_])

    def interleave(gens):
        gens = list(gens)
        while gens:
            for gz in list(gens):
                try:
                    next(gz)
                except StopIteration:
                    gens.remove(gz)

    def run_chunks(make_gen):
        gens = [make_gen(c) for c in range(NCH)]

        def step(c):
            try:
                next(gens[c])
            except StopIteration:
                pass
        pairs = NCH // 2
        step(0); step(1)
        for k in range(pairs):
            a, b = 2 * k, 2 * k + 1
            step(a); step(b)
            step(a); step(b)
            step(a); step(a)
            if k + 1 < pairs:
                step(2 * k + 2)
            step(b); step(b)
            if k + 1 < pairs:
                step(2 * k + 3)

    def next_pb():
        state["pb_i"] = (state["pb_i"] + 1) % 8
        return PB[(0, 1, 2, 7, 3, 5, 4, 6)[state["pb_i"]]]

    def load_slab(l, name, ncols=512):
        s = slab[state["slab_i"]]
        state["slab_i"] = (state["slab_i"] + 1) % NSLABBUF
        dma(s[:, :, 0:ncols], wt_d[l, SLAB_ID[name], :, :, 0:ncols], eng="pool")
        return s

    def load_wo(l, si):
        s = slab[state["slab_i"]]
        state["slab_i"] = (state["slab_i"] + 1) % NSLABBUF
        dma(s[:, :, :], wo_d[l, si].rearrange("p f n -> p (f n)").rearrange("p (a b) -> p a b", a=8), eng="pool")
        return s

    def cvc(l, name, i=0):
        o, w = CM.off[name]
        return cv[l][:, o + i:o + i + 1]

    def fm_proj(s, gi, out_cols=T):
        pb = next_pb()
        for kc in range(8):
            mm(pb[:, 0:T], s[:, kc, gi * 128:(gi + 1) * 128], hT[:, kc, :], start=(kc == 0), stop=(kc == 7))
        return pb

    dma(ident[:, :], ident_d)
    for l in range(DEPTH):
        dma(cv[l][:, :], cv_d[l])
        dma(lora_up[l][:, :], lora_d[l])
        memset(b_hist[l][:, :], 0.0)
        memset(b_S[l][:, :, :], 0.0)
    dma(d_dmat[:, :, :], dmat_d)
    dma(d_xi[:, :, :, :], dxi_d.rearrange("p (g j o) -> p g j o", g=4, j=2))
    dma(d_zeta[:, :, :], dzeta_d.rearrange("p (h o) -> p h o", o=1))
    dma(d_gch[:, :, :], dgch_d.rearrange("p (g o) -> p g o", o=1))
    dma(a_mask[:, :], amask_d)
    dma(a_sel[:, :], asel_d)
    for l in range(DEPTH):
        memset(d_R[l][:, :, :], 0.0)
        memset(a_hist[l][:, :, :], 0.0)
        memset(a_C[l][:, :, :], 0.0)
        memset(a_n[l][:, :, :], 0.0)
        memset(a_car[l][:, :], 0.0)
        ts(a_car[l][0:8, 2:3], cv[l][0:8, CM.off["a_fb"][0]:CM.off["a_fb"][0] + 1], -1.0, None, ALU.mult)
    dma(b_ones[:, :], bones_d)
    dma(b_i2[:, 0, :], bi2_d)
    dma(b_m5[:, :, :], bm5_d)
    memset(ones[:, :], 1.0)
    cp(ident16[:, :], ident[:, :])
    memset(ones16[:, :], 1.0)
    memset(eps12[:, :], 1e-12)
    memset(zeros[:, :], 0.0)
    for l in range(DEPTH):
        memset(c_hist[l][:, :, :], 0.0)
        memset(c_state[l][:, :], 0.0)
        o, w = CM.off["c_lam"]
        act(sm[:, 0:4], cv[l][:, o:o + 4], AF.Exp, scale=-1.0)
        act(sm[:, 4:8], sm[:, 0:4], AF.Ln, bias=ones[:, 0:1])
        ts(c_coef[l][:, 0:4], sm[:, 4:8], -8.0, None, ALU.mult)
        ts(c_coef[l][:, 4:8], sm[:, 4:8], -16.0, None, ALU.mult)

    def rmsnorm_pre(l, X):
        scrs = [scr, scr2]

        def stream(tb):
            sc = scrs[tb % 2]
            act(sc[:, :], X[:, tb, :], AF.Square)
            yield
            reduce(sm[:, 8 + tb:9 + tb], sc[:, :], ALU.add)
            yield
            ts(sm[:, 12 + tb:13 + tb], sm[:, 8 + tb:9 + tb], 1.0 / D_MODEL, 1e-6, ALU.mult, ALU.add)
            yield
            act(sm[:, 16 + tb:17 + tb], sm[:, 12 + tb:13 + tb], AF.Sqrt)
            yield
            recip(sm[:, 20 + tb:21 + tb], sm[:, 16 + tb:17 + tb])
            yield
            act(sc[:, :], X[:, tb, :], AF.Copy, scale=sm[:, 20 + tb:21 + tb])
            yield
            for half in range(2):
                pb = next_pb()
                for q in range(4):
                    kc = half * 4 + q
                    tr(pb[:, q * 128:(q + 1) * 128], sc[:, kc * 128:(kc + 1) * 128], ident[:, :])
                for q in range(4):
                    kc = half * 4 + q
                    ts(hT[:, kc, tb * 128:(tb + 1) * 128], pb[:, q * 128:(q + 1) * 128],
                       cvc(l, "gpre", kc), None, ALU.mult)
                yield
        interleave([stream(tb) for tb in range(NTB)])

    def mixer_C(l):
        cx, xc, rg, ig, aa, uu, hh = FS[0], FS[1], FS[2], FS[3], FS[4], FS[5], FS[6]
        dma(lru[l][:, :, :, :], lru_d[l])
        wx = load_slab(l, "Cx")

        def cstream(g):
            pb = fm_proj(wx, g)
            cp(cx[:, g, 0:3], c_hist[l][:, g, :])
            act(cx[:, g, 3:3 + T], pb[:, 0:T], AF.Copy)
            yield
            cp(c_hist[l][:, g, :], cx[:, g, T:T + 3])
            o, w = CM.off["c_cw"]
            ts(xc[:, g, 0:T], cx[:, g, 0:T], cv[l][:, o + g:o + g + 1], cvc(l, "c_cb", g), ALU.mult, ALU.add)
            yield
            for j in range(1, 4):
                stt(xc[:, g, 0:T], cx[:, g, j:j + T], cv[l][:, o + 4 * j + g:o + 4 * j + g + 1], xc[:, g, 0:T],
                    ALU.mult, ALU.add)
                yield
        interleave([cstream(g) for g in range(4)])
        for g in range(4):
            pb = next_pb()
            mm(pb[:, 0:T], lru[l][:, 0, g, :], xc[:, g, 0:T])
            act(rg[:, g, 0:T], pb[:, 0:T], AF.Sigmoid, bias=cvc(l, "c_br", g))
            pb = next_pb()
            mm(pb[:, 0:T], lru[l][:, 1, g, :], xc[:, g, 0:T])
            act(ig[:, g, 0:T], pb[:, 0:T], AF.Sigmoid, bias=cvc(l, "c_bi", g))
        for g in range(4):
            act(aa[:, g, 0:T], rg[:, g, 0:T], AF.Exp, scale=c_coef[l][:, g:g + 1])
            act(uu[:, g, 0:T], rg[:, g, 0:T], AF.Exp, scale=c_coef[l][:, 4 + g:5 + g])
        ts(uu[:, :, 0:T], uu[:, :, 0:T], -1.0, 1.0, ALU.mult, ALU.add)
        act(uu[:, :, 0:T], uu[:, :, 0:T], AF.Sqrt)
        tt(ig[:, :, 0:T], ig[:, :, 0:T], xc[:, :, 0:T], ALU.mult)
        tt(uu[:, :, 0:T], uu[:, :, 0:T], ig[:, :, 0:T], ALU.mult)
        for g in range(4):
            scan(hh[:, g, 0:T], aa[:, g, 0:T], uu[:, g, 0:T], c_state[l][:, g:g + 1], ALU.mult, ALU.add)
            cp(c_state[l][:, g:g + 1], hh[:, g, T - 1:T])
        wz = load_slab(l, "Cz")
        for g in range(4):
            pb = fm_proj(wz, g)
            act(rg[:, g, 0:T], pb[:, 0:T], AF.Silu)
            tt(yT[:, 8 + g, :], hh[:, g, 0:T], rg[:, g, 0:T], ALU.mult)

    def tm_proj(sl, tb):
        pb = next_pb()
        for kc in range(8):
            mm(pb[:, :], hT[:, kc, tb * 128:(tb + 1) * 128], sl[:, kc, :], start=(kc == 0), stop=(kc == 7))
        return pb

    def to_fm(dst, src, tb):
        pb = next_pb()
        for g in range(4):
            tr(pb[:, g * 128:(g + 1) * 128], src[:, tb, 2 * g:2 * g + 2, 0:64], ident[:, :])
        cp(dst[:, 0:4, tb * 128:(tb + 1) * 128], pb[:, :].rr("p (a b) -> p a b", a=4))

    def to_tm(dst, src, tb):
        pb = next_pb()
        for g in range(4):
            tr(pb[:, g * 128:(g + 1) * 128], src[:, g, tb * 128:(tb + 1) * 128], ident[:, :])
        cp(dst[:, tb, :, 0:64], pb[:, :].rr("p (h e) -> p h e", h=8))

    def head_ln_stream(dst, src, tb):
        X3 = src[:, tb, :, 0:64]
        D3 = dst[:, tb, :, 0:64]
        st_ = lnsts[tb % 2]
        sq = scr[:, (tb % 2) * 512:(tb % 2) * 512 + 512].rr("p (h e) -> p h e", h=8)
        reduce(st_[:, 0:8], X3, ALU.add)
        yield
        stt(D3, st_[:, 0:8].rr("p (h o) -> p h o", o=1).bc([128, 8, 64]), -1.0 / 64, X3, ALU.mult, ALU.add)
        yield
        tt(sq, D3, D3, ALU.mult)
        yield
        reduce(st_[:, 8:16], sq, ALU.add)
        yield
        ts(st_[:, 16:24], st_[:, 8:16], 1.0 / 64, 1e-5, ALU.mult, ALU.add)
        yield
        act(st_[:, 24:32], st_[:, 16:24], AF.Sqrt)
        yield
        recip(st_[:, 32:40], st_[:, 24:32])
        yield
        tt(D3, D3, st_[:, 32:40].rr("p (h o) -> p h o", o=1).bc([128, 8, 64]), ALU.mult)
        yield

    def head_ln_all(dst, src):
        interleave([head_ln_stream(dst, src, tb) for tb in range(NTB)])

    def mixer_D(l, ti):
        qr, kr, osb, xn = TM[0], TM[1], TM[4], TM[5]
        vv, kz = TM16[1], TM16[2]
        qT, kT, sz = AlT16, RhT16, FS[2]
        AT = bm16[0]
        d_R16 = b_S16
        cp(d_R16[:, :, :], d_R[l][:, :, :])
        S = [PB[3], PB[4]]; O = [PB[5], PB[6]]; ST = [PB[2], PB[7]]
        dma(rope_sb[:, :, :, :], rope_d.rearrange("(n tb p) (o e) -> n p tb o e", tb=NTB, p=128, o=1)[ti])
        for name, dst in (("Dq", qr), ("Dk", kr)):
            sl = load_slab(l, name)

            def rstream(tb, name=name, dst=dst, sl=sl):
                o5 = (tb % 2) * 512
                pb = tm_proj(sl, tb)
                act(scr[:, o5:o5 + 512], pb[:, :], AF.Copy)
                yield
                raw = scr[:, o5:o5 + 512].rr("p (h e) -> p h e", h=8)
                x1, x2 = raw[:, :, 0:32], raw[:, :, 32:64]
                cos = rope_sb[:, tb, :, 0:32].bc([128, 8, 32]); sin = rope_sb[:, tb, :, 32:64].bc([128, 8, 32])
                t1 = scr2[:, o5:o5 + 256].rr("p (h e) -> p h e", h=8)
                t2 = scr2[:, o5 + 256:o5 + 512].rr("p (h e) -> p h e", h=8)
                d1, d2 = dst[:, tb, :, 0:32], dst[:, tb, :, 32:64]
                tt(d1, x1, cos, ALU.mult); tt(t1, x2, sin, ALU.mult)
                yield
                tt(d2, x2, cos, ALU.mult); tt(t2, x1, sin, ALU.mult)
                yield
                tt(d1, d1, t1, ALU.subtract)
                yield
                tt(d2, d2, t2, ALU.add)
                yield
                to_fm(qT if name == "Dq" else kT, dst, tb)
                yield
            interleave([rstream(tb) for tb in range(NTB)])
        sl = load_slab(l, "Dv")
        for tb in range(NTB):
            pb = tm_proj(sl, tb)
            act(vv[:, tb, :, 0:64], pb[:, :].rr("p (h e) -> p h e", h=8), AF.Copy)
            tt(kz[:, tb, :, 0:64], kr[:, tb, :, 0:64], d_zeta[:, :, :].bc([128, 8, 64]), ALU.mult)
        sl = load_slab(l, "Dz")
        for g in range(4):
            pb = fm_proj(sl, g)
            act(sz[:, g, 0:T], pb[:, 0:T], AF.Silu)
        def d_chunk(c):
            tb, p = c // 2, c % 2
            rows = slice(64 * p, 64 * p + 64)
            cols = slice(c * 64, (c + 1) * 64)
            for j in range(2):
                jr = slice(64 * j, 64 * j + 64)
                for g in range(4):
                    mm(S[j][rows, g * 64:(g + 1) * 64], kT[jr, g, cols], qT[jr, g, cols])
            for h in range(8):
                g, j = h // 2, h % 2
                mm(ST[p][64 * j:64 * j + 64, g * 64:(g + 1) * 64], kz[rows, tb, h, 0:64], vv[rows, tb, h, 0:64])
            yield
            for j in range(2):
                tt(AT[rows, 4 * j:4 * j + 4, :], S[j][rows, 0:256].rr("p (a b) -> p a b", a=4),
                   d_dmat[rows, 4 * j:4 * j + 4, :], ALU.mult)
            yield
            for h in range(8):
                g, j = h // 2, h % 2
                mm(O[p][rows, h * 64:(h + 1) * 64], AT[rows, j * 4 + g, :], vv[rows, tb, h, 0:64])
            yield
            for j in range(2):
                jr = slice(64 * j, 64 * j + 64)
                for g in range(4):
                    mm(S[j][rows, 256 + g * 64:256 + (g + 1) * 64], qT[jr, g, cols], d_R16[jr, g, :])
            yield
            tt(d_R[l][:, :, :], d_R[l][:, :, :], d_gch[:, :, :].bc([128, 4, 64]), ALU.mult)
            tt(d_R[l][:, :, :], d_R[l][:, :, :], ST[p][:, 0:256].rr("p (a b) -> p a b", a=4), ALU.add)
            for j in range(2):
                tt(tmp4[rows, :, j, 0:64], S[j][rows, 256:512].rr("p (a b) -> p a b", a=4),
                   d_xi[rows, :, j, :].bc([64, 4, 64]), ALU.mult)
            cp(d_R16[:, :, :], d_R[l][:, :, :])
            tt(osb[rows, tb, :, 0:64], O[p][rows, :].rr("p (h e) -> p h e", h=8),
               tmp4[rows, :, :, 0:64].rr("p g j e -> p (g j) e"), ALU.add)
            yield
        run_chunks(d_chunk)
        head_ln_all(xn, osb)
        for tb in range(NTB):
            pb = next_pb()
            for g in range(4):
                tr(pb[:, g * 128:(g + 1) * 128], xn[:, tb, 2 * g:2 * g + 2, 0:64], ident[:, :])
            for g in range(4):
                stt(yT[:, 12 + g, tb * 128:(tb + 1) * 128], pb[:, g * 128:(g + 1) * 128], cvc(l, "d_nw", g),
                    sz[:, g, tb * 128:(tb + 1) * 128], ALU.mult, ALU.mult)

    def conv_silu(l, sl, g8, dst, gdst, cx, whichhist, cwname, cbname, ngroups_total, silu=True):
        pass

    def mixer_A(l, ti):
        cx, qT32, kT32, sz = FS[0], FS[1], FS[2], FS[3]
        qT, kT = AlT16, RhT16
        so, osb, xn = TM[2], TM[3], TM[4]
        ktm, vv = TM16[0], TM16[1]
        AT, vw = bm16[0], bm16[1]
        a_C16 = b_S16
        cp(a_C16[:, :, :], a_C[l][:, :, :])
        cp(a_n16[:, :, :], a_n[l][:, :, :])
        S = [PB[3], PB[4]]; O = [PB[5], PB[6]]; ST = [PB[2], PB[7]]; JX = [PB[0], PB[1]]
        R_I, R_SP, R_F, R_G, R_P, R_MU, R_PE, R_SI, R_WJ, R_NM, R_NP = range(11)
        ocw = CM.off["a_cw"][0]
        for which, (name, dstT, dst16) in enumerate((("Aq", qT32, qT), ("Ak", kT32, kT))):
            sl = load_slab(l, name)

            def astream(g, which=which, sl=sl, dstT=dstT, dst16=dst16):
                g8 = which * 4 + g
                pb = fm_proj(sl, g)
                cp(cx[:, g, 0:3], a_hist[l][:, g8, :])
                act(cx[:, g, 3:3 + T], pb[:, 0:T], AF.Copy)
                yield
                cp(a_hist[l][:, g8, :], cx[:, g, T:T + 3])
                ts(dstT[:, g, 0:T], cx[:, g, 0:T], cv[l][:, ocw + g8:ocw + g8 + 1], cvc(l, "a_cb", g8),
                   ALU.mult, ALU.add)
                yield
                for j in range(1, 4):
                    stt(dstT[:, g, 0:T], cx[:, g, j:j + T], cv[l][:, ocw + 8 * j + g8:ocw + 8 * j + g8 + 1],
                        dstT[:, g, 0:T], ALU.mult, ALU.add)
                    yield
                act(dst16[:, g, 0:T], dstT[:, g, 0:T], AF.Silu)
                yield
            interleave([astream(g) for g in range(4)])
        sl = load_slab(l, "Az")
        for g in range(4):
            pb = fm_proj(sl, g)
            act(sz[:, g, 0:T], pb[:, 0:T], AF.Silu)
        acut = int(os.environ.get("ACUT", "99"))
        if acut < 1:
            return
        sl = load_slab(l, "Ag", ncols=128)
        pbi = next_pb()
        for kc in range(8):
            mm(pbi[0:8, 0:T], sl[:, kc, 0:8], hT[:, kc, :], start=(kc == 0), stop=(kc == 7))
        act(ga[0:8, R_I, 0:T], pbi[0:8, 0:T], AF.Identity, bias=cv[l][0:8, CM.off["a_ib"][0]:CM.off["a_ib"][0] + 1])
        gcut = int(os.environ.get("GCUT", "99"))
        if gcut < 1:
            return
        pbf = next_pb()
        for kc in range(8):
            mm(pbf[0:8, 0:T], sl[:, kc, 8:16], hT[:, kc, :], start=(kc == 0), stop=(kc == 7))
        act(ga[0:8, R_SP, 0:T], pbf[0:8, 0:T], AF.Exp, bias=a_car[l][0:8, 2:3], scale=-1.0)
        act(ga[0:8, R_SP, 0:T], ga[0:8, R_SP, 0:T], AF.Ln, bias=ones[0:8, 0:1])
        if gcut < 2:
            return
        scan(ga[0:8, R_F, 0:T], ones[0:8, 0:T], ga[0:8, R_SP, 0:T], a_car[l][0:8, 0:1], ALU.mult, ALU.subtract)
        cp(a_car[l][0:8, 0:1], ga[0:8, R_F, T - 1:T])
        tt(ga[0:8, R_G, 0:T], ga[0:8, R_I, 0:T], ga[0:8, R_F, 0:T], ALU.subtract)
        if gcut < 3:
            return
        cp(ga[0:8, R_P, 0:1], a_car[l][0:8, 1:2])
        scan(ga[0:8, R_P, 1:T + 1], ones[0:8, 0:T], ga[0:8, R_G, 0:T], a_car[l][0:8, 1:2], ALU.mult, ALU.max)
        cp(a_car[l][0:8, 1:2], ga[0:8, R_P, T:T + 1])
        if gcut < 4:
            return
        for c in range(NCH):
            cols = slice(c * 64, (c + 1) * 64)
            ts(ga[0:8, R_MU, cols], zeros[0:8, 0:64], ga[0:8, R_P, c * 64:c * 64 + 1], None, ALU.add)
            ts(ga[0:8, R_PE, cols], zeros[0:8, 0:64], ga[0:8, R_P, c * 64 + 64:c * 64 + 65], None, ALU.add)
            tt(scl[0:8, c:c + 1], ga[0:8, R_P, c * 64:c * 64 + 1], ga[0:8, R_P, c * 64 + 64:c * 64 + 65], ALU.subtract)
        if gcut < 5:
            return
        Pv = ga[0:8, R_P, 1:T + 1]
        tt(ga[0:8, R_SI, 0:T], ga[0:8, R_MU, 0:T], Pv, ALU.subtract)
        tt(ga[0:8, R_WJ, 0:T], ga[0:8, R_G, 0:T], ga[0:8, R_PE, 0:T], ALU.subtract)
        stt(ga[0:8, R_NM, 0:T], ga[0:8, R_F, 0:T], -1.0, Pv, ALU.mult, ALU.subtract)
        ts(ga[0:8, R_NP, 0:T], Pv, -1.0, None, ALU.mult)
        if acut < 2:
            return
        for tb in range(NTB):
            pb = next_pb()
            for r, R in enumerate((R_SI, R_WJ, R_NM)):
                tr(pb[:, r * 8:(r + 1) * 8], ga[0:8, R, tb * 128:(tb + 1) * 128], ident[0:8, 0:8])
            act(tmsc[:, tb, :, :], pb[:, 0:24].rr("p (a b) -> p a b", a=3), AF.Exp)
        if acut < 3:
            return
        tt(rhs_sc[0:8, :, :], scl[0:8, :].rr("p (c o) -> p c o", o=1).bc([8, NCH, 4]),
           a_sel[0:8, 512:516].rr("p (o g) -> p o g", o=1).bc([8, NCH, 4]), ALU.mult)
        pb = next_pb()
        mm(pb[:, 0:NCH * 4], a_sel[0:8, 516:644], rhs_sc[0:8, :, :].rr("p c g -> p (c g)"))
        act(scb[:, :, :, :].rr("p c g o -> p (c g o)"), pb[:, 0:NCH * 4], AF.Exp)
        if acut < 4:
            return
        for tb in range(NTB):
            pbt = next_pb()
            for g in range(4):
                mm(pbt[:, g * 128:(g + 1) * 128], kT[:, g, tb * 128:(tb + 1) * 128], ident16[:, :])
            cp(ktm[:, tb, :, :], pbt[:, :].rr("p (h e) -> p h e", h=8))
            cp(wj16[:, tb, :], tmsc[:, tb, 1, :])
        sl = load_slab(l, "Av")
        for tb in range(NTB):
            pb = tm_proj(sl, tb)
            act(vv[:, tb, :, :], pb[:, :].rr("p (h e) -> p h e", h=8), AF.Copy)
        sl = load_slab(l, "Ao")
        for tb in range(NTB):
            pb = tm_proj(sl, tb)
            act(so[:, tb, :, :], pb[:, :].rr("p (h e) -> p h e", h=8), AF.Sigmoid)
        ETs = [ET, ET1]
        for tb in range(NTB):
            E = next_pb()
            mm(E[:, :], ident[:, :], a_mask[:, :], start=True, stop=False)
            mm(E[:, :], ga[0:8, R_G, tb * 128:(tb + 1) * 128], a_sel[0:8, 0:512], start=False, stop=False)
            for p in range(2):
                c = tb * 2 + p
                tt(negPx[0:8, p, :, :], ga[0:8, R_NP:R_NP + 1, c * 64:(c + 1) * 64].bc([8, 8, 64]),
                   a_sel[0:8, 0:512].rr("p (h e) -> p h e", h=8), ALU.mult)
                mm(E[64 * p:64 * p + 64, :], ones[0:8, 0:64], negPx[0:8, p, :, :].rr("p h e -> p (h e)"),
                   start=False, stop=True)
            act(ETs[tb][:, :, :].rr("p h e -> p (h e)"), E[:, :], AF.Exp)

        def a_chunk(c):
            tb, p = c // 2, c % 2
            rows = slice(64 * p, 64 * p + 64)
            cols = slice(c * 64, (c + 1) * 64)
            ETt = ETs[tb]
            sI4 = tmsc[:, tb, 0, :].rr("p (g j o) -> p g j o", g=4, j=2)
            for j in range(2):
                jr = slice(64 * j, 64 * j + 64)
                for g in range(4):
                    mm(S[j][rows, g * 64:(g + 1) * 64], kT[jr, g, cols], qT[jr, g, cols])
            tt(vw[rows, :, :], vv[rows, tb, :, :],
               tmsc[rows, tb, 1, :].rr("p (h o) -> p h o", o=1).bc([64, 8, 64]), ALU.mult)
            yield
            for j in range(2):
                stt(AT[rows, 4 * j:4 * j + 4, :], S[j][rows, 0:256].rr("p (a b) -> p a b", a=4), 0.125,
                    ETt[rows, 4 * j:4 * j + 4, :], ALU.mult, ALU.mult)
            yield
            for h in range(8):
                g, j = h // 2, h % 2
                mm(O[p][rows, h * 64:(h + 1) * 64], AT[rows, j * 4 + g, :], vv[rows, tb, h, :])
                mm(ST[p][rows, 384 + h:385 + h], AT[rows, j * 4 + g, :], ones16[rows, 0:1])
            for h in range(8):
                g, j = h // 2, h % 2
                jr = slice(64 * j, 64 * j + 64)
                mm(ST[p][jr, g * 64:(g + 1) * 64], ktm[rows, tb, h, :], vw[rows, h, :])
                mm(ST[p][jr, 256 + g:257 + g], ktm[rows, tb, h, :], wj16[rows, tb, h:h + 1])
            yield
            for j in range(2):
                jr = slice(64 * j, 64 * j + 64)
                for g in range(4):
                    mm(JX[j][rows, g * 64:(g + 1) * 64], qT[jr, g, cols], a_C16[jr, g, :])
                    mm(JX[j][rows, 256 + g:257 + g], qT[jr, g, cols], a_n16[jr, g, :])
            yield
            for j in range(2):
                tt(tmp4[rows, :, j, :], JX[j][rows, 0:256].rr("p (a b) -> p a b", a=4),
                   sI4[rows, :, j, :].bc([64, 4, 64]), ALU.mult)
                tt(lnst[rows, 40:48].rr("p (g j) -> p g j", j=2)[:, :, j], JX[j][rows, 256:260],
                   tmsc[rows, tb, 0, :].rr("p (g j) -> p g j", j=2)[:, :, j], ALU.mult)
            tt(a_C[l][:, :, :], a_C[l][:, :, :], scb[:, c, :, :].bc([128, 4, 64]), ALU.mult)
            stt(a_C[l][:, :, :], ST[p][:, 0:256].rr("p (a b) -> p a b", a=4), 0.125, a_C[l][:, :, :],
                ALU.mult, ALU.add)
            tt(a_n[l][:, :, :], a_n[l][:, :, :], scb[:, c, :, :], ALU.mult)
            stt(a_n[l][:, :, :], ST[p][:, 256:260].rr("p (a b) -> p a b", b=1), 0.125, a_n[l][:, :, :],
                ALU.mult, ALU.add)
            cp(a_C16[:, :, :], a_C[l][:, :, :])
            cp(a_n16[:, :, :], a_n[l][:, :, :])
            tt(hnum[rows, :, :], O[p][rows, :].rr("p (h e) -> p h e", h=8),
               tmp4[rows, :, :, :].rr("p g j e -> p (g j) e"), ALU.add)
            tt(lnst[rows, 48:56], ST[p][rows, 384:392], lnst[rows, 40:48], ALU.add)
            stt(lnst[rows, 48:56], lnst[rows, 48:56], -1.0, lnst[rows, 48:56], ALU.mult, ALU.max)
            tt(lnst[rows, 48:56], lnst[rows, 48:56], tmsc[rows, tb, 2, :], ALU.max)
            recip(lnst[rows, 56:64], lnst[rows, 48:56])
            tt(hnum[rows, :, :], hnum[rows, :, :],
               lnst[rows, 56:64].rr("p (h o) -> p h o", o=1).bc([64, 8, 64]), ALU.mult)
            tt(osb[rows, tb, :, :], hnum[rows, :, :], so[rows, tb, :, :], ALU.mult)
            yield
        run_chunks(a_chunk)
        head_ln_all(xn, osb)
        for tb in range(NTB):
            pb = next_pb()
            for g in range(4):
                tr(pb[:, g * 128:(g + 1) * 128], xn[:, tb, 2 * g:2 * g + 2, 0:64], ident[:, :])
            for g in range(4):
                stt(yT[:, g, tb * 128:(tb + 1) * 128], pb[:, g * 128:(g + 1) * 128], cvc(l, "a_nw", g),
                    sz[:, g, tb * 128:(tb + 1) * 128], ALU.mult, ALU.mult)

    def mixer_B(l, ti):
        rS, kS, vS, sz = FS[0], FS[1], FS[2], FS[3]
        RhT, AlT, bon = rS, kS, vS
        pools = (FS[4], FS[5], FS[6])
        pools2 = (FS[7], FS[8], FS[9])

        def slot(i):
            return pools[i // 4][:, i % 4, :]

        def slot2(i):
            return pools2[i // 4][:, i % 4, :]
        raw, lora, aT, lw, lc, kt, kh, beta, eg, egi, egm, tA = [slot(i) for i in range(12)]
        SETS = [dict(raw=raw, aT=aT, lw=lw, lc=lc, kt=kt, kh=kh, beta=beta, eg=eg, egi=egi, egm=egm, tA=tA,
                     beta16=bk16[:, 0, :], kt16=bk16[:, 1, :]),
                dict(raw=slot2(0), aT=slot2(2), lw=slot2(3), lc=slot2(4), kt=slot2(5), kh=slot2(6), beta=slot2(7),
                     eg=slot2(8), egi=slot2(9), egm=slot2(10), tA=slot2(11),
                     beta16=bk16[:, 2, :], kt16=bk16[:, 3, :])]
        wkv, xn = TM[3], TM[4]
        Vtm, Btm, Ktm = TM16[0], TM16[1], TM16[2]
        BJ = [PB[3], PB[4]]; PA = [PB[2], PB[7]]; PQ = [PB[5], PB[6]]; PX = [PB[0], PB[1]]
        Pn, Qn, Xm = bm16
        W2 = AT
        beta16, kt16 = bk16[:, 0, :], bk16[:, 1, :]
        omu = CM.off["b_mu"][0]

        def shift_mix(dst, sl, gi, hidx, mucol, raw_, tA_):
            pb = fm_proj(sl, gi)
            cp(raw_[:, 0:1], b_hist[l][:, hidx:hidx + 1])
            act(raw_[:, 1:T + 1], pb[:, 0:T], AF.Copy)
            yield
            cp(b_hist[l][:, hidx:hidx + 1], raw_[:, T:T + 1])
            tt(tA_[:, 0:T], raw_[:, 0:T], raw_[:, 1:T + 1], ALU.subtract)
            yield
            stt(dst, tA_[:, 0:T], cv[l][:, omu + mucol:omu + mucol + 1], raw_[:, 1:T + 1], ALU.mult, ALU.add)
            yield

        cp(b_S16[:, :, :], b_S[l][:, :, :])
        sl = load_slab(l, "Bl", ncols=128)
        interleave([shift_mix(lora[:, 0:T], sl, 0, 12, 12, raw, tA)])
        act(lora[0:64, 0:T], lora[0:64, 0:T], AF.Tanh)
        tslots = [(slot(2 + 2 * i), slot(3 + 2 * i)) for i in range(4)]
        for wi, (name, dst) in enumerate((("Br", rS), ("Bk", kS), ("Bv", vS))):
            sl = load_slab(l, name)
            interleave([shift_mix(dst[:, g, 0:T], sl, g, wi * 4 + g, wi * 4 + g, tslots[g][0], tslots[g][1])
                        for g in range(4)])
        sl = load_slab(l, "Bz")
        for g in range(4):
            pb = fm_proj(sl, g)
            act(sz[:, g, 0:T], pb[:, 0:T], AF.Silu)
        bcut = int(os.environ.get("BCUT", "99"))
        if bcut < 1:
            return
        def group_chains(g, Z):
            gc = slice(g * 128, (g + 1) * 128)
            raw, aT, lw, lc, kt, kh, beta = Z["raw"], Z["aT"], Z["lw"], Z["lc"], Z["kt"], Z["kh"], Z["beta"]
            eg, egi, egm, tA, beta16, kt16 = Z["eg"], Z["egi"], Z["egm"], Z["tA"], Z["beta16"], Z["kt16"]

            def chainW():
                pb = next_pb()
                mm(pb[:, 0:T], lora_up[l][0:64, gc], lora[0:64, 0:T])
                act(lw[:, 0:T], pb[:, 0:T], AF.Sigmoid, bias=cvc(l, "b_w0", g))
                yield
                for c in range(NCH):
                    cols = slice(c * 64, (c + 1) * 64)
                    scan(lc[:, cols], ones[:, 0:64], lw[:, cols], 0.0, ALU.mult, ALU.add)
                    yield
                act(eg[:, 0:T], lc[:, 0:T], AF.Exp, scale=-0.606531)
                act(egi[:, 0:T], lc[:, 0:T], AF.Exp, scale=0.606531)
                tt(tA[:, 0:T], lc[:, 0:T], lw[:, 0:T], ALU.subtract)
                yield
                act(egm[:, 0:T], tA[:, 0:T], AF.Exp, scale=-0.606531)
                cp(gLt[:, g, :], eg[:, 63:T:64])
                yield

            def chainA():
                pb = next_pb()
                mm(pb[:, 0:T], lora_up[l][64:128, gc], lora[64:128, 0:T])
                act(aT[:, 0:T], pb[:, 0:T], AF.Sigmoid, bias=cvc(l, "b_a0", g))
                yield
                ts(beta[:, 0:T], aT[:, 0:T], -1.0, cvc(l, "b_ka", g), ALU.add, ALU.mult)
                yield
                stt(kt[:, 0:T], beta[:, 0:T], 1.0, kS[:, g, 0:T], ALU.add, ALU.mult)
                yield

            def chainK():
                ts(kh[:, 0:T], kS[:, g, 0:T], cvc(l, "b_kk", g), None, ALU.mult)
                yield
                tt(raw[:, 0:T], kh[:, 0:T], kh[:, 0:T], ALU.mult)
                yield
                pb = next_pb()
                mm(pb[:, 0:T], b_ones[:, :], raw[:, 0:T])
                act(raw[:, 0:T], pb[:, 0:T], AF.Sqrt, bias=eps12[:, 0:1])
                yield
                recip(raw[:, 0:T], raw[:, 0:T])
                yield
                tt(kh[:, 0:T], kh[:, 0:T], raw[:, 0:T], ALU.mult)
                yield

            def chainV():
                for tb in range(NTB):
                    pbt = next_pb()
                    tr(pbt[:, 0:128], vS[:, g, tb * 128:(tb + 1) * 128], ident[:, :])
                    cp(Vtm[:, tb, 2 * g:2 * g + 2, :], pbt[:, 0:128].rr("p (a b) -> p a b", a=2))
                    yield

            def chainR():
                stt(raw[:, 0:T], rS[:, g, 0:T], cvc(l, "b_rk", g), kt[:, 0:T], ALU.mult, ALU.mult)
                yield
                pbb = next_pb()
                mm(pbb[:, 0:T], b_ones[:, :], raw[:, 0:T])
                tt(bon[:, g, 0:T], pbb[:, 0:T], vS[:, g, 0:T], ALU.mult)
                yield

            def chainB():
                tt(beta[:, 0:T], aT[:, 0:T], kh[:, 0:T], ALU.mult)
                yield
                tt(beta16[:, 0:T], beta[:, 0:T], egi[:, 0:T], ALU.mult)
                yield

            def chainO():
                tt(AlT16[:, g, 0:T], kh[:, 0:T], egm[:, 0:T], ALU.mult)
                yield
                tt(RhT16[:, g, 0:T], rS[:, g, 0:T], eg[:, 0:T], ALU.mult)
                yield
                tt(kt16[:, 0:T], kt[:, 0:T], egi[:, 0:T], ALU.mult)
                yield

            def tail():
                for tb in range(NTB):
                    pbt = next_pb()
                    mm(pbt[:, 0:128], beta16[:, tb * 128:(tb + 1) * 128], ident16[:, :])
                    mm(pbt[:, 128:256], kt16[:, tb * 128:(tb + 1) * 128], ident16[:, :])
                    cp(Btm[:, tb, 2 * g:2 * g + 2, :], pbt[:, 0:128].rr("p (a b) -> p a b", a=2))
                    cp(Ktm[:, tb, 2 * g:2 * g + 2, :], pbt[:, 128:256].rr("p (a b) -> p a b", a=2))
                    yield
            return [chainW(), chainA(), chainK(), chainV()], [chainR(), chainB(), chainO()], tail

        def products(g, Z):
            beta16, kt16 = Z["beta16"], Z["kt16"]
            for tb in range(NTB):
                for j in range(2):
                    jr = slice(64 * j, 64 * j + 64)
                    for p in range(2):
                        c = tb * 2 + p
                        rows = slice(64 * p, 64 * p + 64)
                        cols = slice(c * 64, (c + 1) * 64)
                        A_, B_, K_, R_ = AlT16[jr, g, cols], beta16[jr, cols], kt16[jr, cols], RhT16[jr, g, cols]
                        mm(BJ[j][rows, 0:64], A_, B_)
                        mm(BJ[j][rows, 64:128], B_, A_)
                        mm(BJ[j][rows, 128:192], K_, A_)
                        mm(BJ[j][rows, 192:256], B_, R_)
                        mm(BJ[j][rows, 256:320], K_, R_)
                    tt(PRall[tb][:, :, 2 * g + j, :], BJ[j][:, 0:320].rr("p (k t) -> p k t", k=5), b_m5[:, :, :],
                       ALU.mult)

        for g0 in (0, 2):
            ph1, ph2, tails = [], [], []
            for gi, g in enumerate((g0, g0 + 1)):
                a1, a2, tl = group_chains(g, SETS[gi])
                ph1 += a1; ph2 += a2; tails.append(tl())
            interleave(ph1)
            interleave(ph2)
            interleave(tails)
            for gi, g in enumerate((g0, g0 + 1)):
                products(g, SETS[gi])
        if bcut < 2:
            return
        for tb in range(NTB):
            PRt = PRall[tb]
            P0, Q0, MkT, NbT, NkT = (PRt[:, k, :, :] for k in range(5))
            Wsb, Usb = PRt[:, 2, :, :], PRt[:, 4, :, :]
            for p in range(2):
                rows = slice(64 * p, 64 * p + 64)
                c = tb * 2 + p
                for h in range(8):
                    g, j = h // 2, h % 2
                    mm(PA[p][rows, h * 64:(h + 1) * 64], MkT[rows, h, :], Vtm[rows, tb, h, :])
                    mm(PQ[p][rows, h * 64:(h + 1) * 64], NkT[rows, h, :], Vtm[rows, tb, h, :])
                    mm(PX[p][64 * j:64 * j + 64, g * 64:(g + 1) * 64], Ktm[rows, tb, h, :], Vtm[rows, tb, h, :])
                act(W2[rows, :, :], PA[p][rows, :].rr("p (h e) -> p h e", h=8), AF.Copy)
                act(Y2[rows, :, :], PQ[p][rows, :].rr("p (h e) -> p h e", h=8), AF.Copy)
                tt(K2g[:, p, :, :], PX[p][:, 0:256].rr("p (a b) -> p a b", a=4),
                   gLt[:, :, c:c + 1].bc([128, 4, 64]), ALU.mult)
            tt(Xm[:, :, :], b_i2[:, :, :].bc([128, 8, 64]), Q0, ALU.subtract)
            Pc, Qc = P0, Q0
            nxt = [(Pn, Qn), (P0, Q0)]
            for i in range(1, 6):
                Pd, Qd = nxt[(i - 1) % 2]
                for p in range(2):
                    rows = slice(64 * p, 64 * p + 64)
                    for h in range(8):
                        hc = slice(h * 64, (h + 1) * 64)
                        mm(PA[p][rows, hc], Qc[rows, h, :], Pc[rows, h, :])
                        if i < 5:
                            mm(PQ[p][rows, hc], Pc[rows, h, :], Qc[rows, h, :])
                for p in range(2):
                    rows = slice(64 * p, 64 * p + 64)
                    act(Pd[rows, :, :], PA[p][rows, :].rr("p (h e) -> p h e", h=8), AF.Copy)
                    if i < 5:
                        cp(Qd[rows, :, :], PQ[p][rows, :].rr("p (h e) -> p h e", h=8), force=True)
                Pc, Qc = Pd, Qd
                for p in range(2):
                    rows = slice(64 * p, 64 * p + 64)
                    for h in range(8):
                        mm(PX[p][rows, h * 64:(h + 1) * 64], Pc[rows, h, :], Xm[rows, h, :])
                for p in range(2):
                    rows = slice(64 * p, 64 * p + 64)
                    tt(Xm[rows, :, :], Xm[rows, :, :], PX[p][rows, :].rr("p (h e) -> p h e", h=8), ALU.add)
            if bcut < 3:
                continue
            for p in range(2):
                c = tb * 2 + p
                rows = slice(64 * p, 64 * p + 64)
                cols = slice(c * 64, (c + 1) * 64)
                for j in range(2):
                    jr = slice(64 * j, 64 * j + 64)
                    for g in range(4):
                        mm(BJ[j][rows, g * 64:(g + 1) * 64], AlT16[jr, g, cols], b_S16[jr, g, :])
                        mm(BJ[j][rows, 256 + g * 64:256 + (g + 1) * 64], RhT16[jr, g, cols], b_S16[jr, g, :])
                W24 = W2[rows, :, :].rr("p (g j) e -> p g j e", j=2)
                Ws4 = Wsb[rows, :, :].rr("p (g j) e -> p g j e", j=2)
                Y24 = Y2[rows, :, :].rr("p (g j) e -> p g j e", j=2)
                for j in range(2):
                    tt(Ws4[:, :, j, :], BJ[j][rows, 0:256].rr("p (a b) -> p a b", a=4), W24[:, :, j, :], ALU.add)
                    tt(tmp4[rows, :, j, :], BJ[j][rows, 256:512].rr("p (a b) -> p a b", a=4), Y24[:, :, j, :],
                       ALU.add)
                for h in range(8):
                    mm(PX[p][rows, h * 64:(h + 1) * 64], Xm[rows, h, :], Wsb[rows, h, :])
                act(Usb[rows, :, :], PX[p][rows, :].rr("p (h e) -> p h e", h=8), AF.Copy)
                for h in range(8):
                    g, j = h // 2, h % 2
                    mm(PA[p][rows, h * 64:(h + 1) * 64], NbT[rows, h, :], Usb[rows, h, :])
                    mm(PQ[p][64 * j:64 * j + 64, g * 64:(g + 1) * 64], Btm[rows, tb, h, :], Usb[rows, h, :])
                tt(wkv[rows, tb, :, :], tmp4[rows, :, :, :].rr("p g j e -> p (g j) e"),
                   PA[p][rows, :].rr("p (h e) -> p h e", h=8), ALU.subtract)
                tt(b_S[l][:, :, :], b_S[l][:, :, :], PQ[p][:, 0:256].rr("p (a b) -> p a b", a=4), ALU.subtract)
                tt(b_S[l][:, :, :], b_S[l][:, :, :], gLt[:, :, c:c + 1].bc([128, 4, 64]), ALU.mult)
                tt(b_S[l][:, :, :], b_S[l][:, :, :], K2g[:, p, :, :], ALU.add)
                cp(b_S16[:, :, :], b_S[l][:, :, :])
        ogw, ogb = CM.off["b_gw"][0], CM.off["b_gb"][0]
        head_ln_all(xn, wkv)
        for tb in range(NTB):
            pb = next_pb()
            for g in range(4):
                tr(pb[:, g * 128:(g + 1) * 128], xn[:, tb, 2 * g:2 * g + 2, 0:64], ident[:, :])
            for g in range(4):
                tc_ = slice(tb * 128, (tb + 1) * 128)
                ts(scr[:, 0:128], pb[:, g * 128:(g + 1) * 128], cv[l][:, ogw + g:ogw + g + 1],
                   cv[l][:, ogb + g:ogb + g + 1], ALU.mult, ALU.add)
                tt(scr[:, 0:128], scr[:, 0:128], bon[:, g, tc_], ALU.add)
                tt(yT[:, 4 + g, tc_], scr[:, 0:128], sz[:, g, tc_], ALU.mult)

    def out_proj_residual(l, X):
        dma(gpost[l][:, :], gpost_d[l])
        for si in range(4):
            s = load_wo(l, si)
            for tb in range(NTB):
                for half in range(2):
                    pb = PB[4 + tb * 2 + half]
                    for q in range(4):
                        fc = si * 4 + q
                        mm(pb[:, :], yT[:, fc, tb * 128:(tb + 1) * 128], s[:, q * 2 + half, :],
                           start=(fc == 0), stop=(fc == 15))
        if cut < 4:
            return
        for tb in range(NTB):
            for half in range(2):
                act(scr[:, half * 512:(half + 1) * 512], PB[4 + tb * 2 + half][:, :], AF.Copy)
            if cut < 5:
                continue
            act(scr2[:, :], scr[:, :], AF.Square)
            reduce(sm[:, 24:25], scr2[:, :], ALU.add)
            ts(sm[:, 25:26], sm[:, 24:25], 1.0 / D_MODEL, 1e-6, ALU.mult, ALU.add)
            act(sm[:, 26:27], sm[:, 25:26], AF.Sqrt)
            recip(sm[:, 27:28], sm[:, 26:27])
            if cut < 6:
                continue
            stt(scr2[:, :], scr[:, :], sm[:, 27:28], gpost[l][:, :], ALU.mult, ALU.mult)
            if cut < 7:
                continue
            tt(X[:, tb, :], X[:, tb, :], scr2[:, :], ALU.add)

    xv = x_d.rearrange("(n tb p) d -> n p tb d", tb=NTB, p=128)
    yv = y_d.rearrange("(n tb p) d -> n p tb d", tb=NTB, p=128)
    for ti in range(ntiles):
        X = xt[0]
        dma(X[:, :, :], xv[ti])
        for l in range(nlayers):
            if cut >= 1:
                rmsnorm_pre(l, X)
            memset(yT[:, :, :], 0.0)
            if "C" in mixers and cut >= 2:
                mixer_C(l)
            if "D" in mixers:
                mixer_D(l, ti)
            if "A" in mixers:
                mixer_A(l, ti)
            if "B" in mixers:
                mixer_B(l, ti)
            if cut >= 3:
                out_proj_residual(l, X)
        dma(yv[ti], X[:, :, :])

    P.emit()
    return nc, stack


def kernel(**inputs):
    inputs = {k: np.asarray(v) for k, v in inputs.items()}
    return run(inputs)


def make_in_map(xc, hp, tb):
    m = dict(x=xc, wt=hp["wt"], wo=hp["wo"], cv=hp["cv"], lora=hp["lora"], lru=hp["lru"], gpost=hp["gpost"])
    m.update(tb)
    return m


def run(inputs, ntiles=SEQ // T, nlayers=DEPTH, mixers="ABCD"):
    hp = host_prepare(inputs)
    tb = host_tables()
    nc, stack = build_program(ntiles, nlayers, mixers)
    x = np.ascontiguousarray(inputs["x"], dtype=np.float32)
    in_maps = [make_in_map(x[c], hp, tb) for c in range(NCORES)]
    with stack:
        res = run_bass_kernel_spmd(nc, in_maps, core_ids=list(range(NCORES)))
    out = np.stack([np.asarray(r["y"]) for r in res.results], axis=0)
    return out.astype(np.float32)
```

```python
import contextlib
import numpy as np
import concourse.bass as bass
import concourse.mybir as mybir
from concourse.bass_utils import run_bass_kernel_spmd

F32 = mybir.dt.float32
BF16 = mybir.dt.bfloat16
AF = mybir.ActivationFunctionType
ALU = mybir.AluOpType
AX = mybir.AxisListType

D_MODEL = 1024
SEQ = 4096
BATCH = 4
DEPTH = 2
G = 512
D_IN = 7824
T = 256
NTB = T // 128
NCH = T // 64
NCORES = 4
NSLABBUF = 3
import os as _os
INORDER = tuple(_os.environ.get("INORDER", "pe").split(","))


class Buf:
    def __init__(self, name, tile, psum=False):
        self.name = name
        self.tile = tile
        self.psum = psum
        self.acc = []
        self.dma_sem = None
        self.dma_ops = []

    def __getitem__(self, idx):
        return Ref(self, self.tile[idx])


def _box(ap):
    pat = ap.ap
    pstep = pat[0][0]
    off = int(ap.offset)
    p0 = off // pstep if pstep else 0
    f0 = off % pstep if pstep else off
    ext = 0
    for st, cnt in pat[1:]:
        ext += abs(st) * (cnt - 1)
    return (p0, p0 + pat[0][1], f0, f0 + ext + 1)


class Ref:
    def __init__(self, buf, ap, box=None):
        self.buf = buf
        self.ap = ap
        if box is None:
            box = _box(ap)
            if buf.psum:
                box = ((box[0] // 32) * 32, ((box[1] + 31) // 32) * 32, 0, 512)
        self.box = box

    def bc(self, shape):
        return Ref(self.buf, self.ap.to_broadcast(list(shape)), self.box)

    def __getitem__(self, idx):
        return Ref(self.buf, self.ap[idx])

    def rr(self, pat, **kw):
        return Ref(self.buf, self.ap.rearrange(pat, **kw), self.box)


def _ovl(a, b):
    return a[0] < b[1] and b[0] < a[1] and a[2] < b[3] and b[2] < a[3]


def _cov(a, b):
    return a[0] <= b[0] and a[1] >= b[1] and a[2] <= b[2] and a[3] >= b[3]


class Prog:
    ENG = ("pe", "act", "dve", "pool", "sp")

    def __init__(self, nc, stack):
        self.nc = nc
        self.stack = stack
        self.ops = []
        self.nbuf = 0

    def sb(self, name, shape, dtype=F32):
        t = self.stack.enter_context(self.nc.sbuf_tensor("s_" + name, list(shape), dtype))
        return Buf(name, t)

    def ps(self, name):
        t = self.stack.enter_context(self.nc.psum_tensor("p_" + name, [128, 512], F32))
        return Buf(name, t, psum=True)

    def op(self, eng, fn, outs, ins, dma=False):
        oid = len(self.ops)
        deps = set()
        for r, w in [(x, True) for x in outs] + [(x, False) for x in ins]:
            if r is None or not isinstance(r, Ref):
                continue
            b = r.buf
            for (bx, o2, w2) in b.acc:
                if (w or w2) and _ovl(bx, r.box):
                    deps.add(o2)
        for r, w in [(x, True) for x in outs] + [(x, False) for x in ins]:
            if r is None or not isinstance(r, Ref):
                continue
            b = r.buf
            if w:
                b.acc = [a for a in b.acc if not _cov(r.box, a[0])]
            b.acc.append((r.box, oid, w))
            if len(b.acc) > 48:
                merged = {}
                for (bx, o2, w2) in b.acc:
                    k = (self.ops[o2]["eng"] if o2 < oid else eng, w2)
                    if k in merged:
                        m = merged[k]
                        merged[k] = ((min(m[0][0], bx[0]), max(m[0][1], bx[1]), min(m[0][2], bx[2]),
                                      max(m[0][3], bx[3])), max(m[1], o2), w2)
                    else:
                        merged[k] = (bx, o2, w2)
                b.acc = list(merged.values())
        dbuf = None
        if dma:
            for r in list(outs) + list(ins):
                if isinstance(r, Ref):
                    dbuf = r.buf
            dbuf.dma_ops.append(oid)
        deps.discard(oid)
        self.ops.append(dict(eng=eng, fn=fn, deps=sorted(deps), dma=dma, dbuf=dbuf, sig=False))
        return oid

    def emit(self):
        nc = self.nc
        ops = self.ops
        for o in ops:
            for d in o["deps"]:
                od = ops[d]
                if od["dma"]:
                    continue
                if od["eng"] == o["eng"] and o["eng"] in INORDER and not o["dma"]:
                    continue
                od["sig"] = True
        sems = {e: self.stack.enter_context(nc.semaphore("sem_" + e)) for e in self.ENG}
        cnt = {e: 0 for e in self.ENG}
        for o in ops:
            if o["dma"]:
                b = o["dbuf"]
                if b.dma_sem is None:
                    b.dma_sem = self.stack.enter_context(nc.semaphore("dsem_" + b.name))
            elif o["sig"]:
                cnt[o["eng"]] += 1
                o["signo"] = cnt[o["eng"]]
        per_eng = {e: [] for e in self.ENG}
        for i, o in enumerate(ops):
            per_eng[o["eng"]].append(i)
        import bisect

        def gen(engname, e):
            waited = {}
            for i in per_eng[engname]:
                o = ops[i]
                need = {}
                for d in o["deps"]:
                    od = ops[d]
                    if od["dma"]:
                        b = od["dbuf"]
                        n = bisect.bisect_left(b.dma_ops, i)
                        key = ("d", id(b))
                        if need.get(key, (None, 0))[1] < 16 * n:
                            need[key] = (b.dma_sem, 16 * n)
                    else:
                        if od["eng"] == engname and engname in INORDER and not o["dma"]:
                            continue
                        key = ("e", od["eng"])
                        if need.get(key, (None, 0))[1] < od["signo"]:
                            need[key] = (sems[od["eng"]], od["signo"])
                for key, (sem, val) in need.items():
                    if waited.get(key, 0) >= val:
                        continue
                    e.wait_ge(sem, val)
                    waited[key] = val
                ins = o["fn"](e)
                if o["dma"]:
                    ins.then_inc(o["dbuf"].dma_sem, 16)
                elif o["sig"]:
                    ins.then_inc(sems[engname], 1)

        with nc.Block() as block:
            @block.tensor
            def _(e):
                gen("pe", e)

            @block.scalar
            def _(e):
                gen("act", e)

            @block.vector
            def _(e):
                gen("dve", e)

            @block.gpsimd
            def _(e):
                gen("pool", e)

            @block.sync
            def _(e):
                gen("sp", e)
                seen = set()
                for o in ops:
                    if o["dma"] and id(o["dbuf"]) not in seen:
                        seen.add(id(o["dbuf"]))
                        e.wait_ge(o["dbuf"].dma_sem, 16 * len(o["dbuf"].dma_ops))


def _a(x):
    return x.ap if isinstance(x, Ref) else x


def _fm4(v):
    return np.ascontiguousarray(v.reshape(-1, 128).T)


class CMap:
    def __init__(self):
        self.off = {}
        self.n = 0

    def add(self, name, w):
        self.off[name] = (self.n, w)
        self.n += w


def build_cmap():
    c = CMap()
    c.add("gpre", 8)
    c.add("a_cw", 32); c.add("a_cb", 8); c.add("a_ib", 1); c.add("a_fb", 1); c.add("a_nw", 4)
    c.add("b_mu", 13); c.add("b_w0", 4); c.add("b_a0", 4); c.add("b_kk", 4); c.add("b_ka", 4)
    c.add("b_rk", 4); c.add("b_gw", 4); c.add("b_gb", 4)
    c.add("c_cw", 16); c.add("c_cb", 4); c.add("c_br", 4); c.add("c_bi", 4); c.add("c_lam", 4)
    c.add("d_nw", 4)
    return c


CM = build_cmap()

def _cols(a, b):
    return list(range(a, b))


def build_slabs():
    slabs = []
    def fm(name, start):
        slabs.append((name, _cols(start, start + 512)))
    fm("Aq", 0); fm("Ak", 512); fm("Az", 2048)
    slabs.append(("Ag", _cols(2560, 2576) + [-1] * (512 - 16)))
    fm("Br", 2576); fm("Bk", 3088); fm("Bv", 3600)
    slabs.append(("Bl", _cols(4112, 4240) + [-1] * (512 - 128)))
    fm("Bz", 4240); fm("Cx", 4752); fm("Cz", 5264); fm("Dz", 7312)
    fm("Av", 1024); fm("Ao", 1536); fm("Dq", 5776); fm("Dk", 6288); fm("Dv", 6800)
    return slabs


SLABS = build_slabs()
SLAB_ID = {s[0]: i for i, s in enumerate(SLABS)}
NSLAB = len(SLABS)


def host_prepare(inp):
    f = np.float32
    w_in = inp["w_in"]
    w_in_p = np.concatenate([w_in, np.zeros((DEPTH, D_MODEL, 1), f)], axis=2)
    wt = np.empty((DEPTH, NSLAB, 128, 8, 512), f)
    for si, (name, cols) in enumerate(SLABS):
        blk = w_in_p[:, :, cols]
        wt[:, si] = blk.reshape(DEPTH, 8, 128, 512).transpose(0, 2, 1, 3)
    w_out = inp["w_out"]
    wo = np.ascontiguousarray(w_out.reshape(DEPTH, 4, 4, 128, 1024).transpose(0, 1, 3, 2, 4))
    cv = np.zeros((DEPTH, 128, CM.n), f)
    def put(name, arr):
        o, w = CM.off[name]
        cv[:, :, o:o + w] = arr
    put("gpre", inp["norm_pre"].reshape(DEPTH, 8, 128).transpose(0, 2, 1))
    cw = inp["mlstm_conv_w"]
    put("a_cw", cw.reshape(DEPTH, 4, 8, 128).transpose(0, 3, 1, 2).reshape(DEPTH, 128, 32))
    put("a_cb", inp["mlstm_conv_b"].reshape(DEPTH, 8, 128).transpose(0, 2, 1))
    ib = np.zeros((DEPTH, 128, 1), f); ib[:, 0:8, 0] = inp["mlstm_i_bias"]; put("a_ib", ib)
    fb = np.zeros((DEPTH, 128, 1), f); fb[:, 0:8, 0] = inp["mlstm_f_bias"]; put("a_fb", fb)
    def fm4(x):
        return x.reshape(DEPTH, -1, 128).transpose(0, 2, 1)
    put("a_nw", fm4(inp["mlstm_norm_w"]))
    mu = inp["rwkv_mu"]
    put("b_mu", fm4(mu))
    for k, nm in [("b_w0", "rwkv_w0"), ("b_a0", "rwkv_a0"), ("b_kk", "rwkv_k_k"), ("b_ka", "rwkv_k_a"),
                  ("b_rk", "rwkv_r_k"), ("b_gw", "rwkv_gn_w"), ("b_gb", "rwkv_gn_b"),
                  ("c_cb", "lru_conv_b"), ("c_br", "lru_b_r"), ("c_bi", "lru_b_i"), ("c_lam", "lru_lambda"),
                  ("d_nw", "ret_norm_w")]:
        put(k, fm4(inp[nm]))
    lw = inp["lru_conv_w"]
    put("c_cw", lw.reshape(DEPTH, 4, 4, 128).transpose(0, 3, 1, 2).reshape(DEPTH, 128, 16))
    lora = np.concatenate([inp["rwkv_w_up"], inp["rwkv_a_up"]], axis=1)
    lru = np.zeros((DEPTH, 128, 2, 4, 128), f)
    for which, nm in enumerate(["lru_w_r", "lru_w_i"]):
        w = inp[nm]
        for g in range(4):
            for j in range(2):
                lru[:, 64 * j:64 * j + 64, which, g, 64 * j:64 * j + 64] = w[:, 2 * g + j]
    gpost = np.ascontiguousarray(np.broadcast_to(inp["norm_post"][:, None, :], (DEPTH, 128, D_MODEL)))
    return dict(wt=wt, wo=wo, cv=cv, lora=np.ascontiguousarray(lora), lru=lru, gpost=gpost)


def host_tables():
    f = np.float32
    t = {}
    t["ident"] = np.eye(128, dtype=f)
    hd = 64
    log_g = np.log1p(-np.exp2(-5.0 - np.arange(8, dtype=np.float64)))
    idx = np.arange(64, dtype=np.float64)
    dmat = np.exp(log_g[:, None, None] * np.abs(idx[:, None] - idx[None, :])) * hd ** -0.5
    xi = np.exp(log_g[:, None] * (idx + 1.0))
    zeta = np.exp(log_g[:, None] * (63.0 - idx)) * hd ** -0.5
    gch = np.exp(log_g * 64.0)
    hs2h = [2 * (s % 4) + (s // 4) for s in range(8)]
    dm = np.zeros((128, 8, 64));
    for s in range(8):
        dm[0:64, s] = dmat[hs2h[s]]; dm[64:128, s] = dmat[hs2h[s]]
    t["d_dmat"] = dm.astype(f)
    t["d_xi"] = np.tile(xi.T, (2, 1)).astype(f)
    t["d_zeta"] = np.tile(zeta.T, (2, 1)).astype(f)
    gc = np.zeros((128, 4))
    for g in range(4):
        for j in range(2):
            gc[64 * j:64 * j + 64, g] = gch[2 * g + j]
    t["d_gch"] = gc.astype(f)
    mk = np.zeros((128, 8, 64))
    sidx = np.arange(128) % 64
    mk[:] = np.where(sidx[:, None, None] <= np.arange(64)[None, None, :], 0.0, -30000.0)
    t["a_mask"] = mk.reshape(128, 512).astype(f)
    sel = np.zeros((8, 512 + 4 + 128))
    for hp_ in range(8):
        for hs in range(8):
            if hp_ == hs2h[hs]:
                sel[hp_, hs * 64:(hs + 1) * 64] = 1.0
        sel[hp_, 512 + hp_ // 2] = 1.0
        jj = hp_ % 2
        sel[hp_, 516 + 64 * jj:516 + 64 * jj + 64] = 1.0
    t["a_sel"] = sel.astype(f)
    bo = np.zeros((128, 128)); bo[0:64, 0:64] = 1.0; bo[64:128, 64:128] = 1.0
    t["b_ones"] = bo.astype(f)
    i2 = np.zeros((128, 64)); i2[0:64] = np.eye(64); i2[64:128] = np.eye(64)
    t["b_i2"] = i2.astype(f)
    rr_ = (np.arange(128) % 64)[:, None]; cc_ = np.arange(64)[None, :]
    m5 = np.zeros((128, 5, 64))
    m5[:, 0] = rr_ > cc_; m5[:, 1] = cc_ > rr_; m5[:, 2] = cc_ > rr_; m5[:, 3] = cc_ >= rr_; m5[:, 4] = cc_ >= rr_
    t["b_m5"] = m5.astype(f)
    half = 32
    pos = np.arange(SEQ, dtype=np.float32)
    inv_freq = (np.float32(10000.0) ** (-np.arange(half, dtype=np.float32) / np.float32(half))).astype(np.float32)
    ang = (pos[:, None] * inv_freq[None, :]).astype(np.float32).astype(np.float64)
    t["rope"] = np.concatenate([np.cos(ang), np.sin(ang)], axis=1).astype(f)
    return t


def build_program(ntiles=SEQ // T, nlayers=DEPTH, mixers="ABCD"):
    nc = bass.Bass("TRN2", target_bir_lowering=False)
    stack = contextlib.ExitStack()
    P = Prog(nc, stack)
    dram = {}

    def din(name, shape):
        dram[name] = nc.dram_tensor(name, list(shape), F32, kind="ExternalInput").ap()
        return dram[name]

    x_d = din("x", [SEQ, D_MODEL])
    wt_d = din("wt", [DEPTH, NSLAB, 128, 8, 512])
    wo_d = din("wo", [DEPTH, 4, 128, 4, 1024])
    cv_d = din("cv", [DEPTH, 128, CM.n])
    lora_d = din("lora", [DEPTH, 128, 512])
    lru_d = din("lru", [DEPTH, 128, 2, 4, 128])
    gpost_d = din("gpost", [DEPTH, 128, D_MODEL])
    ident_d = din("ident", [128, 128])
    dmat_d = din("d_dmat", [128, 8, 64])
    dxi_d = din("d_xi", [128, 8])
    dzeta_d = din("d_zeta", [128, 8])
    dgch_d = din("d_gch", [128, 4])
    rope_d = din("rope", [SEQ, 64])
    bones_d = din("b_ones", [128, 128])
    bi2_d = din("b_i2", [128, 64])
    bm5_d = din("b_m5", [128, 5, 64])
    amask_d = din("a_mask", [128, 512])
    asel_d = din("a_sel", [8, 644])
    y_d = nc.dram_tensor("y", [SEQ, D_MODEL], F32, kind="ExternalOutput").ap()

    ident = P.sb("ident", [128, 128])
    cv = [P.sb(f"cv{l}", [128, CM.n]) for l in range(DEPTH)]
    lru1 = P.sb("lru", [128, 2, 4, 128]); lru = [lru1, lru1]
    gpost1 = P.sb("gpost", [128, D_MODEL]); gpost = [gpost1, gpost1]
    lora_up = [P.sb(f"lora_up{l}", [128, 512]) for l in range(DEPTH)]
    b_ones = P.sb("b_ones", [128, 128]); b_i2 = P.sb("b_i2", [128, 1, 64]); b_m5 = P.sb("b_m5", [128, 5, 64])
    b_hist = [P.sb(f"b_hist{l}", [128, 13]) for l in range(DEPTH)]
    b_S = [P.sb(f"b_S{l}", [128, 4, 64]) for l in range(DEPTH)]
    ident16 = P.sb("ident16", [128, 128], BF16)
    AlT16 = P.sb("AlT16", [128, 4, T], BF16); RhT16 = P.sb("RhT16", [128, 4, T], BF16)
    bk16 = P.sb("bk16", [128, 4, T], BF16)
    bm16 = [P.sb(f"bm16_{i}", [128, 8, 64], BF16) for i in range(3)]
    b_S16 = P.sb("b_S16", [128, 4, 64], BF16)
    a_n16 = P.sb("a_n16", [128, 4, 1], BF16); wj16 = P.sb("wj16", [128, NTB, 8], BF16)
    ones16 = P.sb("ones16", [128, 64], BF16); eps12 = P.sb("eps12", [128, 1])
    gLt = P.sb("gLt", [128, 4, NCH]); PRall = [P.sb(f"PR{i}", [128, 5, 8, 64], BF16) for i in range(NTB)]; Y2 = P.sb("Y2", [128, 8, 64])
    K2g = P.sb("K2g", [128, 2, 4, 64])
    ones = P.sb("ones", [128, T]); zeros = P.sb("zeros", [128, T])
    xt = [P.sb(f"xt{i}", [128, NTB, D_MODEL]) for i in range(1)]
    hT = P.sb("hT", [128, 8, T], BF16)
    yT = P.sb("yT", [128, 16, T], BF16)
    slab = [P.sb(f"slab{i}", [128, 8, 512], BF16) for i in range(NSLABBUF)]
    sm = P.sb("small", [128, 64])
    scr = P.sb("scr", [128, D_MODEL])
    scr2 = P.sb("scr2", [128, D_MODEL])
    FS = [P.sb(f"fs{i}", [128, 4, T + 4]) for i in range(10)]
    c_hist = [P.sb(f"c_hist{l}", [128, 4, 3]) for l in range(DEPTH)]
    c_state = [P.sb(f"c_state{l}", [128, 4]) for l in range(DEPTH)]
    c_coef = [P.sb(f"c_coef{l}", [128, 8]) for l in range(DEPTH)]
    PB = [P.ps(f"pb{i}") for i in range(8)]
    TM = [P.sb(f"tm{i}", [128, NTB, 8, 64]) for i in range(6)]
    TM16 = [P.sb(f"tm16_{i}", [128, NTB, 8, 64], BF16) for i in range(3)]
    d_dmat = P.sb("d_dmat", [128, 8, 64]); d_xi = P.sb("d_xi", [128, 4, 2, 1]); d_zeta = P.sb("d_zeta", [128, 8, 1])
    d_gch = P.sb("d_gch", [128, 4, 1]); rope_sb = P.sb("rope_sb", [128, NTB, 1, 64])
    d_R = [P.sb(f"d_R{l}", [128, 4, 64]) for l in range(DEPTH)]
    a_mask = P.sb("a_mask", [128, 512]); a_sel = P.sb("a_sel", [8, 644])
    ga = P.sb("ga", [8, 12, T + 1]); scl = P.sb("scl", [8, NCH]); rhs_sc = P.sb("rhs_sc", [8, NCH, 4])
    scb = P.sb("scb", [128, NCH, 4, 1]); negPx = P.sb("negPx", [8, 2, 8, 64]); ET = P.sb("ET", [128, 8, 64])
    ET1 = P.sb("ET1", [128, 8, 64])
    tmsc = P.sb("tmsc", [128, NTB, 3, 8]); hnum = P.sb("hnum", [128, 8, 64]); vw = P.sb("vw", [128, 8, 64])
    a_hist = [P.sb(f"a_hist{l}", [128, 8, 3]) for l in range(DEPTH)]
    a_C = [P.sb(f"a_C{l}", [128, 4, 64]) for l in range(DEPTH)]
    a_n = [P.sb(f"a_n{l}", [128, 4, 1]) for l in range(DEPTH)]
    a_car = [P.sb(f"a_car{l}", [8, 4]) for l in range(DEPTH)]
    AT = P.sb("AT", [128, 8, 64]); tmp4 = P.sb("tmp4", [128, 4, 2, 64]); lnst = P.sb("lnst", [128, 64]); lnst_b = P.sb("lnst_b", [128, 40]); lnsts = [lnst, lnst_b]
    state = dict(slab_i=0, pb_i=0)

    import os
    cut = int(os.environ.get("KCUT", "9"))
    def dma(out, in_, eng="sp"):
        return P.op(eng, lambda e: e.dma_start(out=_a(out), in_=_a(in_)), [out], [in_], dma=True)

    def mm(out, lhsT, rhs, start=True, stop=True):
        P.op("pe", lambda e: e.matmul(_a(out), _a(lhsT), _a(rhs), start=start, stop=stop),
             [out], [lhsT, rhs] + ([] if start else [out]))

    def tr(out, in_, idn):
        P.op("pe", lambda e: e.transpose(_a(out), _a(in_), _a(idn)), [out], [in_, idn])

    def act(out, in_, func, bias=None, scale=1.0, accum=None, eng="act"):
        kw = {}
        if bias is not None:
            kw["bias"] = _a(bias)
        if accum is not None:
            kw["accum_out"] = _a(accum)
        P.op("act", lambda e: e.activation(_a(out), _a(in_), func, scale=_a(scale), **kw),
             [out, accum], [in_, bias, scale])

    def tt(out, a, b, op, eng="dve"):
        P.op(eng, lambda e: e.tensor_tensor(_a(out), _a(a), _a(b), op), [out], [a, b])

    def ts(out, a, s1, s2, op0, op1=ALU.bypass, eng="dve"):
        P.op(eng, lambda e: e.tensor_scalar(_a(out), _a(a), _a(s1), _a(s2), op0, op1), [out], [a, s1, s2])

    def stt(out, a, s, b, op0, op1):
        P.op("dve", lambda e: e.scalar_tensor_tensor(_a(out), _a(a), _a(s), _a(b), op0, op1), [out], [a, s, b])

    def scan(out, d0, d1, init, op0, op1):
        P.op("dve", lambda e: e.tensor_tensor_scan(_a(out), _a(d0), _a(d1), _a(init), op0, op1),
             [out], [d0, d1, init])

    def cp(out, in_, eng="dve", force=False):
        if isinstance(in_, Ref) and in_.buf.psum and eng == "dve" and not force:
            return act(out, in_, AF.Copy)
        P.op(eng, lambda e: e.tensor_copy(_a(out), _a(in_)), [out], [in_])

    def recip(out, in_):
        P.op("dve", lambda e: e.reciprocal(_a(out), _a(in_)), [out], [in_])

    def memset(out, val, eng="dve"):
        P.op(eng, lambda e: e.memset(_a(out), val), [out], [])

    def reduce(out, in_, op, axis=AX.X):
        P.op("dve", lambda e: e.tensor_reduce(_a(out), _a(in_), axis, op), [out], [in_])

    def interleave(gens):
        gens = list(gens)
        while gens:
            for gz in list(gens):
                try:
                    next(gz)
                except StopIteration:
                    gens.remove(gz)

    def run_chunks(make_gen):
        gens = [make_gen(c) for c in range(NCH)]

        def step(c):
            try:
                next(gens[c])
            except StopIteration:
                pass
        pairs = NCH // 2
        step(0); step(1)
        for k in range(pairs):
            a, b = 2 * k, 2 * k + 1
            step(a); step(b)
            step(a); step(b)
            step(a); step(a)
            if k + 1 < pairs:
                step(2 * k + 2)
            step(b); step(b)
            if k + 1 < pairs:
                step(2 * k + 3)

    def next_pb():
        state["pb_i"] = (state["pb_i"] + 1) % 8
        return PB[(0, 1, 2, 7, 3, 5, 4, 6)[state["pb_i"]]]

    def load_slab(l, name, ncols=512):
        s = slab[state["slab_i"]]
        state["slab_i"] = (state["slab_i"] + 1) % NSLABBUF
        dma(s[:, :, 0:ncols], wt_d[l, SLAB_ID[name], :, :, 0:ncols], eng="pool")
        return s

    def load_wo(l, si):
        s = slab[state["slab_i"]]
        state["slab_i"] = (state["slab_i"] + 1) % NSLABBUF
        dma(s[:, :, :], wo_d[l, si].rearrange("p f n -> p (f n)").rearrange("p (a b) -> p a b", a=8), eng="pool")
        return s

    def cvc(l, name, i=0):
        o, w = CM.off[name]
        return cv[l][:, o + i:o + i + 1]

    def fm_proj(s, gi, out_cols=T):
        pb = next_pb()
        for kc in range(8):
            mm(pb[:, 0:T], s[:, kc, gi * 128:(gi + 1) * 128], hT[:, kc, :], start=(kc == 0), stop=(kc == 7))
        return pb

    dma(ident[:, :], ident_d)
    for l in range(DEPTH):
        dma(cv[l][:, :], cv_d[l])
        dma(lora_up[l][:, :], lora_d[l])
        memset(b_hist[l][:, :], 0.0)
        memset(b_S[l][:, :, :], 0.0)
    dma(d_dmat[:, :, :], dmat_d)
    dma(d_xi[:, :, :, :], dxi_d.rearrange("p (g j o) -> p g j o", g=4, j=2))
    dma(d_zeta[:, :, :], dzeta_d.rearrange("p (h o) -> p h o", o=1))
    dma(d_gch[:, :, :], dgch_d.rearrange("p (g o) -> p g o", o=1))
    dma(a_mask[:, :], amask_d)
    dma(a_sel[:, :], asel_d)
    for l in range(DEPTH):
        memset(d_R[l][:, :, :], 0.0)
        memset(a_hist[l][:, :, :], 0.0)
        memset(a_C[l][:, :, :], 0.0)
        memset(a_n[l][:, :, :], 0.0)
        memset(a_car[l][:, :], 0.0)
        ts(a_car[l][0:8, 2:3], cv[l][0:8, CM.off["a_fb"][0]:CM.off["a_fb"][0] + 1], -1.0, None, ALU.mult)
    dma(b_ones[:, :], bones_d)
    dma(b_i2[:, 0, :], bi2_d)
    dma(b_m5[:, :, :], bm5_d)
    memset(ones[:, :], 1.0)
    cp(ident16[:, :], ident[:, :])
    memset(ones16[:, :], 1.0)
    memset(eps12[:, :], 1e-12)
    memset(zeros[:, :], 0.0)
    for l in range(DEPTH):
        memset(c_hist[l][:, :, :], 0.0)
        memset(c_state[l][:, :], 0.0)
        o, w = CM.off["c_lam"]
        act(sm[:, 0:4], cv[l][:, o:o + 4], AF.Exp, scale=-1.0)
        act(sm[:, 4:8], sm[:, 0:4], AF.Ln, bias=ones[:, 0:1])
        ts(c_coef[l][:, 0:4], sm[:, 4:8], -8.0, None, ALU.mult)
        ts(c_coef[l][:, 4:8], sm[:, 4:8], -16.0, None, ALU.mult)

    def rmsnorm_pre(l, X):
        scrs = [scr, scr2]

        def stream(tb):
            sc = scrs[tb % 2]
            act(sc[:, :], X[:, tb, :], AF.Square)
            yield
            reduce(sm[:, 8 + tb:9 + tb], sc[:, :], ALU.add)
            yield
            ts(sm[:, 12 + tb:13 + tb], sm[:, 8 + tb:9 + tb], 1.0 / D_MODEL, 1e-6, ALU.mult, ALU.add)
            yield
            act(sm[:, 16 + tb:17 + tb], sm[:, 12 + tb:13 + tb], AF.Sqrt)
            yield
            recip(sm[:, 20 + tb:21 + tb], sm[:, 16 + tb:17 + tb])
            yield
            act(sc[:, :], X[:, tb, :], AF.Copy, scale=sm[:, 20 + tb:21 + tb])
            yield
            for half in range(2):
                pb = next_pb()
                for q in range(4):
                    kc = half * 4 + q
                    tr(pb[:, q * 128:(q + 1) * 128], sc[:, kc * 128:(kc + 1) * 128], ident[:, :])
                for q in range(4):
                    kc = half * 4 + q
                    ts(hT[:, kc, tb * 128:(tb + 1) * 128], pb[:, q * 128:(q + 1) * 128],
                       cvc(l, "gpre", kc), None, ALU.mult)
                yield
        interleave([stream(tb) for tb in range(NTB)])

    def mixer_C(l):
        cx, xc, rg, ig, aa, uu, hh = FS[0], FS[1], FS[2], FS[3], FS[4], FS[5], FS[6]
        dma(lru[l][:, :, :, :], lru_d[l])
        wx = load_slab(l, "Cx")

        def cstream(g):
            pb = fm_proj(wx, g)
            cp(cx[:, g, 0:3], c_hist[l][:, g, :])
            act(cx[:, g, 3:3 + T], pb[:, 0:T], AF.Copy)
            yield
            cp(c_hist[l][:, g, :], cx[:, g, T:T + 3])
            o, w = CM.off["c_cw"]
            ts(xc[:, g, 0:T], cx[:, g, 0:T], cv[l][:, o + g:o + g + 1], cvc(l, "c_cb", g), ALU.mult, ALU.add)
            yield
            for j in range(1, 4):
                stt(xc[:, g, 0:T], cx[:, g, j:j + T], cv[l][:, o + 4 * j + g:o + 4 * j + g + 1], xc[:, g, 0:T],
                    ALU.mult, ALU.add)
                yield
        interleave([cstream(g) for g in range(4)])
        for g in range(4):
            pb = next_pb()
            mm(pb[:, 0:T], lru[l][:, 0, g, :], xc[:, g, 0:T])
            act(rg[:, g, 0:T], pb[:, 0:T], AF.Sigmoid, bias=cvc(l, "c_br", g))
            pb = next_pb()
            mm(pb[:, 0:T], lru[l][:, 1, g, :], xc[:, g, 0:T])
            act(ig[:, g, 0:T], pb[:, 0:T], AF.Sigmoid, bias=cvc(l, "c_bi", g))
        for g in range(4):
            act(aa[:, g, 0:T], rg[:, g, 0:T], AF.Exp, scale=c_coef[l][:, g:g + 1])
            act(uu[:, g, 0:T], rg[:, g, 0:T], AF.Exp, scale=c_coef[l][:, 4 + g:5 + g])
        ts(uu[:, :, 0:T], uu[:, :, 0:T], -1.0, 1.0, ALU.mult, ALU.add)
        act(uu[:, :, 0:T], uu[:, :, 0:T], AF.Sqrt)
        tt(ig[:, :, 0:T], ig[:, :, 0:T], xc[:, :, 0:T], ALU.mult)
        tt(uu[:, :, 0:T], uu[:, :, 0:T], ig[:, :, 0:T], ALU.mult)
        for g in range(4):
            scan(hh[:, g, 0:T], aa[:, g, 0:T], uu[:, g, 0:T], c_state[l][:, g:g + 1], ALU.mult, ALU.add)
            cp(c_state[l][:, g:g + 1], hh[:, g, T - 1:T])
        wz = load_slab(l, "Cz")
        for g in range(4):
            pb = fm_proj(wz, g)
            act(rg[:, g, 0:T], pb[:, 0:T], AF.Silu)
            tt(yT[:, 8 + g, :], hh[:, g, 0:T], rg[:, g, 0:T], ALU.mult)

    def tm_proj(sl, tb):
        pb = next_pb()
        for kc in range(8):
            mm(pb[:, :], hT[:, kc, tb * 128:(tb + 1) * 128], sl[:, kc, :], start=(kc == 0), stop=(kc == 7))
        return pb

    def to_fm(dst, src, tb):
        pb = next_pb()
        for g in range(4):
            tr(pb[:, g * 128:(g + 1) * 128], src[:, tb, 2 * g:2 * g + 2, 0:64], ident[:, :])
        cp(dst[:, 0:4, tb * 128:(tb + 1) * 128], pb[:, :].rr("p (a b) -> p a b", a=4))

    def to_tm(dst, src, tb):
        pb = next_pb()
        for g in range(4):
            tr(pb[:, g * 128:(g + 1) * 128], src[:, g, tb * 128:(tb + 1) * 128], ident[:, :])
        cp(dst[:, tb, :, 0:64], pb[:, :].rr("p (h e) -> p h e", h=8))

    def head_ln_stream(dst, src, tb):
        X3 = src[:, tb, :, 0:64]
        D3 = dst[:, tb, :, 0:64]
        st_ = lnsts[tb % 2]
        sq = scr[:, (tb % 2) * 512:(tb % 2) * 512 + 512].rr("p (h e) -> p h e", h=8)
        reduce(st_[:, 0:8], X3, ALU.add)
        yield
        stt(D3, st_[:, 0:8].rr("p (h o) -> p h o", o=1).bc([128, 8, 64]), -1.0 / 64, X3, ALU.mult, ALU.add)
        yield
        tt(sq, D3, D3, ALU.mult)
        yield
        reduce(st_[:, 8:16], sq, ALU.add)
        yield
        ts(st_[:, 16:24], st_[:, 8:16], 1.0 / 64, 1e-5, ALU.mult, ALU.add)
        yield
        act(st_[:, 24:32], st_[:, 16:24], AF.Sqrt)
        yield
        recip(st_[:, 32:40], st_[:, 24:32])
        yield
        tt(D3, D3, st_[:, 32:40].rr("p (h o) -> p h o", o=1).bc([128, 8, 64]), ALU.mult)
        yield

    def head_ln_all(dst, src):
        interleave([head_ln_stream(dst, src, tb) for tb in range(NTB)])

    def mixer_D(l, ti):
        qr, kr, osb, xn = TM[0], TM[1], TM[4], TM[5]
        vv, kz = TM16[1], TM16[2]
        qT, kT, sz = AlT16, RhT16, FS[2]
        AT = bm16[0]
        d_R16 = b_S16
        cp(d_R16[:, :, :], d_R[l][:, :, :])
        S = [PB[3], PB[4]]; O = [PB[5], PB[6]]; ST = [PB[2], PB[7]]
        dma(rope_sb[:, :, :, :], rope_d.rearrange("(n tb p) (o e) -> n p tb o e", tb=NTB, p=128, o=1)[ti])
        for name, dst in (("Dq", qr), ("Dk", kr)):
            sl = load_slab(l, name)

            def rstream(tb, name=name, dst=dst, sl=sl):
                o5 = (tb % 2) * 512
                pb = tm_proj(sl, tb)
                act(scr[:, o5:o5 + 512], pb[:, :], AF.Copy)
                yield
                raw = scr[:, o5:o5 + 512].rr("p (h e) -> p h e", h=8)
                x1, x2 = raw[:, :, 0:32], raw[:, :, 32:64]
                cos = rope_sb[:, tb, :, 0:32].bc([128, 8, 32]); sin = rope_sb[:, tb, :, 32:64].bc([128, 8, 32])
                t1 = scr2[:, o5:o5 + 256].rr("p (h e) -> p h e", h=8)
                t2 = scr2[:, o5 + 256:o5 + 512].rr("p (h e) -> p h e", h=8)
                d1, d2 = dst[:, tb, :, 0:32], dst[:, tb, :, 32:64]
                tt(d1, x1, cos, ALU.mult); tt(t1, x2, sin, ALU.mult)
                yield
                tt(d2, x2, cos, ALU.mult); tt(t2, x1, sin, ALU.mult)
                yield
                tt(d1, d1, t1, ALU.subtract)
                yield
                tt(d2, d2, t2, ALU.add)
                yield
                to_fm(qT if name == "Dq" else kT, dst, tb)
                yield
            interleave([rstream(tb) for tb in range(NTB)])
        sl = load_slab(l, "Dv")
        for tb in range(NTB):
            pb = tm_proj(sl, tb)
            act(vv[:, tb, :, 0:64], pb[:, :].rr("p (h e) -> p h e", h=8), AF.Copy)
            tt(kz[:, tb, :, 0:64], kr[:, tb, :, 0:64], d_zeta[:, :, :].bc([128, 8, 64]), ALU.mult)
        sl = load_slab(l, "Dz")
        for g in range(4):
            pb = fm_proj(sl, g)
            act(sz[:, g, 0:T], pb[:, 0:T], AF.Silu)
        def d_chunk(c):
            tb, p = c // 2, c % 2
            rows = slice(64 * p, 64 * p + 64)
            cols = slice(c * 64, (c + 1) * 64)
            for j in range(2):
                jr = slice(64 * j, 64 * j + 64)
                for g in range(4):
                    mm(S[j][rows, g * 64:(g + 1) * 64], kT[jr, g, cols], qT[jr, g, cols])
            for h in range(8):
                g, j = h // 2, h % 2
                mm(ST[p][64 * j:64 * j + 64, g * 64:(g + 1) * 64], kz[rows, tb, h, 0:64], vv[rows, tb, h, 0:64])
            yield
            for j in range(2):
                tt(AT[rows, 4 * j:4 * j + 4, :], S[j][rows, 0:256].rr("p (a b) -> p a b", a=4),
                   d_dmat[rows, 4 * j:4 * j + 4, :], ALU.mult)
            yield
            for h in range(8):
                g, j = h // 2, h % 2
                mm(O[p][rows, h * 64:(h + 1) * 64], AT[rows, j * 4 + g, :], vv[rows, tb, h, 0:64])
            yield
            for j in range(2):
                jr = slice(64 * j, 64 * j + 64)
                for g in range(4):
                    mm(S[j][rows, 256 + g * 64:256 + (g + 1) * 64], qT[jr, g, cols], d_R16[jr, g, :])
            yield
            tt(d_R[l][:, :, :], d_R[l][:, :, :], d_gch[:, :, :].bc([128, 4, 64]), ALU.mult)
            tt(d_R[l][:, :, :], d_R[l][:, :, :], ST[p][:, 0:256].rr("p (a b) -> p a b", a=4), ALU.add)
            for j in range(2):
                tt(tmp4[rows, :, j, 0:64], S[j][rows, 256:512].rr("p (a b) -> p a b", a=4),
                   d_xi[rows, :, j, :].bc([64, 4, 64]), ALU.mult)
            cp(d_R16[:, :, :], d_R[l][:, :, :])
            tt(osb[rows, tb, :, 0:64], O[p][rows, :].rr("p (h e) -> p h e", h=8),
               tmp4[rows, :, :, 0:64].rr("p g j e -> p (g j) e"), ALU.add)
            yield
        run_chunks(d_chunk)
        head_ln_all(xn, osb)
        for tb in range(NTB):
            pb = next_pb()
            for g in range(4):
                tr(pb[:, g * 128:(g + 1) * 128], xn[:, tb, 2 * g:2 * g + 2, 0:64], ident[:, :])
            for g in range(4):
                stt(yT[:, 12 + g, tb * 128:(tb + 1) * 128], pb[:, g * 128:(g + 1) * 128], cvc(l, "d_nw", g),
                    sz[:, g, tb * 128:(tb + 1) * 128], ALU.mult, ALU.mult)

    def conv_silu(l, sl, g8, dst, gdst, cx, whichhist, cwname, cbname, ngroups_total, silu=True):
        pass

    def mixer_A(l, ti):
        cx, qT32, kT32, sz = FS[0], FS[1], FS[2], FS[3]
        qT, kT = AlT16, RhT16
        so, osb, xn = TM[2], TM[3], TM[4]
        ktm, vv = TM16[0], TM16[1]
        AT, vw = bm16[0], bm16[1]
        a_C16 = b_S16
        cp(a_C16[:, :, :], a_C[l][:, :, :])
        cp(a_n16[:, :, :], a_n[l][:, :, :])
        S = [PB[3], PB[4]]; O = [PB[5], PB[6]]; ST = [PB[2], PB[7]]; JX = [PB[0], PB[1]]
        R_I, R_SP, R_F, R_G, R_P, R_MU, R_PE, R_SI, R_WJ, R_NM, R_NP = range(11)
        ocw = CM.off["a_cw"][0]
        for which, (name, dstT, dst16) in enumerate((("Aq", qT32, qT), ("Ak", kT32, kT))):
            sl = load_slab(l, name)

            def astream(g, which=which, sl=sl, dstT=dstT, dst16=dst16):
                g8 = which * 4 + g
                pb = fm_proj(sl, g)
                cp(cx[:, g, 0:3], a_hist[l][:, g8, :])
                act(cx[:, g, 3:3 + T], pb[:, 0:T], AF.Copy)
                yield
                cp(a_hist[l][:, g8, :], cx[:, g, T:T + 3])
                ts(dstT[:, g, 0:T], cx[:, g, 0:T], cv[l][:, ocw + g8:ocw + g8 + 1], cvc(l, "a_cb", g8),
                   ALU.mult, ALU.add)
                yield
                for j in range(1, 4):
                    stt(dstT[:, g, 0:T], cx[:, g, j:j + T], cv[l][:, ocw + 8 * j + g8:ocw + 8 * j + g8 + 1],
                        dstT[:, g, 0:T], ALU.mult, ALU.add)
                    yield
                act(dst16[:, g, 0:T], dstT[:, g, 0:T], AF.Silu)
                yield
            interleave([astream(g) for g in range(4)])
        sl = load_slab(l, "Az")
        for g in range(4):
            pb = fm_proj(sl, g)
            act(sz[:, g, 0:T], pb[:, 0:T], AF.Silu)
        acut = int(os.environ.get("ACUT", "99"))
        if acut < 1:
            return
        sl = load_slab(l, "Ag", ncols=128)
        pbi = next_pb()
        for kc in range(8):
            mm(pbi[0:8, 0:T], sl[:, kc, 0:8], hT[:, kc, :], start=(kc == 0), stop=(kc == 7))
        act(ga[0:8, R_I, 0:T], pbi[0:8, 0:T], AF.Identity, bias=cv[l][0:8, CM.off["a_ib"][0]:CM.off["a_ib"][0] + 1])
        gcut = int(os.environ.get("GCUT", "99"))
        if gcut < 1:
            return
        pbf = next_pb()
        for kc in range(8):
            mm(pbf[0:8, 0:T], sl[:, kc, 8:16], hT[:, kc, :], start=(kc == 0), stop=(kc == 7))
        act(ga[0:8, R_SP, 0:T], pbf[0:8, 0:T], AF.Exp, bias=a_car[l][0:8, 2:3], scale=-1.0)
        act(ga[0:8, R_SP, 0:T], ga[0:8, R_SP, 0:T], AF.Ln, bias=ones[0:8, 0:1])
        if gcut < 2:
            return
        scan(ga[0:8, R_F, 0:T], ones[0:8, 0:T], ga[0:8, R_SP, 0:T], a_car[l][0:8, 0:1], ALU.mult, ALU.subtract)
        cp(a_car[l][0:8, 0:1], ga[0:8, R_F, T - 1:T])
        tt(ga[0:8, R_G, 0:T], ga[0:8, R_I, 0:T], ga[0:8, R_F, 0:T], ALU.subtract)
        if gcut < 3:
            return
        cp(ga[0:8, R_P, 0:1], a_car[l][0:8, 1:2])
        scan(ga[0:8, R_P, 1:T + 1], ones[0:8, 0:T], ga[0:8, R_G, 0:T], a_car[l][0:8, 1:2], ALU.mult, ALU.max)
        cp(a_car[l][0:8, 1:2], ga[0:8, R_P, T:T + 1])
        if gcut < 4:
            return
        for c in range(NCH):
            cols = slice(c * 64, (c + 1) * 64)
            ts(ga[0:8, R_MU, cols], zeros[0:8, 0:64], ga[0:8, R_P, c * 64:c * 64 + 1], None, ALU.add)
            ts(ga[0:8, R_PE, cols], zeros[0:8, 0:64], ga[0:8, R_P, c * 64 + 64:c * 64 + 65], None, ALU.add)
            tt(scl[0:8, c:c + 1], ga[0:8, R_P, c * 64:c * 64 + 1], ga[0:8, R_P, c * 64 + 64:c * 64 + 65], ALU.subtract)
        if gcut < 5:
            return
        Pv = ga[0:8, R_P, 1:T + 1]
        tt(ga[0:8, R_SI, 0:T], ga[0:8, R_MU, 0:T], Pv, ALU.subtract)
        tt(ga[0:8, R_WJ, 0:T], ga[0:8, R_G, 0:T], ga[0:8, R_PE, 0:T], ALU.subtract)
        stt(ga[0:8, R_NM, 0:T], ga[0:8, R_F, 0:T], -1.0, Pv, ALU.mult, ALU.subtract)
        ts(ga[0:8, R_NP, 0:T], Pv, -1.0, None, ALU.mult)
        if acut < 2:
            return
        for tb in range(NTB):
            pb = next_pb()
            for r, R in enumerate((R_SI, R_WJ, R_NM)):
                tr(pb[:, r * 8:(r + 1) * 8], ga[0:8, R, tb * 128:(tb + 1) * 128], ident[0:8, 0:8])
            act(tmsc[:, tb, :, :], pb[:, 0:24].rr("p (a b) -> p a b", a=3), AF.Exp)
        if acut < 3:
            return
        tt(rhs_sc[0:8, :, :], scl[0:8, :].rr("p (c o) -> p c o", o=1).bc([8, NCH, 4]),
           a_sel[0:8, 512:516].rr("p (o g) -> p o g", o=1).bc([8, NCH, 4]), ALU.mult)
        pb = next_pb()
        mm(pb[:, 0:NCH * 4], a_sel[0:8, 516:644], rhs_sc[0:8, :, :].rr("p c g -> p (c g)"))
        act(scb[:, :, :, :].rr("p c g o -> p (c g o)"), pb[:, 0:NCH * 4], AF.Exp)
        if acut < 4:
            return
        for tb in range(NTB):
            pbt = next_pb()
            for g in range(4):
                mm(pbt[:, g * 128:(g + 1) * 128], kT[:, g, tb * 128:(tb + 1) * 128], ident16[:, :])
            cp(ktm[:, tb, :, :], pbt[:, :].rr("p (h e) -> p h e", h=8))
            cp(wj16[:, tb, :], tmsc[:, tb, 1, :])
        sl = load_slab(l, "Av")
        for tb in range(NTB):
            pb = tm_proj(sl, tb)
            act(vv[:, tb, :, :], pb[:, :].rr("p (h e) -> p h e", h=8), AF.Copy)
        sl = load_slab(l, "Ao")
        for tb in range(NTB):
            pb = tm_proj(sl, tb)
            act(so[:, tb, :, :], pb[:, :].rr("p (h e) -> p h e", h=8), AF.Sigmoid)
        ETs = [ET, ET1]
        for tb in range(NTB):
            E = next_pb()
            mm(E[:, :], ident[:, :], a_mask[:, :], start=True, stop=False)
            mm(E[:, :], ga[0:8, R_G, tb * 128:(tb + 1) * 128], a_sel[0:8, 0:512], start=False, stop=False)
            for p in range(2):
                c = tb * 2 + p
                tt(negPx[0:8, p, :, :], ga[0:8, R_NP:R_NP + 1, c * 64:(c + 1) * 64].bc([8, 8, 64]),
                   a_sel[0:8, 0:512].rr("p (h e) -> p h e", h=8), ALU.mult)
                mm(E[64 * p:64 * p + 64, :], ones[0:8, 0:64], negPx[0:8, p, :, :].rr("p h e -> p (h e)"),
                   start=False, stop=True)
            act(ETs[tb][:, :, :].rr("p h e -> p (h e)"), E[:, :], AF.Exp)

        def a_chunk(c):
            tb, p = c // 2, c % 2
            rows = slice(64 * p, 64 * p + 64)
            cols = slice(c * 64, (c + 1) * 64)
            ETt = ETs[tb]
            sI4 = tmsc[:, tb, 0, :].rr("p (g j o) -> p g j o", g=4, j=2)
            for j in range(2):
                jr = slice(64 * j, 64 * j + 64)
                for g in range(4):
                    mm(S[j][rows, g * 64:(g + 1) * 64], kT[jr, g, cols], qT[jr, g, cols])
            tt(vw[rows, :, :], vv[rows, tb, :, :],
               tmsc[rows, tb, 1, :].rr("p (h o) -> p h o", o=1).bc([64, 8, 64]), ALU.mult)
            yield
            for j in range(2):
                stt(AT[rows, 4 * j:4 * j + 4, :], S[j][rows, 0:256].rr("p (a b) -> p a b", a=4), 0.125,
                    ETt[rows, 4 * j:4 * j + 4, :], ALU.mult, ALU.mult)
            yield
            for h in range(8):
                g, j = h // 2, h % 2
                mm(O[p][rows, h * 64:(h + 1) * 64], AT[rows, j * 4 + g, :], vv[rows, tb, h, :])
                mm(ST[p][rows, 384 + h:385 + h], AT[rows, j * 4 + g, :], ones16[rows, 0:1])
            for h in range(8):
                g, j = h // 2, h % 2
                jr = slice(64 * j, 64 * j + 64)
                mm(ST[p][jr, g * 64:(g + 1) * 64], ktm[rows, tb, h, :], vw[rows, h, :])
                mm(ST[p][jr, 256 + g:257 + g], ktm[rows, tb, h, :], wj16[rows, tb, h:h + 1])
            yield
            for j in range(2):
                jr = slice(64 * j, 64 * j + 64)
                for g in range(4):
                    mm(JX[j][rows, g * 64:(g + 1) * 64], qT[jr, g, cols], a_C16[jr, g, :])
                    mm(JX[j][rows, 256 + g:257 + g], qT[jr, g, cols], a_n16[jr, g, :])
            yield
            for j in range(2):
                tt(tmp4[rows, :, j, :], JX[j][rows, 0:256].rr("p (a b) -> p a b", a=4),
                   sI4[rows, :, j, :].bc([64, 4, 64]), ALU.mult)
                tt(lnst[rows, 40:48].rr("p (g j) -> p g j", j=2)[:, :, j], JX[j][rows, 256:260],
                   tmsc[rows, tb, 0, :].rr("p (g j) -> p g j", j=2)[:, :, j], ALU.mult)
            tt(a_C[l][:, :, :], a_C[l][:, :, :], scb[:, c, :, :].bc([128, 4, 64]), ALU.mult)
            stt(a_C[l][:, :, :], ST[p][:, 0:256].rr("p (a b) -> p a b", a=4), 0.125, a_C[l][:, :, :],
                ALU.mult, ALU.add)
            tt(a_n[l][:, :, :], a_n[l][:, :, :], scb[:, c, :, :], ALU.mult)
            stt(a_n[l][:, :, :], ST[p][:, 256:260].rr("p (a b) -> p a b", b=1), 0.125, a_n[l][:, :, :],
                ALU.mult, ALU.add)
            cp(a_C16[:, :, :], a_C[l][:, :, :])
            cp(a_n16[:, :, :], a_n[l][:, :, :])
            tt(hnum[rows, :, :], O[p][rows, :].rr("p (h e) -> p h e", h=8),
               tmp4[rows, :, :, :].rr("p g j e -> p (g j) e"), ALU.add)
            tt(lnst[rows, 48:56], ST[p][rows, 384:392], lnst[rows, 40:48], ALU.add)
            stt(lnst[rows, 48:56], lnst[rows, 48:56], -1.0, lnst[rows, 48:56], ALU.mult, ALU.max)
            tt(lnst[rows, 48:56], lnst[rows, 48:56], tmsc[rows, tb, 2, :], ALU.max)
            recip(lnst[rows, 56:64], lnst[rows, 48:56])
            tt(hnum[rows, :, :], hnum[rows, :, :],
               lnst[rows, 56:64].rr("p (h o) -> p h o", o=1).bc([64, 8, 64]), ALU.mult)
            tt(osb[rows, tb, :, :], hnum[rows, :, :], so[rows, tb, :, :], ALU.mult)
            yield
        run_chunks(a_chunk)
        head_ln_all(xn, osb)
        for tb in range(NTB):
            pb = next_pb()
            for g in range(4):
                tr(pb[:, g * 128:(g + 1) * 128], xn[:, tb, 2 * g:2 * g + 2, 0:64], ident[:, :])
            for g in range(4):
                stt(yT[:, g, tb * 128:(tb + 1) * 128], pb[:, g * 128:(g + 1) * 128], cvc(l, "a_nw", g),
                    sz[:, g, tb * 128:(tb + 1) * 128], ALU.mult, ALU.mult)

    def mixer_B(l, ti):
        rS, kS, vS, sz = FS[0], FS[1], FS[2], FS[3]
        RhT, AlT, bon = rS, kS, vS
        pools = (FS[4], FS[5], FS[6])
        pools2 = (FS[7], FS[8], FS[9])

        def slot(i):
            return pools[i // 4][:, i % 4, :]

        def slot2(i):
            return pools2[i // 4][:, i % 4, :]
        raw, lora, aT, lw, lc, kt, kh, beta, eg, egi, egm, tA = [slot(i) for i in range(12)]
        SETS = [dict(raw=raw, aT=aT, lw=lw, lc=lc, kt=kt, kh=kh, beta=beta, eg=eg, egi=egi, egm=egm, tA=tA,
                     beta16=bk16[:, 0, :], kt16=bk16[:, 1, :]),
                dict(raw=slot2(0), aT=slot2(2), lw=slot2(3), lc=slot2(4), kt=slot2(5), kh=slot2(6), beta=slot2(7),
                     eg=slot2(8), egi=slot2(9), egm=slot2(10), tA=slot2(11),
                     beta16=bk16[:, 2, :], kt16=bk16[:, 3, :])]
        wkv, xn = TM[3], TM[4]
        Vtm, Btm, Ktm = TM16[0], TM16[1], TM16[2]
        BJ = [PB[3], PB[4]]; PA = [PB[2], PB[7]]; PQ = [PB[5], PB[6]]; PX = [PB[0], PB[1]]
        Pn, Qn, Xm = bm16
        W2 = AT
        beta16, kt16 = bk16[:, 0, :], bk16[:, 1, :]
        omu = CM.off["b_mu"][0]

        def shift_mix(dst, sl, gi, hidx, mucol, raw_, tA_):
            pb = fm_proj(sl, gi)
            cp(raw_[:, 0:1], b_hist[l][:, hidx:hidx + 1])
            act(raw_[:, 1:T + 1], pb[:, 0:T], AF.Copy)
            yield
            cp(b_hist[l][:, hidx:hidx + 1], raw_[:, T:T + 1])
            tt(tA_[:, 0:T], raw_[:, 0:T], raw_[:, 1:T + 1], ALU.subtract)
            yield
            stt(dst, tA_[:, 0:T], cv[l][:, omu + mucol:omu + mucol + 1], raw_[:, 1:T + 1], ALU.mult, ALU.add)
            yield

        cp(b_S16[:, :, :], b_S[l][:, :, :])
        sl = load_slab(l, "Bl", ncols=128)
        interleave([shift_mix(lora[:, 0:T], sl, 0, 12, 12, raw, tA)])
        act(lora[0:64, 0:T], lora[0:64, 0:T], AF.Tanh)
        tslots = [(slot(2 + 2 * i), slot(3 + 2 * i)) for i in range(4)]
        for wi, (name, dst) in enumerate((("Br", rS), ("Bk", kS), ("Bv", vS))):
            sl = load_slab(l, name)
            interleave([shift_mix(dst[:, g, 0:T], sl, g, wi * 4 + g, wi * 4 + g, tslots[g][0], tslots[g][1])
                        for g in range(4)])
        sl = load_slab(l, "Bz")
        for g in range(4):
            pb = fm_proj(sl, g)
            act(sz[:, g, 0:T], pb[:, 0:T], AF.Silu)
        bcut = int(os.environ.get("BCUT", "99"))
        if bcut < 1:
            return
        def group_chains(g, Z):
            gc = slice(g * 128, (g + 1) * 128)
            raw, aT, lw, lc, kt, kh, beta = Z["raw"], Z["aT"], Z["lw"], Z["lc"], Z["kt"], Z["kh"], Z["beta"]
            eg, egi, egm, tA, beta16, kt16 = Z["eg"], Z["egi"], Z["egm"], Z["tA"], Z["beta16"], Z["kt16"]

            def chainW():
                pb = next_pb()
                mm(pb[:, 0:T], lora_up[l][0:64, gc], lora[0:64, 0:T])
                act(lw[:, 0:T], pb[:, 0:T], AF.Sigmoid, bias=cvc(l, "b_w0", g))
                yield
                for c in range(NCH):
                    cols = slice(c * 64, (c + 1) * 64)
                    scan(lc[:, cols], ones[:, 0:64], lw[:, cols], 0.0, ALU.mult, ALU.add)
                    yield
                act(eg[:, 0:T], lc[:, 0:T], AF.Exp, scale=-0.606531)
                act(egi[:, 0:T], lc[:, 0:T], AF.Exp, scale=0.606531)
                tt(tA[:, 0:T], lc[:, 0:T], lw[:, 0:T], ALU.subtract)
                yield
                act(egm[:, 0:T], tA[:, 0:T], AF.Exp, scale=-0.606531)
                cp(gLt[:, g, :], eg[:, 63:T:64])
                yield

            def chainA():
                pb = next_pb()
                mm(pb[:, 0:T], lora_up[l][64:128, gc], lora[64:128, 0:T])
                act(aT[:, 0:T], pb[:, 0:T], AF.Sigmoid, bias=cvc(l, "b_a0", g))
                yield
                ts(beta[:, 0:T], aT[:, 0:T], -1.0, cvc(l, "b_ka", g), ALU.add, ALU.mult)
                yield
                stt(kt[:, 0:T], beta[:, 0:T], 1.0, kS[:, g, 0:T], ALU.add, ALU.mult)
                yield

            def chainK():
                ts(kh[:, 0:T], kS[:, g, 0:T], cvc(l, "b_kk", g), None, ALU.mult)
                yield
                tt(raw[:, 0:T], kh[:, 0:T], kh[:, 0:T], ALU.mult)
                yield
                pb = next_pb()
                mm(pb[:, 0:T], b_ones[:, :], raw[:, 0:T])
                act(raw[:, 0:T], pb[:, 0:T], AF.Sqrt, bias=eps12[:, 0:1])
                yield
                recip(raw[:, 0:T], raw[:, 0:T])
                yield
                tt(kh[:, 0:T], kh[:, 0:T], raw[:, 0:T], ALU.mult)
                yield

            def chainV():
                for tb in range(NTB):
                    pbt = next_pb()
                    tr(pbt[:, 0:128], vS[:, g, tb * 128:(tb + 1) * 128], ident[:, :])
                    cp(Vtm[:, tb, 2 * g:2 * g + 2, :], pbt[:, 0:128].rr("p (a b) -> p a b", a=2))
                    yield

            def chainR():
                stt(raw[:, 0:T], rS[:, g, 0:T], cvc(l, "b_rk", g), kt[:, 0:T], ALU.mult, ALU.mult)
                yield
                pbb = next_pb()
                mm(pbb[:, 0:T], b_ones[:, :], raw[:, 0:T])
                tt(bon[:, g, 0:T], pbb[:, 0:T], vS[:, g, 0:T], ALU.mult)
                yield

            def chainB():
                tt(beta[:, 0:T], aT[:, 0:T], kh[:, 0:T], ALU.mult)
                yield
                tt(beta16[:, 0:T], beta[:, 0:T], egi[:, 0:T], ALU.mult)
                yield

            def chainO():
                tt(AlT16[:, g, 0:T], kh[:, 0:T], egm[:, 0:T], ALU.mult)
                yield
                tt(RhT16[:, g, 0:T], rS[:, g, 0:T], eg[:, 0:T], ALU.mult)
                yield
                tt(kt16[:, 0:T], kt[:, 0:T], egi[:, 0:T], ALU.mult)
                yield

            def tail():
                for tb in range(NTB):
                    pbt = next_pb()
                    pbt2 = next_pb()
                    mm(pbt[:, 0:128], beta16[:, tb * 128:(tb + 1) * 128], ident16[:, :])
                    mm(pbt2[:, 0:128], kt16[:, tb * 128:(tb + 1) * 128], ident16[:, :])
                    cp(Btm[:, tb, 2 * g:2 * g + 2, :], pbt[:, 0:128].rr("p (a b) -> p a b", a=2))
                    cp(Ktm[:, tb, 2 * g:2 * g + 2, :], pbt2[:, 0:128].rr("p (a b) -> p a b", a=2), force=True)
                    yield
            return [chainW(), chainA(), chainK(), chainV()], [chainR(), chainB(), chainO()], tail

        def products(g, Z):
            beta16, kt16 = Z["beta16"], Z["kt16"]
            for tb in range(NTB):
                for j in range(2):
                    jr = slice(64 * j, 64 * j + 64)
                    for p in range(2):
                        c = tb * 2 + p
                        rows = slice(64 * p, 64 * p + 64)
                        cols = slice(c * 64, (c + 1) * 64)
                        A_, B_, K_, R_ = AlT16[jr, g, cols], beta16[jr, cols], kt16[jr, cols], RhT16[jr, g, cols]
                        mm(BJ[j][rows, 0:64], A_, B_)
                        mm(BJ[j][rows, 64:128], B_, A_)
                        mm(BJ[j][rows, 128:192], K_, A_)
                        mm(BJ[j][rows, 192:256], B_, R_)
                        mm(BJ[j][rows, 256:320], K_, R_)
                    tt(PRall[tb][:, :, 2 * g + j, :], BJ[j][:, 0:320].rr("p (k t) -> p k t", k=5), b_m5[:, :, :],
                       ALU.mult)

        for g0 in (0, 2):
            ph1, ph2, tails = [], [], []
            for gi, g in enumerate((g0, g0 + 1)):
                a1, a2, tl = group_chains(g, SETS[gi])
                ph1 += a1; ph2 += a2; tails.append(tl())
            interleave(ph1)
            interleave(ph2)
            interleave(tails)
            for gi, g in enumerate((g0, g0 + 1)):
                products(g, SETS[gi])
        if bcut < 2:
            return
        for tb in range(NTB):
            PRt = PRall[tb]
            P0, Q0, MkT, NbT, NkT = (PRt[:, k, :, :] for k in range(5))
            Wsb, Usb = PRt[:, 2, :, :], PRt[:, 4, :, :]
            for p in range(2):
                rows = slice(64 * p, 64 * p + 64)
                c = tb * 2 + p
                for h in range(8):
                    g, j = h // 2, h % 2
                    mm(PA[p][rows, h * 64:(h + 1) * 64], MkT[rows, h, :], Vtm[rows, tb, h, :])
                    mm(PQ[p][rows, h * 64:(h + 1) * 64], NkT[rows, h, :], Vtm[rows, tb, h, :])
                    mm(PX[p][64 * j:64 * j + 64, g * 64:(g + 1) * 64], Ktm[rows, tb, h, :], Vtm[rows, tb, h, :])
                act(W2[rows, :, :], PA[p][rows, :].rr("p (h e) -> p h e", h=8), AF.Copy)
                cp(Y2[rows, :, :], PQ[p][rows, :].rr("p (h e) -> p h e", h=8), force=True)
                tt(K2g[:, p, :, :], PX[p][:, 0:256].rr("p (a b) -> p a b", a=4),
                   gLt[:, :, c:c + 1].bc([128, 4, 64]), ALU.mult)
            tt(Xm[:, :, :], b_i2[:, :, :].bc([128, 8, 64]), Q0, ALU.subtract)
            Pc, Qc = P0, Q0
            nxt = [(Pn, Qn), (P0, Q0)]
            for i in range(1, 6):
                Pd, Qd = nxt[(i - 1) % 2]
                for p in range(2):
                    rows = slice(64 * p, 64 * p + 64)
                    for h in range(8):
                        hc = slice(h * 64, (h + 1) * 64)
                        mm(PA[p][rows, hc], Qc[rows, h, :], Pc[rows, h, :])
                        if i < 5:
                            mm(PQ[p][rows, hc], Pc[rows, h, :], Qc[rows, h, :])
                for p in range(2):
                    rows = slice(64 * p, 64 * p + 64)
                    act(Pd[rows, :, :], PA[p][rows, :].rr("p (h e) -> p h e", h=8), AF.Copy)
                    if i < 5:
                        cp(Qd[rows, :, :], PQ[p][rows, :].rr("p (h e) -> p h e", h=8), force=True)
                Pc, Qc = Pd, Qd
                for p in range(2):
                    rows = slice(64 * p, 64 * p + 64)
                    for h in range(8):
                        mm(PX[p][rows, h * 64:(h + 1) * 64], Pc[rows, h, :], Xm[rows, h, :])
                for p in range(2):
                    rows = slice(64 * p, 64 * p + 64)
                    tt(Xm[rows, :, :], Xm[rows, :, :], PX[p][rows, :].rr("p (h e) -> p h e", h=8), ALU.add)
            if bcut < 3:
                continue
            for p in range(2):
                c = tb * 2 + p
                rows = slice(64 * p, 64 * p + 64)
                cols = slice(c * 64, (c + 1) * 64)
                for j in range(2):
                    jr = slice(64 * j, 64 * j + 64)
                    for g in range(4):
                        mm(BJ[j][rows, g * 64:(g + 1) * 64], AlT16[jr, g, cols], b_S16[jr, g, :])
                        mm(BJ[j][rows, 256 + g * 64:256 + (g + 1) * 64], RhT16[jr, g, cols], b_S16[jr, g, :])
                W24 = W2[rows, :, :].rr("p (g j) e -> p g j e", j=2)
                Ws4 = Wsb[rows, :, :].rr("p (g j) e -> p g j e", j=2)
                Y24 = Y2[rows, :, :].rr("p (g j) e -> p g j e", j=2)
                for j in range(2):
                    tt(Ws4[:, :, j, :], BJ[j][rows, 0:256].rr("p (a b) -> p a b", a=4), W24[:, :, j, :], ALU.add)
                    tt(tmp4[rows, :, j, :], BJ[j][rows, 256:512].rr("p (a b) -> p a b", a=4), Y24[:, :, j, :],
                       ALU.add)
                for h in range(8):
                    mm(PX[p][rows, h * 64:(h + 1) * 64], Xm[rows, h, :], Wsb[rows, h, :])
                act(Usb[rows, :, :], PX[p][rows, :].rr("p (h e) -> p h e", h=8), AF.Copy)
                for h in range(8):
                    g, j = h // 2, h % 2
                    mm(PA[p][rows, h * 64:(h + 1) * 64], NbT[rows, h, :], Usb[rows, h, :])
                    mm(PQ[p][64 * j:64 * j + 64, g * 64:(g + 1) * 64], Btm[rows, tb, h, :], Usb[rows, h, :])
                tt(wkv[rows, tb, :, :], tmp4[rows, :, :, :].rr("p g j e -> p (g j) e"),
                   PA[p][rows, :].rr("p (h e) -> p h e", h=8), ALU.subtract)
                tt(b_S[l][:, :, :], b_S[l][:, :, :], PQ[p][:, 0:256].rr("p (a b) -> p a b", a=4), ALU.subtract)
                tt(b_S[l][:, :, :], b_S[l][:, :, :], gLt[:, :, c:c + 1].bc([128, 4, 64]), ALU.mult)
                tt(b_S[l][:, :, :], b_S[l][:, :, :], K2g[:, p, :, :], ALU.add)
                cp(b_S16[:, :, :], b_S[l][:, :, :])
        ogw, ogb = CM.off["b_gw"][0], CM.off["b_gb"][0]
        head_ln_all(xn, wkv)
        for tb in range(NTB):
            pb = next_pb()
            for g in range(4):
                tr(pb[:, g * 128:(g + 1) * 128], xn[:, tb, 2 * g:2 * g + 2, 0:64], ident[:, :])
            for g in range(4):
                tc_ = slice(tb * 128, (tb + 1) * 128)
                ts(scr[:, 0:128], pb[:, g * 128:(g + 1) * 128], cv[l][:, ogw + g:ogw + g + 1],
                   cv[l][:, ogb + g:ogb + g + 1], ALU.mult, ALU.add)
                tt(scr[:, 0:128], scr[:, 0:128], bon[:, g, tc_], ALU.add)
                tt(yT[:, 4 + g, tc_], scr[:, 0:128], sz[:, g, tc_], ALU.mult)

    def out_proj_residual(l, X):
        dma(gpost[l][:, :], gpost_d[l])
        for si in range(4):
            s = load_wo(l, si)
            for tb in range(NTB):
                for half in range(2):
                    pb = PB[4 + tb * 2 + half]
                    for q in range(4):
                        fc = si * 4 + q
                        mm(pb[:, :], yT[:, fc, tb * 128:(tb + 1) * 128], s[:, q * 2 + half, :],
                           start=(fc == 0), stop=(fc == 15))
        if cut < 4:
            return
        for tb in range(NTB):
            for half in range(2):
                act(scr[:, half * 512:(half + 1) * 512], PB[4 + tb * 2 + half][:, :], AF.Copy)
            if cut < 5:
                continue
            act(scr2[:, :], scr[:, :], AF.Square)
            reduce(sm[:, 24:25], scr2[:, :], ALU.add)
            ts(sm[:, 25:26], sm[:, 24:25], 1.0 / D_MODEL, 1e-6, ALU.mult, ALU.add)
            act(sm[:, 26:27], sm[:, 25:26], AF.Sqrt)
            recip(sm[:, 27:28], sm[:, 26:27])
            if cut < 6:
                continue
            stt(scr2[:, :], scr[:, :], sm[:, 27:28], gpost[l][:, :], ALU.mult, ALU.mult)
            if cut < 7:
                continue
            tt(X[:, tb, :], X[:, tb, :], scr2[:, :], ALU.add)

    xv = x_d.rearrange("(n tb p) d -> n p tb d", tb=NTB, p=128)
    yv = y_d.rearrange("(n tb p) d -> n p tb d", tb=NTB, p=128)
    for ti in range(ntiles):
        X = xt[0]
        dma(X[:, :, :], xv[ti])
        for l in range(nlayers):
            if cut >= 1:
                rmsnorm_pre(l, X)
            memset(yT[:, :, :], 0.0)
            if "C" in mixers and cut >= 2:
                mixer_C(l)
            if "D" in mixers:
                mixer_D(l, ti)
            if "A" in mixers:
                mixer_A(l, ti)
            if "B" in mixers:
                mixer_B(l, ti)
            if cut >= 3:
                out_proj_residual(l, X)
        dma(yv[ti], X[:, :, :])

    P.emit()
    return nc, stack


def kernel(**inputs):
    inputs = {k: np.asarray(v) for k, v in inputs.items()}
    return run(inputs)


def make_in_map(xc, hp, tb):
    m = dict(x=xc, wt=hp["wt"], wo=hp["wo"], cv=hp["cv"], lora=hp["lora"], lru=hp["lru"], gpost=hp["gpost"])
    m.update(tb)
    return m


def run(inputs, ntiles=SEQ // T, nlayers=DEPTH, mixers="ABCD"):
    hp = host_prepare(inputs)
    tb = host_tables()
    nc, stack = build_program(ntiles, nlayers, mixers)
    x = np.ascontiguousarray(inputs["x"], dtype=np.float32)
    in_maps = [make_in_map(x[c], hp, tb) for c in range(NCORES)]
    with stack:
        res = run_bass_kernel_spmd(nc, in_maps, core_ids=list(range(NCORES)))
    out = np.stack([np.asarray(r["y"]) for r in res.results], axis=0)
    return out.astype(np.float32)
```
